# Optimizing a Trainium2 kernel written in Bass

```python
import math
import jax
import jax.numpy as jnp
from jax import lax
import numpy as np

D_MODEL = 1024
BATCH = 8
SEQ = 2048
DEPTH = 2
DEC_BATCH = 128
DEC_SEQ = 1
PAST_LEN = 2048
PAGE_SIZE = 128

HEAD_DIM = 64
D_SB = D_MODEL // 2
D_RWKV = D_MODEL // 2
N_HEADS_SB = D_SB // HEAD_DIM
N_HEADS_RWKV = D_RWKV // HEAD_DIM
RANK_DECAY = D_MODEL // 16
RANK_ICLR = D_MODEL // 16
RANK_GATE = D_MODEL // 8
SB_COLS = 3 * D_SB
RWKV_SPLITS = [D_RWKV, 2 * D_RWKV, 3 * D_RWKV, 3 * D_RWKV + RANK_DECAY, 3 * D_RWKV + RANK_DECAY + RANK_ICLR]
RWKV_COLS = 3 * D_RWKV + RANK_DECAY + RANK_ICLR + RANK_GATE
IN_COLS = SB_COLS + RWKV_COLS
Q_BLOCK = 128
SB_BIAS_LO = -7.0
SB_BIAS_HI = -4.0
S5_GROUP = 16
S5_GROUPS = D_MODEL // S5_GROUP
S5_STATE = 64
D_FF = ((8 * D_MODEL // 3 + 127) // 128) * 128
N_EVEN = (DEPTH + 1) // 2
N_ODD = DEPTH // 2
DN_ALPHA = (2.0 * DEPTH) ** 0.25
DN_BETA = (8.0 * DEPTH) ** -0.25
LN_EPS = 1e-5
GN_EPS = 64e-5

kernel_name = 'hybrid_sb_rwkv7_s5_macaron_deepnorm_step'


def layer_norm(x, g, b):
    xf = x.astype(jnp.float32)
    mu = jnp.mean(xf, axis=-1, keepdims=True)
    var = jnp.mean(jnp.square(xf - mu), axis=-1, keepdims=True)
    return ((xf - mu) * lax.rsqrt(var + LN_EPS) * g + b).astype(x.dtype)


def swiglu(x, wg, wu, wd):
    return (jax.nn.silu(x @ wg) * (x @ wu)) @ wd


def heads(t):
    return t.reshape(t.shape[:-1] + (-1, HEAD_DIM))


def stick_breaking(q, k, v, bias, q_offset):
    tq = q.shape[1]
    scale = HEAD_DIM ** -0.5
    bias = bias.astype(jnp.float32)[None, :, None, None]
    outs = []
    for qs in range(0, tq, Q_BLOCK):
        qe = min(qs + Q_BLOCK, tq)
        kend = q_offset + qe
        z = jnp.einsum('bqhd,bkhd->bhqk', q[:, qs:qe], k[:, :kend],
                       preferred_element_type=jnp.float32) * scale + bias
        q_pos = q_offset + jnp.arange(qs, qe)
        k_pos = jnp.arange(kend)
        mask = k_pos[None, :] < q_pos[:, None]
        log1m = jnp.where(mask, jax.nn.log_sigmoid(-z), 0.0)
        later = lax.cumsum(log1m, axis=3, reverse=True) - log1m
        w = jnp.where(mask, jnp.exp(jax.nn.log_sigmoid(z) + later), 0.0)
        outs.append(jnp.einsum('bhqk,bkhd->bqhd', w.astype(v.dtype), v[:, :kend]))
    return jnp.concatenate(outs, axis=1)


def rwkv7_recurrence(r, w, k, v, kk, a, s0):
    def step(s, inp):
        r_t, w_t, k_t, v_t, kk_t, a_t = inp
        sa = jnp.einsum('bhij,bhj->bhi', s, kk_t)
        s = (s * w_t[:, :, None, :] - sa[..., None] * (kk_t * a_t)[:, :, None, :]
             + v_t[..., None] * k_t[:, :, None, :])
        return s, jnp.einsum('bhij,bhj->bhi', s, r_t)
    xs = tuple(jnp.swapaxes(t, 0, 1) for t in (r, w, k, v, kk, a))
    s_final, y = lax.scan(step, s0, xs)
    return jnp.swapaxes(y, 0, 1), s_final


def even_mixer(h, past_k, past_v, wkv0, shift0, prm, i):
    f32 = jnp.float32
    bsz, t, _ = h.shape
    proj = h @ prm['w_in_even'][i]
    q, k, v = (heads(c) for c in jnp.split(proj[..., :SB_COLS], 3, axis=-1))
    if past_k is None:
        o_sb = stick_breaking(q, k, v, prm['sb_bias'][i], 0)
    else:
        k_all = jnp.concatenate([past_k.astype(k.dtype), k], axis=1)
        v_all = jnp.concatenate([past_v.astype(v.dtype), v], axis=1)
        o_sb = stick_breaking(q, k_all, v_all, prm['sb_bias'][i], past_k.shape[1])
    o_sb = o_sb.reshape(bsz, t, D_SB).astype(h.dtype)

    pb = proj[..., SB_COLS:]
    prev = jnp.concatenate([shift0[:, None, :].astype(pb.dtype), pb[:, :-1]], axis=1)
    pm = pb + prm['mu_shift'][i] * (prev - pb)
    new_shift = pb[:, -1]
    r, kr, vr, wd, ad, gd = jnp.split(pm, RWKV_SPLITS, axis=-1)
    z_w = (prm['w0'][i] + jnp.tanh(wd) @ prm['w_w2'][i]).astype(f32)
    decay = jnp.exp(-jnp.exp(-jax.nn.softplus(-z_w) - 0.5))
    iclr = jax.nn.sigmoid((prm['a0'][i] + ad @ prm['w_a2'][i]).astype(f32))
    gate = (jax.nn.sigmoid(gd) @ prm['w_g2'][i]).astype(f32)
    kk = heads(kr.astype(f32) * prm['k_k'][i].astype(f32))
    kk = kk * lax.rsqrt(jnp.maximum(jnp.sum(kk * kk, axis=-1, keepdims=True), 1e-24))
    kf = heads(kr.astype(f32) * (1.0 + (iclr - 1.0) * prm['k_a'][i].astype(f32)))
    rf = heads(r.astype(f32))
    vf = heads(vr.astype(f32))
    y, wkv_final = rwkv7_recurrence(rf, heads(decay), kf, vf, kk, heads(iclr), wkv0.astype(f32))
    mu = jnp.mean(y, axis=-1, keepdims=True)
    var = jnp.mean(jnp.square(y - mu), axis=-1, keepdims=True)
    yn = ((y - mu) * lax.rsqrt(var + GN_EPS)).reshape(bsz, t, D_RWKV)
    yn = yn * prm['gn_g'][i].astype(f32) + prm['gn_b'][i].astype(f32)
    bonus = jnp.sum(rf * kf * prm['r_k'][i].astype(f32), axis=-1, keepdims=True) * vf
    o_rwkv = ((yn + bonus.reshape(bsz, t, D_RWKV)) * gate).astype(h.dtype)

    out = jnp.concatenate([o_sb, o_rwkv], axis=-1) @ prm['w_out_even'][i]
    return out, k, v, wkv_final, new_shift


def s5_mixer(h, s0, prm, i):
    f32 = jnp.float32
    bsz, t, _ = h.shape
    u = h.astype(f32).reshape(bsz, t, S5_GROUPS, S5_GROUP)
    dt = jnp.exp(prm['log_dt'][i].astype(f32))[:, None]
    lr = prm['lam_re'][i].astype(f32)
    li = prm['lam_im'][i].astype(f32)
    mag = jnp.exp(lr * dt)
    ar = mag * jnp.cos(li * dt)
    ai = mag * jnp.sin(li * dt)
    den = lr * lr + li * li
    cr = ((ar - 1.0) * lr + ai * li) / den
    ci = (ai * lr - (ar - 1.0) * li) / den
    br = prm['b_re'][i].astype(f32)
    bi = prm['b_im'][i].astype(f32)
    bbr = cr[..., None] * br - ci[..., None] * bi
    bbi = cr[..., None] * bi + ci[..., None] * br
    bu_r = jnp.einsum('gpc,btgc->btgp', bbr, u)
    bu_i = jnp.einsum('gpc,btgc->btgp', bbi, u)
    a_r = jnp.broadcast_to(ar, (1, t, S5_GROUPS, S5_STATE))
    a_i = jnp.broadcast_to(ai, (1, t, S5_GROUPS, S5_STATE))

    def combine(e1, e2):
        a1r, a1i, b1r, b1i = e1
        a2r, a2i, b2r, b2i = e2
        return (a1r * a2r - a1i * a2i, a1r * a2i + a1i * a2r,
                a2r * b1r - a2i * b1i + b2r, a2r * b1i + a2i * b1r + b2i)

    pr, pi, sr, si = lax.associative_scan(combine, (a_r, a_i, bu_r, bu_i), axis=1)
    s0r = s0[..., 0].astype(f32)[:, None]
    s0i = s0[..., 1].astype(f32)[:, None]
    sr = sr + pr * s0r - pi * s0i
    si = si + pr * s0i + pi * s0r
    y = (jnp.einsum('gcp,btgp->btgc', prm['c_re'][i].astype(f32), sr)
         - jnp.einsum('gcp,btgp->btgc', prm['c_im'][i].astype(f32), si)
         + prm['d_skip'][i].astype(f32).reshape(S5_GROUPS, S5_GROUP) * u)
    zg = jax.nn.gelu(y.reshape(bsz, t, D_MODEL)).astype(h.dtype)
    out = (zg @ prm['w_glu_out'][i]) * jax.nn.sigmoid(zg @ prm['w_glu_gate'][i])
    s_new = jnp.stack([sr[:, -1], si[:, -1]], axis=-1)
    return out, s_new


def trunk(x, cache_k, cache_v, page_table, wkv0, shift0, s50, prm):
    out_k, out_v, out_wkv, out_shift, out_s5 = [], [], [], [], []
    for layer in range(DEPTH):
        i = layer // 2
        g = prm['ln_g'][layer]
        b = prm['ln_b'][layer]
        f = swiglu(x, prm['ffn1_wg'][layer], prm['ffn1_wu'][layer], prm['ffn1_wd'][layer])
        x = layer_norm(DN_ALPHA * x + 0.5 * f, g[0], b[0])
        if layer % 2 == 0:
            if cache_k is None:
                past_k = None
                past_v = None
            else:
                nb = page_table.shape[0]
                past_k = cache_k[i][page_table].reshape(nb, -1, N_HEADS_SB, HEAD_DIM)
                past_v = cache_v[i][page_table].reshape(nb, -1, N_HEADS_SB, HEAD_DIM)
            mix, k, v, wkv, shift = even_mixer(x, past_k, past_v, wkv0[i], shift0[i], prm, i)
            out_k.append(k)
            out_v.append(v)
            out_wkv.append(wkv)
            out_shift.append(shift)
        else:
            mix, s5 = s5_mixer(x, s50[i], prm, i)
            out_s5.append(s5)
        x = layer_norm(DN_ALPHA * x + mix, g[1], b[1])
        f = swiglu(x, prm['ffn2_wg'][layer], prm['ffn2_wu'][layer], prm['ffn2_wd'][layer])
        x = layer_norm(DN_ALPHA * x + 0.5 * f, g[2], b[2])
    return (x, jnp.stack(out_k), jnp.stack(out_v), jnp.stack(out_wkv),
            jnp.stack(out_shift), jnp.stack(out_s5))


def setup_inputs(seed: int = 0) -> dict:
    key = jax.random.key(seed)
    ks = iter(jax.random.split(key, 64))
    f32 = jnp.float32

    def nrm(shape, s=1.0):
        return s * jax.random.normal(next(ks), shape, f32)

    def uni(shape, lo, hi):
        return jax.random.uniform(next(ks), shape, f32, lo, hi)

    n_pages = PAST_LEN // PAGE_SIZE
    n_pool = (5 * DEC_BATCH * n_pages + 3) // 4
    x_prompt = nrm((BATCH, SEQ, D_MODEL))
    x_sample = nrm((DEC_BATCH, DEC_SEQ, D_MODEL))
    cache_k_sb = nrm((N_EVEN, n_pool, PAGE_SIZE, N_HEADS_SB, HEAD_DIM))
    cache_v_sb = nrm((N_EVEN, n_pool, PAGE_SIZE, N_HEADS_SB, HEAD_DIM))
    page_table = jax.random.permutation(next(ks), n_pool)[:DEC_BATCH * n_pages]
    page_table = page_table.reshape(DEC_BATCH, n_pages).astype(jnp.int32)
    state_wkv = nrm((N_EVEN, DEC_BATCH, N_HEADS_RWKV, HEAD_DIM, HEAD_DIM), 0.3)
    state_shift = nrm((N_EVEN, DEC_BATCH, RWKV_COLS))
    state_s5 = nrm((N_ODD, DEC_BATCH, S5_GROUPS, S5_STATE, 2), 0.3)
    ln_g = 1.0 + nrm((DEPTH, 3, D_MODEL), 0.02)
    ln_b = nrm((DEPTH, 3, D_MODEL), 0.02)
    ffn1_wg = nrm((DEPTH, D_MODEL, D_FF), D_MODEL ** -0.5)
    ffn1_wu = nrm((DEPTH, D_MODEL, D_FF), D_MODEL ** -0.5)
    ffn1_wd = nrm((DEPTH, D_FF, D_MODEL), DN_BETA * D_FF ** -0.5)
    ffn2_wg = nrm((DEPTH, D_MODEL, D_FF), D_MODEL ** -0.5)
    ffn2_wu = nrm((DEPTH, D_MODEL, D_FF), D_MODEL ** -0.5)
    ffn2_wd = nrm((DEPTH, D_FF, D_MODEL), DN_BETA * D_FF ** -0.5)
    w_in_even = nrm((N_EVEN, D_MODEL, IN_COLS), D_MODEL ** -0.5)
    w_out_even = nrm((N_EVEN, D_SB + D_RWKV, D_MODEL), DN_BETA * (D_SB + D_RWKV) ** -0.5)
    sb_bias = (jnp.linspace(SB_BIAS_LO, SB_BIAS_HI, N_HEADS_SB, dtype=f32)[None, :]
               + nrm((N_EVEN, N_HEADS_SB), 0.05))
    mu_shift = uni((N_EVEN, RWKV_COLS), 0.0, 1.0)
    w0 = uni((N_EVEN, D_RWKV), -6.0, -1.0)
    w_w2 = nrm((N_EVEN, RANK_DECAY, D_RWKV), 0.5 * RANK_DECAY ** -0.5)
    a0 = nrm((N_EVEN, D_RWKV), 0.1)
    w_a2 = nrm((N_EVEN, RANK_ICLR, D_RWKV), 0.5 * RANK_ICLR ** -0.5)
    w_g2 = nrm((N_EVEN, RANK_GATE, D_RWKV), RANK_GATE ** -0.5)
    k_k = 0.85 + nrm((N_EVEN, D_RWKV), 0.02)
    k_a = 1.0 + nrm((N_EVEN, D_RWKV), 0.02)
    r_k = nrm((N_EVEN, N_HEADS_RWKV, HEAD_DIM), 0.1)
    gn_g = 1.0 + nrm((N_EVEN, D_RWKV), 0.02)
    gn_b = nrm((N_EVEN, D_RWKV), 0.02)
    lam_re = -0.5 + nrm((N_ODD, S5_GROUPS, S5_STATE), 0.01)
    lam_im = math.pi * jnp.arange(S5_STATE, dtype=f32) + nrm((N_ODD, S5_GROUPS, S5_STATE), 0.01)
    log_dt = uni((N_ODD, S5_GROUPS), math.log(1e-3), math.log(1e-1))
    b_re = nrm((N_ODD, S5_GROUPS, S5_STATE, S5_GROUP), (2.0 * S5_GROUP) ** -0.5)
    b_im = nrm((N_ODD, S5_GROUPS, S5_STATE, S5_GROUP), (2.0 * S5_GROUP) ** -0.5)
    c_re = nrm((N_ODD, S5_GROUPS, S5_GROUP, S5_STATE), S5_STATE ** -0.5)
    c_im = nrm((N_ODD, S5_GROUPS, S5_GROUP, S5_STATE), S5_STATE ** -0.5)
    d_skip = nrm((N_ODD, D_MODEL))
    w_glu_out = nrm((N_ODD, D_MODEL, D_MODEL), DN_BETA * D_MODEL ** -0.5)
    w_glu_gate = nrm((N_ODD, D_MODEL, D_MODEL), D_MODEL ** -0.5)
    return {'x_prompt': x_prompt, 'x_sample': x_sample, 'cache_k_sb': cache_k_sb,
            'cache_v_sb': cache_v_sb, 'page_table': page_table, 'state_wkv': state_wkv,
            'state_shift': state_shift, 'state_s5': state_s5, 'ln_g': ln_g, 'ln_b': ln_b,
            'ffn1_wg': ffn1_wg, 'ffn1_wu': ffn1_wu, 'ffn1_wd': ffn1_wd,
            'ffn2_wg': ffn2_wg, 'ffn2_wu': ffn2_wu, 'ffn2_wd': ffn2_wd,
            'w_in_even': w_in_even, 'w_out_even': w_out_even, 'sb_bias': sb_bias,
            'mu_shift': mu_shift,
            'w0': w0, 'w_w2': w_w2, 'a0': a0, 'w_a2': w_a2, 'w_g2': w_g2,
            'k_k': k_k, 'k_a': k_a, 'r_k': r_k, 'gn_g': gn_g, 'gn_b': gn_b,
            'lam_re': lam_re, 'lam_im': lam_im, 'log_dt': log_dt, 'b_re': b_re, 'b_im': b_im,
            'c_re': c_re, 'c_im': c_im, 'd_skip': d_skip, 'w_glu_out': w_glu_out,
            'w_glu_gate': w_glu_gate}


def reference(x_prompt, x_sample, cache_k_sb, cache_v_sb, page_table, state_wkv, state_shift,
              state_s5, ln_g, ln_b, ffn1_wg, ffn1_wu, ffn1_wd, ffn2_wg, ffn2_wu, ffn2_wd,
              w_in_even, w_out_even, sb_bias, mu_shift, w0, w_w2, a0, w_a2, w_g2, k_k, k_a, r_k,
              gn_g, gn_b, lam_re, lam_im, log_dt, b_re, b_im, c_re, c_im, d_skip,
              w_glu_out, w_glu_gate):
    prm = dict(ln_g=ln_g, ln_b=ln_b, ffn1_wg=ffn1_wg, ffn1_wu=ffn1_wu, ffn1_wd=ffn1_wd,
               ffn2_wg=ffn2_wg, ffn2_wu=ffn2_wu, ffn2_wd=ffn2_wd, w_in_even=w_in_even,
               w_out_even=w_out_even, sb_bias=sb_bias, mu_shift=mu_shift, w0=w0, w_w2=w_w2,
               a0=a0, w_a2=w_a2,
               w_g2=w_g2, k_k=k_k, k_a=k_a, r_k=r_k, gn_g=gn_g, gn_b=gn_b, lam_re=lam_re,
               lam_im=lam_im, log_dt=log_dt, b_re=b_re, b_im=b_im, c_re=c_re, c_im=c_im,
               d_skip=d_skip, w_glu_out=w_glu_out, w_glu_gate=w_glu_gate)
    nb = x_prompt.shape[0]
    wkv0 = jnp.zeros((N_EVEN, nb, N_HEADS_RWKV, HEAD_DIM, HEAD_DIM), jnp.float32)
    shift0 = jnp.zeros((N_EVEN, nb, RWKV_COLS), x_prompt.dtype)
    s50 = jnp.zeros((N_ODD, nb, S5_GROUPS, S5_STATE, 2), jnp.float32)
    y_prompt, p_k, p_v, p_wkv, p_shift, p_s5 = trunk(
        x_prompt, None, None, None, wkv0, shift0, s50, prm)
    y_sample, s_k, s_v, s_wkv, s_shift, s_s5 = trunk(
        x_sample, cache_k_sb, cache_v_sb, page_table, state_wkv, state_shift, state_s5, prm)
    return (y_prompt, y_sample, p_k, p_v, p_wkv, p_shift, p_s5, s_k, s_v, s_wkv, s_shift, s_s5)
```

```python
from contextlib import ExitStack
import numpy as np
import concourse.bass as bass
import concourse.mybir as mybir
from concourse.bass_utils import run_bass_kernel_spmd

F32 = mybir.dt.float32
BF16 = mybir.dt.bfloat16
I32 = mybir.dt.int32
AF = mybir.ActivationFunctionType
ALU = mybir.AluOpType
AX = mybir.AxisListType

D = 1024
KC = 8
DFF = 2816
JC = 22
NP = 2048
NS = 16
NT = NP + NS
TW = 344
NTL = NT // TW
NG = 3
GW = NT // NG
ALPHA = 4.0 ** 0.25
LN_EPS = 1e-5
GN_EPS = 64e-5
INC = 3328
RWC = 1792
SKIP_FFN = False
import os
VAR = os.environ.get('KVAR', '')


class K:
    __slots__ = ("name", "lw", "rd")

    def __init__(self, name=""):
        self.name = name
        self.lw = None
        self.rd = []


class FW:
    def __init__(self, nc, stack, n_dma_sems=8):
        self.nc = nc
        self.eng = {"pe": nc.tensor, "dve": nc.vector, "act": nc.scalar,
                    "pool": nc.gpsimd, "sp": nc.sync}
        self.sem = {}
        self.cnt = {}
        self.seen = {e: {} for e in self.eng}
        for e in self.eng:
            self.sem[e] = stack.enter_context(nc.semaphore("s_" + e))
            self.cnt[e] = 0
        self.dsem = {}
        self.dcnt = {}
        self.dnext = {}
        for q in ("sp", "pool"):
            self.dsem[q] = [stack.enter_context(nc.semaphore("d_%s_%d" % (q, i)))
                            for i in range(n_dma_sems)]
            self.dcnt[q] = [0] * n_dma_sems
            self.dnext[q] = 0
        self.ninst = 0
        self.dram_writes = []

    def _semobj(self, sk):
        if isinstance(sk, tuple):
            return self.dsem[sk[0]][sk[1]]
        return self.sem[sk]

    def _wait(self, e, sk, val):
        if sk == "pe" and e == "pe":
            return
        if self.seen[e].get(sk, 0) >= val:
            return
        self.seen[e][sk] = val
        self.eng[e].wait_ge(self._semobj(sk), val)
        self.ninst += 1

    def _deps(self, e, reads, writes):
        for k in reads:
            if k.lw is not None:
                self._wait(e, k.lw[0], k.lw[1])
        for k in writes:
            if k.lw is not None:
                self._wait(e, k.lw[0], k.lw[1])
            for (sk, v) in k.rd:
                self._wait(e, sk, v)

    def _mark(self, sk, val, reads, writes):
        for k in reads:
            k.rd.append((sk, val))
            if len(k.rd) > 16:
                m = {}
                for (s, v) in k.rd:
                    if m.get(s, 0) < v:
                        m[s] = v
                k.rd = list(m.items())
        for k in writes:
            k.lw = (sk, val)
            k.rd = []

    def op(self, e, fn, reads=(), writes=()):
        if e == "dve" and self.dram_writes:
            for (sk, v) in self.dram_writes:
                self._wait(e, sk, v)
            self.dram_writes = []
        self._deps(e, reads, writes)
        ins = fn(self.eng[e])
        self.cnt[e] += 1
        ins.then_inc(self.sem[e], 1)
        self._mark(e, self.cnt[e], reads, writes)
        self.ninst += 1
        return ins

    def _dslot(self, q):
        i = self.dnext[q]
        self.dnext[q] = (i + 1) % len(self.dsem[q])
        if self.dcnt[q][i] > 0:
            self._wait(q, (q, i), self.dcnt[q][i])
        return i

    def dma(self, q, out, in_, reads=(), writes=(), **kw):
        i = self._dslot(q)
        self._deps(q, reads, writes)
        ins = self.eng[q].dma_start(out=out, in_=in_, **kw)
        self.dcnt[q][i] += 16
        ins.then_inc(self.dsem[q][i], 16)
        self._mark((q, i), self.dcnt[q][i], reads, writes)
        self.ninst += 1
        if "DRAM" in str(getattr(out.tensor, "space", "")).upper() or type(out.tensor).__name__.startswith("DRam"):
            self.dram_writes.append(((q, i), self.dcnt[q][i]))
        return ins

    def gather(self, out, in_rows, idx_ap, nrows, reads=(), writes=()):
        q = "pool"
        i = self._dslot(q)
        self._deps(q, reads, writes)
        if getattr(self, "_breg", None) is None or self._breg[0] != nrows:
            self._breg = (nrows, self.nc.gpsimd.to_reg(nrows - 1))
        ins = self.nc.gpsimd.indirect_dma_start(
            out=out, out_offset=None, in_=in_rows,
            in_offset=bass.IndirectOffsetOnAxis(ap=idx_ap, axis=0),
            bounds_check=self._breg[1], oob_is_err=False)
        self.dcnt[q][i] += 16
        ins.then_inc(self.dsem[q][i], 16)
        self._mark((q, i), self.dcnt[q][i], reads, writes)
        self.ninst += 1
        return ins

    def barrier(self):
        for e in self.eng:
            for q in self.dsem:
                for i, v in enumerate(self.dcnt[q]):
                    if v > 0:
                        self._wait(e, (q, i), v)
            for e2 in self.eng:
                if self.cnt[e2] > 0 and not (e2 == e and e == "pe"):
                    self._wait(e, e2, self.cnt[e2])

    def finish(self):
        for q in self.dsem:
            for i, v in enumerate(self.dcnt[q]):
                if v > 0:
                    self._wait("sp", (q, i), v)
        for e in self.eng:
            if e != "sp" and self.cnt[e] > 0:
                self._wait("sp", e, self.cnt[e])


class Ctx:
    pass


def sb(st, nc, name, shape, dt):
    return st.enter_context(nc.sbuf_tensor(name, list(shape), dt))


def ffn_ln(c, wg, wu, wd, lnidx, tag):
    nc, fw = c.nc, c.fw
    XF, kXF = c.XF, c.kXF
    with ExitStack() as st:
        XB = sb(st, nc, "XB" + tag, [128, KC, GW], BF16)
        H = sb(st, nc, "H" + tag, [128, JC, GW], BF16)
        wgb = [sb(st, nc, "wgb%d%s" % (i, tag), [128, KC, 256], BF16) for i in range(2)]
        wub = [sb(st, nc, "wub%d%s" % (i, tag), [128, KC, 256], BF16) for i in range(2)]
        wdb = [sb(st, nc, "wdb%d%s" % (i, tag), [128, JC, 256], BF16) for i in range(2)]
        sgt = [sb(st, nc, "sgt%d%s" % (i, tag), [128, TW], F32) for i in range(2)]
        alloc_ln(c, st)
        kXB = [K() for _ in range(KC)]
        kH = [[K() for _ in range(2)] for _ in range(JC)]
        kwg = [K(), K()]
        kwu = [K(), K()]
        kwd = [K(), K()]
        ksg = [K(), K()]
        wgv = wg.rearrange("(kc p) n -> p kc n", p=128)
        wuv = wu.rearrange("(kc p) n -> p kc n", p=128)
        wdv = wd.rearrange("(j p) n -> p j n", p=128)
        PS, kPS = c.PS, c.kPS
        nsg = 0
        nb = 0
        for g in range(NG):
            c0 = g * GW
            for kc in range(KC):
                fw.op("act", lambda e, kc=kc: e.activation(out=XB[:, kc, :], in_=XF[:, kc, c0:c0 + GW], func=AF.Identity),
                      reads=[kXF[kc][2 * g], kXF[kc][2 * g + 1]], writes=[kXB[kc]])
            for jp in range(JC // 2):
                s = jp % 2
                fw.dma("pool", wgb[s][:], wgv[:, :, jp * 256:(jp + 1) * 256], writes=[kwg[s]])
                fw.dma("pool", wub[s][:], wuv[:, :, jp * 256:(jp + 1) * 256], writes=[kwu[s]])
                for jj in range(2):
                    j = 2 * jp + jj
                    for tl in range(2):
                        bg, bu = 2 * (nb % 2), 2 * (nb % 2) + 1
                        nb += 1
                        cs = slice(tl * TW, (tl + 1) * TW)
                        for kc in range(KC):
                            fw.op("pe", lambda e, kc=kc: e.matmul(PS[:, bg, 0:TW], wgb[s][:, kc, jj * 128:(jj + 1) * 128],
                                                                  XB[:, kc, cs], start=(kc == 0), stop=(kc == KC - 1)),
                                  reads=[kwg[s], kXB[kc]], writes=[kPS[bg]])
                        for kc in range(KC):
                            fw.op("pe", lambda e, kc=kc: e.matmul(PS[:, bu, 0:TW], wub[s][:, kc, jj * 128:(jj + 1) * 128],
                                                                  XB[:, kc, cs], start=(kc == 0), stop=(kc == KC - 1)),
                                  reads=[kwu[s], kXB[kc]], writes=[kPS[bu]])
                        ss = nsg % 2
                        nsg += 1
                        fw.op("act", lambda e: e.activation(out=sgt[ss][:], in_=PS[:, bg, 0:TW], func=AF.Silu),
                              reads=[kPS[bg]], writes=[ksg[ss]])
                        fw.op("dve", lambda e: e.scalar_tensor_tensor(out=H[:, j, cs], in0=PS[:, bu, 0:TW], scalar=0.5,
                                                                      in1=sgt[ss][:], op0=ALU.mult, op1=ALU.mult),
                              reads=[kPS[bu], ksg[ss]], writes=[kH[j][tl]])
            for mp in range(KC // 2):
                s = mp % 2
                fw.dma("pool", wdb[s][:], wdv[:, :, mp * 256:(mp + 1) * 256], writes=[kwd[s]])
                for mm in range(2):
                    m = 2 * mp + mm
                    for tl in range(2):
                        by = 4 + (nb % 2)
                        nb += 1
                        cs = slice(tl * TW, (tl + 1) * TW)
                        gc = slice(c0 + tl * TW, c0 + (tl + 1) * TW)
                        for j in range(JC):
                            fw.op("pe", lambda e, j=j: e.matmul(PS[:, by, 0:TW], wdb[s][:, j, mm * 128:(mm + 1) * 128],
                                                                H[:, j, cs], start=(j == 0), stop=(j == JC - 1)),
                                  reads=[kwd[s], kH[j][tl]], writes=[kPS[by]])
                        fw.op("dve", lambda e: e.scalar_tensor_tensor(out=XF[:, m, gc], in0=XF[:, m, gc], scalar=ALPHA,
                                                                      in1=PS[:, by, 0:TW], op0=ALU.mult, op1=ALU.add),
                              reads=[kPS[by], kXF[m][2 * g + tl]], writes=[kXF[m][2 * g + tl]])
            for tl in range(2):
                layer_norm_tile(c, 2 * g + tl, lnidx)


def alloc_ln(c, st):
    c.nln += 1
    c.sq = [sb(st, c.nc, "sq%d_%d" % (i, c.nln), [128, TW], F32) for i in range(2)]
    c.ksq = [K(), K()]
    c.lnt = [sb(st, c.nc, "lnt%d_%d" % (i, c.nln), [128, TW], F32) for i in range(4)]
    c.kln = [K() for _ in range(4)]


def layer_norm_tile(c, ti, lnidx):
    nc, fw = c.nc, c.fw
    XF, kXF, PS, kPS = c.XF, c.kXF, c.PS, c.kPS
    cs = slice(ti * TW, (ti + 1) * TW)
    b1, b2 = 6, 7
    for m in range(KC):
        s = c.nsq % 2
        c.nsq += 1
        fw.op("act", lambda e: e.activation(out=c.sq[s][:], in_=XF[:, m, cs], func=AF.Square),
              reads=[kXF[m][ti]], writes=[c.ksq[s]])
        fw.op("pe", lambda e: e.matmul(PS[:, b1, 0:TW], c.onesf[:], XF[:, m, cs], start=(m == 0), stop=(m == KC - 1)),
              reads=[kXF[m][ti], c.kconst], writes=[kPS[b1]])
        fw.op("pe", lambda e: e.matmul(PS[:, b2, 0:TW], c.onesf[:], c.sq[s][:], start=(m == 0), stop=(m == KC - 1)),
              reads=[c.ksq[s], c.kconst], writes=[kPS[b2]])
    mean, msq, var, rstd = c.lnt
    kln = c.kln
    fw.op("dve", lambda e: e.tensor_scalar(out=mean[:], in0=PS[:, b1, 0:TW], scalar1=1.0 / D, scalar2=None, op0=ALU.mult),
          reads=[kPS[b1]], writes=[kln[0]])
    fw.op("dve", lambda e: e.tensor_tensor(out=msq[:], in0=mean[:], in1=mean[:], op=ALU.mult),
          reads=[kln[0]], writes=[kln[1]])
    fw.op("dve", lambda e: e.scalar_tensor_tensor(out=var[:], in0=PS[:, b2, 0:TW], scalar=1.0 / D, in1=msq[:],
                                                  op0=ALU.mult, op1=ALU.subtract),
          reads=[kPS[b2], kln[1]], writes=[kln[2]])
    fw.op("act", lambda e: e.activation(out=var[:], in_=var[:], func=AF.Sqrt, bias=c.epsc[:, 0:1]),
          reads=[kln[2], c.kconst], writes=[kln[2]])
    fw.op("dve", lambda e: e.reciprocal(out=rstd[:], in_=var[:]), reads=[kln[2]], writes=[kln[3]])
    for m in range(KC):
        s = c.nsq % 2
        c.nsq += 1
        t = c.sq[s]
        fw.op("pool", lambda e: e.tensor_tensor(out=t[:], in0=XF[:, m, cs], in1=mean[:], op=ALU.subtract),
              reads=[kXF[m][ti], kln[0]], writes=[c.ksq[s]])
        fw.op("dve", lambda e: e.tensor_tensor(out=t[:], in0=t[:], in1=rstd[:], op=ALU.mult),
              reads=[c.ksq[s], kln[3]], writes=[c.ksq[s]])
        col = lnidx * KC + m
        fw.op("act", lambda e: e.activation(out=XF[:, m, cs], in_=t[:], func=AF.Identity,
                                            scale=c.lng[:, col:col + 1], bias=c.lnb[:, col:col + 1]),
              reads=[c.ksq[s], c.kconst], writes=[kXF[m][ti]])


def pipeline(streams):
    nst = max(len(b) for s in streams for b in s)
    nmax = max(len(s) for s in streams)
    for t in range(nmax + nst - 1):
        for s in streams:
            for k in range(nst - 1, -1, -1):
                bi = t - k
                if 0 <= bi < len(s) and k < len(s[bi]):
                    s[bi][k]()


def mixer_even(c, d, stage):
    nc, fw = c.nc, c.fw
    XF, kXF, PS, kPS = c.XF, c.kXF, c.PS, c.kPS
    st = ExitStack()
    with st:
        OSB = sb(st, nc, "OSB", [128, 4, NT], BF16)
        kOSB = [[K() for _ in range(5)] for _ in range(4)]
        ORW = sb(st, nc, "ORW", [128, 4, NT], BF16)
        kORW = [K() for _ in range(4)]
        fw.op("pool", lambda e: e.memset(OSB[:, :, NP:NT], 0.0), writes=[kOSB[cc][4] for cc in range(4)])
        QS = sb(st, nc, "QS", [NS, 512], F32); kQS = K()
        win_v = d["w_in"].rearrange("(kc p) n -> p kc n", p=128)
        with ExitStack() as sta:
            QT = sb(sta, nc, "QT", [128, 4, NT], BF16)
            KT = sb(sta, nc, "KT", [128, 4, NT], BF16)
            kQT = [K() for _ in range(4)]
            kKT = [K() for _ in range(4)]
            Vtok = sb(sta, nc, "Vtok", [128, 17, 512], BF16)
            kV = [K() for _ in range(17)]
            with ExitStack() as st2:
                XB = sb(st2, nc, "XBm", [128, KC, NT], BF16)
                kXB = [K() for _ in range(KC)]
                for kc in range(KC):
                    fw.op("act" if kc % 2 else "dve",
                          (lambda e, kc=kc: e.activation(out=XB[:, kc, :], in_=XF[:, kc, :], func=AF.Identity)) if kc % 2 else
                          (lambda e, kc=kc: e.tensor_copy(out=XB[:, kc, :], in_=XF[:, kc, :])),
                          reads=kXF[kc], writes=[kXB[kc]])
                WB = [sb(st2, nc, "WBm%d" % i, [128, KC, 256], BF16) for i in range(2)]
                kWB = [K(), K()]
                stg = [sb(st2, nc, "stg%d" % i, [128, 256], F32) for i in range(2)]
                kstg = [K(), K()]
                nb = 0
                nstg = 0
                for wc in range(6):
                    sl = wc % 2
                    fw.dma("pool", WB[sl][:], win_v[:, :, wc * 256:(wc + 1) * 256], writes=[kWB[sl]])
                    if wc < 4:
                        for oo in range(2):
                            oc = 2 * wc + oo
                            dst, kd = (QT, kQT) if oc < 4 else (KT, kKT)
                            for ti in range(NTL):
                                bk = nb % 2
                                nb += 1
                                cs = slice(ti * TW, (ti + 1) * TW)
                                for kc in range(KC):
                                    fw.op("pe", lambda e, kc=kc: e.matmul(PS[:, bk, 0:TW], WB[sl][:, kc, oo * 128:(oo + 1) * 128], XB[:, kc, cs],
                                                                          start=(kc == 0), stop=(kc == KC - 1)),
                                          reads=[kWB[sl], kXB[kc]], writes=[kPS[bk]])
                                fw.op("act", lambda e: e.activation(out=dst[:, oc % 4, cs], in_=PS[:, bk, 0:TW], func=AF.Identity),
                                      reads=[kPS[bk]], writes=[kd[oc % 4]])
                    if wc < 2 and stage >= 4:
                        bk = 2 + (nb % 2)
                        nb += 1
                        for kc in range(KC):
                            fw.op("pe", lambda e, kc=kc: e.matmul(PS[0:NS, bk, 0:256], XB[:, kc, NP:NT], WB[sl][:, kc, :], start=(kc == 0), stop=(kc == KC - 1)),
                                  reads=[kWB[sl], kXB[kc]], writes=[kPS[bk]])
                        fw.op("act", lambda e: e.activation(out=QS[0:NS, wc * 256:(wc + 1) * 256], in_=PS[0:NS, bk, 0:256], func=AF.Identity), reads=[kPS[bk]], writes=[kQS])
                    if wc >= 2:
                        which = 0 if wc < 4 else 1
                        coff = (wc % 2) * 256
                        for tt in range(17):
                            rows = 128 if tt < 16 else NS
                            bk = 2 + (nb % 2)
                            nb += 1
                            for kc in range(KC):
                                fw.op("pe", lambda e, kc=kc: e.matmul(PS[0:rows, bk, 0:256], XB[:, kc, tt * 128:tt * 128 + rows], WB[sl][:, kc, :],
                                                                      start=(kc == 0), stop=(kc == KC - 1)),
                                      reads=[kWB[sl], kXB[kc]], writes=[kPS[bk]])
                            ss = nstg % 2
                            nstg += 1
                            fw.op("act", lambda e: e.activation(out=stg[ss][0:rows, :], in_=PS[0:rows, bk, 0:256], func=AF.Identity),
                                  reads=[kPS[bk]], writes=[kstg[ss]])
                            if tt < 16:
                                dstd = (d["pk"] if which == 0 else d["pv"])[tt * 128:(tt + 1) * 128, coff:coff + 256]
                            else:
                                dstd = (d["sk"] if which == 0 else d["sv"])[:, coff:coff + 256]
                            fw.dma("sp", dstd, stg[ss][0:rows, :], reads=[kstg[ss]])
                            if which == 1:
                                fw.op("act", lambda e: e.activation(out=Vtok[0:rows, tt, coff:coff + 256], in_=PS[0:rows, bk, 0:256], func=AF.Identity),
                                      reads=[kPS[bk]], writes=[kV[tt]])
                fw.barrier()
            with ExitStack() as st2:
                if stage >= 3:
                    sb_attention_prompt(c, d, st2, QT, KT, kQT, kKT, Vtok, kV, OSB, kOSB)
                fw.barrier()
            fw.barrier()
        if stage >= 4 and 'nosamp' not in VAR:
            with ExitStack() as st3:
                sb_attention_sample(c, d, st3, QS, kQS, OSB, kOSB)
                fw.barrier()
        if stage >= 5:
            with ExitStack() as st3:
                rwkv_all(c, d, st3, ORW, kORW)
                fw.barrier()
        if stage >= 6:
            with ExitStack() as st3:
                alloc_ln(c, st3)
                WOUT = sb(st3, nc, "WOUT", [128, KC, D], BF16)
                kWO = [K() for _ in range(4)]
                wo_v = d["w_out"].rearrange("(kc p) n -> p kc n", p=128)
                for i in range(4):
                    fw.dma("pool", WOUT[:, :, i * 256:(i + 1) * 256], wo_v[:, :, i * 256:(i + 1) * 256], writes=[kWO[i]])
                nb2 = 0
                for ti in range(NTL):
                    cs = slice(ti * TW, (ti + 1) * TW)
                    for m in range(KC):
                        bk = nb2 % 2
                        nb2 += 1
                        for kc in range(KC):
                            src, ks = (OSB, kOSB[kc % 4]) if kc < 4 else (ORW, [kORW[kc % 4]])
                            fw.op("pe", lambda e, kc=kc, src=src: e.matmul(PS[:, bk, 0:TW], WOUT[:, kc, m * 128:(m + 1) * 128], src[:, kc % 4, cs],
                                                                           start=(kc == 0), stop=(kc == KC - 1)),
                                  reads=[kWO[m // 2]] + list(ks), writes=[kPS[bk]])
                        fw.op("dve", lambda e: e.scalar_tensor_tensor(out=XF[:, m, cs], in0=XF[:, m, cs], scalar=ALPHA, in1=PS[:, bk, 0:TW],
                                                                      op0=ALU.mult, op1=ALU.add),
                              reads=[kPS[bk], kXF[m][ti]], writes=[kXF[m][ti]])
                    layer_norm_tile(c, ti, 1)
                fw.barrier()
        if c.dbg and 'nodbg' not in VAR:
            fw.dma("sp", d["dbg_osb"], OSB[:], reads=[k for kk in kOSB for k in kk])
        fw.barrier()


def sb_attention_prompt(c, d, st, QT, KT, kQT, kKT, Vtok, kV, OSB, kOSB):
    nc, fw = c.nc, c.fw
    PS, kPS = c.PS, c.kPS
    NSTR = 2
    onesb = c.onesb
    c.TRI = sb(st, nc, "TRI", [128, 128], BF16)
    c.STRICT = sb(st, nc, "STRICT", [128, 128], BF16)
    c.MASK = [sb(st, nc, "MASK%d" % i, [128, 512], BF16) for i in range(4)]
    fw.op("pool", lambda e: e.affine_select(out=c.TRI[:], in_=onesb[:, 0:128], pattern=[[-1, 128]], base=0, channel_multiplier=1,
                                            compare_op=ALU.is_ge, fill=0.0), reads=[c.kconst], writes=[c.kconst])
    for i in range(4):
        fw.op("pool", lambda e, i=i: e.affine_select(out=c.MASK[i][:], in_=onesb[:], pattern=[[1, 512]], base=-128 * i, channel_multiplier=-1,
                                                     compare_op=ALU.is_gt, fill=0.0), reads=[c.kconst], writes=[c.kconst])
    Et = [[sb(st, nc, "Et%d_%d" % (s, i), [128, 512], F32) for i in range(3)] for s in range(NSTR)]
    spt = [[sb(st, nc, "spt%d_%d" % (s, i), [128, 512], BF16) for i in range(3)] for s in range(NSTR)]
    e2t = [[sb(st, nc, "e2t%d_%d" % (s, i), [128, 512], F32) for i in range(2)] for s in range(NSTR)]
    wt = [[sb(st, nc, "wt%d_%d" % (s, i), [128, 512], BF16) for i in range(2)] for s in range(NSTR)]
    kEt = [[K() for _ in range(3)] for _ in range(NSTR)]
    kspt = [[K() for _ in range(3)] for _ in range(NSTR)]
    ke2t = [[K() for _ in range(2)] for _ in range(NSTR)]
    kwt = [[K() for _ in range(2)] for _ in range(NSTR)]
    sacc = [sb(st, nc, "sacc%d" % s, [128, 512], BF16) for s in range(NSTR)]
    ksacc = [K() for _ in range(NSTR)]
    streams = []
    for s in range(NSTR):
        blocks = []
        bi = 0
        bA = [4 * s, 4 * s + 1]
        bC = 4 * s + 2
        bO = 4 * s + 3
        for h in range(s, 8, NSTR):
            po = (h % 2) * 64
            ch = h // 2
            for qt in range(4):
                q0 = qt * 512
                nkb = 4 * qt + 4
                for n, kb in enumerate(range(nkb - 1, -1, -1)):
                    first = (n == 0)
                    last = (kb == 0)
                    diag = kb - 4 * qt
                    i3, i2 = bi % 3, bi % 2
                    bi += 1

                    def st1(s=s, h=h, po=po, ch=ch, q0=q0, kb=kb, diag=diag, i3=i3, bA=bA[bi % 2]):
                        fw.op("pe", lambda e: e.matmul(PS[:, bA, :], KT[po:po + 64, ch, kb * 128:(kb + 1) * 128], QT[po:po + 64, ch, q0:q0 + 512],
                                                       start=True, stop=True),
                              reads=[kQT[ch], kKT[ch]], writes=[kPS[bA]])
                        fw.op("act", lambda e: e.activation(out=Et[s][i3][:], in_=PS[:, bA, :], func=AF.Exp, scale=0.125,
                                                            bias=c.sbb[:, h:h + 1]),
                              reads=[kPS[bA], c.kconst], writes=[kEt[s][i3]])
                        fw.op("act", lambda e: e.activation(out=spt[s][i3][:], in_=Et[s][i3][:], func=AF.Ln, bias=c.cst[:, 1:2]),
                              reads=[kEt[s][i3], c.kconst], writes=[kspt[s][i3]])
                        if diag >= 0:
                            fw.op("dve", lambda e: e.tensor_tensor(out=spt[s][i3][:], in0=spt[s][i3][:], in1=c.MASK[diag][:], op=ALU.mult),
                                  reads=[kspt[s][i3], c.kconst], writes=[kspt[s][i3]])
                            fw.op("pool", lambda e: e.tensor_tensor(out=Et[s][i3][:], in0=Et[s][i3][:], in1=c.MASK[diag][:], op=ALU.mult),
                                  reads=[kEt[s][i3], c.kconst], writes=[kEt[s][i3]])

                    def st2(s=s, i3=i3, i2=i2, first=first, last=last, bC=bC):
                        fw.op("pe", lambda e: e.matmul(PS[:, bC, :], c.TRI[:], spt[s][i3][:], start=True, stop=first),
                              reads=[kspt[s][i3], c.kconst], writes=[kPS[bC]])
                        if not first:
                            fw.op("pe", lambda e: e.matmul(PS[:, bC, :], c.ONESB[:], sacc[s][:], start=False, stop=True),
                                  reads=[ksacc[s], c.kconst], writes=[kPS[bC]])
                        fw.op("act", lambda e: e.activation(out=e2t[s][i2][:], in_=PS[:, bC, :], func=AF.Exp, scale=-1.0),
                              reads=[kPS[bC]], writes=[ke2t[s][i2]])
                        if not last:
                            if first:
                                fw.op("pool", lambda e: e.tensor_copy(out=sacc[s][:], in_=spt[s][i3][:]),
                                      reads=[kspt[s][i3]], writes=[ksacc[s]])
                            else:
                                fw.op("pool", lambda e: e.tensor_tensor(out=sacc[s][:], in0=sacc[s][:], in1=spt[s][i3][:], op=ALU.add),
                                      reads=[kspt[s][i3], ksacc[s]], writes=[ksacc[s]])

                    def st3(s=s, h=h, po=po, ch=ch, q0=q0, qt=qt, kb=kb, i3=i3, i2=i2, first=first, last=last, bC=bC, bO=bO):
                        fw.op("dve", lambda e: e.tensor_tensor(out=wt[s][i2][:], in0=Et[s][i3][:], in1=e2t[s][i2][:], op=ALU.mult),
                              reads=[kEt[s][i3], ke2t[s][i2]], writes=[kwt[s][i2]])
                        fw.op("pe", lambda e: e.matmul(PS[po:po + 64, bO, :], Vtok[:, kb, h * 64:(h + 1) * 64], wt[s][i2][:],
                                                       start=first, stop=last),
                              reads=[kV[kb], kwt[s][i2]], writes=[kPS[bO]])
                        if last:
                            fw.op("act", lambda e: e.activation(out=OSB[po:po + 64, ch, q0:q0 + 512], in_=PS[po:po + 64, bO, :], func=AF.Identity),
                                  reads=[kPS[bO]], writes=[kOSB[ch][qt]])
                    blocks.append([st1, st2, st3])
        streams.append(blocks)
    pipeline(streams)


CS = 24


def transpose_to(c, out_ps, in_ap, kin, kout, ident):
    c.fw.op("pe", lambda e: e.transpose(out_ps, in_ap, ident), reads=kin + [c.kconst], writes=kout)


def rwkv_all(c, d, st, ORW, kORW):
    nc, fw = c.nc, c.fw
    XF, kXF, PS, kPS = c.XF, c.kXF, c.PS, c.kPS
    IDF = c.IDF
    ST = sb(st, nc, "ST", [128, 256], F32); kST = K()
    STw = sb(st, nc, "STw", [128, 256], F32); kSTw = K()
    SAY = sb(st, nc, "SAY", [128, 64], BF16); kSAY = K()
    PB = sb(st, nc, "PB", [128, 14, TW + 1], F32); kPB = [K() for _ in range(14)]
    PM = sb(st, nc, "PM", [128, 14, TW], F32); kPM = [K() for _ in range(14)]
    XBt = sb(st, nc, "XBt", [128, KC, TW], BF16); kXBt = [K() for _ in range(KC)]
    WRb = [sb(st, nc, "WRb%d" % i, [128, KC, 256], BF16) for i in range(2)]; kWRb = [K(), K()]
    WW2 = sb(st, nc, "WW2", [128, 512], BF16)
    WA2 = sb(st, nc, "WA2", [128, 512], BF16)
    WG2 = sb(st, nc, "WG2", [128, 512], BF16)
    PC = sb(st, nc, "PC", [128, 48], F32)
    GNG = sb(st, nc, "GNG", [128, 512], F32)
    GNB = sb(st, nc, "GNB", [128, 512], F32)
    BLK = sb(st, nc, "BLK", [128, 128], F32)
    kW = K()
    tmpA = sb(st, nc, "tmpA", [128, TW], F32); ktA = K()
    tmpB = sb(st, nc, "tmpB", [128, TW], F32); ktB = K()
    tmpH = sb(st, nc, "tmpH", [128, TW], BF16); ktH = K()
    NBrow = sb(st, nc, "NBrow", [128, CS, 128], BF16)
    KBrow = sb(st, nc, "KBrow", [128, CS, 128], BF16)
    Vrow = sb(st, nc, "Vrow", [128, CS, 64], BF16)
    kRow = K()
    TOK = sb(st, nc, "TOK", [128, 3, 512], BF16); kTOK = [K(), K(), K()]
    LKc = sb(st, nc, "LKc", [128, CS, 4, 2], F32)
    RKc = sb(st, nc, "RKc", [128, CS, 4, 2], F32)
    kLK = K()
    Ybuf = sb(st, nc, "Ybuf", [128, CS, 64], BF16); kY = K()
    YTOK = sb(st, nc, "YTOK", [128, 512], BF16); kYT = K()
    YC = sb(st, nc, "YC", [128, 512], F32); kYC = K()
    YS = sb(st, nc, "YS", [128, 512], F32); kYS = K()
    gst = sb(st, nc, "gst", [128, 32], F32); kgst = K()
    SHX = sb(st, nc, "SHX", [NS + 1, RWC], F32); kSHT = K()
    SHTOK = SHX
    SHT = sb(st, nc, "SHT", [128, 14, NS], F32); kSH = K()
    SSH = SHX; kSSH = kSHT
    SLD = sb(st, nc, "SLD", [64, 4, 128], F32); kSLD = K()
    SSTt = SLD; kSST = kSLD
    fw.dma("pool", WW2[0:64, :], d["w_w2"], writes=[kW])
    fw.dma("pool", WA2[64:128, :], d["w_a2"], writes=[kW])
    fw.dma("pool", WG2[:, :], d["w_g2"], writes=[kW])
    fw.dma("pool", PC[:], d["pcol"], writes=[kW])
    fw.dma("pool", GNG[:], d["gng"], writes=[kW])
    fw.dma("pool", GNB[:], d["gnb"], writes=[kW])
    fw.dma("pool", SHTOK[0:NS, :], d["sshift0"], writes=[kSHT])
    fw.op("pool", lambda e: e.memset(BLK[:], 0.0), writes=[kW])
    fw.op("pool", lambda e: e.memset(BLK[0:64, 0:64], 1.0), reads=[kW], writes=[kW])
    fw.op("pool", lambda e: e.memset(BLK[64:128, 64:128], 1.0), reads=[kW], writes=[kW])
    fw.op("pool", lambda e: e.memset(ST[:], 0.0), writes=[kST])
    fw.op("pool", lambda e: e.memset(NBrow[:], 0.0), writes=[kRow])
    fw.op("pool", lambda e: e.memset(KBrow[:], 0.0), writes=[kRow])
    fw.op("pool", lambda e: e.memset(Vrow[:], 0.0), writes=[kRow])
    fw.op("pool", lambda e: e.memset(LKc[:], 0.0), writes=[kLK])
    fw.op("pool", lambda e: e.memset(RKc[:], 0.0), writes=[kLK])
    fw.op("pool", lambda e: e.memset(PB[:, :, 0:1], 0.0), writes=kPB)
    MU, W0, A0, KKc, KAc, RKp = 0, 14, 18, 22, 26, 30
    for bz in (2, 3, 4, 5, 6, 7):
        fw.op("dve", lambda e, bz=bz: e.memset(PS[:, bz, :], 0.0), writes=[kPS[bz]])
    for m in range(14):
        transpose_to(c, PS[:, 7, m * NS:(m + 1) * NS], SHTOK[0:NS, m * 128:(m + 1) * 128], [kSHT], [kPS[7]], IDF[0:NS, 0:NS])
    fw.op("act", lambda e: e.activation(out=SHT[:].rearrange("p m b -> p (m b)"), in_=PS[:, 7, 0:14 * NS], func=AF.Identity),
          reads=[kPS[7]], writes=[kSH])
    wr_v = d["w_in"].rearrange("(kc p) n -> p kc n", p=128)
    nwr = 0
    nbk = 0

    def store_state(dst):
        for h4 in range(4):
            transpose_to(c, PS[0:64, 0, h4 * 128:(h4 + 1) * 128], ST[:, h4 * 64:(h4 + 1) * 64], [kST], [kPS[0]], IDF[:, :])
        fw.op("act", lambda e: e.activation(out=SSTt[:].rearrange("p a b -> p (a b)"), in_=PS[0:64, 0, :], func=AF.Identity),
              reads=[kPS[0]], writes=[kSST])
        fw.dma("pool", dst.rearrange("(h4 h2) i j -> i h4 h2 j", h2=2), SSTt[:].rearrange("p a (h2 j) -> p a h2 j", h2=2), reads=[kSST])

    def load_state(src):
        fw.dma("pool", SLD[:].rearrange("p a (h2 j) -> p a h2 j", h2=2), src.rearrange("(h4 h2) i j -> i h4 h2 j", h2=2), writes=[kSLD])
        for h4 in range(4):
            transpose_to(c, PS[:, 0, h4 * 64:(h4 + 1) * 64], SLD[:, h4, :], [kSLD], [kPS[0]], IDF[0:64, 0:64])
        fw.op("act", lambda e: e.activation(out=ST[:], in_=PS[:, 0, 0:256], func=AF.Identity), reads=[kPS[0]], writes=[kST])

    for ti in ([5] if 'rw1' in VAR else [0] if 'rw0' in VAR else range(NTL)):
        c0 = ti * TW
        npr = min(TW, NP - c0)
        for kc in range(KC):
            fw.op("act", lambda e, kc=kc: e.activation(out=XBt[:, kc, :], in_=XF[:, kc, c0:c0 + TW], func=AF.Identity),
                  reads=[kXF[kc][ti]], writes=[kXBt[kc]])
        for mp in range(7):
            sl = nwr % 2
            nwr += 1
            fw.dma("pool", WRb[sl][:], wr_v[:, :, 1536 + mp * 256:1536 + (mp + 1) * 256], writes=[kWRb[sl]])
            for mm in range(2):
                m = 2 * mp + mm
                bk = nbk % 2
                nbk += 1
                for kc in range(KC):
                    fw.op("pe", lambda e, kc=kc: e.matmul(PS[:, bk, 0:TW], WRb[sl][:, kc, mm * 128:(mm + 1) * 128], XBt[:, kc, :],
                                                          start=(kc == 0), stop=(kc == KC - 1)),
                          reads=[kWRb[sl], kXBt[kc]], writes=[kPS[bk]])
                fw.op("act", lambda e: e.activation(out=PB[:, m, 1:TW + 1], in_=PS[:, bk, 0:TW], func=AF.Identity),
                      reads=[kPS[bk]], writes=[kPB[m]])
        for m in range(14):
            fw.op("pool", lambda e, m=m: e.tensor_tensor(out=PM[:, m, 0:npr], in0=PB[:, m, 0:npr], in1=PB[:, m, 1:npr + 1], op=ALU.subtract),
                  reads=[kPB[m]], writes=[kPM[m]])
            if npr < TW:
                fw.op("pool", lambda e, m=m: e.tensor_tensor(out=PM[:, m, npr:TW], in0=SHT[:, m, :], in1=PB[:, m, npr + 1:TW + 1], op=ALU.subtract),
                      reads=[kPB[m], kSH], writes=[kPM[m]])
            fw.op("dve", lambda e, m=m: e.scalar_tensor_tensor(out=PM[:, m, :], in0=PM[:, m, :], scalar=PC[:, MU + m:MU + m + 1],
                                                               in1=PB[:, m, 1:TW + 1], op0=ALU.mult, op1=ALU.add),
                  reads=[kPB[m], kPM[m], kW], writes=[kPM[m]])
        if npr < TW:
            for m in range(14):
                transpose_to(c, PS[0:NS + 1, 7, (m % 4) * 128:(m % 4 + 1) * 128], PB[:, m, npr:TW + 1], [kPB[m]], [kPS[7]], IDF[:, :])
                if m % 4 == 3 or m == 13:
                    m0 = (m // 4) * 4
                    fw.op("act", lambda e, m0=m0, m=m: e.activation(out=SSH[:, m0 * 128:(m + 1) * 128], in_=PS[0:NS + 1, 7, 0:(m - m0 + 1) * 128], func=AF.Identity),
                          reads=[kPS[7]], writes=[kSSH])
            fw.dma("pool", d["pshift"], SSH[0:1, :], reads=[kSSH])
            fw.dma("pool", d["sshift"], SSH[1:NS + 1, :], reads=[kSSH])
        fw.op("dve", lambda e: e.tensor_copy(out=PB[:, :, 0:1], in_=PB[:, :, TW:TW + 1]), reads=kPB, writes=kPB)
        Wt = lambda cc: PB[:, cc, 1:TW + 1]
        KKt = lambda cc: PB[:, 4 + cc, 1:TW + 1]
        NBt = lambda cc: PB[:, 8 + cc, 1:TW + 1]
        fw.op("act", lambda e: e.activation(out=tmpH[0:64, :], in_=PM[0:64, 12, :], func=AF.Tanh), reads=[kPM[12]], writes=[ktH])
        fw.op("act", lambda e: e.activation(out=tmpH[64:128, :], in_=PM[64:128, 12, :], func=AF.Identity), reads=[kPM[12]], writes=[ktH])
        for cc in range(4):
            bk = nbk % 2
            nbk += 1
            fw.op("pe", lambda e: e.matmul(PS[:, bk, 0:TW], WW2[0:64, cc * 128:(cc + 1) * 128], tmpH[0:64, :], start=True, stop=True),
                  reads=[kW, ktH], writes=[kPS[bk]])
            fw.op("act", lambda e: e.activation(out=tmpA[:], in_=PS[:, bk, 0:TW], func=AF.Sigmoid, bias=PC[:, W0 + cc:W0 + cc + 1]),
                  reads=[kPS[bk], kW], writes=[ktA])
            fw.op("act", lambda e: e.activation(out=Wt(cc), in_=tmpA[:], func=AF.Exp, scale=-0.6065306597126334),
                  reads=[ktA], writes=[kPB[cc]])
        for cc in range(4):
            bk = nbk % 2
            nbk += 1
            fw.op("pe", lambda e: e.matmul(PS[:, bk, 0:TW], WA2[64:128, cc * 128:(cc + 1) * 128], tmpH[64:128, :], start=True, stop=True),
                  reads=[kW, ktH], writes=[kPS[bk]])
            fw.op("act", lambda e: e.activation(out=tmpA[:], in_=PS[:, bk, 0:TW], func=AF.Sigmoid, bias=PC[:, A0 + cc:A0 + cc + 1]),
                  reads=[kPS[bk], kW], writes=[ktA])
            fw.op("dve", lambda e: e.tensor_scalar(out=KKt(cc), in0=PM[:, 4 + cc, :], scalar1=PC[:, KKc + cc:KKc + cc + 1], scalar2=None, op0=ALU.mult),
                  reads=[kPM[4 + cc], kW], writes=[kPB[4 + cc]])
            fw.op("act", lambda e: e.activation(out=tmpB[:], in_=KKt(cc), func=AF.Square), reads=[kPB[4 + cc]], writes=[ktB])
            b2 = 2 + (nbk % 2)
            fw.op("pe", lambda e: e.matmul(PS[:, b2, 0:TW], BLK[:], tmpB[:], start=True, stop=True), reads=[kW, ktB], writes=[kPS[b2]])
            fw.op("dve", lambda e: e.tensor_scalar(out=tmpB[:], in0=PS[:, b2, 0:TW], scalar1=1e-24, scalar2=None, op0=ALU.max),
                  reads=[kPS[b2]], writes=[ktB])
            fw.op("act", lambda e: e.activation(out=tmpB[:], in_=tmpB[:], func=AF.Sqrt), reads=[ktB], writes=[ktB])
            fw.op("dve", lambda e: e.reciprocal(out=tmpB[:], in_=tmpB[:]), reads=[ktB], writes=[ktB])
            fw.op("dve", lambda e: e.tensor_tensor(out=KKt(cc), in0=KKt(cc), in1=tmpB[:], op=ALU.mult), reads=[kPB[4 + cc], ktB], writes=[kPB[4 + cc]])
            fw.op("dve", lambda e: e.scalar_tensor_tensor(out=NBt(cc), in0=KKt(cc), scalar=-1.0, in1=tmpA[:], op0=ALU.mult, op1=ALU.mult),
                  reads=[kPB[4 + cc], ktA], writes=[kPB[8 + cc]])
            fw.op("dve", lambda e: e.tensor_scalar(out=tmpA[:], in0=tmpA[:], scalar1=-1.0, scalar2=PC[:, KAc + cc:KAc + cc + 1], op0=ALU.add, op1=ALU.mult),
                  reads=[ktA, kW], writes=[ktA])
            fw.op("dve", lambda e: e.scalar_tensor_tensor(out=PM[:, 4 + cc, :], in0=tmpA[:], scalar=1.0, in1=PM[:, 4 + cc, :], op0=ALU.add, op1=ALU.mult),
                  reads=[ktA, kPM[4 + cc]], writes=[kPM[4 + cc]])
        if 'rwA' in VAR:
            continue
        for a in range(0, TW, CS):
            ncol = min(CS, TW - a)
            for vi, src in enumerate((lambda cc: PM[:, 4 + cc, a:a + ncol], lambda cc: NBt(cc)[:, a:a + ncol], lambda cc: PM[:, 8 + cc, a:a + ncol])):
                kk_ = (lambda cc: kPM[4 + cc], lambda cc: kPB[8 + cc], lambda cc: kPM[8 + cc])[vi]
                bk = nbk % 2
                nbk += 1
                for cc in range(4):
                    transpose_to(c, PS[0:ncol, bk, cc * 128:(cc + 1) * 128], src(cc), [kk_(cc)], [kPS[bk]], IDF[:, :])
                fw.op("act", lambda e: e.activation(out=TOK[0:ncol, vi, :], in_=PS[0:ncol, bk, :], func=AF.Identity), reads=[kPS[bk]], writes=[kTOK[vi]])
            fw.dma("pool", c.TOKd[0:ncol], TOK[0:ncol, :, :], reads=kTOK, writes=[c.kTOKd])
            for h2 in range(2):
                for vi, dstt in enumerate((KBrow, NBrow)):
                    fw.dma("pool", dstt[h2:128:32, 0:ncol, h2 * 64:(h2 + 1) * 64],
                           c.TOKd[0:ncol, vi, :].rearrange("s (h4 h2 j) -> h4 s h2 j", h4=4, h2=2)[:, :, h2, :], reads=[c.kTOKd], writes=[kRow])
                fw.dma("pool", Vrow[h2:128:32, 0:ncol, :],
                       c.TOKd[0:ncol, 2, :].rearrange("s (h4 h2 j) -> h4 s h2 j", h4=4, h2=2)[:, :, h2, :], reads=[c.kTOKd], writes=[kRow])
            for h2 in range(2):
                ps_ = slice(h2 * 64, (h2 + 1) * 64)
                fw.op("pool", lambda e: e.tensor_copy(out=LKc[ps_, 0:ncol, :, h2], in_=PB[ps_, 4:8, 1 + a:1 + a + ncol].rearrange("p c s -> p s c")),
                      reads=kPB[4:8], writes=[kLK])
                fw.op("pool", lambda e: e.tensor_copy(out=RKc[ps_, 0:ncol, :, h2], in_=PM[ps_, 0:4, a:a + ncol].rearrange("p c s -> p s c")),
                      reads=kPM[0:4], writes=[kLK])
            for s_ in range(ncol if 'rwB' not in VAR else 0):
                gcol = c0 + a + s_
                is_sample = gcol >= NP
                if is_sample:
                    load_state(d["swkv0"][gcol - NP])
                for h4 in range(4):
                    fw.op("pe", lambda e, h4=h4: e.matmul(PS[32 * h4:32 * h4 + 2, 2, 0:64], LKc[:, s_, h4, :], ST[:, h4 * 64:(h4 + 1) * 64],
                                                          start=True, stop=True, tile_position=(0, 32 * h4)),
                          reads=[kLK, kST], writes=[kPS[2]])
                if 'noSTw' not in VAR:
                    fw.op("pool", lambda e: e.tensor_tensor(out=STw[:].rearrange("p (a b) -> p a b", a=4), in0=ST[:].rearrange("p (a b) -> p a b", a=4),
                                                            in1=PB[:, 0:4, 1 + a + s_:2 + a + s_].to_broadcast([128, 4, 64]), op=ALU.mult),
                          reads=[kST] + kPB[0:4], writes=[kSTw])
                else:
                    fw.op("pool", lambda e: e.tensor_copy(out=STw[:], in_=ST[:]), reads=[kST], writes=[kSTw])
                for h4 in range(4 if 'noPE2' not in VAR else 0):
                    fw.op("pe", lambda e, h4=h4: e.matmul(PS[:, 3 + h4, 0:64], KBrow[32 * h4:32 * h4 + 2, s_, :], Vrow[32 * h4:32 * h4 + 2, s_, :],
                                                          start=True, stop=False, tile_position=(32 * h4, 0)),
                          reads=[kRow], writes=[kPS[3 + h4]])
                fw.op("act", lambda e: e.activation(out=SAY[:], in_=PS[:, 2, 0:64], func=AF.Identity), reads=[kPS[2]], writes=[kSAY])
                for h4 in range(4 if 'noPE2' not in VAR else 0):
                    fw.op("pe", lambda e, h4=h4: e.matmul(PS[:, 3 + h4, 0:64], NBrow[32 * h4:32 * h4 + 2, s_, :], SAY[32 * h4:32 * h4 + 2, :],
                                                          start=False, stop=True, tile_position=(32 * h4, 0)),
                          reads=[kRow, kSAY], writes=[kPS[3 + h4]])
                fw.op("dve", lambda e: e.tensor_tensor(out=ST[:].rearrange("p (a b) -> p a b", a=4), in0=STw[:].rearrange("p (a b) -> p a b", a=4),
                                                       in1=PS[:, 3:7, 0:64], op=ALU.add),
                      reads=[kSTw] + kPS[3:7], writes=[kST])
                yb = s_ % 8
                for h4 in range(4 if 'noY' not in VAR else 0):
                    fw.op("pe", lambda e, h4=h4: e.matmul(PS[32 * h4:32 * h4 + 2, 7, yb * 64:(yb + 1) * 64], RKc[:, s_, h4, :], ST[:, h4 * 64:(h4 + 1) * 64],
                                                          start=True, stop=True, tile_position=(0, 32 * h4)),
                          reads=[kLK, kST], writes=[kPS[7]])
                if (yb == 7 or s_ == ncol - 1) and 'noY' not in VAR:
                    s0_ = s_ - yb
                    fw.op("act", lambda e: e.activation(out=Ybuf[:, s0_:s_ + 1, :].rearrange("p s i -> p (s i)"), in_=PS[:, 7, 0:(yb + 1) * 64], func=AF.Identity),
                          reads=[kPS[7]], writes=[kY])
                if is_sample:
                    store_state(d["swkv"][gcol - NP])
                if gcol == NP - 1:
                    store_state(d["pwkv"])
            if 'rwB' in VAR or 'rwC' in VAR:
                continue
            for h2 in range(2):
                fw.dma("pool", c.Yd[:, h2, 0:ncol, :], Ybuf[h2:128:32, 0:ncol, :], reads=[kY], writes=[c.kYd])
            fw.dma("pool", YTOK[0:ncol, :].rearrange("s (h i) -> s h i", h=8), c.Yd[:, :, 0:ncol, :].rearrange("a b s i -> s (a b) i"),
                   reads=[c.kYd], writes=[kYT])
            Y3 = YTOK[0:ncol, :].rearrange("s (h i) -> s h i", h=8)
            C3 = YC[0:ncol, :].rearrange("s (h i) -> s h i", h=8)
            S3 = YS[0:ncol, :].rearrange("s (h i) -> s h i", h=8)
            fw.op("dve", lambda e: e.tensor_reduce(out=gst[0:ncol, 0:8], in_=Y3, axis=AX.X, op=ALU.add), reads=[kYT], writes=[kgst])
            fw.op("dve", lambda e: e.tensor_scalar(out=gst[0:ncol, 0:8], in0=gst[0:ncol, 0:8], scalar1=1.0 / 64, scalar2=None, op0=ALU.mult), reads=[kgst], writes=[kgst])
            fw.op("dve", lambda e: e.tensor_tensor(out=C3, in0=Y3, in1=gst[0:ncol, 0:8].unsqueeze(2).to_broadcast([ncol, 8, 64]), op=ALU.subtract),
                  reads=[kYT, kgst], writes=[kYC])
            fw.op("act", lambda e: e.activation(out=YS[0:ncol, :], in_=YC[0:ncol, :], func=AF.Square), reads=[kYC], writes=[kYS])
            fw.op("dve", lambda e: e.tensor_reduce(out=gst[0:ncol, 8:16], in_=S3, axis=AX.X, op=ALU.add), reads=[kYS], writes=[kgst])
            fw.op("act", lambda e: e.activation(out=gst[0:ncol, 8:16], in_=gst[0:ncol, 8:16], func=AF.Sqrt, scale=1.0 / 64, bias=c.cst[0:ncol, 2:3]),
                  reads=[kgst, c.kconst], writes=[kgst])
            fw.op("dve", lambda e: e.reciprocal(out=gst[0:ncol, 8:16], in_=gst[0:ncol, 8:16]), reads=[kgst], writes=[kgst])
            fw.op("dve", lambda e: e.tensor_tensor(out=C3, in0=C3, in1=gst[0:ncol, 8:16].unsqueeze(2).to_broadcast([ncol, 8, 64]), op=ALU.mult),
                  reads=[kYC, kgst], writes=[kYC])
            fw.op("pool", lambda e: e.tensor_tensor(out=YC[0:ncol, :], in0=YC[0:ncol, :], in1=GNG[0:ncol, :], op=ALU.mult), reads=[kYC, kW], writes=[kYC])
            fw.op("pool", lambda e: e.tensor_tensor(out=YC[0:ncol, :], in0=YC[0:ncol, :], in1=GNB[0:ncol, :], op=ALU.add), reads=[kYC, kW], writes=[kYC])
            bk = 2
            for cc in range(4):
                transpose_to(c, PS[:, bk, cc * 64:cc * 64 + ncol], YC[0:ncol, cc * 128:(cc + 1) * 128], [kYC], [kPS[bk]], IDF[0:ncol, 0:ncol])
            for cc in range(4):
                fw.op("dve", lambda e: e.scalar_tensor_tensor(out=tmpA[:, 0:ncol], in0=PM[:, cc, a:a + ncol], scalar=PC[:, RKp + cc:RKp + cc + 1],
                                                              in1=PM[:, 4 + cc, a:a + ncol], op0=ALU.mult, op1=ALU.mult),
                      reads=[kPM[cc], kPM[4 + cc], kW], writes=[ktA])
                b2 = nbk % 2
                nbk += 1
                fw.op("pe", lambda e: e.matmul(PS[:, b2, 0:ncol], BLK[:], tmpA[:, 0:ncol], start=True, stop=True), reads=[kW, ktA], writes=[kPS[b2]])
                fw.op("dve", lambda e: e.tensor_tensor(out=tmpB[:, 0:ncol], in0=PS[:, b2, 0:ncol], in1=PM[:, 8 + cc, a:a + ncol], op=ALU.mult),
                      reads=[kPS[b2], kPM[8 + cc]], writes=[ktB])
                fw.op("dve", lambda e: e.tensor_tensor(out=tmpB[:, 0:ncol], in0=tmpB[:, 0:ncol], in1=PS[:, bk, cc * 64:cc * 64 + ncol], op=ALU.add),
                      reads=[ktB, kPS[bk]], writes=[ktB])
                if cc == 0:
                    fw.op("act", lambda e: e.activation(out=tmpH[:, 0:ncol], in_=PM[:, 13, a:a + ncol], func=AF.Sigmoid), reads=[kPM[13]], writes=[ktH])
                b3 = nbk % 2
                nbk += 1
                fw.op("pe", lambda e: e.matmul(PS[:, b3, 0:ncol], WG2[:, cc * 128:(cc + 1) * 128], tmpH[:, 0:ncol], start=True, stop=True),
                      reads=[kW, ktH], writes=[kPS[b3]])
                fw.op("dve", lambda e: e.tensor_tensor(out=ORW[:, cc, c0 + a:c0 + a + ncol], in0=tmpB[:, 0:ncol], in1=PS[:, b3, 0:ncol], op=ALU.mult),
                      reads=[ktB, kPS[b3]], writes=[kORW[cc]])


TWO_PI = 6.283185307179586
C1 = 6.28125
C2 = TWO_PI - 6.28125
PI = 3.141592653589793


def trig_tables(c, X, kX, Sout, Cout, kS, kC, tmpI, tmpF, ktmp, width):
    fw = c.fw
    w = slice(0, width)
    fw.op("dve", lambda e: e.tensor_scalar(out=tmpI[:, w], in0=X[:, w], scalar1=1.0 / TWO_PI, scalar2=None, op0=ALU.mult), reads=[kX], writes=[ktmp])
    fw.op("dve", lambda e: e.tensor_copy(out=tmpF[:, w], in_=tmpI[:, w]), reads=[ktmp], writes=[ktmp])
    fw.op("dve", lambda e: e.scalar_tensor_tensor(out=X[:, w], in0=tmpF[:, w], scalar=-C1, in1=X[:, w], op0=ALU.mult, op1=ALU.add), reads=[ktmp, kX], writes=[kX])
    fw.op("dve", lambda e: e.scalar_tensor_tensor(out=X[:, w], in0=tmpF[:, w], scalar=-C2, in1=X[:, w], op0=ALU.mult, op1=ALU.add), reads=[ktmp, kX], writes=[kX])
    fw.op("dve", lambda e: e.tensor_scalar(out=X[:, w], in0=X[:, w], scalar1=PI, scalar2=-PI, op0=ALU.min, op1=ALU.max), reads=[kX], writes=[kX])
    fw.op("act", lambda e: e.activation(out=Sout, in_=X[:, w], func=AF.Sin), reads=[kX], writes=[kS])
    fw.op("act", lambda e: e.activation(out=tmpF[:, w], in_=X[:, w], func=AF.Abs), reads=[kX, ktmp], writes=[ktmp])
    fw.op("act", lambda e: e.activation(out=Cout, in_=tmpF[:, w], func=AF.Sin, scale=-1.0, bias=c.cst[:, 3:4]), reads=[ktmp, c.kconst], writes=[kC])


def s5_mixer(c, d):
    nc, fw = c.nc, c.fw
    XF, kXF, PS, kPS = c.XF, c.kXF, c.PS, c.kPS
    IDF = c.IDF
    with ExitStack() as st:
        ZG = sb(st, nc, "ZG", [128, KC, NT], BF16); kZG = [[K() for _ in range(NTL)] for _ in range(KC)]
        with ExitStack() as s2:
            XB = sb(s2, nc, "XBs", [128, KC, NT], BF16); kXB = [K() for _ in range(KC)]
            for kc in range(KC):
                fw.op("act", lambda e, kc=kc: e.activation(out=XB[:, kc, :], in_=XF[:, kc, :], func=AF.Identity), reads=kXF[kc], writes=[kXB[kc]])
            PRM = sb(s2, nc, "PRM", [128, 3, 32], F32); kP = K()
            fw.dma("sp", PRM[:], d["s5prm"], writes=[kP])
            DSK = sb(s2, nc, "DSK", [128, 8], F32)
            fw.dma("sp", DSK[:], d["dskip"], writes=[kP])
            S0 = sb(s2, nc, "S0", [128, 32, NS, 2], F32); kS0 = K()
            for t_ in range(32):
                fw.dma("sp", S0[:, t_, :, :], d["s5_0"][:, t_ * 128:(t_ + 1) * 128, :].rearrange("b p r -> p b r"), writes=[kS0])
            cn = {nm: sb(s2, nc, "c_" + nm, [128, 32], F32) for nm in ("DT", "MAGL", "TH", "MAG", "X", "SI", "CO", "AR", "AI", "CR", "CI", "T1", "T2", "RD")}
            tI = sb(s2, nc, "tI32", [128, TW], I32)
            tF = sb(s2, nc, "tF32", [128, TW], F32)
            ktmp = K()
            LR, LI, LDT = PRM[:, 0, :], PRM[:, 1, :], PRM[:, 2, :]
            kc_ = K()

            def o(eng, fn, r=(), w=()):
                fw.op(eng, fn, reads=[kP, kc_] + list(r), writes=[kc_] + list(w))
            o("act", lambda e: e.activation(out=cn["DT"][:], in_=LDT, func=AF.Exp))
            o("dve", lambda e: e.tensor_tensor(out=cn["MAGL"][:], in0=LR, in1=cn["DT"][:], op=ALU.mult))
            o("dve", lambda e: e.tensor_tensor(out=cn["TH"][:], in0=LI, in1=cn["DT"][:], op=ALU.mult))
            o("act", lambda e: e.activation(out=cn["MAG"][:], in_=cn["MAGL"][:], func=AF.Exp))
            o("dve", lambda e: e.tensor_copy(out=cn["X"][:], in_=cn["TH"][:]))
            trig_tables(c, cn["X"], kc_, cn["SI"][:], cn["CO"][:], kc_, kc_, tI, tF, ktmp, 32)
            o("dve", lambda e: e.tensor_tensor(out=cn["AR"][:], in0=cn["MAG"][:], in1=cn["CO"][:], op=ALU.mult))
            o("dve", lambda e: e.tensor_tensor(out=cn["AI"][:], in0=cn["MAG"][:], in1=cn["SI"][:], op=ALU.mult))
            o("dve", lambda e: e.tensor_tensor(out=cn["T1"][:], in0=LR, in1=LR, op=ALU.mult))
            o("dve", lambda e: e.tensor_tensor(out=cn["T2"][:], in0=LI, in1=LI, op=ALU.mult))
            o("dve", lambda e: e.tensor_tensor(out=cn["T1"][:], in0=cn["T1"][:], in1=cn["T2"][:], op=ALU.add))
            o("dve", lambda e: e.reciprocal(out=cn["RD"][:], in_=cn["T1"][:]))
            o("dve", lambda e: e.tensor_scalar(out=cn["T1"][:], in0=cn["AR"][:], scalar1=-1.0, scalar2=None, op0=ALU.add))
            o("dve", lambda e: e.tensor_tensor(out=cn["CR"][:], in0=cn["T1"][:], in1=LR, op=ALU.mult))
            o("dve", lambda e: e.tensor_tensor(out=cn["T2"][:], in0=cn["AI"][:], in1=LI, op=ALU.mult))
            o("dve", lambda e: e.tensor_tensor(out=cn["CR"][:], in0=cn["CR"][:], in1=cn["T2"][:], op=ALU.add))
            o("dve", lambda e: e.tensor_tensor(out=cn["CR"][:], in0=cn["CR"][:], in1=cn["RD"][:], op=ALU.mult))
            o("dve", lambda e: e.tensor_tensor(out=cn["CI"][:], in0=cn["AI"][:], in1=LR, op=ALU.mult))
            o("dve", lambda e: e.tensor_tensor(out=cn["T2"][:], in0=cn["T1"][:], in1=LI, op=ALU.mult))
            o("dve", lambda e: e.tensor_tensor(out=cn["CI"][:], in0=cn["CI"][:], in1=cn["T2"][:], op=ALU.subtract))
            o("dve", lambda e: e.tensor_tensor(out=cn["CI"][:], in0=cn["CI"][:], in1=cn["RD"][:], op=ALU.mult))
            IOT = sb(s2, nc, "IOT", [128, TW], F32)
            o("pool", lambda e: e.iota(IOT[:], [[1, TW]], base=1, channel_multiplier=0, allow_small_or_imprecise_dtypes=True))
            TB = [[sb(s2, nc, "TB%d_%d" % (k, j), [128, TW], F32) for j in range(4)] for k in range(4)]
            kTB = [[K() for _ in range(4)] for _ in range(4)]
            XA = sb(s2, nc, "XA", [128, TW], F32); kXA = K()
            Pt = [sb(s2, nc, "Pt%d" % i, [128, TW], F32) for i in range(6)]; kPt = [K() for _ in range(6)]
            Qr = sb(s2, nc, "Qr", [128, TW], F32); Qi = sb(s2, nc, "Qi", [128, TW], F32); kQ = [K(), K()]
            S16 = [sb(s2, nc, "S16_%d" % i, [128, TW], BF16) for i in range(2)]; kS16 = [K(), K()]
            BCm = sb(s2, nc, "BCm", [128, 4, 4, 128], BF16); kBC = K()
            SE = sb(s2, nc, "SE", [128, 32, 2], F32); kSE = K()
            SN = sb(s2, nc, "SN", [128, 2, NS], F32); kSN = K()
            OUT16 = [sb(s2, nc, "OUT16_%d" % i, [NS, 256], F32) for i in range(2)]; kO16 = [K(), K()]
            RHO = sb(s2, nc, "RHO", [128, TW], F32); kRHO = K()
            fw.op("pool", lambda e: e.memset(SE[:], 0.0), writes=[kSE])
            fw.op("pool", lambda e: e.memset(RHO[:], 1.0), writes=[kRHO])
            nb = 0
            for m in range(KC):
                fw.dma("pool", BCm[:].rearrange("p a b n -> p (a b) n"), d["s5bc"][m].rearrange("a p n -> p a n"), writes=[kBC])
                for k in range(4):
                    stn = 4 * m + k
                    col = slice(stn, stn + 1)
                    TR, TI, CC, SS = TB[k]
                    fw.op("dve", lambda e: e.tensor_scalar(out=XA[:], in0=IOT[:], scalar1=cn["TH"][:, col], scalar2=None, op0=ALU.mult),
                          reads=[kc_], writes=[kXA])
                    trig_tables(c, XA, kXA, SS[:], CC[:], kTB[k][3], kTB[k][2], tI, tF, ktmp, TW)
                    fw.op("dve", lambda e: e.tensor_scalar(out=TR[:], in0=CC[:], scalar1=cn["CR"][:, col], scalar2=None, op0=ALU.mult), reads=[kTB[k][2], kc_], writes=[kTB[k][0]])
                    fw.op("dve", lambda e: e.scalar_tensor_tensor(out=TR[:], in0=SS[:], scalar=cn["CI"][:, col], in1=TR[:], op0=ALU.mult, op1=ALU.add), reads=[kTB[k][3], kTB[k][0], kc_], writes=[kTB[k][0]])
                    fw.op("dve", lambda e: e.tensor_scalar(out=TI[:], in0=CC[:], scalar1=cn["CI"][:, col], scalar2=None, op0=ALU.mult), reads=[kTB[k][2], kc_], writes=[kTB[k][1]])
                    fw.op("dve", lambda e: e.tensor_scalar(out=XA[:], in0=SS[:], scalar1=cn["CR"][:, col], scalar2=None, op0=ALU.mult), reads=[kTB[k][3], kc_], writes=[kXA])
                    fw.op("dve", lambda e: e.tensor_tensor(out=TI[:], in0=TI[:], in1=XA[:], op=ALU.subtract), reads=[kTB[k][1], kXA], writes=[kTB[k][1]])
                for ti in range(NTL):
                    c0 = ti * TW
                    npr = min(TW, NP - c0)
                    yb = 6 + (ti % 2)
                    for k in range(4):
                        stn = 4 * m + k
                        col = slice(stn, stn + 1)
                        TR, TI, CC, SS = TB[k]
                        br, bi = 2 * (nb % 2), 2 * (nb % 2) + 1
                        nb += 1
                        fw.op("pe", lambda e: e.matmul(PS[:, br, 0:TW], BCm[:, k, 0, :], XB[:, m, c0:c0 + TW], start=True, stop=True), reads=[kBC, kXB[m]], writes=[kPS[br]])
                        fw.op("pe", lambda e: e.matmul(PS[:, bi, 0:TW], BCm[:, k, 1, :], XB[:, m, c0:c0 + TW], start=True, stop=True), reads=[kBC, kXB[m]], writes=[kPS[bi]])
                        w = slice(0, npr)
                        fw.op("pool", lambda e: e.tensor_scalar(out=RHO[:, w], in0=IOT[:, w], scalar1=0.0, scalar2=cn["MAG"][:, col], op0=ALU.mult, op1=ALU.add),
                              reads=[kc_], writes=[kRHO])
                        fw.op("dve", lambda e: e.tensor_tensor(out=Pt[0][:, w], in0=PS[:, br, w], in1=TR[:, w], op=ALU.mult), reads=[kPS[br], kTB[k][0]], writes=[kPt[0]])
                        fw.op("dve", lambda e: e.tensor_tensor(out=Pt[1][:, w], in0=PS[:, bi, w], in1=TI[:, w], op=ALU.mult), reads=[kPS[bi], kTB[k][1]], writes=[kPt[1]])
                        fw.op("dve", lambda e: e.tensor_tensor(out=Pt[2][:, w], in0=PS[:, bi, w], in1=TR[:, w], op=ALU.mult), reads=[kPS[bi], kTB[k][0]], writes=[kPt[2]])
                        fw.op("dve", lambda e: e.tensor_tensor(out=Pt[3][:, w], in0=PS[:, br, w], in1=TI[:, w], op=ALU.mult), reads=[kPS[br], kTB[k][1]], writes=[kPt[3]])
                        fw.op("pool", lambda e: e.tensor_tensor(out=Pt[0][:, w], in0=Pt[0][:, w], in1=Pt[1][:, w], op=ALU.subtract), reads=[kPt[0], kPt[1]], writes=[kPt[0]])
                        fw.op("pool", lambda e: e.tensor_tensor(out=Pt[2][:, w], in0=Pt[2][:, w], in1=Pt[3][:, w], op=ALU.add), reads=[kPt[2], kPt[3]], writes=[kPt[2]])
                        fw.op("dve", lambda e: e.tensor_tensor_scan(Qr[:, w], RHO[:, w], Pt[0][:, w], SE[:, stn, 0:1], ALU.mult, ALU.add), reads=[kRHO, kPt[0], kSE], writes=[kQ[0]])
                        fw.op("dve", lambda e: e.tensor_tensor_scan(Qi[:, w], RHO[:, w], Pt[2][:, w], SE[:, stn, 1:2], ALU.mult, ALU.add), reads=[kRHO, kPt[2], kSE], writes=[kQ[1]])
                        fw.op("pool", lambda e: e.tensor_tensor(out=Pt[4][:, w], in0=CC[:, w], in1=Qr[:, w], op=ALU.mult), reads=[kTB[k][2], kQ[0]], writes=[kPt[4]])
                        fw.op("pool", lambda e: e.tensor_tensor(out=Pt[5][:, w], in0=SS[:, w], in1=Qi[:, w], op=ALU.mult), reads=[kTB[k][3], kQ[1]], writes=[kPt[5]])
                        fw.op("dve", lambda e: e.tensor_tensor(out=S16[0][:, w], in0=Pt[4][:, w], in1=Pt[5][:, w], op=ALU.subtract), reads=[kPt[4], kPt[5]], writes=[kS16[0]])
                        fw.op("pool", lambda e: e.tensor_tensor(out=Pt[1][:, w], in0=SS[:, w], in1=Qr[:, w], op=ALU.mult), reads=[kTB[k][3], kQ[0], kPt[1]], writes=[kPt[1]])
                        fw.op("pool", lambda e: e.tensor_tensor(out=Pt[3][:, w], in0=CC[:, w], in1=Qi[:, w], op=ALU.mult), reads=[kTB[k][2], kQ[1], kPt[3]], writes=[kPt[3]])
                        fw.op("dve", lambda e: e.scalar_tensor_tensor(out=S16[1][:, w], in0=Pt[1][:, w], scalar=-1.0, in1=Pt[3][:, w], op0=ALU.mult, op1=ALU.subtract),
                              reads=[kPt[1], kPt[3]], writes=[kS16[1]])
                        L = npr - 1
                        fw.op("dve", lambda e: e.tensor_tensor(out=SE[:, stn, 0:1], in0=Pt[4][:, L:L + 1], in1=Pt[5][:, L:L + 1], op=ALU.subtract), reads=[kPt[4], kPt[5], kSE], writes=[kSE])
                        fw.op("dve", lambda e: e.tensor_tensor(out=SE[:, stn, 1:2], in0=Pt[1][:, L:L + 1], in1=Pt[3][:, L:L + 1], op=ALU.add), reads=[kPt[1], kPt[3], kSE], writes=[kSE])
                        if npr < TW:
                            ws = slice(npr, TW)
                            fw.op("dve", lambda e: e.tensor_scalar(out=Pt[0][:, ws], in0=PS[:, bi, ws], scalar1=cn["CI"][:, col], scalar2=None, op0=ALU.mult), reads=[kPS[bi], kc_, kPt[0]], writes=[kPt[0]])
                            fw.op("dve", lambda e: e.scalar_tensor_tensor(out=Pt[0][:, ws], in0=PS[:, br, ws], scalar=cn["CR"][:, col], in1=Pt[0][:, ws], op0=ALU.mult, op1=ALU.subtract), reads=[kPS[br], kPt[0], kc_], writes=[kPt[0]])
                            fw.op("dve", lambda e: e.tensor_scalar(out=Pt[2][:, ws], in0=PS[:, br, ws], scalar1=cn["CI"][:, col], scalar2=None, op0=ALU.mult), reads=[kPS[br], kc_, kPt[2]], writes=[kPt[2]])
                            fw.op("dve", lambda e: e.scalar_tensor_tensor(out=Pt[2][:, ws], in0=PS[:, bi, ws], scalar=cn["CR"][:, col], in1=Pt[2][:, ws], op0=ALU.mult, op1=ALU.add), reads=[kPS[bi], kPt[2], kc_], writes=[kPt[2]])
                            s0r, s0i = S0[:, stn, :, 0], S0[:, stn, :, 1]
                            fw.op("dve", lambda e: e.scalar_tensor_tensor(out=Pt[0][:, ws], in0=s0r, scalar=cn["AR"][:, col], in1=Pt[0][:, ws], op0=ALU.mult, op1=ALU.add), reads=[kS0, kPt[0], kc_], writes=[kPt[0]])
                            fw.op("dve", lambda e: e.tensor_scalar(out=Pt[1][:, ws], in0=s0i, scalar1=cn["AI"][:, col], scalar2=None, op0=ALU.mult), reads=[kS0, kc_, kPt[1]], writes=[kPt[1]])
                            fw.op("dve", lambda e: e.tensor_tensor(out=SN[:, 0, :], in0=Pt[0][:, ws], in1=Pt[1][:, ws], op=ALU.subtract), reads=[kPt[0], kPt[1]], writes=[kSN])
                            fw.op("dve", lambda e: e.scalar_tensor_tensor(out=Pt[2][:, ws], in0=s0i, scalar=cn["AR"][:, col], in1=Pt[2][:, ws], op0=ALU.mult, op1=ALU.add), reads=[kS0, kPt[2], kc_], writes=[kPt[2]])
                            fw.op("dve", lambda e: e.scalar_tensor_tensor(out=SN[:, 1, :], in0=s0r, scalar=cn["AI"][:, col], in1=Pt[2][:, ws], op0=ALU.mult, op1=ALU.add), reads=[kS0, kPt[2], kc_], writes=[kSN])
                            fw.op("act", lambda e: e.activation(out=S16[0][:, ws], in_=SN[:, 0, :], func=AF.Identity), reads=[kSN, kS16[0]], writes=[kS16[0]])
                            fw.op("act", lambda e: e.activation(out=S16[1][:, ws], in_=SN[:, 1, :], func=AF.Identity, scale=-1.0), reads=[kSN, kS16[1]], writes=[kS16[1]])
                            for r_ in range(2):
                                transpose_to(c, PS[0:NS, 5, r_ * 128:(r_ + 1) * 128], SN[:, r_, :], [kSN], [kPS[5]], IDF[:, :])
                            oi = stn % 2
                            fw.op("act", lambda e: e.activation(out=OUT16[oi][:, :].rearrange("b (p r) -> b r p", r=2),
                                                                in_=PS[0:NS, 5, 0:256].rearrange("b (r p) -> b r p", r=2), func=AF.Identity),
                                  reads=[kPS[5]], writes=[kO16[oi]])
                            fw.dma("sp", d["ss5"][:, stn * 256:(stn + 1) * 256], OUT16[oi][:, :], reads=[kO16[oi]])
                        fw.op("pe", lambda e: e.matmul(PS[:, yb, 0:TW], BCm[:, k, 2, :], S16[0][:], start=(k == 0), stop=False), reads=[kBC, kS16[0]], writes=[kPS[yb]])
                        fw.op("pe", lambda e: e.matmul(PS[:, yb, 0:TW], BCm[:, k, 3, :], S16[1][:], start=False, stop=(k == 3)), reads=[kBC, kS16[1]], writes=[kPS[yb]])
                    fw.op("dve", lambda e: e.scalar_tensor_tensor(out=XA[:], in0=XF[:, m, c0:c0 + TW], scalar=DSK[:, m:m + 1], in1=PS[:, yb, 0:TW], op0=ALU.mult, op1=ALU.add),
                          reads=[kXF[m][ti], kPS[yb], kP, kXA], writes=[kXA])
                    fw.op("act", lambda e: e.activation(out=ZG[:, m, c0:c0 + TW], in_=XA[:], func=AF.Gelu), reads=[kXA], writes=[kZG[m][ti]])
            fw.dma("sp", d["ps5"].rearrange("(t p) r -> p t r", p=128), SE[:], reads=[kSE])
            fw.barrier()
        with ExitStack() as s3:
            alloc_ln(c, s3)
            WGb = [[sb(s3, nc, "WG%d_%d" % (a, i), [128, KC, 256], BF16) for i in range(2)] for a in range(2)]
            kWG = [[K(), K()], [K(), K()]]
            sgl = [sb(s3, nc, "sgl%d" % i, [128, TW], F32) for i in range(2)]; ksgl = [K(), K()]
            wv = [d["w_glu_out"].rearrange("(kc p) n -> p kc n", p=128), d["w_glu_gate"].rearrange("(kc p) n -> p kc n", p=128)]
            nb = 0
            for mp in range(4):
                sl = mp % 2
                for a in range(2):
                    fw.dma("pool", WGb[a][sl][:], wv[a][:, :, mp * 256:(mp + 1) * 256], writes=[kWG[a][sl]])
                for mm in range(2):
                    m = 2 * mp + mm
                    for ti in range(NTL):
                        cs = slice(ti * TW, (ti + 1) * TW)
                        bo, bg = 2 * (nb % 2), 2 * (nb % 2) + 1
                        nb += 1
                        for a, bk in ((0, bo), (1, bg)):
                            for kc in range(KC):
                                fw.op("pe", lambda e, kc=kc: e.matmul(PS[:, bk, 0:TW], WGb[a][sl][:, kc, mm * 128:(mm + 1) * 128], ZG[:, kc, cs], start=(kc == 0), stop=(kc == KC - 1)),
                                      reads=[kWG[a][sl], kZG[kc][ti]], writes=[kPS[bk]])
                        ss = nb % 2
                        fw.op("act", lambda e: e.activation(out=sgl[ss][:], in_=PS[:, bg, 0:TW], func=AF.Sigmoid), reads=[kPS[bg]], writes=[ksgl[ss]])
                        fw.op("dve", lambda e: e.tensor_tensor(out=sgl[ss][:], in0=PS[:, bo, 0:TW], in1=sgl[ss][:], op=ALU.mult), reads=[kPS[bo], ksgl[ss]], writes=[ksgl[ss]])
                        fw.op("dve", lambda e: e.scalar_tensor_tensor(out=XF[:, m, cs], in0=XF[:, m, cs], scalar=ALPHA, in1=sgl[ss][:], op0=ALU.mult, op1=ALU.add),
                              reads=[ksgl[ss], kXF[m][ti]], writes=[kXF[m][ti]])
            for ti in range(NTL):
                layer_norm_tile(c, ti, 4)
            fw.barrier()


def sb_attention_sample(c, d, st, QS, kQS, OSB, kOSB):
    nc, fw = c.nc, c.fw
    PS, kPS = c.PS, c.kPS
    NE = NS * 16
    nrows = c.npool * 128
    QBC = sb(st, nc, "QBC", [128, NS, 512], F32); kQBC = K()
    fw.dma("sp", c.Qd, QS[0:NS, :], reads=[kQS], writes=[c.kQd])
    fw.dma("sp", QBC[:].rearrange("p b n -> p (b n)"), c.Qd.rearrange("b n -> (b n)").unsqueeze(0).to_broadcast([128, NS * 512]), reads=[c.kQd], writes=[kQBC])
    PTB = sb(st, nc, "PTB", [128, NE], I32); kPT = K()
    IDXF = sb(st, nc, "IDXF", [128, NE], F32)
    IOP = sb(st, nc, "IOP", [128, NE], F32)
    fw.dma("sp", PTB[:], d["pt"].to_broadcast([128, NE]), writes=[kPT])
    fw.op("pool", lambda e: e.iota(IOP[:], [[0, NE]], base=0, channel_multiplier=1, allow_small_or_imprecise_dtypes=True), writes=[kPT])
    fw.op("dve", lambda e: e.tensor_copy(out=IDXF[:], in_=PTB[:]), reads=[kPT], writes=[kPT])
    fw.op("dve", lambda e: e.scalar_tensor_tensor(out=IDXF[:], in0=IDXF[:], scalar=128.0, in1=IOP[:], op0=ALU.mult, op1=ALU.add), reads=[kPT], writes=[kPT])
    fw.op("dve", lambda e: e.tensor_copy(out=PTB[:], in_=IDXF[:]), reads=[kPT], writes=[kPT])
    NSL = 3
    IDc = [sb(st, nc, "IDc%d" % i, [128, 1], I32) for i in range(NSL)]; kID = [K() for _ in range(NSL)]
    PG = [sb(st, nc, "PG%d" % i, [128, 512], F32) for i in range(NSL)]; kPG = [K() for _ in range(NSL)]
    PR = [sb(st, nc, "PR%d" % i, [128, 512], F32) for i in range(2)]; kPR = [K(), K()]
    ZA = sb(st, nc, "ZA", [128, NE * 8], F32); kZA = K()
    EA = sb(st, nc, "EA", [128, NE * 8], F32); kEA = K()
    SPA = sb(st, nc, "SPA", [128, NE * 8], F32); kSPA = K()
    CIN = sb(st, nc, "CIN", [128, NE * 8], F32); kCIN = K()
    TOT = sb(st, nc, "TOT", [128, NE * 8], F32); kTOT = K()
    TRIF = sb(st, nc, "TRIF", [128, 128], F32); kTF = K()
    fw.op("pool", lambda e: e.affine_select(out=TRIF[:], in_=c.onesf[:], pattern=[[-1, 128]], base=0, channel_multiplier=1,
                                            compare_op=ALU.is_ge, fill=0.0), reads=[c.kconst], writes=[kTF])
    n = 0
    for e_ in range(NE):
        b = e_ // 16
        sl = n % NSL
        n += 1
        fw.op("dve", lambda e: e.tensor_copy(out=IDc[sl][:], in_=PTB[:, e_:e_ + 1]), reads=[kPT], writes=[kID[sl]])
        fw.gather(PG[sl][:, :], d["ck"], IDc[sl][:, :], nrows, reads=[kID[sl]], writes=[kPG[sl]])
        ps = e_ % 2
        fw.op("pool", lambda e: e.tensor_tensor(out=PR[ps][:], in0=PG[sl][:], in1=QBC[:, b, :], op=ALU.mult), reads=[kPG[sl], kQBC], writes=[kPR[ps]])
        fw.op("dve", lambda e: e.tensor_reduce(out=ZA[:, e_ * 8:(e_ + 1) * 8], in_=PR[ps][:].rearrange("p (h d) -> p h d", h=8), axis=AX.X, op=ALU.add),
              reads=[kPR[ps]], writes=[kZA])
    fw.op("dve", lambda e: e.scalar_tensor_tensor(out=ZA[:].rearrange("p (e h) -> p e h", h=8), in0=ZA[:].rearrange("p (e h) -> p e h", h=8), scalar=0.125,
                                                  in1=c.sbb[:, :].unsqueeze(1).to_broadcast([128, NE, 8]), op0=ALU.mult, op1=ALU.add),
          reads=[kZA, c.kconst], writes=[kZA])
    fw.op("act", lambda e: e.activation(out=EA[:], in_=ZA[:], func=AF.Exp), reads=[kZA], writes=[kEA])
    fw.op("act", lambda e: e.activation(out=SPA[:], in_=EA[:], func=AF.Ln, bias=c.cst[:, 1:2]), reads=[kEA, c.kconst], writes=[kSPA])
    for q4 in range(4):
        fw.op("pe", lambda e: e.matmul(PS[:, q4, :], TRIF[:], SPA[:, q4 * 512:(q4 + 1) * 512], start=True, stop=True), reads=[kTF, kSPA], writes=[kPS[q4]])
        fw.op("pe", lambda e: e.matmul(PS[:, 4 + q4, :], c.onesf[:], SPA[:, q4 * 512:(q4 + 1) * 512], start=True, stop=True), reads=[c.kconst, kSPA], writes=[kPS[4 + q4]])
        fw.op("act", lambda e: e.activation(out=CIN[:, q4 * 512:(q4 + 1) * 512], in_=PS[:, q4, :], func=AF.Identity), reads=[kPS[q4]], writes=[kCIN])
        fw.op("act", lambda e: e.activation(out=TOT[:, q4 * 512:(q4 + 1) * 512], in_=PS[:, 4 + q4, :], func=AF.Identity), reads=[kPS[4 + q4]], writes=[kTOT])
    T4 = TOT[:].rearrange("p (b q h) -> p b q h", b=NS, q=16)
    C4 = CIN[:].rearrange("p (b q h) -> p b q h", b=NS, q=16)
    RUN = sb(st, nc, "RUN", [128, NS, 8], F32); kRUN = K()
    fw.op("pool", lambda e: e.memset(RUN[:], 0.0), writes=[kRUN])
    for p_ in range(14, -1, -1):
        fw.op("dve", lambda e: e.tensor_tensor(out=RUN[:], in0=RUN[:], in1=T4[:, :, p_ + 1, :], op=ALU.add), reads=[kRUN, kTOT], writes=[kRUN])
        fw.op("dve", lambda e: e.tensor_tensor(out=C4[:, :, p_, :], in0=C4[:, :, p_, :], in1=RUN[:], op=ALU.add), reads=[kRUN, kCIN], writes=[kCIN])
    fw.op("act", lambda e: e.activation(out=CIN[:], in_=CIN[:], func=AF.Exp, scale=-1.0), reads=[kCIN], writes=[kCIN])
    fw.op("dve", lambda e: e.tensor_tensor(out=EA[:], in0=EA[:], in1=CIN[:], op=ALU.mult), reads=[kEA, kCIN], writes=[kEA])
    for e_ in range(NE):
        b, p_ = e_ // 16, e_ % 16
        sl = n % NSL
        n += 1
        fw.op("dve", lambda e: e.tensor_copy(out=IDc[sl][:], in_=PTB[:, e_:e_ + 1]), reads=[kPT], writes=[kID[sl]])
        fw.gather(PG[sl][:, :], d["cv"], IDc[sl][:, :], nrows, reads=[kID[sl]], writes=[kPG[sl]])
        ps = e_ % 2
        fw.op("pool", lambda e: e.tensor_tensor(out=PR[ps][:].rearrange("p (h d) -> p h d", h=8), in0=PG[sl][:].rearrange("p (h d) -> p h d", h=8),
                                                in1=EA[:, e_ * 8:(e_ + 1) * 8].unsqueeze(2).to_broadcast([128, 8, 64]), op=ALU.mult),
              reads=[kPG[sl], kEA], writes=[kPR[ps]])
        for c4 in range(4):
            fw.op("pe", lambda e, c4=c4: e.matmul(PS[:, 4 + c4, b:b + 1], PR[ps][:, c4 * 128:(c4 + 1) * 128], c.onesf[:, 0:1], start=(p_ == 0), stop=(p_ == 15)),
                  reads=[kPR[ps], c.kconst], writes=[kPS[4 + c4]])
    for c4 in range(4):
        fw.op("act", lambda e, c4=c4: e.activation(out=OSB[:, c4, NP:NT], in_=PS[:, 4 + c4, 0:NS], func=AF.Identity), reads=[kPS[4 + c4]], writes=[kOSB[c4][4]])


def build(stage=99, dbg=False, npool=2560):
    nc = bass.Bass("TRN2", target_bir_lowering=False)
    c = Ctx()
    c.nc = nc

    DECL.clear()

    def din(name, shape, dt=F32):
        DECL.append(name)
        return nc.dram_tensor(name, list(shape), dt, kind="ExternalInput").ap()

    def dout(name, shape, dt=F32):
        return nc.dram_tensor(name, list(shape), dt, kind="ExternalOutput").ap()

    xT = din("xT", [D, NT])
    lngT = din("lngT", [128, 6 * KC])
    lnbT = din("lnbT", [128, 6 * KC])
    fw_ = {}
    for nm in ("ffn1_wg", "ffn1_wu", "ffn2_wg", "ffn2_wu"):
        fw_[nm] = din(nm, [2, D, DFF])
    for nm in ("ffn1_wd", "ffn2_wd"):
        fw_[nm] = din(nm, [2, DFF, D])
    yT = dout("yT", [D, NT])
    d = {}
    c.dbg = dbg
    if stage >= 2:
        d["w_in"] = din("w_in", [D, INC])
        d["sbb"] = din("sbb", [128, 8])
        d["pk"] = dout("pk", [NP, 512]); d["pv"] = dout("pv", [NP, 512])
        d["sk"] = dout("sk", [NS, 512]); d["sv"] = dout("sv", [NS, 512])
        if dbg:
            d["dbg_osb"] = dout("dbg_osb", [128, 4, NT], BF16)
    c.npool = npool
    if stage >= 4:
        d["ck"] = din("ck", [npool * 128, 512]); d["cv"] = din("cv", [npool * 128, 512])
        d["pt"] = din("pt", [1, NS * 16], I32)
        c.Qd = nc.dram_tensor("Qd", [NS, 512], F32, kind="Internal").ap(); c.kQd = K()
    if stage >= 5:
        c.TOKd = nc.dram_tensor("TOKd", [CS, 3, 512], BF16, kind="Internal").ap()
        c.Yd = nc.dram_tensor("Yd", [4, 2, CS, 64], BF16, kind="Internal").ap()
        c.kTOKd = K(); c.kYd = K()
        d["w_w2"] = din("w_w2", [64, 512]); d["w_a2"] = din("w_a2", [64, 512]); d["w_g2"] = din("w_g2", [128, 512])
        d["pcol"] = din("pcol", [128, 48]); d["gng"] = din("gng", [128, 512]); d["gnb"] = din("gnb", [128, 512])
        d["sshift0"] = din("sshift0", [NS, RWC]); d["swkv0"] = din("swkv0", [NS, 8, 64, 64])
        d["pwkv"] = dout("pwkv", [8, 64, 64]); d["swkv"] = dout("swkv", [NS, 8, 64, 64])
        d["pshift"] = dout("pshift", [1, RWC]); d["sshift"] = dout("sshift", [NS, RWC])
        d["w_out"] = din("w_out", [D, D])
    if stage >= 8:
        d["s5prm"] = din("s5prm", [128, 3, 32]); d["dskip"] = din("dskip", [128, 8])
        d["s5_0"] = din("s5_0", [NS, 4096, 2]); d["s5bc"] = din("s5bc", [8, 16, 128, 128])
        d["w_glu_out"] = din("w_glu_out", [D, D]); d["w_glu_gate"] = din("w_glu_gate", [D, D])
        d["ps5"] = dout("ps5", [4096, 2]); d["ss5"] = dout("ss5", [NS, 8192])

    with ExitStack() as st:
        fw = FW(nc, st)
        c.fw = fw
        c.XF = sb(st, nc, "XF", [128, KC, NT], F32)
        c.kXF = [[K() for _ in range(NTL)] for _ in range(KC)]
        c.PS = st.enter_context(nc.psum_tensor("PS", [128, 8, 512], F32))
        c.kPS = [K() for _ in range(8)]
        c.onesf = sb(st, nc, "onesf", [128, 128], F32)
        c.epsc = sb(st, nc, "epsc", [128, 1], F32)
        c.lng = sb(st, nc, "lng", [128, 6 * KC], F32)
        c.lnb = sb(st, nc, "lnb", [128, 6 * KC], F32)
        c.kconst = K()
        c.nsq = 0
        c.nln = 0
        fw.op("pool", lambda e: e.memset(c.onesf[:], 1.0), writes=[c.kconst])
        fw.op("pool", lambda e: e.memset(c.epsc[:], LN_EPS), writes=[c.kconst])
        fw.dma("sp", c.lng[:], lngT, writes=[c.kconst])
        fw.dma("sp", c.lnb[:], lnbT, writes=[c.kconst])
        xv = xT.rearrange("(kc p) t -> p kc t", p=128)
        for kc in range(KC):
            fw.dma("sp", c.XF[:, kc, :], xv[:, kc, :], writes=c.kXF[kc])

        if not SKIP_FFN:
            ffn_ln(c, fw_["ffn1_wg"][0], fw_["ffn1_wu"][0], fw_["ffn1_wd"][0], 0, "a")
        fw.barrier()
        if stage >= 2:
            c.cst = sb(st, nc, "cst", [128, 4], F32)
            c.sbb = sb(st, nc, "sbb_s", [128, 8], F32)
            fw.op("pool", lambda e: e.memset(c.cst[:, 0:1], LN_EPS), writes=[c.kconst])
            fw.op("pool", lambda e: e.memset(c.cst[:, 1:2], 1.0), writes=[c.kconst])
            fw.op("pool", lambda e: e.memset(c.cst[:, 2:3], GN_EPS), writes=[c.kconst])
            fw.op("pool", lambda e: e.memset(c.cst[:, 3:4], 1.5707963267948966), writes=[c.kconst])
            fw.dma("sp", c.sbb[:], d["sbb"], writes=[c.kconst])
            onesb = sb(st, nc, "onesb", [128, 512], BF16)
            fw.op("pool", lambda e: e.memset(onesb[:], 1.0), writes=[c.kconst])
            c.ONESB = onesb[:, 0:128]
            c.onesb = onesb
            c.IDF = sb(st, nc, "IDF", [128, 128], F32)
            fw.op("pool", lambda e: e.memset(c.IDF[:], 1.0), writes=[c.kconst])
            fw.op("pool", lambda e: e.affine_select(out=c.IDF[:], in_=c.IDF[:], pattern=[[-1, 128]], base=0, channel_multiplier=1,
                                                    compare_op=ALU.is_equal, fill=0.0), reads=[c.kconst], writes=[c.kconst])
            mixer_even(c, d, stage)
        if stage >= 7 and not SKIP_FFN:
            ffn_ln(c, fw_["ffn2_wg"][0], fw_["ffn2_wu"][0], fw_["ffn2_wd"][0], 2, "b")
            fw.barrier()
            ffn_ln(c, fw_["ffn1_wg"][1], fw_["ffn1_wu"][1], fw_["ffn1_wd"][1], 3, "c")
            fw.barrier()
        if stage >= 8:
            s5_mixer(c, d)
        if stage >= 9 and not SKIP_FFN:
            ffn_ln(c, fw_["ffn2_wg"][1], fw_["ffn2_wu"][1], fw_["ffn2_wd"][1], 5, "d")
            fw.barrier()

        yv = yT.rearrange("(kc p) t -> p kc t", p=128)
        for kc in range(KC):
            fw.dma("sp", yv[:, kc, :], c.XF[:, kc, :], reads=c.kXF[kc])
        fw.finish()
        print("instructions:", fw.ninst, {e: fw.cnt[e] for e in fw.cnt})
    return nc


DECL = []


def make_in_maps(inp, n_cores=8):
    f = np.float32
    maps = []
    ln_g = np.ascontiguousarray(np.asarray(inp["ln_g"], f).reshape(6, KC, 128).transpose(2, 0, 1).reshape(128, 6 * KC))
    ln_b = np.ascontiguousarray(np.asarray(inp["ln_b"], f).reshape(6, KC, 128).transpose(2, 0, 1).reshape(128, 6 * KC))
    shared = {"lngT": ln_g, "lnbT": ln_b}
    for nm in ("ffn1_wg", "ffn1_wu", "ffn1_wd", "ffn2_wg", "ffn2_wu", "ffn2_wd"):
        shared[nm] = np.asarray(inp[nm], f)
    shared["w_in"] = np.asarray(inp["w_in_even"][0], f)
    for nm in ("w_w2", "w_a2", "w_g2"):
        shared[nm] = np.asarray(inp[nm][0], f)
    shared["w_out"] = np.asarray(inp["w_out_even"][0], f)
    def colT(v, n):
        return np.asarray(v, f).reshape(n, 128).T
    pc = np.zeros((128, 48), f)
    pc[:, 0:14] = colT(inp["mu_shift"][0], 14)
    pc[:, 14:18] = colT(inp["w0"][0], 4); pc[:, 18:22] = colT(inp["a0"][0], 4)
    pc[:, 22:26] = colT(inp["k_k"][0], 4); pc[:, 26:30] = colT(inp["k_a"][0], 4)
    pc[:, 30:34] = colT(inp["r_k"][0].reshape(-1), 4)
    shared["pcol"] = pc
    shared["gng"] = np.ascontiguousarray(np.broadcast_to(np.asarray(inp["gn_g"][0], f)[None, :], (128, 512)))
    shared["gnb"] = np.ascontiguousarray(np.broadcast_to(np.asarray(inp["gn_b"][0], f)[None, :], (128, 512)))
    lre = np.asarray(inp["lam_re"][0], f); lim = np.asarray(inp["lam_im"][0], f); ldt = np.asarray(inp["log_dt"][0], f)
    prm = np.zeros((128, 3, 32), f)
    prm[:, 0, :] = lre.reshape(32, 128).T; prm[:, 1, :] = lim.reshape(32, 128).T
    prm[:, 2, :] = np.repeat(ldt, 64).reshape(32, 128).T
    shared["s5prm"] = prm
    shared["dskip"] = np.ascontiguousarray(np.asarray(inp["d_skip"][0], f).reshape(8, 128).T)
    bre = np.asarray(inp["b_re"][0], f); bim = np.asarray(inp["b_im"][0], f)
    cre = np.asarray(inp["c_re"][0], f); cim = np.asarray(inp["c_im"][0], f)
    bc = np.zeros((8, 4, 4, 128, 128), f)
    for g in range(64):
        m_, gl = g // 8, g % 8
        k_, g2 = (g % 8) // 2, g % 2
        rows = slice(gl * 16, gl * 16 + 16); cols = slice(g2 * 64, g2 * 64 + 64)
        bc[m_, k_, 0][rows, cols] = bre[g].T
        bc[m_, k_, 1][rows, cols] = bim[g].T
        bc[m_, k_, 2][cols, rows] = cre[g].T
        bc[m_, k_, 3][cols, rows] = cim[g].T
    shared["s5bc"] = bc.reshape(8, 16, 128, 128)
    shared["w_glu_out"] = np.asarray(inp["w_glu_out"][0], f); shared["w_glu_gate"] = np.asarray(inp["w_glu_gate"][0], f)
    shared["sbb"] = np.ascontiguousarray(np.broadcast_to(np.asarray(inp["sb_bias"][0], f)[None, :], (128, 8)))
    for cidx in range(n_cores):
        m = dict(shared)
        xp = np.asarray(inp["x_prompt"][cidx], f)
        xs = np.asarray(inp["x_sample"][cidx * NS:(cidx + 1) * NS, 0], f)
        m["xT"] = np.ascontiguousarray(np.concatenate([xp, xs], axis=0).T)
        sl = slice(cidx * NS, (cidx + 1) * NS)
        m["sshift0"] = np.asarray(inp["state_shift"][0, sl], f)
        m["swkv0"] = np.asarray(inp["state_wkv"][0, sl], f)
        m["pt"] = np.ascontiguousarray(np.asarray(inp["page_table"][sl], np.int32).reshape(1, NS * 16))
        m["ck"] = np.asarray(inp["cache_k_sb"][0], f).reshape(-1, 512)
        m["cv"] = np.asarray(inp["cache_v_sb"][0], f).reshape(-1, 512)
        m["s5_0"] = np.asarray(inp["state_s5"][0, sl], f).reshape(NS, 4096, 2)
        maps.append({k: v for k, v in m.items() if k in DECL})
    return maps


def dev_compare(stage, r, ref, cmp):
    if stage >= 2:
        pp = ref["p_proj"][0]; sp_ = ref["s_proj"][:, 0]
        cmp("pk", r["pk"], pp[:, 512:1024]); cmp("pv", r["pv"], pp[:, 1024:1536])
        cmp("sk", r["sk"], sp_[:, 512:1024]); cmp("sv", r["sv"], sp_[:, 1024:1536])
    if stage >= 4 and "dbg_osb" in r:
        import ml_dtypes
        x_ = r["dbg_osb"]
        if x_.dtype.kind == "V":
            x_ = x_.view(ml_dtypes.bfloat16)
        o = np.asarray(x_).astype(np.float32).transpose(1, 0, 2).reshape(512, NT).T
        cmp("osb_s", o[NP:], ref["s_osb"][:, 0])
    if stage >= 3 and "dbg_osb" in r and False:
        o = np.asarray(r["dbg_osb"]).astype(np.float32).transpose(1, 0, 2).reshape(512, NT).T
        cmp("osb_p", o[:NP], ref["p_osb"][0])
        for qq in range(4):
            cmp("osb_p q%d" % qq, o[qq*512:(qq+1)*512], ref["p_osb"][0][qq*512:(qq+1)*512])
    if stage >= 5:
        cmp("pwkv", r["pwkv"], ref["p_wkv"][0]); cmp("swkv", r["swkv"], ref["s_wkv"])
        cmp("pshift", r["pshift"][0], ref["p_proj"][0, -1, 1536:]); cmp("sshift", r["sshift"], ref["s_proj"][:, 0, 1536:])
    if stage >= 8:
        cmp("ps5", r["ps5"].reshape(64, 64, 2), ref["p_s5"][0]); cmp("ss5", r["ss5"].reshape(NS, 64, 64, 2), ref["s_s5"])
    y = r["yT"].T
    key = {1: "L0_x1", 2: "L0_x1", 3: "L0_x1", 4: "L0_x1", 5: "L0_x1", 6: "L0_x2", 7: "L1_x1", 8: "L1_x2", 9: "L1_x3"}.get(stage, "L1_x3")
    cmp("y_prompt", y[:NP], ref["p_" + key][0])
    cmp("y_sample", y[NP:], ref["s_" + key][:, 0])


def kernel(**inputs):
    n = 8
    npool = int(np.asarray(inputs["cache_k_sb"]).shape[1])
    nc = build(stage=9, dbg=False, npool=npool)
    maps = make_in_maps(inputs, n_cores=n)
    res = run_bass_kernel_spmd(nc, maps, core_ids=list(range(n)))
    R = res.results
    f = np.float32
    yp = np.stack([np.asarray(R[i]["yT"], f).T[:NP] for i in range(n)], axis=0)
    ys = np.concatenate([np.asarray(R[i]["yT"], f).T[NP:] for i in range(n)], axis=0)[:, None, :]
    pk = np.stack([np.asarray(R[i]["pk"], f).reshape(NP, 8, 64) for i in range(n)], axis=0)[None]
    pv = np.stack([np.asarray(R[i]["pv"], f).reshape(NP, 8, 64) for i in range(n)], axis=0)[None]
    pwkv = np.stack([np.asarray(R[i]["pwkv"], f) for i in range(n)], axis=0)[None]
    pshift = np.stack([np.asarray(R[i]["pshift"], f).reshape(RWC) for i in range(n)], axis=0)[None]
    ps5 = np.stack([np.asarray(R[i]["ps5"], f).reshape(64, 64, 2) for i in range(n)], axis=0)[None]
    sk = np.concatenate([np.asarray(R[i]["sk"], f).reshape(NS, 1, 8, 64) for i in range(n)], axis=0)[None]
    sv = np.concatenate([np.asarray(R[i]["sv"], f).reshape(NS, 1, 8, 64) for i in range(n)], axis=0)[None]
    swkv = np.concatenate([np.asarray(R[i]["swkv"], f) for i in range(n)], axis=0)[None]
    sshift = np.concatenate([np.asarray(R[i]["sshift"], f) for i in range(n)], axis=0)[None]
    ss5 = np.concatenate([np.asarray(R[i]["ss5"], f).reshape(NS, 64, 64, 2) for i in range(n)], axis=0)[None]
    return (yp, ys, pk, pv, pwkv, pshift, ps5, sk, sv, swkv, sshift, ss5)
```

```python
from contextlib import ExitStack
import numpy as np
import concourse.bass as bass
import concourse.mybir as mybir
from concourse.bass_utils import run_bass_kernel_spmd

F32 = mybir.dt.float32
BF16 = mybir.dt.bfloat16
I32 = mybir.dt.int32
AF = mybir.ActivationFunctionType
ALU = mybir.AluOpType
AX = mybir.AxisListType

D = 1024
KC = 8
DFF = 2816
JC = 22
NP = 2048
NS = 16
NT = NP + NS
TW = 344
NTL = NT // TW
NG = 3
GW = NT // NG
ALPHA = 4.0 ** 0.25
LN_EPS = 1e-5
GN_EPS = 64e-5
INC = 3328
RWC = 1792
SKIP_FFN = False
import os
VAR = os.environ.get('KVAR', '')


class K:
    __slots__ = ("name", "lw", "rd")

    def __init__(self, name=""):
        self.name = name
        self.lw = None
        self.rd = []


class FW:
    def __init__(self, nc, stack, n_dma_sems=8):
        self.nc = nc
        self.eng = {"pe": nc.tensor, "dve": nc.vector, "act": nc.scalar,
                    "pool": nc.gpsimd, "sp": nc.sync}
        self.sem = {}
        self.cnt = {}
        self.seen = {e: {} for e in self.eng}
        for e in self.eng:
            self.sem[e] = stack.enter_context(nc.semaphore("s_" + e))
            self.cnt[e] = 0
        self.dsem = {}
        self.dcnt = {}
        self.dnext = {}
        for q in ("sp", "pool"):
            self.dsem[q] = [stack.enter_context(nc.semaphore("d_%s_%d" % (q, i)))
                            for i in range(n_dma_sems)]
            self.dcnt[q] = [0] * n_dma_sems
            self.dnext[q] = 0
        self.ninst = 0
        self.dram_writes = []

    def _semobj(self, sk):
        if isinstance(sk, tuple):
            return self.dsem[sk[0]][sk[1]]
        return self.sem[sk]

    def _wait(self, e, sk, val):
        if sk == "pe" and e == "pe":
            return
        if self.seen[e].get(sk, 0) >= val:
            return
        self.seen[e][sk] = val
        self.eng[e].wait_ge(self._semobj(sk), val)
        self.ninst += 1

    def _deps(self, e, reads, writes):
        for k in reads:
            if k.lw is not None:
                self._wait(e, k.lw[0], k.lw[1])
        for k in writes:
            if k.lw is not None:
                self._wait(e, k.lw[0], k.lw[1])
            for (sk, v) in k.rd:
                self._wait(e, sk, v)

    def _mark(self, sk, val, reads, writes):
        for k in reads:
            k.rd.append((sk, val))
            if len(k.rd) > 16:
                m = {}
                for (s, v) in k.rd:
                    if m.get(s, 0) < v:
                        m[s] = v
                k.rd = list(m.items())
        for k in writes:
            k.lw = (sk, val)
            k.rd = []

    def op(self, e, fn, reads=(), writes=()):
        if e == "dve" and self.dram_writes:
            for (sk, v) in self.dram_writes:
                self._wait(e, sk, v)
            self.dram_writes = []
        self._deps(e, reads, writes)
        ins = fn(self.eng[e])
        self.cnt[e] += 1
        ins.then_inc(self.sem[e], 1)
        self._mark(e, self.cnt[e], reads, writes)
        self.ninst += 1
        return ins

    def _dslot(self, q):
        i = self.dnext[q]
        self.dnext[q] = (i + 1) % len(self.dsem[q])
        if self.dcnt[q][i] > 0:
            self._wait(q, (q, i), self.dcnt[q][i])
        return i

    def dma(self, q, out, in_, reads=(), writes=(), **kw):
        i = self._dslot(q)
        self._deps(q, reads, writes)
        ins = self.eng[q].dma_start(out=out, in_=in_, **kw)
        self.dcnt[q][i] += 16
        ins.then_inc(self.dsem[q][i], 16)
        self._mark((q, i), self.dcnt[q][i], reads, writes)
        self.ninst += 1
        if "DRAM" in str(getattr(out.tensor, "space", "")).upper() or type(out.tensor).__name__.startswith("DRam"):
            self.dram_writes.append(((q, i), self.dcnt[q][i]))
        return ins

    def gather(self, out, in_rows, idx_ap, nrows, reads=(), writes=()):
        q = "pool"
        i = self._dslot(q)
        self._deps(q, reads, writes)
        if getattr(self, "_breg", None) is None or self._breg[0] != nrows:
            self._breg = (nrows, self.nc.gpsimd.to_reg(nrows - 1))
        ins = self.nc.gpsimd.indirect_dma_start(
            out=out, out_offset=None, in_=in_rows,
            in_offset=bass.IndirectOffsetOnAxis(ap=idx_ap, axis=0),
            bounds_check=self._breg[1], oob_is_err=False)
        self.dcnt[q][i] += 16
        ins.then_inc(self.dsem[q][i], 16)
        self._mark((q, i), self.dcnt[q][i], reads, writes)
        self.ninst += 1
        return ins

    def barrier(self):
        for e in self.eng:
            for q in self.dsem:
                for i, v in enumerate(self.dcnt[q]):
                    if v > 0:
                        self._wait(e, (q, i), v)
            for e2 in self.eng:
                if self.cnt[e2] > 0 and not (e2 == e and e == "pe"):
                    self._wait(e, e2, self.cnt[e2])

    def finish(self):
        for q in self.dsem:
            for i, v in enumerate(self.dcnt[q]):
                if v > 0:
                    self._wait("sp", (q, i), v)
        for e in self.eng:
            if e != "sp" and self.cnt[e] > 0:
                self._wait("sp", e, self.cnt[e])


class Ctx:
    pass


def sb(st, nc, name, shape, dt):
    return st.enter_context(nc.sbuf_tensor(name, list(shape), dt))


def ffn_ln(c, wg, wu, wd, lnidx, tag):
    nc, fw = c.nc, c.fw
    XF, kXF = c.XF, c.kXF
    with ExitStack() as st:
        XB = sb(st, nc, "XB" + tag, [128, KC, GW], BF16)
        H = sb(st, nc, "H" + tag, [128, JC, GW], BF16)
        wgb = [sb(st, nc, "wgb%d%s" % (i, tag), [128, KC, 256], BF16) for i in range(2)]
        wub = [sb(st, nc, "wub%d%s" % (i, tag), [128, KC, 256], BF16) for i in range(2)]
        wdb = [sb(st, nc, "wdb%d%s" % (i, tag), [128, JC, 256], BF16) for i in range(2)]
        sgt = [sb(st, nc, "sgt%d%s" % (i, tag), [128, TW], F32) for i in range(2)]
        alloc_ln(c, st)
        kXB = [K() for _ in range(KC)]
        kH = [[K() for _ in range(2)] for _ in range(JC)]
        kwg = [K(), K()]
        kwu = [K(), K()]
        kwd = [K(), K()]
        ksg = [K(), K()]
        wgv = wg.rearrange("(kc p) n -> p kc n", p=128)
        wuv = wu.rearrange("(kc p) n -> p kc n", p=128)
        wdv = wd.rearrange("(j p) n -> p j n", p=128)
        PS, kPS = c.PS, c.kPS
        nsg = 0
        nb = 0
        for g in range(NG):
            c0 = g * GW
            for kc in range(KC):
                fw.op("act", lambda e, kc=kc: e.activation(out=XB[:, kc, :], in_=XF[:, kc, c0:c0 + GW], func=AF.Identity),
                      reads=[kXF[kc][2 * g], kXF[kc][2 * g + 1]], writes=[kXB[kc]])
            for jp in range(JC // 2):
                s = jp % 2
                fw.dma("pool", wgb[s][:], wgv[:, :, jp * 256:(jp + 1) * 256], writes=[kwg[s]])
                fw.dma("pool", wub[s][:], wuv[:, :, jp * 256:(jp + 1) * 256], writes=[kwu[s]])
                for jj in range(2):
                    j = 2 * jp + jj
                    for tl in range(2):
                        bg, bu = 2 * (nb % 2), 2 * (nb % 2) + 1
                        nb += 1
                        cs = slice(tl * TW, (tl + 1) * TW)
                        for kc in range(KC):
                            fw.op("pe", lambda e, kc=kc: e.matmul(PS[:, bg, 0:TW], wgb[s][:, kc, jj * 128:(jj + 1) * 128],
                                                                  XB[:, kc, cs], start=(kc == 0), stop=(kc == KC - 1)),
                                  reads=[kwg[s], kXB[kc]], writes=[kPS[bg]])
                        for kc in range(KC):
                            fw.op("pe", lambda e, kc=kc: e.matmul(PS[:, bu, 0:TW], wub[s][:, kc, jj * 128:(jj + 1) * 128],
                                                                  XB[:, kc, cs], start=(kc == 0), stop=(kc == KC - 1)),
                                  reads=[kwu[s], kXB[kc]], writes=[kPS[bu]])
                        ss = nsg % 2
                        nsg += 1
                        fw.op("act", lambda e: e.activation(out=sgt[ss][:], in_=PS[:, bg, 0:TW], func=AF.Silu),
                              reads=[kPS[bg]], writes=[ksg[ss]])
                        fw.op("dve", lambda e: e.scalar_tensor_tensor(out=H[:, j, cs], in0=PS[:, bu, 0:TW], scalar=0.5,
                                                                      in1=sgt[ss][:], op0=ALU.mult, op1=ALU.mult),
                              reads=[kPS[bu], ksg[ss]], writes=[kH[j][tl]])
            for mp in range(KC // 2):
                s = mp % 2
                fw.dma("pool", wdb[s][:], wdv[:, :, mp * 256:(mp + 1) * 256], writes=[kwd[s]])
                for mm in range(2):
                    m = 2 * mp + mm
                    for tl in range(2):
                        by = 4 + (nb % 2)
                        nb += 1
                        cs = slice(tl * TW, (tl + 1) * TW)
                        gc = slice(c0 + tl * TW, c0 + (tl + 1) * TW)
                        for j in range(JC):
                            fw.op("pe", lambda e, j=j: e.matmul(PS[:, by, 0:TW], wdb[s][:, j, mm * 128:(mm + 1) * 128],
                                                                H[:, j, cs], start=(j == 0), stop=(j == JC - 1)),
                                  reads=[kwd[s], kH[j][tl]], writes=[kPS[by]])
                        fw.op("dve", lambda e: e.scalar_tensor_tensor(out=XF[:, m, gc], in0=XF[:, m, gc], scalar=ALPHA,
                                                                      in1=PS[:, by, 0:TW], op0=ALU.mult, op1=ALU.add),
                              reads=[kPS[by], kXF[m][2 * g + tl]], writes=[kXF[m][2 * g + tl]])
            for tl in range(2):
                layer_norm_tile(c, 2 * g + tl, lnidx)


def alloc_ln(c, st):
    c.nln += 1
    c.sq = [sb(st, c.nc, "sq%d_%d" % (i, c.nln), [128, TW], F32) for i in range(2)]
    c.ksq = [K(), K()]
    c.lnt = [sb(st, c.nc, "lnt%d_%d" % (i, c.nln), [128, TW], F32) for i in range(4)]
    c.kln = [K() for _ in range(4)]


def layer_norm_tile(c, ti, lnidx):
    nc, fw = c.nc, c.fw
    XF, kXF, PS, kPS = c.XF, c.kXF, c.PS, c.kPS
    cs = slice(ti * TW, (ti + 1) * TW)
    b1, b2 = 6, 7
    for m in range(KC):
        s = c.nsq % 2
        c.nsq += 1
        fw.op("act", lambda e: e.activation(out=c.sq[s][:], in_=XF[:, m, cs], func=AF.Square),
              reads=[kXF[m][ti]], writes=[c.ksq[s]])
        fw.op("pe", lambda e: e.matmul(PS[:, b1, 0:TW], c.onesf[:], XF[:, m, cs], start=(m == 0), stop=(m == KC - 1)),
              reads=[kXF[m][ti], c.kconst], writes=[kPS[b1]])
        fw.op("pe", lambda e: e.matmul(PS[:, b2, 0:TW], c.onesf[:], c.sq[s][:], start=(m == 0), stop=(m == KC - 1)),
              reads=[c.ksq[s], c.kconst], writes=[kPS[b2]])
    mean, msq, var, rstd = c.lnt
    kln = c.kln
    fw.op("dve", lambda e: e.tensor_scalar(out=mean[:], in0=PS[:, b1, 0:TW], scalar1=1.0 / D, scalar2=None, op0=ALU.mult),
          reads=[kPS[b1]], writes=[kln[0]])
    fw.op("dve", lambda e: e.tensor_tensor(out=msq[:], in0=mean[:], in1=mean[:], op=ALU.mult),
          reads=[kln[0]], writes=[kln[1]])
    fw.op("dve", lambda e: e.scalar_tensor_tensor(out=var[:], in0=PS[:, b2, 0:TW], scalar=1.0 / D, in1=msq[:],
                                                  op0=ALU.mult, op1=ALU.subtract),
          reads=[kPS[b2], kln[1]], writes=[kln[2]])
    fw.op("act", lambda e: e.activation(out=var[:], in_=var[:], func=AF.Sqrt, bias=c.epsc[:, 0:1]),
          reads=[kln[2], c.kconst], writes=[kln[2]])
    fw.op("dve", lambda e: e.reciprocal(out=rstd[:], in_=var[:]), reads=[kln[2]], writes=[kln[3]])
    for m in range(KC):
        s = c.nsq % 2
        c.nsq += 1
        t = c.sq[s]
        fw.op("pool", lambda e: e.tensor_tensor(out=t[:], in0=XF[:, m, cs], in1=mean[:], op=ALU.subtract),
              reads=[kXF[m][ti], kln[0]], writes=[c.ksq[s]])
        fw.op("dve", lambda e: e.tensor_tensor(out=t[:], in0=t[:], in1=rstd[:], op=ALU.mult),
              reads=[c.ksq[s], kln[3]], writes=[c.ksq[s]])
        col = lnidx * KC + m
        fw.op("act", lambda e: e.activation(out=XF[:, m, cs], in_=t[:], func=AF.Identity,
                                            scale=c.lng[:, col:col + 1], bias=c.lnb[:, col:col + 1]),
              reads=[c.ksq[s], c.kconst], writes=[kXF[m][ti]])


def pipeline(streams):
    nst = max(len(b) for s in streams for b in s)
    nmax = max(len(s) for s in streams)
    for t in range(nmax + nst - 1):
        for s in streams:
            for k in range(nst - 1, -1, -1):
                bi = t - k
                if 0 <= bi < len(s) and k < len(s[bi]):
                    s[bi][k]()


def mixer_even(c, d, stage):
    nc, fw = c.nc, c.fw
    XF, kXF, PS, kPS = c.XF, c.kXF, c.PS, c.kPS
    st = ExitStack()
    with st:
        OSB = sb(st, nc, "OSB", [128, 4, NT], BF16)
        kOSB = [[K() for _ in range(5)] for _ in range(4)]
        ORW = sb(st, nc, "ORW", [128, 4, NT], BF16)
        kORW = [K() for _ in range(4)]
        fw.op("pool", lambda e: e.memset(OSB[:, :, NP:NT], 0.0), writes=[kOSB[cc][4] for cc in range(4)])
        QS = sb(st, nc, "QS", [NS, 512], F32); kQS = K()
        win_v = d["w_in"].rearrange("(kc p) n -> p kc n", p=128)
        with ExitStack() as sta:
            QT = sb(sta, nc, "QT", [128, 4, NT], BF16)
            KT = sb(sta, nc, "KT", [128, 4, NT], BF16)
            kQT = [K() for _ in range(4)]
            kKT = [K() for _ in range(4)]
            Vtok = sb(sta, nc, "Vtok", [128, 17, 512], BF16)
            kV = [K() for _ in range(17)]
            with ExitStack() as st2:
                XB = sb(st2, nc, "XBm", [128, KC, NT], BF16)
                kXB = [K() for _ in range(KC)]
                for kc in range(KC):
                    fw.op("act" if kc % 2 else "dve",
                          (lambda e, kc=kc: e.activation(out=XB[:, kc, :], in_=XF[:, kc, :], func=AF.Identity)) if kc % 2 else
                          (lambda e, kc=kc: e.tensor_copy(out=XB[:, kc, :], in_=XF[:, kc, :])),
                          reads=kXF[kc], writes=[kXB[kc]])
                WB = [sb(st2, nc, "WBm%d" % i, [128, KC, 256], BF16) for i in range(2)]
                kWB = [K(), K()]
                stg = [sb(st2, nc, "stg%d" % i, [128, 256], F32) for i in range(2)]
                kstg = [K(), K()]
                nb = 0
                nstg = 0
                for wc in range(6):
                    sl = wc % 2
                    fw.dma("pool", WB[sl][:], win_v[:, :, wc * 256:(wc + 1) * 256], writes=[kWB[sl]])
                    if wc < 4:
                        for oo in range(2):
                            oc = 2 * wc + oo
                            dst, kd = (QT, kQT) if oc < 4 else (KT, kKT)
                            for ti in range(NTL):
                                bk = nb % 2
                                nb += 1
                                cs = slice(ti * TW, (ti + 1) * TW)
                                for kc in range(KC):
                                    fw.op("pe", lambda e, kc=kc: e.matmul(PS[:, bk, 0:TW], WB[sl][:, kc, oo * 128:(oo + 1) * 128], XB[:, kc, cs],
                                                                          start=(kc == 0), stop=(kc == KC - 1)),
                                          reads=[kWB[sl], kXB[kc]], writes=[kPS[bk]])
                                fw.op("act", lambda e: e.activation(out=dst[:, oc % 4, cs], in_=PS[:, bk, 0:TW], func=AF.Identity),
                                      reads=[kPS[bk]], writes=[kd[oc % 4]])
                    if wc < 2 and stage >= 4:
                        bk = 2 + (nb % 2)
                        nb += 1
                        for kc in range(KC):
                            fw.op("pe", lambda e, kc=kc: e.matmul(PS[0:NS, bk, 0:256], XB[:, kc, NP:NT], WB[sl][:, kc, :], start=(kc == 0), stop=(kc == KC - 1)),
                                  reads=[kWB[sl], kXB[kc]], writes=[kPS[bk]])
                        fw.op("act", lambda e: e.activation(out=QS[0:NS, wc * 256:(wc + 1) * 256], in_=PS[0:NS, bk, 0:256], func=AF.Identity), reads=[kPS[bk]], writes=[kQS])
                    if wc >= 2:
                        which = 0 if wc < 4 else 1
                        coff = (wc % 2) * 256
                        for tt in range(17):
                            rows = 128 if tt < 16 else NS
                            bk = 2 + (nb % 2)
                            nb += 1
                            for kc in range(KC):
                                fw.op("pe", lambda e, kc=kc: e.matmul(PS[0:rows, bk, 0:256], XB[:, kc, tt * 128:tt * 128 + rows], WB[sl][:, kc, :],
                                                                      start=(kc == 0), stop=(kc == KC - 1)),
                                      reads=[kWB[sl], kXB[kc]], writes=[kPS[bk]])
                            ss = nstg % 2
                            nstg += 1
                            fw.op("act", lambda e: e.activation(out=stg[ss][0:rows, :], in_=PS[0:rows, bk, 0:256], func=AF.Identity),
                                  reads=[kPS[bk]], writes=[kstg[ss]])
                            if tt < 16:
                                dstd = (d["pk"] if which == 0 else d["pv"])[tt * 128:(tt + 1) * 128, coff:coff + 256]
                            else:
                                dstd = (d["sk"] if which == 0 else d["sv"])[:, coff:coff + 256]
                            fw.dma("sp", dstd, stg[ss][0:rows, :], reads=[kstg[ss]])
                            if which == 1:
                                fw.op("act", lambda e: e.activation(out=Vtok[0:rows, tt, coff:coff + 256], in_=PS[0:rows, bk, 0:256], func=AF.Identity),
                                      reads=[kPS[bk]], writes=[kV[tt]])
                fw.barrier()
            with ExitStack() as st2:
                if stage >= 3:
                    sb_attention_prompt(c, d, st2, QT, KT, kQT, kKT, Vtok, kV, OSB, kOSB)
                fw.barrier()
            fw.barrier()
        if stage >= 4 and 'nosamp' not in VAR:
            with ExitStack() as st3:
                sb_attention_sample(c, d, st3, QS, kQS, OSB, kOSB)
                fw.barrier()
        if stage >= 5:
            with ExitStack() as st3:
                rwkv_all(c, d, st3, ORW, kORW)
                fw.barrier()
        if stage >= 6:
            with ExitStack() as st3:
                alloc_ln(c, st3)
                WOUT = sb(st3, nc, "WOUT", [128, KC, D], BF16)
                kWO = [K() for _ in range(4)]
                wo_v = d["w_out"].rearrange("(kc p) n -> p kc n", p=128)
                for i in range(4):
                    fw.dma("pool", WOUT[:, :, i * 256:(i + 1) * 256], wo_v[:, :, i * 256:(i + 1) * 256], writes=[kWO[i]])
                nb2 = 0
                for ti in range(NTL):
                    cs = slice(ti * TW, (ti + 1) * TW)
                    for m in range(KC):
                        bk = nb2 % 2
                        nb2 += 1
                        for kc in range(KC):
                            src, ks = (OSB, kOSB[kc % 4]) if kc < 4 else (ORW, [kORW[kc % 4]])
                            fw.op("pe", lambda e, kc=kc, src=src: e.matmul(PS[:, bk, 0:TW], WOUT[:, kc, m * 128:(m + 1) * 128], src[:, kc % 4, cs],
                                                                           start=(kc == 0), stop=(kc == KC - 1)),
                                  reads=[kWO[m // 2]] + list(ks), writes=[kPS[bk]])
                        fw.op("dve", lambda e: e.scalar_tensor_tensor(out=XF[:, m, cs], in0=XF[:, m, cs], scalar=ALPHA, in1=PS[:, bk, 0:TW],
                                                                      op0=ALU.mult, op1=ALU.add),
                              reads=[kPS[bk], kXF[m][ti]], writes=[kXF[m][ti]])
                    layer_norm_tile(c, ti, 1)
                fw.barrier()
        if c.dbg and 'nodbg' not in VAR:
            fw.dma("sp", d["dbg_osb"], OSB[:], reads=[k for kk in kOSB for k in kk])
        fw.barrier()


def sb_attention_prompt(c, d, st, QT, KT, kQT, kKT, Vtok, kV, OSB, kOSB):
    nc, fw = c.nc, c.fw
    PS, kPS = c.PS, c.kPS
    NSTR = 2
    onesb = c.onesb
    c.TRI = sb(st, nc, "TRI", [128, 128], BF16)
    c.STRICT = sb(st, nc, "STRICT", [128, 128], BF16)
    c.MASK = [sb(st, nc, "MASK%d" % i, [128, 512], BF16) for i in range(4)]
    fw.op("pool", lambda e: e.affine_select(out=c.TRI[:], in_=onesb[:, 0:128], pattern=[[-1, 128]], base=0, channel_multiplier=1,
                                            compare_op=ALU.is_ge, fill=0.0), reads=[c.kconst], writes=[c.kconst])
    for i in range(4):
        fw.op("pool", lambda e, i=i: e.affine_select(out=c.MASK[i][:], in_=onesb[:], pattern=[[1, 512]], base=-128 * i, channel_multiplier=-1,
                                                     compare_op=ALU.is_gt, fill=0.0), reads=[c.kconst], writes=[c.kconst])
    Et = [[sb(st, nc, "Et%d_%d" % (s, i), [128, 512], F32) for i in range(3)] for s in range(NSTR)]
    spt = [[sb(st, nc, "spt%d_%d" % (s, i), [128, 512], BF16) for i in range(3)] for s in range(NSTR)]
    e2t = [[sb(st, nc, "e2t%d_%d" % (s, i), [128, 512], F32) for i in range(2)] for s in range(NSTR)]
    wt = [[sb(st, nc, "wt%d_%d" % (s, i), [128, 512], BF16) for i in range(2)] for s in range(NSTR)]
    kEt = [[K() for _ in range(3)] for _ in range(NSTR)]
    kspt = [[K() for _ in range(3)] for _ in range(NSTR)]
    ke2t = [[K() for _ in range(2)] for _ in range(NSTR)]
    kwt = [[K() for _ in range(2)] for _ in range(NSTR)]
    sacc = [sb(st, nc, "sacc%d" % s, [128, 512], BF16) for s in range(NSTR)]
    ksacc = [K() for _ in range(NSTR)]
    streams = []
    for s in range(NSTR):
        blocks = []
        bi = 0
        bA = [4 * s, 4 * s + 1]
        bC = 4 * s + 2
        bO = 4 * s + 3
        for h in range(s, 8, NSTR):
            po = (h % 2) * 64
            ch = h // 2
            for qt in range(4):
                q0 = qt * 512
                nkb = 4 * qt + 4
                for n, kb in enumerate(range(nkb - 1, -1, -1)):
                    first = (n == 0)
                    last = (kb == 0)
                    diag = kb - 4 * qt
                    i3, i2 = bi % 3, bi % 2
                    bi += 1

                    def st1(s=s, h=h, po=po, ch=ch, q0=q0, kb=kb, diag=diag, i3=i3, bA=bA[bi % 2]):
                        fw.op("pe", lambda e: e.matmul(PS[:, bA, :], KT[po:po + 64, ch, kb * 128:(kb + 1) * 128], QT[po:po + 64, ch, q0:q0 + 512],
                                                       start=True, stop=True),
                              reads=[kQT[ch], kKT[ch]], writes=[kPS[bA]])
                        fw.op("act", lambda e: e.activation(out=Et[s][i3][:], in_=PS[:, bA, :], func=AF.Exp, scale=0.125,
                                                            bias=c.sbb[:, h:h + 1]),
                              reads=[kPS[bA], c.kconst], writes=[kEt[s][i3]])
                        fw.op("act", lambda e: e.activation(out=spt[s][i3][:], in_=Et[s][i3][:], func=AF.Ln, bias=c.cst[:, 1:2]),
                              reads=[kEt[s][i3], c.kconst], writes=[kspt[s][i3]])
                        if diag >= 0:
                            fw.op("dve", lambda e: e.tensor_tensor(out=spt[s][i3][:], in0=spt[s][i3][:], in1=c.MASK[diag][:], op=ALU.mult),
                                  reads=[kspt[s][i3], c.kconst], writes=[kspt[s][i3]])
                            fw.op("pool", lambda e: e.tensor_tensor(out=Et[s][i3][:], in0=Et[s][i3][:], in1=c.MASK[diag][:], op=ALU.mult),
                                  reads=[kEt[s][i3], c.kconst], writes=[kEt[s][i3]])

                    def st2(s=s, i3=i3, i2=i2, first=first, last=last, bC=bC):
                        fw.op("pe", lambda e: e.matmul(PS[:, bC, :], c.TRI[:], spt[s][i3][:], start=True, stop=first),
                              reads=[kspt[s][i3], c.kconst], writes=[kPS[bC]])
                        if not first:
                            fw.op("pe", lambda e: e.matmul(PS[:, bC, :], c.ONESB[:], sacc[s][:], start=False, stop=True),
                                  reads=[ksacc[s], c.kconst], writes=[kPS[bC]])
                        fw.op("act", lambda e: e.activation(out=e2t[s][i2][:], in_=PS[:, bC, :], func=AF.Exp, scale=-1.0),
                              reads=[kPS[bC]], writes=[ke2t[s][i2]])
                        if not last:
                            if first:
                                fw.op("pool", lambda e: e.tensor_copy(out=sacc[s][:], in_=spt[s][i3][:]),
                                      reads=[kspt[s][i3]], writes=[ksacc[s]])
                            else:
                                fw.op("pool", lambda e: e.tensor_tensor(out=sacc[s][:], in0=sacc[s][:], in1=spt[s][i3][:], op=ALU.add),
                                      reads=[kspt[s][i3], ksacc[s]], writes=[ksacc[s]])

                    def st3(s=s, h=h, po=po, ch=ch, q0=q0, qt=qt, kb=kb, i3=i3, i2=i2, first=first, last=last, bC=bC, bO=bO):
                        fw.op("dve", lambda e: e.tensor_tensor(out=wt[s][i2][:], in0=Et[s][i3][:], in1=e2t[s][i2][:], op=ALU.mult),
                              reads=[kEt[s][i3], ke2t[s][i2]], writes=[kwt[s][i2]])
                        fw.op("pe", lambda e: e.matmul(PS[po:po + 64, bO, :], Vtok[:, kb, h * 64:(h + 1) * 64], wt[s][i2][:],
                                                       start=first, stop=last),
                              reads=[kV[kb], kwt[s][i2]], writes=[kPS[bO]])
                        if last:
                            fw.op("act", lambda e: e.activation(out=OSB[po:po + 64, ch, q0:q0 + 512], in_=PS[po:po + 64, bO, :], func=AF.Identity),
                                  reads=[kPS[bO]], writes=[kOSB[ch][qt]])
                    blocks.append([st1, st2, st3])
        streams.append(blocks)
    pipeline(streams)


CS = 24


def transpose_to(c, out_ps, in_ap, kin, kout, ident):
    c.fw.op("pe", lambda e: e.transpose(out_ps, in_ap, ident), reads=kin + [c.kconst], writes=kout)


def rwkv_all(c, d, st, ORW, kORW):
    nc, fw = c.nc, c.fw
    XF, kXF, PS, kPS = c.XF, c.kXF, c.PS, c.kPS
    IDF = c.IDF
    ST = sb(st, nc, "ST", [128, 256], F32); kST = K()
    STw = sb(st, nc, "STw", [128, 256], F32); kSTw = K()
    SAY = sb(st, nc, "SAY", [128, 64], BF16); kSAY = K()
    PB = sb(st, nc, "PB", [128, 14, TW + 1], F32); kPB = [K() for _ in range(14)]
    PM = sb(st, nc, "PM", [128, 14, TW], F32); kPM = [K() for _ in range(14)]
    XBt = sb(st, nc, "XBt", [128, KC, TW], BF16); kXBt = [K() for _ in range(KC)]
    WRb = [sb(st, nc, "WRb%d" % i, [128, KC, 256], BF16) for i in range(2)]; kWRb = [K(), K()]
    WW2 = sb(st, nc, "WW2", [128, 512], BF16)
    WA2 = sb(st, nc, "WA2", [128, 512], BF16)
    WG2 = sb(st, nc, "WG2", [128, 512], BF16)
    PC = sb(st, nc, "PC", [128, 48], F32)
    GNG = sb(st, nc, "GNG", [128, 512], F32)
    GNB = sb(st, nc, "GNB", [128, 512], F32)
    BLK = sb(st, nc, "BLK", [128, 128], F32)
    kW = K()
    tmpA = sb(st, nc, "tmpA", [128, TW], F32); ktA = K()
    tmpB = sb(st, nc, "tmpB", [128, TW], F32); ktB = K()
    tmpH = sb(st, nc, "tmpH", [128, TW], BF16); ktH = K()
    NBrow = sb(st, nc, "NBrow", [128, CS, 128], BF16)
    KBrow = sb(st, nc, "KBrow", [128, CS, 128], BF16)
    Vrow = sb(st, nc, "Vrow", [128, CS, 64], BF16)
    kRow = K()
    TOK = sb(st, nc, "TOK", [128, 3, 512], BF16); kTOK = [K(), K(), K()]
    LKc = sb(st, nc, "LKc", [128, CS, 4, 2], F32)
    RKc = sb(st, nc, "RKc", [128, CS, 4, 2], F32)
    kLK = K()
    Ybuf = sb(st, nc, "Ybuf", [128, CS, 64], BF16); kY = K()
    YTOK = sb(st, nc, "YTOK", [128, 512], BF16); kYT = K()
    YC = sb(st, nc, "YC", [128, 512], F32); kYC = K()
    YS = sb(st, nc, "YS", [128, 512], F32); kYS = K()
    gst = sb(st, nc, "gst", [128, 32], F32); kgst = K()
    SHX = sb(st, nc, "SHX", [NS + 1, RWC], F32); kSHT = K()
    SHTOK = SHX
    SHT = sb(st, nc, "SHT", [128, 14, NS], F32); kSH = K()
    SSH = SHX; kSSH = kSHT
    SLD = sb(st, nc, "SLD", [64, 4, 128], F32); kSLD = K()
    SSTt = SLD; kSST = kSLD
    fw.dma("pool", WW2[0:64, :], d["w_w2"], writes=[kW])
    fw.dma("pool", WA2[64:128, :], d["w_a2"], writes=[kW])
    fw.dma("pool", WG2[:, :], d["w_g2"], writes=[kW])
    fw.dma("pool", PC[:], d["pcol"], writes=[kW])
    fw.dma("pool", GNG[:], d["gng"], writes=[kW])
    fw.dma("pool", GNB[:], d["gnb"], writes=[kW])
    fw.dma("pool", SHTOK[0:NS, :], d["sshift0"], writes=[kSHT])
    fw.op("pool", lambda e: e.memset(BLK[:], 0.0), writes=[kW])
    fw.op("pool", lambda e: e.memset(BLK[0:64, 0:64], 1.0), reads=[kW], writes=[kW])
    fw.op("pool", lambda e: e.memset(BLK[64:128, 64:128], 1.0), reads=[kW], writes=[kW])
    fw.op("pool", lambda e: e.memset(ST[:], 0.0), writes=[kST])
    fw.op("pool", lambda e: e.memset(NBrow[:], 0.0), writes=[kRow])
    fw.op("pool", lambda e: e.memset(KBrow[:], 0.0), writes=[kRow])
    fw.op("pool", lambda e: e.memset(Vrow[:], 0.0), writes=[kRow])
    fw.op("pool", lambda e: e.memset(LKc[:], 0.0), writes=[kLK])
    fw.op("pool", lambda e: e.memset(RKc[:], 0.0), writes=[kLK])
    fw.op("pool", lambda e: e.memset(PB[:, :, 0:1], 0.0), writes=kPB)
    MU, W0, A0, KKc, KAc, RKp = 0, 14, 18, 22, 26, 30
    for bz in (2, 3, 4, 5, 6, 7):
        fw.op("dve", lambda e, bz=bz: e.memset(PS[:, bz, :], 0.0), writes=[kPS[bz]])
    for m in range(14):
        transpose_to(c, PS[:, 7, m * NS:(m + 1) * NS], SHTOK[0:NS, m * 128:(m + 1) * 128], [kSHT], [kPS[7]], IDF[0:NS, 0:NS])
    fw.op("act", lambda e: e.activation(out=SHT[:].rearrange("p m b -> p (m b)"), in_=PS[:, 7, 0:14 * NS], func=AF.Identity),
          reads=[kPS[7]], writes=[kSH])
    wr_v = d["w_in"].rearrange("(kc p) n -> p kc n", p=128)
    nwr = 0
    nbk = 0

    def store_state(dst):
        for h4 in range(4):
            transpose_to(c, PS[0:64, 0, h4 * 128:(h4 + 1) * 128], ST[:, h4 * 64:(h4 + 1) * 64], [kST], [kPS[0]], IDF[:, :])
        fw.op("act", lambda e: e.activation(out=SSTt[:].rearrange("p a b -> p (a b)"), in_=PS[0:64, 0, :], func=AF.Identity),
              reads=[kPS[0]], writes=[kSST])
        fw.dma("pool", dst.rearrange("(h4 h2) i j -> i h4 h2 j", h2=2), SSTt[:].rearrange("p a (h2 j) -> p a h2 j", h2=2), reads=[kSST])

    def load_state(src):
        fw.dma("pool", SLD[:].rearrange("p a (h2 j) -> p a h2 j", h2=2), src.rearrange("(h4 h2) i j -> i h4 h2 j", h2=2), writes=[kSLD])
        for h4 in range(4):
            transpose_to(c, PS[:, 0, h4 * 64:(h4 + 1) * 64], SLD[:, h4, :], [kSLD], [kPS[0]], IDF[0:64, 0:64])
        fw.op("act", lambda e: e.activation(out=ST[:], in_=PS[:, 0, 0:256], func=AF.Identity), reads=[kPS[0]], writes=[kST])

    for ti in ([5] if 'rw1' in VAR else [0] if 'rw0' in VAR else range(NTL)):
        c0 = ti * TW
        npr = min(TW, NP - c0)
        for kc in range(KC):
            fw.op("act", lambda e, kc=kc: e.activation(out=XBt[:, kc, :], in_=XF[:, kc, c0:c0 + TW], func=AF.Identity),
                  reads=[kXF[kc][ti]], writes=[kXBt[kc]])
        for mp in range(7):
            sl = nwr % 2
            nwr += 1
            fw.dma("pool", WRb[sl][:], wr_v[:, :, 1536 + mp * 256:1536 + (mp + 1) * 256], writes=[kWRb[sl]])
            for mm in range(2):
                m = 2 * mp + mm
                bk = nbk % 2
                nbk += 1
                for kc in range(KC):
                    fw.op("pe", lambda e, kc=kc: e.matmul(PS[:, bk, 0:TW], WRb[sl][:, kc, mm * 128:(mm + 1) * 128], XBt[:, kc, :],
                                                          start=(kc == 0), stop=(kc == KC - 1)),
                          reads=[kWRb[sl], kXBt[kc]], writes=[kPS[bk]])
                fw.op("act", lambda e: e.activation(out=PB[:, m, 1:TW + 1], in_=PS[:, bk, 0:TW], func=AF.Identity),
                      reads=[kPS[bk]], writes=[kPB[m]])
        for m in range(14):
            fw.op("pool", lambda e, m=m: e.tensor_tensor(out=PM[:, m, 0:npr], in0=PB[:, m, 0:npr], in1=PB[:, m, 1:npr + 1], op=ALU.subtract),
                  reads=[kPB[m]], writes=[kPM[m]])
            if npr < TW:
                fw.op("pool", lambda e, m=m: e.tensor_tensor(out=PM[:, m, npr:TW], in0=SHT[:, m, :], in1=PB[:, m, npr + 1:TW + 1], op=ALU.subtract),
                      reads=[kPB[m], kSH], writes=[kPM[m]])
            fw.op("dve", lambda e, m=m: e.scalar_tensor_tensor(out=PM[:, m, :], in0=PM[:, m, :], scalar=PC[:, MU + m:MU + m + 1],
                                                               in1=PB[:, m, 1:TW + 1], op0=ALU.mult, op1=ALU.add),
                  reads=[kPB[m], kPM[m], kW], writes=[kPM[m]])
        if npr < TW:
            for m in range(14):
                transpose_to(c, PS[0:NS + 1, 7, (m % 4) * 128:(m % 4 + 1) * 128], PB[:, m, npr:TW + 1], [kPB[m]], [kPS[7]], IDF[:, :])
                if m % 4 == 3 or m == 13:
                    m0 = (m // 4) * 4
                    fw.op("act", lambda e, m0=m0, m=m: e.activation(out=SSH[:, m0 * 128:(m + 1) * 128], in_=PS[0:NS + 1, 7, 0:(m - m0 + 1) * 128], func=AF.Identity),
                          reads=[kPS[7]], writes=[kSSH])
            fw.dma("pool", d["pshift"], SSH[0:1, :], reads=[kSSH])
            fw.dma("pool", d["sshift"], SSH[1:NS + 1, :], reads=[kSSH])
        fw.op("dve", lambda e: e.tensor_copy(out=PB[:, :, 0:1], in_=PB[:, :, TW:TW + 1]), reads=kPB, writes=kPB)
        Wt = lambda cc: PB[:, cc, 1:TW + 1]
        KKt = lambda cc: PB[:, 4 + cc, 1:TW + 1]
        NBt = lambda cc: PB[:, 8 + cc, 1:TW + 1]
        fw.op("act", lambda e: e.activation(out=tmpH[0:64, :], in_=PM[0:64, 12, :], func=AF.Tanh), reads=[kPM[12]], writes=[ktH])
        fw.op("act", lambda e: e.activation(out=tmpH[64:128, :], in_=PM[64:128, 12, :], func=AF.Identity), reads=[kPM[12]], writes=[ktH])
        for cc in range(4):
            bk = nbk % 2
            nbk += 1
            fw.op("pe", lambda e: e.matmul(PS[:, bk, 0:TW], WW2[0:64, cc * 128:(cc + 1) * 128], tmpH[0:64, :], start=True, stop=True),
                  reads=[kW, ktH], writes=[kPS[bk]])
            fw.op("act", lambda e: e.activation(out=tmpA[:], in_=PS[:, bk, 0:TW], func=AF.Sigmoid, bias=PC[:, W0 + cc:W0 + cc + 1]),
                  reads=[kPS[bk], kW], writes=[ktA])
            fw.op("act", lambda e: e.activation(out=Wt(cc), in_=tmpA[:], func=AF.Exp, scale=-0.6065306597126334),
                  reads=[ktA], writes=[kPB[cc]])
        for cc in range(4):
            bk = nbk % 2
            nbk += 1
            fw.op("pe", lambda e: e.matmul(PS[:, bk, 0:TW], WA2[64:128, cc * 128:(cc + 1) * 128], tmpH[64:128, :], start=True, stop=True),
                  reads=[kW, ktH], writes=[kPS[bk]])
            fw.op("act", lambda e: e.activation(out=tmpA[:], in_=PS[:, bk, 0:TW], func=AF.Sigmoid, bias=PC[:, A0 + cc:A0 + cc + 1]),
                  reads=[kPS[bk], kW], writes=[ktA])
            fw.op("dve", lambda e: e.tensor_scalar(out=KKt(cc), in0=PM[:, 4 + cc, :], scalar1=PC[:, KKc + cc:KKc + cc + 1], scalar2=None, op0=ALU.mult),
                  reads=[kPM[4 + cc], kW], writes=[kPB[4 + cc]])
            fw.op("act", lambda e: e.activation(out=tmpB[:], in_=KKt(cc), func=AF.Square), reads=[kPB[4 + cc]], writes=[ktB])
            b2 = 2 + (nbk % 2)
            fw.op("pe", lambda e: e.matmul(PS[:, b2, 0:TW], BLK[:], tmpB[:], start=True, stop=True), reads=[kW, ktB], writes=[kPS[b2]])
            fw.op("dve", lambda e: e.tensor_scalar(out=tmpB[:], in0=PS[:, b2, 0:TW], scalar1=1e-24, scalar2=None, op0=ALU.max),
                  reads=[kPS[b2]], writes=[ktB])
            fw.op("act", lambda e: e.activation(out=tmpB[:], in_=tmpB[:], func=AF.Sqrt), reads=[ktB], writes=[ktB])
            fw.op("dve", lambda e: e.reciprocal(out=tmpB[:], in_=tmpB[:]), reads=[ktB], writes=[ktB])
            fw.op("dve", lambda e: e.tensor_tensor(out=KKt(cc), in0=KKt(cc), in1=tmpB[:], op=ALU.mult), reads=[kPB[4 + cc], ktB], writes=[kPB[4 + cc]])
            fw.op("dve", lambda e: e.scalar_tensor_tensor(out=NBt(cc), in0=KKt(cc), scalar=-1.0, in1=tmpA[:], op0=ALU.mult, op1=ALU.mult),
                  reads=[kPB[4 + cc], ktA], writes=[kPB[8 + cc]])
            fw.op("dve", lambda e: e.tensor_scalar(out=tmpA[:], in0=tmpA[:], scalar1=-1.0, scalar2=PC[:, KAc + cc:KAc + cc + 1], op0=ALU.add, op1=ALU.mult),
                  reads=[ktA, kW], writes=[ktA])
            fw.op("dve", lambda e: e.scalar_tensor_tensor(out=PM[:, 4 + cc, :], in0=tmpA[:], scalar=1.0, in1=PM[:, 4 + cc, :], op0=ALU.add, op1=ALU.mult),
                  reads=[ktA, kPM[4 + cc]], writes=[kPM[4 + cc]])
        if 'rwA' in VAR:
            continue
        for a in range(0, TW, CS):
            ncol = min(CS, TW - a)
            for vi, src in enumerate((lambda cc: PM[:, 4 + cc, a:a + ncol], lambda cc: NBt(cc)[:, a:a + ncol], lambda cc: PM[:, 8 + cc, a:a + ncol])):
                kk_ = (lambda cc: kPM[4 + cc], lambda cc: kPB[8 + cc], lambda cc: kPM[8 + cc])[vi]
                bk = nbk % 2
                nbk += 1
                for cc in range(4):
                    transpose_to(c, PS[0:ncol, bk, cc * 128:(cc + 1) * 128], src(cc), [kk_(cc)], [kPS[bk]], IDF[:, :])
                fw.op("act", lambda e: e.activation(out=TOK[0:ncol, vi, :], in_=PS[0:ncol, bk, :], func=AF.Identity), reads=[kPS[bk]], writes=[kTOK[vi]])
            fw.dma("pool", c.TOKd[0:ncol], TOK[0:ncol, :, :], reads=kTOK, writes=[c.kTOKd])
            for h2 in range(2):
                for vi, dstt in enumerate((KBrow, NBrow)):
                    fw.dma("pool", dstt[h2:128:32, 0:ncol, h2 * 64:(h2 + 1) * 64],
                           c.TOKd[0:ncol, vi, :].rearrange("s (h4 h2 j) -> h4 s h2 j", h4=4, h2=2)[:, :, h2, :], reads=[c.kTOKd], writes=[kRow])
                fw.dma("pool", Vrow[h2:128:32, 0:ncol, :],
                       c.TOKd[0:ncol, 2, :].rearrange("s (h4 h2 j) -> h4 s h2 j", h4=4, h2=2)[:, :, h2, :], reads=[c.kTOKd], writes=[kRow])
            for h2 in range(2):
                ps_ = slice(h2 * 64, (h2 + 1) * 64)
                fw.op("pool", lambda e: e.tensor_copy(out=LKc[ps_, 0:ncol, :, h2], in_=PB[ps_, 4:8, 1 + a:1 + a + ncol].rearrange("p c s -> p s c")),
                      reads=kPB[4:8], writes=[kLK])
                fw.op("pool", lambda e: e.tensor_copy(out=RKc[ps_, 0:ncol, :, h2], in_=PM[ps_, 0:4, a:a + ncol].rearrange("p c s -> p s c")),
                      reads=kPM[0:4], writes=[kLK])
            def emit_y(sy, ycol_cnt):
                yb = sy % 8
                for h4 in range(4):
                    fw.op("pe", lambda e, h4=h4: e.matmul(PS[32 * h4:32 * h4 + 2, 7, yb * 64:(yb + 1) * 64], RKc[:, sy, h4, :], ST[:, h4 * 64:(h4 + 1) * 64],
                                                          start=True, stop=True, tile_position=(0, 32 * h4)),
                          reads=[kLK, kST], writes=[kPS[7]])
                if yb == 7 or sy == ncol - 1:
                    s0_ = sy - yb
                    fw.op("act", lambda e: e.activation(out=Ybuf[:, s0_:sy + 1, :].rearrange("p s i -> p (s i)"), in_=PS[:, 7, 0:(yb + 1) * 64], func=AF.Identity),
                          reads=[kPS[7]], writes=[kY])

            pending_y = None
            for s_ in range(ncol if 'rwB' not in VAR else 0):
                gcol = c0 + a + s_
                is_sample = gcol >= NP
                if is_sample:
                    if pending_y is not None:
                        emit_y(pending_y, ncol); pending_y = None
                    load_state(d["swkv0"][gcol - NP])
                for h4 in range(4):
                    fw.op("pe", lambda e, h4=h4: e.matmul(PS[32 * h4:32 * h4 + 2, 2, 0:64], LKc[:, s_, h4, :], ST[:, h4 * 64:(h4 + 1) * 64],
                                                          start=True, stop=True, tile_position=(0, 32 * h4)),
                          reads=[kLK, kST], writes=[kPS[2]])
                fw.op("dve", lambda e: e.tensor_tensor(out=STw[:].rearrange("p (a b) -> p a b", a=4), in0=ST[:].rearrange("p (a b) -> p a b", a=4),
                                                       in1=PB[:, 0:4, 1 + a + s_:2 + a + s_].to_broadcast([128, 4, 64]), op=ALU.mult),
                      reads=[kST] + kPB[0:4], writes=[kSTw])
                if pending_y is not None:
                    emit_y(pending_y, ncol); pending_y = None
                for h4 in range(4):
                    fw.op("pe", lambda e, h4=h4: e.matmul(PS[:, 3 + h4, 0:64], KBrow[32 * h4:32 * h4 + 2, s_, :], Vrow[32 * h4:32 * h4 + 2, s_, :],
                                                          start=True, stop=False, tile_position=(32 * h4, 0)),
                          reads=[kRow], writes=[kPS[3 + h4]])
                fw.op("act", lambda e: e.activation(out=SAY[:], in_=PS[:, 2, 0:64], func=AF.Identity), reads=[kPS[2]], writes=[kSAY])
                for h4 in range(4):
                    fw.op("pe", lambda e, h4=h4: e.matmul(PS[:, 3 + h4, 0:64], NBrow[32 * h4:32 * h4 + 2, s_, :], SAY[32 * h4:32 * h4 + 2, :],
                                                          start=False, stop=True, tile_position=(32 * h4, 0)),
                          reads=[kRow, kSAY], writes=[kPS[3 + h4]])
                fw.op("dve", lambda e: e.tensor_tensor(out=ST[:].rearrange("p (a b) -> p a b", a=4), in0=STw[:].rearrange("p (a b) -> p a b", a=4),
                                                       in1=PS[:, 3:7, 0:64], op=ALU.add),
                      reads=[kSTw] + kPS[3:7], writes=[kST])
                pending_y = s_
                if is_sample or gcol == NP - 1 or s_ == ncol - 1:
                    emit_y(pending_y, ncol); pending_y = None
                if is_sample:
                    store_state(d["swkv"][gcol - NP])
                if gcol == NP - 1:
                    store_state(d["pwkv"])
            if 'rwB' in VAR or 'rwC' in VAR:
                continue
            for h2 in range(2):
                fw.dma("pool", c.Yd[:, h2, 0:ncol, :], Ybuf[h2:128:32, 0:ncol, :], reads=[kY], writes=[c.kYd])
            fw.dma("pool", YTOK[0:ncol, :].rearrange("s (h i) -> s h i", h=8), c.Yd[:, :, 0:ncol, :].rearrange("a b s i -> s (a b) i"),
                   reads=[c.kYd], writes=[kYT])
            Y3 = YTOK[0:ncol, :].rearrange("s (h i) -> s h i", h=8)
            C3 = YC[0:ncol, :].rearrange("s (h i) -> s h i", h=8)
            S3 = YS[0:ncol, :].rearrange("s (h i) -> s h i", h=8)
            fw.op("dve", lambda e: e.tensor_reduce(out=gst[0:ncol, 0:8], in_=Y3, axis=AX.X, op=ALU.add), reads=[kYT], writes=[kgst])
            fw.op("dve", lambda e: e.tensor_scalar(out=gst[0:ncol, 0:8], in0=gst[0:ncol, 0:8], scalar1=1.0 / 64, scalar2=None, op0=ALU.mult), reads=[kgst], writes=[kgst])
            fw.op("dve", lambda e: e.tensor_tensor(out=C3, in0=Y3, in1=gst[0:ncol, 0:8].unsqueeze(2).to_broadcast([ncol, 8, 64]), op=ALU.subtract),
                  reads=[kYT, kgst], writes=[kYC])
            fw.op("act", lambda e: e.activation(out=YS[0:ncol, :], in_=YC[0:ncol, :], func=AF.Square), reads=[kYC], writes=[kYS])
            fw.op("dve", lambda e: e.tensor_reduce(out=gst[0:ncol, 8:16], in_=S3, axis=AX.X, op=ALU.add), reads=[kYS], writes=[kgst])
            fw.op("act", lambda e: e.activation(out=gst[0:ncol, 8:16], in_=gst[0:ncol, 8:16], func=AF.Sqrt, scale=1.0 / 64, bias=c.cst[0:ncol, 2:3]),
                  reads=[kgst, c.kconst], writes=[kgst])
            fw.op("dve", lambda e: e.reciprocal(out=gst[0:ncol, 8:16], in_=gst[0:ncol, 8:16]), reads=[kgst], writes=[kgst])
            fw.op("dve", lambda e: e.tensor_tensor(out=C3, in0=C3, in1=gst[0:ncol, 8:16].unsqueeze(2).to_broadcast([ncol, 8, 64]), op=ALU.mult),
                  reads=[kYC, kgst], writes=[kYC])
            fw.op("pool", lambda e: e.tensor_tensor(out=YC[0:ncol, :], in0=YC[0:ncol, :], in1=GNG[0:ncol, :], op=ALU.mult), reads=[kYC, kW], writes=[kYC])
            fw.op("pool", lambda e: e.tensor_tensor(out=YC[0:ncol, :], in0=YC[0:ncol, :], in1=GNB[0:ncol, :], op=ALU.add), reads=[kYC, kW], writes=[kYC])
            bk = 2
            for cc in range(4):
                transpose_to(c, PS[:, bk, cc * 64:cc * 64 + ncol], YC[0:ncol, cc * 128:(cc + 1) * 128], [kYC], [kPS[bk]], IDF[0:ncol, 0:ncol])
            for cc in range(4):
                fw.op("dve", lambda e: e.scalar_tensor_tensor(out=tmpA[:, 0:ncol], in0=PM[:, cc, a:a + ncol], scalar=PC[:, RKp + cc:RKp + cc + 1],
                                                              in1=PM[:, 4 + cc, a:a + ncol], op0=ALU.mult, op1=ALU.mult),
                      reads=[kPM[cc], kPM[4 + cc], kW], writes=[ktA])
                b2 = nbk % 2
                nbk += 1
                fw.op("pe", lambda e: e.matmul(PS[:, b2, 0:ncol], BLK[:], tmpA[:, 0:ncol], start=True, stop=True), reads=[kW, ktA], writes=[kPS[b2]])
                fw.op("dve", lambda e: e.tensor_tensor(out=tmpB[:, 0:ncol], in0=PS[:, b2, 0:ncol], in1=PM[:, 8 + cc, a:a + ncol], op=ALU.mult),
                      reads=[kPS[b2], kPM[8 + cc]], writes=[ktB])
                fw.op("dve", lambda e: e.tensor_tensor(out=tmpB[:, 0:ncol], in0=tmpB[:, 0:ncol], in1=PS[:, bk, cc * 64:cc * 64 + ncol], op=ALU.add),
                      reads=[ktB, kPS[bk]], writes=[ktB])
                if cc == 0:
                    fw.op("act", lambda e: e.activation(out=tmpH[:, 0:ncol], in_=PM[:, 13, a:a + ncol], func=AF.Sigmoid), reads=[kPM[13]], writes=[ktH])
                b3 = nbk % 2
                nbk += 1
                fw.op("pe", lambda e: e.matmul(PS[:, b3, 0:ncol], WG2[:, cc * 128:(cc + 1) * 128], tmpH[:, 0:ncol], start=True, stop=True),
                      reads=[kW, ktH], writes=[kPS[b3]])
                fw.op("dve", lambda e: e.tensor_tensor(out=ORW[:, cc, c0 + a:c0 + a + ncol], in0=tmpB[:, 0:ncol], in1=PS[:, b3, 0:ncol], op=ALU.mult),
                      reads=[ktB, kPS[b3]], writes=[kORW[cc]])


TWO_PI = 6.283185307179586
C1 = 6.28125
C2 = TWO_PI - 6.28125
PI = 3.141592653589793


def trig_tables(c, X, kX, Sout, Cout, kS, kC, tmpI, tmpF, ktmp, width):
    fw = c.fw
    w = slice(0, width)
    fw.op("dve", lambda e: e.tensor_scalar(out=tmpI[:, w], in0=X[:, w], scalar1=1.0 / TWO_PI, scalar2=None, op0=ALU.mult), reads=[kX], writes=[ktmp])
    fw.op("dve", lambda e: e.tensor_copy(out=tmpF[:, w], in_=tmpI[:, w]), reads=[ktmp], writes=[ktmp])
    fw.op("dve", lambda e: e.scalar_tensor_tensor(out=X[:, w], in0=tmpF[:, w], scalar=-C1, in1=X[:, w], op0=ALU.mult, op1=ALU.add), reads=[ktmp, kX], writes=[kX])
    fw.op("dve", lambda e: e.scalar_tensor_tensor(out=X[:, w], in0=tmpF[:, w], scalar=-C2, in1=X[:, w], op0=ALU.mult, op1=ALU.add), reads=[ktmp, kX], writes=[kX])
    fw.op("dve", lambda e: e.tensor_scalar(out=X[:, w], in0=X[:, w], scalar1=PI, scalar2=-PI, op0=ALU.min, op1=ALU.max), reads=[kX], writes=[kX])
    fw.op("act", lambda e: e.activation(out=Sout, in_=X[:, w], func=AF.Sin), reads=[kX], writes=[kS])
    fw.op("act", lambda e: e.activation(out=tmpF[:, w], in_=X[:, w], func=AF.Abs), reads=[kX, ktmp], writes=[ktmp])
    fw.op("act", lambda e: e.activation(out=Cout, in_=tmpF[:, w], func=AF.Sin, scale=-1.0, bias=c.cst[:, 3:4]), reads=[ktmp, c.kconst], writes=[kC])


def s5_mixer(c, d):
    nc, fw = c.nc, c.fw
    XF, kXF, PS, kPS = c.XF, c.kXF, c.PS, c.kPS
    IDF = c.IDF
    with ExitStack() as st:
        ZG = sb(st, nc, "ZG", [128, KC, NT], BF16); kZG = [[K() for _ in range(NTL)] for _ in range(KC)]
        with ExitStack() as s2:
            XB = sb(s2, nc, "XBs", [128, KC, NT], BF16); kXB = [K() for _ in range(KC)]
            for kc in range(KC):
                fw.op("act", lambda e, kc=kc: e.activation(out=XB[:, kc, :], in_=XF[:, kc, :], func=AF.Identity), reads=kXF[kc], writes=[kXB[kc]])
            PRM = sb(s2, nc, "PRM", [128, 3, 32], F32); kP = K()
            fw.dma("sp", PRM[:], d["s5prm"], writes=[kP])
            DSK = sb(s2, nc, "DSK", [128, 8], F32)
            fw.dma("sp", DSK[:], d["dskip"], writes=[kP])
            S0 = sb(s2, nc, "S0", [128, 32, NS, 2], F32); kS0 = K()
            for t_ in range(32):
                fw.dma("sp", S0[:, t_, :, :], d["s5_0"][:, t_ * 128:(t_ + 1) * 128, :].rearrange("b p r -> p b r"), writes=[kS0])
            cn = {nm: sb(s2, nc, "c_" + nm, [128, 32], F32) for nm in ("DT", "MAGL", "TH", "MAG", "X", "SI", "CO", "AR", "AI", "CR", "CI", "T1", "T2", "RD")}
            tI = sb(s2, nc, "tI32", [128, TW], I32)
            tF = sb(s2, nc, "tF32", [128, TW], F32)
            ktmp = K()
            LR, LI, LDT = PRM[:, 0, :], PRM[:, 1, :], PRM[:, 2, :]
            kc_ = K()

            def o(eng, fn, r=(), w=()):
                fw.op(eng, fn, reads=[kP, kc_] + list(r), writes=[kc_] + list(w))
            o("act", lambda e: e.activation(out=cn["DT"][:], in_=LDT, func=AF.Exp))
            o("dve", lambda e: e.tensor_tensor(out=cn["MAGL"][:], in0=LR, in1=cn["DT"][:], op=ALU.mult))
            o("dve", lambda e: e.tensor_tensor(out=cn["TH"][:], in0=LI, in1=cn["DT"][:], op=ALU.mult))
            o("act", lambda e: e.activation(out=cn["MAG"][:], in_=cn["MAGL"][:], func=AF.Exp))
            o("dve", lambda e: e.tensor_copy(out=cn["X"][:], in_=cn["TH"][:]))
            trig_tables(c, cn["X"], kc_, cn["SI"][:], cn["CO"][:], kc_, kc_, tI, tF, ktmp, 32)
            o("dve", lambda e: e.tensor_tensor(out=cn["AR"][:], in0=cn["MAG"][:], in1=cn["CO"][:], op=ALU.mult))
            o("dve", lambda e: e.tensor_tensor(out=cn["AI"][:], in0=cn["MAG"][:], in1=cn["SI"][:], op=ALU.mult))
            o("dve", lambda e: e.tensor_tensor(out=cn["T1"][:], in0=LR, in1=LR, op=ALU.mult))
            o("dve", lambda e: e.tensor_tensor(out=cn["T2"][:], in0=LI, in1=LI, op=ALU.mult))
            o("dve", lambda e: e.tensor_tensor(out=cn["T1"][:], in0=cn["T1"][:], in1=cn["T2"][:], op=ALU.add))
            o("dve", lambda e: e.reciprocal(out=cn["RD"][:], in_=cn["T1"][:]))
            o("dve", lambda e: e.tensor_scalar(out=cn["T1"][:], in0=cn["AR"][:], scalar1=-1.0, scalar2=None, op0=ALU.add))
            o("dve", lambda e: e.tensor_tensor(out=cn["CR"][:], in0=cn["T1"][:], in1=LR, op=ALU.mult))
            o("dve", lambda e: e.tensor_tensor(out=cn["T2"][:], in0=cn["AI"][:], in1=LI, op=ALU.mult))
            o("dve", lambda e: e.tensor_tensor(out=cn["CR"][:], in0=cn["CR"][:], in1=cn["T2"][:], op=ALU.add))
            o("dve", lambda e: e.tensor_tensor(out=cn["CR"][:], in0=cn["CR"][:], in1=cn["RD"][:], op=ALU.mult))
            o("dve", lambda e: e.tensor_tensor(out=cn["CI"][:], in0=cn["AI"][:], in1=LR, op=ALU.mult))
            o("dve", lambda e: e.tensor_tensor(out=cn["T2"][:], in0=cn["T1"][:], in1=LI, op=ALU.mult))
            o("dve", lambda e: e.tensor_tensor(out=cn["CI"][:], in0=cn["CI"][:], in1=cn["T2"][:], op=ALU.subtract))
            o("dve", lambda e: e.tensor_tensor(out=cn["CI"][:], in0=cn["CI"][:], in1=cn["RD"][:], op=ALU.mult))
            IOT = sb(s2, nc, "IOT", [128, TW], F32)
            o("pool", lambda e: e.iota(IOT[:], [[1, TW]], base=1, channel_multiplier=0, allow_small_or_imprecise_dtypes=True))
            TB = [[sb(s2, nc, "TB%d_%d" % (k, j), [128, TW], F32) for j in range(4)] for k in range(4)]
            kTB = [[K() for _ in range(4)] for _ in range(4)]
            XA = sb(s2, nc, "XA", [128, TW], F32); kXA = K()
            Pt = [sb(s2, nc, "Pt%d" % i, [128, TW], F32) for i in range(6)]; kPt = [K() for _ in range(6)]
            Qr = sb(s2, nc, "Qr", [128, TW], F32); Qi = sb(s2, nc, "Qi", [128, TW], F32); kQ = [K(), K()]
            S16 = [sb(s2, nc, "S16_%d" % i, [128, TW], BF16) for i in range(2)]; kS16 = [K(), K()]
            BCm = sb(s2, nc, "BCm", [128, 4, 4, 128], BF16); kBC = K()
            SE = sb(s2, nc, "SE", [128, 32, 2], F32); kSE = K()
            SN = sb(s2, nc, "SN", [128, 2, NS], F32); kSN = K()
            OUT16 = [sb(s2, nc, "OUT16_%d" % i, [NS, 256], F32) for i in range(2)]; kO16 = [K(), K()]
            RHO = sb(s2, nc, "RHO", [128, TW], F32); kRHO = K()
            fw.op("pool", lambda e: e.memset(SE[:], 0.0), writes=[kSE])
            fw.op("pool", lambda e: e.memset(RHO[:], 1.0), writes=[kRHO])
            nb = 0
            for m in range(KC):
                fw.dma("pool", BCm[:].rearrange("p a b n -> p (a b) n"), d["s5bc"][m].rearrange("a p n -> p a n"), writes=[kBC])
                for k in range(4):
                    stn = 4 * m + k
                    col = slice(stn, stn + 1)
                    TR, TI, CC, SS = TB[k]
                    fw.op("dve", lambda e: e.tensor_scalar(out=XA[:], in0=IOT[:], scalar1=cn["TH"][:, col], scalar2=None, op0=ALU.mult),
                          reads=[kc_], writes=[kXA])
                    trig_tables(c, XA, kXA, SS[:], CC[:], kTB[k][3], kTB[k][2], tI, tF, ktmp, TW)
                    fw.op("dve", lambda e: e.tensor_scalar(out=TR[:], in0=CC[:], scalar1=cn["CR"][:, col], scalar2=None, op0=ALU.mult), reads=[kTB[k][2], kc_], writes=[kTB[k][0]])
                    fw.op("dve", lambda e: e.scalar_tensor_tensor(out=TR[:], in0=SS[:], scalar=cn["CI"][:, col], in1=TR[:], op0=ALU.mult, op1=ALU.add), reads=[kTB[k][3], kTB[k][0], kc_], writes=[kTB[k][0]])
                    fw.op("dve", lambda e: e.tensor_scalar(out=TI[:], in0=CC[:], scalar1=cn["CI"][:, col], scalar2=None, op0=ALU.mult), reads=[kTB[k][2], kc_], writes=[kTB[k][1]])
                    fw.op("dve", lambda e: e.tensor_scalar(out=XA[:], in0=SS[:], scalar1=cn["CR"][:, col], scalar2=None, op0=ALU.mult), reads=[kTB[k][3], kc_], writes=[kXA])
                    fw.op("dve", lambda e: e.tensor_tensor(out=TI[:], in0=TI[:], in1=XA[:], op=ALU.subtract), reads=[kTB[k][1], kXA], writes=[kTB[k][1]])
                for ti in range(NTL):
                    c0 = ti * TW
                    npr = min(TW, NP - c0)
                    yb = 6 + (ti % 2)
                    for k in range(4):
                        stn = 4 * m + k
                        col = slice(stn, stn + 1)
                        TR, TI, CC, SS = TB[k]
                        br, bi = 2 * (nb % 2), 2 * (nb % 2) + 1
                        nb += 1
                        fw.op("pe", lambda e: e.matmul(PS[:, br, 0:TW], BCm[:, k, 0, :], XB[:, m, c0:c0 + TW], start=True, stop=True), reads=[kBC, kXB[m]], writes=[kPS[br]])
                        fw.op("pe", lambda e: e.matmul(PS[:, bi, 0:TW], BCm[:, k, 1, :], XB[:, m, c0:c0 + TW], start=True, stop=True), reads=[kBC, kXB[m]], writes=[kPS[bi]])
                        w = slice(0, npr)
                        fw.op("pool", lambda e: e.tensor_scalar(out=RHO[:, w], in0=IOT[:, w], scalar1=0.0, scalar2=cn["MAG"][:, col], op0=ALU.mult, op1=ALU.add),
                              reads=[kc_], writes=[kRHO])
                        fw.op("dve", lambda e: e.tensor_tensor(out=Pt[0][:, w], in0=PS[:, br, w], in1=TR[:, w], op=ALU.mult), reads=[kPS[br], kTB[k][0]], writes=[kPt[0]])
                        fw.op("dve", lambda e: e.tensor_tensor(out=Pt[1][:, w], in0=PS[:, bi, w], in1=TI[:, w], op=ALU.mult), reads=[kPS[bi], kTB[k][1]], writes=[kPt[1]])
                        fw.op("dve", lambda e: e.tensor_tensor(out=Pt[2][:, w], in0=PS[:, bi, w], in1=TR[:, w], op=ALU.mult), reads=[kPS[bi], kTB[k][0]], writes=[kPt[2]])
                        fw.op("dve", lambda e: e.tensor_tensor(out=Pt[3][:, w], in0=PS[:, br, w], in1=TI[:, w], op=ALU.mult), reads=[kPS[br], kTB[k][1]], writes=[kPt[3]])
                        fw.op("pool", lambda e: e.tensor_tensor(out=Pt[0][:, w], in0=Pt[0][:, w], in1=Pt[1][:, w], op=ALU.subtract), reads=[kPt[0], kPt[1]], writes=[kPt[0]])
                        fw.op("pool", lambda e: e.tensor_tensor(out=Pt[2][:, w], in0=Pt[2][:, w], in1=Pt[3][:, w], op=ALU.add), reads=[kPt[2], kPt[3]], writes=[kPt[2]])
                        fw.op("dve", lambda e: e.tensor_tensor_scan(Qr[:, w], RHO[:, w], Pt[0][:, w], SE[:, stn, 0:1], ALU.mult, ALU.add), reads=[kRHO, kPt[0], kSE], writes=[kQ[0]])
                        fw.op("dve", lambda e: e.tensor_tensor_scan(Qi[:, w], RHO[:, w], Pt[2][:, w], SE[:, stn, 1:2], ALU.mult, ALU.add), reads=[kRHO, kPt[2], kSE], writes=[kQ[1]])
                        fw.op("pool", lambda e: e.tensor_tensor(out=Pt[4][:, w], in0=CC[:, w], in1=Qr[:, w], op=ALU.mult), reads=[kTB[k][2], kQ[0]], writes=[kPt[4]])
                        fw.op("pool", lambda e: e.tensor_tensor(out=Pt[5][:, w], in0=SS[:, w], in1=Qi[:, w], op=ALU.mult), reads=[kTB[k][3], kQ[1]], writes=[kPt[5]])
                        fw.op("dve", lambda e: e.tensor_tensor(out=S16[0][:, w], in0=Pt[4][:, w], in1=Pt[5][:, w], op=ALU.subtract), reads=[kPt[4], kPt[5]], writes=[kS16[0]])
                        fw.op("pool", lambda e: e.tensor_tensor(out=Pt[1][:, w], in0=SS[:, w], in1=Qr[:, w], op=ALU.mult), reads=[kTB[k][3], kQ[0], kPt[1]], writes=[kPt[1]])
                        fw.op("pool", lambda e: e.tensor_tensor(out=Pt[3][:, w], in0=CC[:, w], in1=Qi[:, w], op=ALU.mult), reads=[kTB[k][2], kQ[1], kPt[3]], writes=[kPt[3]])
                        fw.op("dve", lambda e: e.scalar_tensor_tensor(out=S16[1][:, w], in0=Pt[1][:, w], scalar=-1.0, in1=Pt[3][:, w], op0=ALU.mult, op1=ALU.subtract),
                              reads=[kPt[1], kPt[3]], writes=[kS16[1]])
                        L = npr - 1
                        fw.op("dve", lambda e: e.tensor_tensor(out=SE[:, stn, 0:1], in0=Pt[4][:, L:L + 1], in1=Pt[5][:, L:L + 1], op=ALU.subtract), reads=[kPt[4], kPt[5], kSE], writes=[kSE])
                        fw.op("dve", lambda e: e.tensor_tensor(out=SE[:, stn, 1:2], in0=Pt[1][:, L:L + 1], in1=Pt[3][:, L:L + 1], op=ALU.add), reads=[kPt[1], kPt[3], kSE], writes=[kSE])
                        if npr < TW:
                            ws = slice(npr, TW)
                            fw.op("dve", lambda e: e.tensor_scalar(out=Pt[0][:, ws], in0=PS[:, bi, ws], scalar1=cn["CI"][:, col], scalar2=None, op0=ALU.mult), reads=[kPS[bi], kc_, kPt[0]], writes=[kPt[0]])
                            fw.op("dve", lambda e: e.scalar_tensor_tensor(out=Pt[0][:, ws], in0=PS[:, br, ws], scalar=cn["CR"][:, col], in1=Pt[0][:, ws], op0=ALU.mult, op1=ALU.subtract), reads=[kPS[br], kPt[0], kc_], writes=[kPt[0]])
                            fw.op("dve", lambda e: e.tensor_scalar(out=Pt[2][:, ws], in0=PS[:, br, ws], scalar1=cn["CI"][:, col], scalar2=None, op0=ALU.mult), reads=[kPS[br], kc_, kPt[2]], writes=[kPt[2]])
                            fw.op("dve", lambda e: e.scalar_tensor_tensor(out=Pt[2][:, ws], in0=PS[:, bi, ws], scalar=cn["CR"][:, col], in1=Pt[2][:, ws], op0=ALU.mult, op1=ALU.add), reads=[kPS[bi], kPt[2], kc_], writes=[kPt[2]])
                            s0r, s0i = S0[:, stn, :, 0], S0[:, stn, :, 1]
                            fw.op("dve", lambda e: e.scalar_tensor_tensor(out=Pt[0][:, ws], in0=s0r, scalar=cn["AR"][:, col], in1=Pt[0][:, ws], op0=ALU.mult, op1=ALU.add), reads=[kS0, kPt[0], kc_], writes=[kPt[0]])
                            fw.op("dve", lambda e: e.tensor_scalar(out=Pt[1][:, ws], in0=s0i, scalar1=cn["AI"][:, col], scalar2=None, op0=ALU.mult), reads=[kS0, kc_, kPt[1]], writes=[kPt[1]])
                            fw.op("dve", lambda e: e.tensor_tensor(out=SN[:, 0, :], in0=Pt[0][:, ws], in1=Pt[1][:, ws], op=ALU.subtract), reads=[kPt[0], kPt[1]], writes=[kSN])
                            fw.op("dve", lambda e: e.scalar_tensor_tensor(out=Pt[2][:, ws], in0=s0i, scalar=cn["AR"][:, col], in1=Pt[2][:, ws], op0=ALU.mult, op1=ALU.add), reads=[kS0, kPt[2], kc_], writes=[kPt[2]])
                            fw.op("dve", lambda e: e.scalar_tensor_tensor(out=SN[:, 1, :], in0=s0r, scalar=cn["AI"][:, col], in1=Pt[2][:, ws], op0=ALU.mult, op1=ALU.add), reads=[kS0, kPt[2], kc_], writes=[kSN])
                            fw.op("act", lambda e: e.activation(out=S16[0][:, ws], in_=SN[:, 0, :], func=AF.Identity), reads=[kSN, kS16[0]], writes=[kS16[0]])
                            fw.op("act", lambda e: e.activation(out=S16[1][:, ws], in_=SN[:, 1, :], func=AF.Identity, scale=-1.0), reads=[kSN, kS16[1]], writes=[kS16[1]])
                            for r_ in range(2):
                                transpose_to(c, PS[0:NS, 5, r_ * 128:(r_ + 1) * 128], SN[:, r_, :], [kSN], [kPS[5]], IDF[:, :])
                            oi = stn % 2
                            fw.op("act", lambda e: e.activation(out=OUT16[oi][:, :].rearrange("b (p r) -> b r p", r=2),
                                                                in_=PS[0:NS, 5, 0:256].rearrange("b (r p) -> b r p", r=2), func=AF.Identity),
                                  reads=[kPS[5]], writes=[kO16[oi]])
                            fw.dma("sp", d["ss5"][:, stn * 256:(stn + 1) * 256], OUT16[oi][:, :], reads=[kO16[oi]])
                        fw.op("pe", lambda e: e.matmul(PS[:, yb, 0:TW], BCm[:, k, 2, :], S16[0][:], start=(k == 0), stop=False), reads=[kBC, kS16[0]], writes=[kPS[yb]])
                        fw.op("pe", lambda e: e.matmul(PS[:, yb, 0:TW], BCm[:, k, 3, :], S16[1][:], start=False, stop=(k == 3)), reads=[kBC, kS16[1]], writes=[kPS[yb]])
                    fw.op("dve", lambda e: e.scalar_tensor_tensor(out=XA[:], in0=XF[:, m, c0:c0 + TW], scalar=DSK[:, m:m + 1], in1=PS[:, yb, 0:TW], op0=ALU.mult, op1=ALU.add),
                          reads=[kXF[m][ti], kPS[yb], kP, kXA], writes=[kXA])
                    fw.op("act", lambda e: e.activation(out=ZG[:, m, c0:c0 + TW], in_=XA[:], func=AF.Gelu), reads=[kXA], writes=[kZG[m][ti]])
            fw.dma("sp", d["ps5"].rearrange("(t p) r -> p t r", p=128), SE[:], reads=[kSE])
            fw.barrier()
        with ExitStack() as s3:
            alloc_ln(c, s3)
            WGb = [[sb(s3, nc, "WG%d_%d" % (a, i), [128, KC, 256], BF16) for i in range(2)] for a in range(2)]
            kWG = [[K(), K()], [K(), K()]]
            sgl = [sb(s3, nc, "sgl%d" % i, [128, TW], F32) for i in range(2)]; ksgl = [K(), K()]
            wv = [d["w_glu_out"].rearrange("(kc p) n -> p kc n", p=128), d["w_glu_gate"].rearrange("(kc p) n -> p kc n", p=128)]
            nb = 0
            for mp in range(4):
                sl = mp % 2
                for a in range(2):
                    fw.dma("pool", WGb[a][sl][:], wv[a][:, :, mp * 256:(mp + 1) * 256], writes=[kWG[a][sl]])
                for mm in range(2):
                    m = 2 * mp + mm
                    for ti in range(NTL):
                        cs = slice(ti * TW, (ti + 1) * TW)
                        bo, bg = 2 * (nb % 2), 2 * (nb % 2) + 1
                        nb += 1
                        for a, bk in ((0, bo), (1, bg)):
                            for kc in range(KC):
                                fw.op("pe", lambda e, kc=kc: e.matmul(PS[:, bk, 0:TW], WGb[a][sl][:, kc, mm * 128:(mm + 1) * 128], ZG[:, kc, cs], start=(kc == 0), stop=(kc == KC - 1)),
                                      reads=[kWG[a][sl], kZG[kc][ti]], writes=[kPS[bk]])
                        ss = nb % 2
                        fw.op("act", lambda e: e.activation(out=sgl[ss][:], in_=PS[:, bg, 0:TW], func=AF.Sigmoid), reads=[kPS[bg]], writes=[ksgl[ss]])
                        fw.op("dve", lambda e: e.tensor_tensor(out=sgl[ss][:], in0=PS[:, bo, 0:TW], in1=sgl[ss][:], op=ALU.mult), reads=[kPS[bo], ksgl[ss]], writes=[ksgl[ss]])
                        fw.op("dve", lambda e: e.scalar_tensor_tensor(out=XF[:, m, cs], in0=XF[:, m, cs], scalar=ALPHA, in1=sgl[ss][:], op0=ALU.mult, op1=ALU.add),
                              reads=[ksgl[ss], kXF[m][ti]], writes=[kXF[m][ti]])
            for ti in range(NTL):
                layer_norm_tile(c, ti, 4)
            fw.barrier()


def sb_attention_sample(c, d, st, QS, kQS, OSB, kOSB):
    nc, fw = c.nc, c.fw
    PS, kPS = c.PS, c.kPS
    NE = NS * 16
    nrows = c.npool * 128
    QBC = sb(st, nc, "QBC", [128, NS, 512], F32); kQBC = K()
    fw.dma("sp", c.Qd, QS[0:NS, :], reads=[kQS], writes=[c.kQd])
    fw.dma("sp", QBC[:].rearrange("p b n -> p (b n)"), c.Qd.rearrange("b n -> (b n)").unsqueeze(0).to_broadcast([128, NS * 512]), reads=[c.kQd], writes=[kQBC])
    PTB = sb(st, nc, "PTB", [128, NE], I32); kPT = K()
    IDXF = sb(st, nc, "IDXF", [128, NE], F32)
    IOP = sb(st, nc, "IOP", [128, NE], F32)
    fw.dma("sp", PTB[:], d["pt"].to_broadcast([128, NE]), writes=[kPT])
    fw.op("pool", lambda e: e.iota(IOP[:], [[0, NE]], base=0, channel_multiplier=1, allow_small_or_imprecise_dtypes=True), writes=[kPT])
    fw.op("dve", lambda e: e.tensor_copy(out=IDXF[:], in_=PTB[:]), reads=[kPT], writes=[kPT])
    fw.op("dve", lambda e: e.scalar_tensor_tensor(out=IDXF[:], in0=IDXF[:], scalar=128.0, in1=IOP[:], op0=ALU.mult, op1=ALU.add), reads=[kPT], writes=[kPT])
    fw.op("dve", lambda e: e.tensor_copy(out=PTB[:], in_=IDXF[:]), reads=[kPT], writes=[kPT])
    NSL = 3
    IDc = [sb(st, nc, "IDc%d" % i, [128, 1], I32) for i in range(NSL)]; kID = [K() for _ in range(NSL)]
    PG = [sb(st, nc, "PG%d" % i, [128, 512], F32) for i in range(NSL)]; kPG = [K() for _ in range(NSL)]
    PR = [sb(st, nc, "PR%d" % i, [128, 512], F32) for i in range(2)]; kPR = [K(), K()]
    ZA = sb(st, nc, "ZA", [128, NE * 8], F32); kZA = K()
    EA = sb(st, nc, "EA", [128, NE * 8], F32); kEA = K()
    SPA = sb(st, nc, "SPA", [128, NE * 8], F32); kSPA = K()
    CIN = sb(st, nc, "CIN", [128, NE * 8], F32); kCIN = K()
    TOT = sb(st, nc, "TOT", [128, NE * 8], F32); kTOT = K()
    TRIF = sb(st, nc, "TRIF", [128, 128], F32); kTF = K()
    fw.op("pool", lambda e: e.affine_select(out=TRIF[:], in_=c.onesf[:], pattern=[[-1, 128]], base=0, channel_multiplier=1,
                                            compare_op=ALU.is_ge, fill=0.0), reads=[c.kconst], writes=[kTF])
    PRb = [sb(st, nc, "PRb%d" % i, [128, 512], BF16) for i in range(2)]; kPRb = [K(), K()]
    OH = sb(st, nc, "OH", [128, NS, NS], BF16); kOH = K()
    fw.op("pool", lambda e: e.memset(OH[:], 0.0), writes=[kOH])
    for b_ in range(NS):
        fw.op("pool", lambda e, b_=b_: e.memset(OH[:, b_, b_:b_ + 1], 1.0), reads=[kOH], writes=[kOH])
    n = 0
    for e_ in range(NE):
        b = e_ // 16
        sl = n % NSL
        n += 1
        fw.op("dve", lambda e: e.tensor_copy(out=IDc[sl][:], in_=PTB[:, e_:e_ + 1]), reads=[kPT], writes=[kID[sl]])
        fw.gather(PG[sl][:, :], d["ck"], IDc[sl][:, :], nrows, reads=[kID[sl]], writes=[kPG[sl]])
        ps = e_ % 2
        fw.op("dve", lambda e: e.tensor_tensor(out=PR[ps][:], in0=PG[sl][:], in1=QBC[:, b, :], op=ALU.mult), reads=[kPG[sl], kQBC], writes=[kPR[ps]])
        fw.op("dve", lambda e: e.tensor_reduce(out=ZA[:, e_ * 8:(e_ + 1) * 8], in_=PR[ps][:].rearrange("p (h d) -> p h d", h=8), axis=AX.X, op=ALU.add),
              reads=[kPR[ps]], writes=[kZA])
    fw.op("dve", lambda e: e.scalar_tensor_tensor(out=ZA[:].rearrange("p (e h) -> p e h", h=8), in0=ZA[:].rearrange("p (e h) -> p e h", h=8), scalar=0.125,
                                                  in1=c.sbb[:, :].unsqueeze(1).to_broadcast([128, NE, 8]), op0=ALU.mult, op1=ALU.add),
          reads=[kZA, c.kconst], writes=[kZA])
    fw.op("act", lambda e: e.activation(out=EA[:], in_=ZA[:], func=AF.Exp), reads=[kZA], writes=[kEA])
    fw.op("act", lambda e: e.activation(out=SPA[:], in_=EA[:], func=AF.Ln, bias=c.cst[:, 1:2]), reads=[kEA, c.kconst], writes=[kSPA])
    for q4 in range(4):
        fw.op("pe", lambda e: e.matmul(PS[:, q4, :], TRIF[:], SPA[:, q4 * 512:(q4 + 1) * 512], start=True, stop=True), reads=[kTF, kSPA], writes=[kPS[q4]])
        fw.op("pe", lambda e: e.matmul(PS[:, 4 + q4, :], c.onesf[:], SPA[:, q4 * 512:(q4 + 1) * 512], start=True, stop=True), reads=[c.kconst, kSPA], writes=[kPS[4 + q4]])
        fw.op("act", lambda e: e.activation(out=CIN[:, q4 * 512:(q4 + 1) * 512], in_=PS[:, q4, :], func=AF.Identity), reads=[kPS[q4]], writes=[kCIN])
        fw.op("act", lambda e: e.activation(out=TOT[:, q4 * 512:(q4 + 1) * 512], in_=PS[:, 4 + q4, :], func=AF.Identity), reads=[kPS[4 + q4]], writes=[kTOT])
    T4 = TOT[:].rearrange("p (b q h) -> p b q h", b=NS, q=16)
    C4 = CIN[:].rearrange("p (b q h) -> p b q h", b=NS, q=16)
    RUN = sb(st, nc, "RUN", [128, NS, 8], F32); kRUN = K()
    fw.op("pool", lambda e: e.memset(RUN[:], 0.0), writes=[kRUN])
    for p_ in range(14, -1, -1):
        fw.op("dve", lambda e: e.tensor_tensor(out=RUN[:], in0=RUN[:], in1=T4[:, :, p_ + 1, :], op=ALU.add), reads=[kRUN, kTOT], writes=[kRUN])
        fw.op("dve", lambda e: e.tensor_tensor(out=C4[:, :, p_, :], in0=C4[:, :, p_, :], in1=RUN[:], op=ALU.add), reads=[kRUN, kCIN], writes=[kCIN])
    fw.op("act", lambda e: e.activation(out=CIN[:], in_=CIN[:], func=AF.Exp, scale=-1.0), reads=[kCIN], writes=[kCIN])
    fw.op("dve", lambda e: e.tensor_tensor(out=EA[:], in0=EA[:], in1=CIN[:], op=ALU.mult), reads=[kEA, kCIN], writes=[kEA])
    for e_ in range(NE):
        b, p_ = e_ // 16, e_ % 16
        sl = n % NSL
        n += 1
        fw.op("dve", lambda e: e.tensor_copy(out=IDc[sl][:], in_=PTB[:, e_:e_ + 1]), reads=[kPT], writes=[kID[sl]])
        fw.gather(PG[sl][:, :], d["cv"], IDc[sl][:, :], nrows, reads=[kID[sl]], writes=[kPG[sl]])
        ps = e_ % 2
        fw.op("dve", lambda e: e.tensor_tensor(out=PRb[ps][:].rearrange("p (h d) -> p h d", h=8), in0=PG[sl][:].rearrange("p (h d) -> p h d", h=8),
                                               in1=EA[:, e_ * 8:(e_ + 1) * 8].unsqueeze(2).to_broadcast([128, 8, 64]), op=ALU.mult),
              reads=[kPG[sl], kEA], writes=[kPRb[ps]])
        fw.op("pe", lambda e: e.matmul(PS[0:NS, 4, :], OH[:, b, :], PRb[ps][:], start=(e_ == 0), stop=(e_ == NE - 1)),
              reads=[kPRb[ps], kOH], writes=[kPS[4]])
    OT = sb(st, nc, "OTs", [NS, 512], F32); kOT = K()
    fw.op("act", lambda e: e.activation(out=OT[:], in_=PS[0:NS, 4, :], func=AF.Identity), reads=[kPS[4]], writes=[kOT])
    for c4 in range(4):
        transpose_to(c, PS[:, 5, c4 * NS:(c4 + 1) * NS], OT[0:NS, c4 * 128:(c4 + 1) * 128], [kOT], [kPS[5]], c.IDF[0:NS, 0:NS])
    fw.op("act", lambda e: e.activation(out=OSB[:, :, NP:NT], in_=PS[:, 5, 0:4 * NS].rearrange("p (c b) -> p c b", c=4), func=AF.Identity),
          reads=[kPS[5]], writes=[kOSB[c4][4] for c4 in range(4)])

def build(stage=99, dbg=False, npool=2560):
    nc = bass.Bass("TRN2", target_bir_lowering=False)
    c = Ctx()
    c.nc = nc

    DECL.clear()

    def din(name, shape, dt=F32):
        DECL.append(name)
        return nc.dram_tensor(name, list(shape), dt, kind="ExternalInput").ap()

    def dout(name, shape, dt=F32):
        return nc.dram_tensor(name, list(shape), dt, kind="ExternalOutput").ap()

    xT = din("xT", [D, NT])
    lngT = din("lngT", [128, 6 * KC])
    lnbT = din("lnbT", [128, 6 * KC])
    fw_ = {}
    for nm in ("ffn1_wg", "ffn1_wu", "ffn2_wg", "ffn2_wu"):
        fw_[nm] = din(nm, [2, D, DFF])
    for nm in ("ffn1_wd", "ffn2_wd"):
        fw_[nm] = din(nm, [2, DFF, D])
    yT = dout("yT", [D, NT])
    d = {}
    c.dbg = dbg
    if stage >= 2:
        d["w_in"] = din("w_in", [D, INC])
        d["sbb"] = din("sbb", [128, 8])
        d["pk"] = dout("pk", [NP, 512]); d["pv"] = dout("pv", [NP, 512])
        d["sk"] = dout("sk", [NS, 512]); d["sv"] = dout("sv", [NS, 512])
        if dbg:
            d["dbg_osb"] = dout("dbg_osb", [128, 4, NT], BF16)
    c.npool = npool
    if stage >= 4:
        d["ck"] = din("ck", [npool * 128, 512]); d["cv"] = din("cv", [npool * 128, 512])
        d["pt"] = din("pt", [1, NS * 16], I32)
        c.Qd = nc.dram_tensor("Qd", [NS, 512], F32, kind="Internal").ap(); c.kQd = K()
    if stage >= 5:
        c.TOKd = nc.dram_tensor("TOKd", [CS, 3, 512], BF16, kind="Internal").ap()
        c.Yd = nc.dram_tensor("Yd", [4, 2, CS, 64], BF16, kind="Internal").ap()
        c.kTOKd = K(); c.kYd = K()
        d["w_w2"] = din("w_w2", [64, 512]); d["w_a2"] = din("w_a2", [64, 512]); d["w_g2"] = din("w_g2", [128, 512])
        d["pcol"] = din("pcol", [128, 48]); d["gng"] = din("gng", [128, 512]); d["gnb"] = din("gnb", [128, 512])
        d["sshift0"] = din("sshift0", [NS, RWC]); d["swkv0"] = din("swkv0", [NS, 8, 64, 64])
        d["pwkv"] = dout("pwkv", [8, 64, 64]); d["swkv"] = dout("swkv", [NS, 8, 64, 64])
        d["pshift"] = dout("pshift", [1, RWC]); d["sshift"] = dout("sshift", [NS, RWC])
        d["w_out"] = din("w_out", [D, D])
    if stage >= 8:
        d["s5prm"] = din("s5prm", [128, 3, 32]); d["dskip"] = din("dskip", [128, 8])
        d["s5_0"] = din("s5_0", [NS, 4096, 2]); d["s5bc"] = din("s5bc", [8, 16, 128, 128])
        d["w_glu_out"] = din("w_glu_out", [D, D]); d["w_glu_gate"] = din("w_glu_gate", [D, D])
        d["ps5"] = dout("ps5", [4096, 2]); d["ss5"] = dout("ss5", [NS, 8192])

    with ExitStack() as st:
        fw = FW(nc, st)
        c.fw = fw
        c.XF = sb(st, nc, "XF", [128, KC, NT], F32)
        c.kXF = [[K() for _ in range(NTL)] for _ in range(KC)]
        c.PS = st.enter_context(nc.psum_tensor("PS", [128, 8, 512], F32))
        c.kPS = [K() for _ in range(8)]
        c.onesf = sb(st, nc, "onesf", [128, 128], F32)
        c.epsc = sb(st, nc, "epsc", [128, 1], F32)
        c.lng = sb(st, nc, "lng", [128, 6 * KC], F32)
        c.lnb = sb(st, nc, "lnb", [128, 6 * KC], F32)
        c.kconst = K()
        c.nsq = 0
        c.nln = 0
        fw.op("pool", lambda e: e.memset(c.onesf[:], 1.0), writes=[c.kconst])
        fw.op("pool", lambda e: e.memset(c.epsc[:], LN_EPS), writes=[c.kconst])
        fw.dma("sp", c.lng[:], lngT, writes=[c.kconst])
        fw.dma("sp", c.lnb[:], lnbT, writes=[c.kconst])
        xv = xT.rearrange("(kc p) t -> p kc t", p=128)
        for kc in range(KC):
            fw.dma("sp", c.XF[:, kc, :], xv[:, kc, :], writes=c.kXF[kc])

        if not SKIP_FFN:
            ffn_ln(c, fw_["ffn1_wg"][0], fw_["ffn1_wu"][0], fw_["ffn1_wd"][0], 0, "a")
        fw.barrier()
        if stage >= 2:
            c.cst = sb(st, nc, "cst", [128, 4], F32)
            c.sbb = sb(st, nc, "sbb_s", [128, 8], F32)
            fw.op("pool", lambda e: e.memset(c.cst[:, 0:1], LN_EPS), writes=[c.kconst])
            fw.op("pool", lambda e: e.memset(c.cst[:, 1:2], 1.0), writes=[c.kconst])
            fw.op("pool", lambda e: e.memset(c.cst[:, 2:3], GN_EPS), writes=[c.kconst])
            fw.op("pool", lambda e: e.memset(c.cst[:, 3:4], 1.5707963267948966), writes=[c.kconst])
            fw.dma("sp", c.sbb[:], d["sbb"], writes=[c.kconst])
            onesb = sb(st, nc, "onesb", [128, 512], BF16)
            fw.op("pool", lambda e: e.memset(onesb[:], 1.0), writes=[c.kconst])
            c.ONESB = onesb[:, 0:128]
            c.onesb = onesb
            c.IDF = sb(st, nc, "IDF", [128, 128], F32)
            fw.op("pool", lambda e: e.memset(c.IDF[:], 1.0), writes=[c.kconst])
            fw.op("pool", lambda e: e.affine_select(out=c.IDF[:], in_=c.IDF[:], pattern=[[-1, 128]], base=0, channel_multiplier=1,
                                                    compare_op=ALU.is_equal, fill=0.0), reads=[c.kconst], writes=[c.kconst])
            mixer_even(c, d, stage)
        if stage >= 7 and not SKIP_FFN:
            ffn_ln(c, fw_["ffn2_wg"][0], fw_["ffn2_wu"][0], fw_["ffn2_wd"][0], 2, "b")
            fw.barrier()
            ffn_ln(c, fw_["ffn1_wg"][1], fw_["ffn1_wu"][1], fw_["ffn1_wd"][1], 3, "c")
            fw.barrier()
        if stage >= 8:
            s5_mixer(c, d)
        if stage >= 9 and not SKIP_FFN:
            ffn_ln(c, fw_["ffn2_wg"][1], fw_["ffn2_wu"][1], fw_["ffn2_wd"][1], 5, "d")
            fw.barrier()

        yv = yT.rearrange("(kc p) t -> p kc t", p=128)
        for kc in range(KC):
            fw.dma("sp", yv[:, kc, :], c.XF[:, kc, :], reads=c.kXF[kc])
        fw.finish()
        print("instructions:", fw.ninst, {e: fw.cnt[e] for e in fw.cnt})
    return nc


DECL = []


def make_in_maps(inp, n_cores=8):
    f = np.float32
    maps = []
    ln_g = np.ascontiguousarray(np.asarray(inp["ln_g"], f).reshape(6, KC, 128).transpose(2, 0, 1).reshape(128, 6 * KC))
    ln_b = np.ascontiguousarray(np.asarray(inp["ln_b"], f).reshape(6, KC, 128).transpose(2, 0, 1).reshape(128, 6 * KC))
    shared = {"lngT": ln_g, "lnbT": ln_b}
    for nm in ("ffn1_wg", "ffn1_wu", "ffn1_wd", "ffn2_wg", "ffn2_wu", "ffn2_wd"):
        shared[nm] = np.asarray(inp[nm], f)
    shared["w_in"] = np.asarray(inp["w_in_even"][0], f)
    for nm in ("w_w2", "w_a2", "w_g2"):
        shared[nm] = np.asarray(inp[nm][0], f)
    shared["w_out"] = np.asarray(inp["w_out_even"][0], f)
    def colT(v, n):
        return np.asarray(v, f).reshape(n, 128).T
    pc = np.zeros((128, 48), f)
    pc[:, 0:14] = colT(inp["mu_shift"][0], 14)
    pc[:, 14:18] = colT(inp["w0"][0], 4); pc[:, 18:22] = colT(inp["a0"][0], 4)
    pc[:, 22:26] = colT(inp["k_k"][0], 4); pc[:, 26:30] = colT(inp["k_a"][0], 4)
    pc[:, 30:34] = colT(inp["r_k"][0].reshape(-1), 4)
    shared["pcol"] = pc
    shared["gng"] = np.ascontiguousarray(np.broadcast_to(np.asarray(inp["gn_g"][0], f)[None, :], (128, 512)))
    shared["gnb"] = np.ascontiguousarray(np.broadcast_to(np.asarray(inp["gn_b"][0], f)[None, :], (128, 512)))
    lre = np.asarray(inp["lam_re"][0], f); lim = np.asarray(inp["lam_im"][0], f); ldt = np.asarray(inp["log_dt"][0], f)
    prm = np.zeros((128, 3, 32), f)
    prm[:, 0, :] = lre.reshape(32, 128).T; prm[:, 1, :] = lim.reshape(32, 128).T
    prm[:, 2, :] = np.repeat(ldt, 64).reshape(32, 128).T
    shared["s5prm"] = prm
    shared["dskip"] = np.ascontiguousarray(np.asarray(inp["d_skip"][0], f).reshape(8, 128).T)
    bre = np.asarray(inp["b_re"][0], f); bim = np.asarray(inp["b_im"][0], f)
    cre = np.asarray(inp["c_re"][0], f); cim = np.asarray(inp["c_im"][0], f)
    bc = np.zeros((8, 4, 4, 128, 128), f)
    for g in range(64):
        m_, gl = g // 8, g % 8
        k_, g2 = (g % 8) // 2, g % 2
        rows = slice(gl * 16, gl * 16 + 16); cols = slice(g2 * 64, g2 * 64 + 64)
        bc[m_, k_, 0][rows, cols] = bre[g].T
        bc[m_, k_, 1][rows, cols] = bim[g].T
        bc[m_, k_, 2][cols, rows] = cre[g].T
        bc[m_, k_, 3][cols, rows] = cim[g].T
    shared["s5bc"] = bc.reshape(8, 16, 128, 128)
    shared["w_glu_out"] = np.asarray(inp["w_glu_out"][0], f); shared["w_glu_gate"] = np.asarray(inp["w_glu_gate"][0], f)
    shared["sbb"] = np.ascontiguousarray(np.broadcast_to(np.asarray(inp["sb_bias"][0], f)[None, :], (128, 8)))
    for cidx in range(n_cores):
        m = dict(shared)
        xp = np.asarray(inp["x_prompt"][cidx], f)
        xs = np.asarray(inp["x_sample"][cidx * NS:(cidx + 1) * NS, 0], f)
        m["xT"] = np.ascontiguousarray(np.concatenate([xp, xs], axis=0).T)
        sl = slice(cidx * NS, (cidx + 1) * NS)
        m["sshift0"] = np.asarray(inp["state_shift"][0, sl], f)
        m["swkv0"] = np.asarray(inp["state_wkv"][0, sl], f)
        m["pt"] = np.ascontiguousarray(np.asarray(inp["page_table"][sl], np.int32).reshape(1, NS * 16))
        m["ck"] = np.asarray(inp["cache_k_sb"][0], f).reshape(-1, 512)
        m["cv"] = np.asarray(inp["cache_v_sb"][0], f).reshape(-1, 512)
        m["s5_0"] = np.asarray(inp["state_s5"][0, sl], f).reshape(NS, 4096, 2)
        maps.append({k: v for k, v in m.items() if k in DECL})
    return maps


def dev_compare(stage, r, ref, cmp):
    if stage >= 2:
        pp = ref["p_proj"][0]; sp_ = ref["s_proj"][:, 0]
        cmp("pk", r["pk"], pp[:, 512:1024]); cmp("pv", r["pv"], pp[:, 1024:1536])
        cmp("sk", r["sk"], sp_[:, 512:1024]); cmp("sv", r["sv"], sp_[:, 1024:1536])
    if stage >= 4 and "dbg_osb" in r:
        import ml_dtypes
        x_ = r["dbg_osb"]
        if x_.dtype.kind == "V":
            x_ = x_.view(ml_dtypes.bfloat16)
        o = np.asarray(x_).astype(np.float32).transpose(1, 0, 2).reshape(512, NT).T
        cmp("osb_s", o[NP:], ref["s_osb"][:, 0])
    if stage >= 3 and "dbg_osb" in r and False:
        o = np.asarray(r["dbg_osb"]).astype(np.float32).transpose(1, 0, 2).reshape(512, NT).T
        cmp("osb_p", o[:NP], ref["p_osb"][0])
        for qq in range(4):
            cmp("osb_p q%d" % qq, o[qq*512:(qq+1)*512], ref["p_osb"][0][qq*512:(qq+1)*512])
    if stage >= 5:
        cmp("pwkv", r["pwkv"], ref["p_wkv"][0]); cmp("swkv", r["swkv"], ref["s_wkv"])
        cmp("pshift", r["pshift"][0], ref["p_proj"][0, -1, 1536:]); cmp("sshift", r["sshift"], ref["s_proj"][:, 0, 1536:])
    if stage >= 8:
        cmp("ps5", r["ps5"].reshape(64, 64, 2), ref["p_s5"][0]); cmp("ss5", r["ss5"].reshape(NS, 64, 64, 2), ref["s_s5"])
    y = r["yT"].T
    key = {1: "L0_x1", 2: "L0_x1", 3: "L0_x1", 4: "L0_x1", 5: "L0_x1", 6: "L0_x2", 7: "L1_x1", 8: "L1_x2", 9: "L1_x3"}.get(stage, "L1_x3")
    cmp("y_prompt", y[:NP], ref["p_" + key][0])
    cmp("y_sample", y[NP:], ref["s_" + key][:, 0])


def kernel(**inputs):
    n = 8
    npool = int(np.asarray(inputs["cache_k_sb"]).shape[1])
    nc = build(stage=9, dbg=False, npool=npool)
    maps = make_in_maps(inputs, n_cores=n)
    res = run_bass_kernel_spmd(nc, maps, core_ids=list(range(n)))
    R = res.results
    f = np.float32
    yp = np.stack([np.asarray(R[i]["yT"], f).T[:NP] for i in range(n)], axis=0)
    ys = np.concatenate([np.asarray(R[i]["yT"], f).T[NP:] for i in range(n)], axis=0)[:, None, :]
    pk = np.stack([np.asarray(R[i]["pk"], f).reshape(NP, 8, 64) for i in range(n)], axis=0)[None]
    pv = np.stack([np.asarray(R[i]["pv"], f).reshape(NP, 8, 64) for i in range(n)], axis=0)[None]
    pwkv = np.stack([np.asarray(R[i]["pwkv"], f) for i in range(n)], axis=0)[None]
    pshift = np.stack([np.asarray(R[i]["pshift"], f).reshape(RWC) for i in range(n)], axis=0)[None]
    ps5 = np.stack([np.asarray(R[i]["ps5"], f).reshape(64, 64, 2) for i in range(n)], axis=0)[None]
    sk = np.concatenate([np.asarray(R[i]["sk"], f).reshape(NS, 1, 8, 64) for i in range(n)], axis=0)[None]
    sv = np.concatenate([np.asarray(R[i]["sv"], f).reshape(NS, 1, 8, 64) for i in range(n)], axis=0)[None]
    swkv = np.concatenate([np.asarray(R[i]["swkv"], f) for i in range(n)], axis=0)[None]
    sshift = np.concatenate([np.asarray(R[i]["sshift"], f) for i in range(n)], axis=0)[None]
    ss5 = np.concatenate([np.asarray(R[i]["ss5"], f).reshape(NS, 64, 64, 2) for i in range(n)], axis=0)[None]
    return (yp, ys, pk, pv, pwkv, pshift, ps5, sk, sv, swkv, sshift, ss5)
```

```python
from contextlib import ExitStack
import numpy as np
import concourse.bass as bass
import concourse.mybir as mybir
from concourse.bass_utils import run_bass_kernel_spmd

F32 = mybir.dt.float32
BF16 = mybir.dt.bfloat16
I32 = mybir.dt.int32
AF = mybir.ActivationFunctionType
ALU = mybir.AluOpType
AX = mybir.AxisListType

D = 1024
KC = 8
DFF = 2816
JC = 22
NP = 2048
NS = 16
NT = NP + NS
TW = 344
NTL = NT // TW
NG = 3
GW = NT // NG
ALPHA = 4.0 ** 0.25
LN_EPS = 1e-5
GN_EPS = 64e-5
INC = 3328
RWC = 1792
SKIP_FFN = False
import os
VAR = os.environ.get('KVAR', '')


class K:
    __slots__ = ("name", "lw", "rd")

    def __init__(self, name=""):
        self.name = name
        self.lw = None
        self.rd = []


class FW:
    def __init__(self, nc, stack, n_dma_sems=8):
        self.nc = nc
        self.eng = {"pe": nc.tensor, "dve": nc.vector, "act": nc.scalar,
                    "pool": nc.gpsimd, "sp": nc.sync}
        self.sem = {}
        self.cnt = {}
        self.seen = {e: {} for e in self.eng}
        for e in self.eng:
            self.sem[e] = stack.enter_context(nc.semaphore("s_" + e))
            self.cnt[e] = 0
        self.dsem = {}
        self.dcnt = {}
        self.dnext = {}
        for q in ("sp", "pool"):
            self.dsem[q] = [stack.enter_context(nc.semaphore("d_%s_%d" % (q, i)))
                            for i in range(n_dma_sems)]
            self.dcnt[q] = [0] * n_dma_sems
            self.dnext[q] = 0
        self.ninst = 0
        self.dram_writes = []

    def _semobj(self, sk):
        if isinstance(sk, tuple):
            return self.dsem[sk[0]][sk[1]]
        return self.sem[sk]

    def _wait(self, e, sk, val):
        if sk == "pe" and e == "pe":
            return
        if self.seen[e].get(sk, 0) >= val:
            return
        self.seen[e][sk] = val
        self.eng[e].wait_ge(self._semobj(sk), val)
        self.ninst += 1

    def _deps(self, e, reads, writes):
        need = {}
        for k in reads:
            if k.lw is not None and need.get(k.lw[0], 0) < k.lw[1]:
                need[k.lw[0]] = k.lw[1]
        for k in writes:
            if k.lw is not None and need.get(k.lw[0], 0) < k.lw[1]:
                need[k.lw[0]] = k.lw[1]
            for (sk, v) in k.rd:
                if need.get(sk, 0) < v:
                    need[sk] = v
        for sk, v in need.items():
            self._wait(e, sk, v)

    def _mark(self, sk, val, reads, writes):
        for k in reads:
            k.rd.append((sk, val))
            if len(k.rd) > 16:
                m = {}
                for (s, v) in k.rd:
                    if m.get(s, 0) < v:
                        m[s] = v
                k.rd = list(m.items())
        for k in writes:
            k.lw = (sk, val)
            k.rd = []

    def op(self, e, fn, reads=(), writes=()):
        if e == "dve" and self.dram_writes:
            for (sk, v) in self.dram_writes:
                self._wait(e, sk, v)
            self.dram_writes = []
        self._deps(e, reads, writes)
        ins = fn(self.eng[e])
        self.cnt[e] += 1
        ins.then_inc(self.sem[e], 1)
        self._mark(e, self.cnt[e], reads, writes)
        self.ninst += 1
        return ins

    def _dslot(self, q):
        i = self.dnext[q]
        self.dnext[q] = (i + 1) % len(self.dsem[q])
        if self.dcnt[q][i] > 0:
            self._wait(q, (q, i), self.dcnt[q][i])
        return i

    def dma(self, q, out, in_, reads=(), writes=(), **kw):
        i = self._dslot(q)
        self._deps(q, reads, writes)
        ins = self.eng[q].dma_start(out=out, in_=in_, **kw)
        self.dcnt[q][i] += 16
        ins.then_inc(self.dsem[q][i], 16)
        self._mark((q, i), self.dcnt[q][i], reads, writes)
        self.ninst += 1
        if "DRAM" in str(getattr(out.tensor, "space", "")).upper() or type(out.tensor).__name__.startswith("DRam"):
            self.dram_writes.append(((q, i), self.dcnt[q][i]))
        return ins

    def gather(self, out, in_rows, idx_ap, nrows, reads=(), writes=()):
        q = "pool"
        i = self._dslot(q)
        self._deps(q, reads, writes)
        if getattr(self, "_breg", None) is None or self._breg[0] != nrows:
            self._breg = (nrows, self.nc.gpsimd.to_reg(nrows - 1))
        ins = self.nc.gpsimd.indirect_dma_start(
            out=out, out_offset=None, in_=in_rows,
            in_offset=bass.IndirectOffsetOnAxis(ap=idx_ap, axis=0),
            bounds_check=self._breg[1], oob_is_err=False)
        self.dcnt[q][i] += 16
        ins.then_inc(self.dsem[q][i], 16)
        self._mark((q, i), self.dcnt[q][i], reads, writes)
        self.ninst += 1
        return ins

    def barrier(self):
        for e in self.eng:
            for q in self.dsem:
                for i, v in enumerate(self.dcnt[q]):
                    if v > 0:
                        self._wait(e, (q, i), v)
            for e2 in self.eng:
                if self.cnt[e2] > 0 and not (e2 == e and e == "pe"):
                    self._wait(e, e2, self.cnt[e2])

    def finish(self):
        for q in self.dsem:
            for i, v in enumerate(self.dcnt[q]):
                if v > 0:
                    self._wait("sp", (q, i), v)
        for e in self.eng:
            if e != "sp" and self.cnt[e] > 0:
                self._wait("sp", e, self.cnt[e])


class Ctx:
    pass


def sb(st, nc, name, shape, dt):
    return st.enter_context(nc.sbuf_tensor(name, list(shape), dt))


def ffn_ln(c, wg, wu, wd, lnidx, tag):
    nc, fw = c.nc, c.fw
    XF, kXF = c.XF, c.kXF
    with ExitStack() as st:
        XB = sb(st, nc, "XB" + tag, [128, KC, GW], BF16)
        H = sb(st, nc, "H" + tag, [128, JC, GW], BF16)
        wgb = [sb(st, nc, "wgb%d%s" % (i, tag), [128, KC, 256], BF16) for i in range(2)]
        wub = [sb(st, nc, "wub%d%s" % (i, tag), [128, KC, 256], BF16) for i in range(2)]
        wdb = [sb(st, nc, "wdb%d%s" % (i, tag), [128, JC, 256], BF16) for i in range(2)]
        sgt = [sb(st, nc, "sgt%d%s" % (i, tag), [128, TW], F32) for i in range(2)]
        alloc_ln(c, st)
        kXB = [K() for _ in range(KC)]
        kH = [[K() for _ in range(2)] for _ in range(JC)]
        kwg = [K(), K()]
        kwu = [K(), K()]
        kwd = [K(), K()]
        ksg = [K(), K()]
        wgv = wg.rearrange("(kc p) n -> p kc n", p=128)
        wuv = wu.rearrange("(kc p) n -> p kc n", p=128)
        wdv = wd.rearrange("(j p) n -> p j n", p=128)
        PS, kPS = c.PS, c.kPS
        nsg = 0
        nb = 0
        for g in range(NG):
            c0 = g * GW
            for kc in range(KC):
                fw.op("act", lambda e, kc=kc: e.activation(out=XB[:, kc, :], in_=XF[:, kc, c0:c0 + GW], func=AF.Identity),
                      reads=[kXF[kc][2 * g], kXF[kc][2 * g + 1]], writes=[kXB[kc]])
            for jp in range(JC // 2):
                s = jp % 2
                fw.dma("pool", wgb[s][:], wgv[:, :, jp * 256:(jp + 1) * 256], writes=[kwg[s]])
                fw.dma("pool", wub[s][:], wuv[:, :, jp * 256:(jp + 1) * 256], writes=[kwu[s]])
                for jj in range(2):
                    j = 2 * jp + jj
                    for tl in range(2):
                        bg, bu = 2 * (nb % 2), 2 * (nb % 2) + 1
                        nb += 1
                        cs = slice(tl * TW, (tl + 1) * TW)
                        for kc in range(KC):
                            fw.op("pe", lambda e, kc=kc: e.matmul(PS[:, bg, 0:TW], wgb[s][:, kc, jj * 128:(jj + 1) * 128],
                                                                  XB[:, kc, cs], start=(kc == 0), stop=(kc == KC - 1)),
                                  reads=[kwg[s], kXB[kc]], writes=[kPS[bg]])
                        for kc in range(KC):
                            fw.op("pe", lambda e, kc=kc: e.matmul(PS[:, bu, 0:TW], wub[s][:, kc, jj * 128:(jj + 1) * 128],
                                                                  XB[:, kc, cs], start=(kc == 0), stop=(kc == KC - 1)),
                                  reads=[kwu[s], kXB[kc]], writes=[kPS[bu]])
                        ss = nsg % 2
                        nsg += 1
                        fw.op("act", lambda e: e.activation(out=sgt[ss][:], in_=PS[:, bg, 0:TW], func=AF.Silu),
                              reads=[kPS[bg]], writes=[ksg[ss]])
                        fw.op("dve", lambda e: e.scalar_tensor_tensor(out=H[:, j, cs], in0=PS[:, bu, 0:TW], scalar=0.5,
                                                                      in1=sgt[ss][:], op0=ALU.mult, op1=ALU.mult),
                              reads=[kPS[bu], ksg[ss]], writes=[kH[j][tl]])
            for mp in range(KC // 2):
                s = mp % 2
                fw.dma("pool", wdb[s][:], wdv[:, :, mp * 256:(mp + 1) * 256], writes=[kwd[s]])
                for mm in range(2):
                    m = 2 * mp + mm
                    for tl in range(2):
                        by = 4 + (nb % 2)
                        nb += 1
                        cs = slice(tl * TW, (tl + 1) * TW)
                        gc = slice(c0 + tl * TW, c0 + (tl + 1) * TW)
                        for j in range(JC):
                            fw.op("pe", lambda e, j=j: e.matmul(PS[:, by, 0:TW], wdb[s][:, j, mm * 128:(mm + 1) * 128],
                                                                H[:, j, cs], start=(j == 0), stop=(j == JC - 1)),
                                  reads=[kwd[s], kH[j][tl]], writes=[kPS[by]])
                        fw.op("dve", lambda e: e.scalar_tensor_tensor(out=XF[:, m, gc], in0=XF[:, m, gc], scalar=ALPHA,
                                                                      in1=PS[:, by, 0:TW], op0=ALU.mult, op1=ALU.add),
                              reads=[kPS[by], kXF[m][2 * g + tl]], writes=[kXF[m][2 * g + tl]])
            for tl in range(2):
                layer_norm_tile(c, 2 * g + tl, lnidx)


def alloc_ln(c, st):
    c.nln += 1
    c.sq = [sb(st, c.nc, "sq%d_%d" % (i, c.nln), [128, TW], F32) for i in range(2)]
    c.ksq = [K(), K()]
    c.lnt = [sb(st, c.nc, "lnt%d_%d" % (i, c.nln), [128, TW], F32) for i in range(4)]
    c.kln = [K() for _ in range(4)]


def layer_norm_tile(c, ti, lnidx):
    nc, fw = c.nc, c.fw
    XF, kXF, PS, kPS = c.XF, c.kXF, c.PS, c.kPS
    cs = slice(ti * TW, (ti + 1) * TW)
    b1, b2 = 6, 7
    for m in range(KC):
        s = c.nsq % 2
        c.nsq += 1
        fw.op("act", lambda e: e.activation(out=c.sq[s][:], in_=XF[:, m, cs], func=AF.Square),
              reads=[kXF[m][ti]], writes=[c.ksq[s]])
        fw.op("pe", lambda e: e.matmul(PS[:, b1, 0:TW], c.onesf[:], XF[:, m, cs], start=(m == 0), stop=(m == KC - 1)),
              reads=[kXF[m][ti], c.kconst], writes=[kPS[b1]])
        fw.op("pe", lambda e: e.matmul(PS[:, b2, 0:TW], c.onesf[:], c.sq[s][:], start=(m == 0), stop=(m == KC - 1)),
              reads=[c.ksq[s], c.kconst], writes=[kPS[b2]])
    mean, msq, var, rstd = c.lnt
    kln = c.kln
    fw.op("dve", lambda e: e.tensor_scalar(out=mean[:], in0=PS[:, b1, 0:TW], scalar1=1.0 / D, scalar2=None, op0=ALU.mult),
          reads=[kPS[b1]], writes=[kln[0]])
    fw.op("dve", lambda e: e.tensor_tensor(out=msq[:], in0=mean[:], in1=mean[:], op=ALU.mult),
          reads=[kln[0]], writes=[kln[1]])
    fw.op("dve", lambda e: e.scalar_tensor_tensor(out=var[:], in0=PS[:, b2, 0:TW], scalar=1.0 / D, in1=msq[:],
                                                  op0=ALU.mult, op1=ALU.subtract),
          reads=[kPS[b2], kln[1]], writes=[kln[2]])
    fw.op("act", lambda e: e.activation(out=var[:], in_=var[:], func=AF.Sqrt, bias=c.epsc[:, 0:1]),
          reads=[kln[2], c.kconst], writes=[kln[2]])
    fw.op("dve", lambda e: e.reciprocal(out=rstd[:], in_=var[:]), reads=[kln[2]], writes=[kln[3]])
    for m in range(KC):
        s = c.nsq % 2
        c.nsq += 1
        t = c.sq[s]
        fw.op("pool", lambda e: e.tensor_tensor(out=t[:], in0=XF[:, m, cs], in1=mean[:], op=ALU.subtract),
              reads=[kXF[m][ti], kln[0]], writes=[c.ksq[s]])
        fw.op("dve", lambda e: e.tensor_tensor(out=t[:], in0=t[:], in1=rstd[:], op=ALU.mult),
              reads=[c.ksq[s], kln[3]], writes=[c.ksq[s]])
        col = lnidx * KC + m
        fw.op("act", lambda e: e.activation(out=XF[:, m, cs], in_=t[:], func=AF.Identity,
                                            scale=c.lng[:, col:col + 1], bias=c.lnb[:, col:col + 1]),
              reads=[c.ksq[s], c.kconst], writes=[kXF[m][ti]])


def pipeline(streams):
    nst = max(len(b) for s in streams for b in s)
    nmax = max(len(s) for s in streams)
    for t in range(nmax + nst - 1):
        for s in streams:
            for k in range(nst - 1, -1, -1):
                bi = t - k
                if 0 <= bi < len(s) and k < len(s[bi]):
                    s[bi][k]()


def mixer_even(c, d, stage):
    nc, fw = c.nc, c.fw
    XF, kXF, PS, kPS = c.XF, c.kXF, c.PS, c.kPS
    st = ExitStack()
    with st:
        OSB = sb(st, nc, "OSB", [128, 4, NT], BF16)
        kOSB = [[K() for _ in range(5)] for _ in range(4)]
        ORW = sb(st, nc, "ORW", [128, 4, NT], BF16)
        kORW = [K() for _ in range(4)]
        fw.op("pool", lambda e: e.memset(OSB[:, :, NP:NT], 0.0), writes=[kOSB[cc][4] for cc in range(4)])
        QS = sb(st, nc, "QS", [NS, 512], F32); kQS = K()
        win_v = d["w_in"].rearrange("(kc p) n -> p kc n", p=128)
        with ExitStack() as sta:
            QT = sb(sta, nc, "QT", [128, 4, NT], BF16)
            KT = sb(sta, nc, "KT", [128, 4, NT], BF16)
            kQT = [K() for _ in range(4)]
            kKT = [K() for _ in range(4)]
            Vtok = sb(sta, nc, "Vtok", [128, 17, 512], BF16)
            kV = [K() for _ in range(17)]
            with ExitStack() as st2:
                XB = sb(st2, nc, "XBm", [128, KC, NT], BF16)
                kXB = [K() for _ in range(KC)]
                for kc in range(KC):
                    fw.op("act" if kc % 2 else "dve",
                          (lambda e, kc=kc: e.activation(out=XB[:, kc, :], in_=XF[:, kc, :], func=AF.Identity)) if kc % 2 else
                          (lambda e, kc=kc: e.tensor_copy(out=XB[:, kc, :], in_=XF[:, kc, :])),
                          reads=kXF[kc], writes=[kXB[kc]])
                WB = [sb(st2, nc, "WBm%d" % i, [128, KC, 256], BF16) for i in range(2)]
                kWB = [K(), K()]
                stg = [sb(st2, nc, "stg%d" % i, [128, 256], F32) for i in range(2)]
                kstg = [K(), K()]
                nb = 0
                nstg = 0
                for wc in range(6):
                    sl = wc % 2
                    fw.dma("pool", WB[sl][:], win_v[:, :, wc * 256:(wc + 1) * 256], writes=[kWB[sl]])
                    if wc < 4:
                        for oo in range(2):
                            oc = 2 * wc + oo
                            dst, kd = (QT, kQT) if oc < 4 else (KT, kKT)
                            for ti in range(NTL):
                                bk = nb % 2
                                nb += 1
                                cs = slice(ti * TW, (ti + 1) * TW)
                                for kc in range(KC):
                                    fw.op("pe", lambda e, kc=kc: e.matmul(PS[:, bk, 0:TW], WB[sl][:, kc, oo * 128:(oo + 1) * 128], XB[:, kc, cs],
                                                                          start=(kc == 0), stop=(kc == KC - 1)),
                                          reads=[kWB[sl], kXB[kc]], writes=[kPS[bk]])
                                fw.op("act", lambda e: e.activation(out=dst[:, oc % 4, cs], in_=PS[:, bk, 0:TW], func=AF.Identity),
                                      reads=[kPS[bk]], writes=[kd[oc % 4]])
                    if wc < 2 and stage >= 4:
                        bk = 2 + (nb % 2)
                        nb += 1
                        for kc in range(KC):
                            fw.op("pe", lambda e, kc=kc: e.matmul(PS[0:NS, bk, 0:256], XB[:, kc, NP:NT], WB[sl][:, kc, :], start=(kc == 0), stop=(kc == KC - 1)),
                                  reads=[kWB[sl], kXB[kc]], writes=[kPS[bk]])
                        fw.op("act", lambda e: e.activation(out=QS[0:NS, wc * 256:(wc + 1) * 256], in_=PS[0:NS, bk, 0:256], func=AF.Identity), reads=[kPS[bk]], writes=[kQS])
                    if wc >= 2:
                        which = 0 if wc < 4 else 1
                        coff = (wc % 2) * 256
                        for tt in range(17):
                            rows = 128 if tt < 16 else NS
                            bk = 2 + (nb % 2)
                            nb += 1
                            for kc in range(KC):
                                fw.op("pe", lambda e, kc=kc: e.matmul(PS[0:rows, bk, 0:256], XB[:, kc, tt * 128:tt * 128 + rows], WB[sl][:, kc, :],
                                                                      start=(kc == 0), stop=(kc == KC - 1)),
                                      reads=[kWB[sl], kXB[kc]], writes=[kPS[bk]])
                            ss = nstg % 2
                            nstg += 1
                            fw.op("act", lambda e: e.activation(out=stg[ss][0:rows, :], in_=PS[0:rows, bk, 0:256], func=AF.Identity),
                                  reads=[kPS[bk]], writes=[kstg[ss]])
                            if tt < 16:
                                dstd = (d["pk"] if which == 0 else d["pv"])[tt * 128:(tt + 1) * 128, coff:coff + 256]
                            else:
                                dstd = (d["sk"] if which == 0 else d["sv"])[:, coff:coff + 256]
                            fw.dma("sp", dstd, stg[ss][0:rows, :], reads=[kstg[ss]])
                            if which == 1:
                                fw.op("act", lambda e: e.activation(out=Vtok[0:rows, tt, coff:coff + 256], in_=PS[0:rows, bk, 0:256], func=AF.Identity),
                                      reads=[kPS[bk]], writes=[kV[tt]])
                fw.barrier()
            with ExitStack() as st2:
                if stage >= 3:
                    sb_attention_prompt(c, d, st2, QT, KT, kQT, kKT, Vtok, kV, OSB, kOSB)
                fw.barrier()
            fw.barrier()
        if stage >= 4 and 'nosamp' not in VAR:
            with ExitStack() as st3:
                sb_attention_sample(c, d, st3, QS, kQS, OSB, kOSB)
                fw.barrier()
        if stage >= 5:
            with ExitStack() as st3:
                rwkv_all(c, d, st3, ORW, kORW)
                fw.barrier()
        if stage >= 6:
            with ExitStack() as st3:
                alloc_ln(c, st3)
                WOUT = sb(st3, nc, "WOUT", [128, KC, D], BF16)
                kWO = [K() for _ in range(4)]
                wo_v = d["w_out"].rearrange("(kc p) n -> p kc n", p=128)
                for i in range(4):
                    fw.dma("pool", WOUT[:, :, i * 256:(i + 1) * 256], wo_v[:, :, i * 256:(i + 1) * 256], writes=[kWO[i]])
                nb2 = 0
                for ti in range(NTL):
                    cs = slice(ti * TW, (ti + 1) * TW)
                    for m in range(KC):
                        bk = nb2 % 2
                        nb2 += 1
                        for kc in range(KC):
                            src, ks = (OSB, kOSB[kc % 4]) if kc < 4 else (ORW, [kORW[kc % 4]])
                            fw.op("pe", lambda e, kc=kc, src=src: e.matmul(PS[:, bk, 0:TW], WOUT[:, kc, m * 128:(m + 1) * 128], src[:, kc % 4, cs],
                                                                           start=(kc == 0), stop=(kc == KC - 1)),
                                  reads=[kWO[m // 2]] + list(ks), writes=[kPS[bk]])
                        fw.op("dve", lambda e: e.scalar_tensor_tensor(out=XF[:, m, cs], in0=XF[:, m, cs], scalar=ALPHA, in1=PS[:, bk, 0:TW],
                                                                      op0=ALU.mult, op1=ALU.add),
                              reads=[kPS[bk], kXF[m][ti]], writes=[kXF[m][ti]])
                    layer_norm_tile(c, ti, 1)
                fw.barrier()
        if c.dbg and 'nodbg' not in VAR:
            fw.dma("sp", d["dbg_osb"], OSB[:], reads=[k for kk in kOSB for k in kk])
        fw.barrier()


def sb_attention_prompt(c, d, st, QT, KT, kQT, kKT, Vtok, kV, OSB, kOSB):
    nc, fw = c.nc, c.fw
    PS, kPS = c.PS, c.kPS
    NSTR = 2
    onesb = c.onesb
    c.TRI = sb(st, nc, "TRI", [128, 128], BF16)
    c.STRICT = sb(st, nc, "STRICT", [128, 128], BF16)
    c.MASK = [sb(st, nc, "MASK%d" % i, [128, 512], BF16) for i in range(4)]
    fw.op("pool", lambda e: e.affine_select(out=c.TRI[:], in_=onesb[:, 0:128], pattern=[[-1, 128]], base=0, channel_multiplier=1,
                                            compare_op=ALU.is_ge, fill=0.0), reads=[c.kconst], writes=[c.kconst])
    for i in range(4):
        fw.op("pool", lambda e, i=i: e.affine_select(out=c.MASK[i][:], in_=onesb[:], pattern=[[1, 512]], base=-128 * i, channel_multiplier=-1,
                                                     compare_op=ALU.is_gt, fill=0.0), reads=[c.kconst], writes=[c.kconst])
    Et = [[sb(st, nc, "Et%d_%d" % (s, i), [128, 512], F32) for i in range(3)] for s in range(NSTR)]
    spt = [[sb(st, nc, "spt%d_%d" % (s, i), [128, 512], BF16) for i in range(3)] for s in range(NSTR)]
    e2t = [[sb(st, nc, "e2t%d_%d" % (s, i), [128, 512], F32) for i in range(2)] for s in range(NSTR)]
    wt = [[sb(st, nc, "wt%d_%d" % (s, i), [128, 512], BF16) for i in range(2)] for s in range(NSTR)]
    kEt = [[K() for _ in range(3)] for _ in range(NSTR)]
    kspt = [[K() for _ in range(3)] for _ in range(NSTR)]
    ke2t = [[K() for _ in range(2)] for _ in range(NSTR)]
    kwt = [[K() for _ in range(2)] for _ in range(NSTR)]
    sacc = [sb(st, nc, "sacc%d" % s, [128, 512], BF16) for s in range(NSTR)]
    ksacc = [K() for _ in range(NSTR)]
    streams = []
    for s in range(NSTR):
        blocks = []
        bi = 0
        bA = [4 * s, 4 * s + 1]
        bC = 4 * s + 2
        bO = 4 * s + 3
        for h in range(s, 8, NSTR):
            po = (h % 2) * 64
            ch = h // 2
            for qt in range(4):
                q0 = qt * 512
                nkb = 4 * qt + 4
                for n, kb in enumerate(range(nkb - 1, -1, -1)):
                    first = (n == 0)
                    last = (kb == 0)
                    diag = kb - 4 * qt
                    i3, i2 = bi % 3, bi % 2
                    bi += 1

                    def st1(s=s, h=h, po=po, ch=ch, q0=q0, kb=kb, diag=diag, i3=i3, bA=bA[bi % 2]):
                        fw.op("pe", lambda e: e.matmul(PS[:, bA, :], KT[po:po + 64, ch, kb * 128:(kb + 1) * 128], QT[po:po + 64, ch, q0:q0 + 512],
                                                       start=True, stop=True),
                              reads=[kQT[ch], kKT[ch]], writes=[kPS[bA]])
                        fw.op("act", lambda e: e.activation(out=Et[s][i3][:], in_=PS[:, bA, :], func=AF.Exp, scale=0.125,
                                                            bias=c.sbb[:, h:h + 1]),
                              reads=[kPS[bA], c.kconst], writes=[kEt[s][i3]])
                        fw.op("act", lambda e: e.activation(out=spt[s][i3][:], in_=Et[s][i3][:], func=AF.Ln, bias=c.cst[:, 1:2]),
                              reads=[kEt[s][i3], c.kconst], writes=[kspt[s][i3]])
                        if diag >= 0:
                            fw.op("dve", lambda e: e.tensor_tensor(out=spt[s][i3][:], in0=spt[s][i3][:], in1=c.MASK[diag][:], op=ALU.mult),
                                  reads=[kspt[s][i3], c.kconst], writes=[kspt[s][i3]])
                            fw.op("pool", lambda e: e.tensor_tensor(out=Et[s][i3][:], in0=Et[s][i3][:], in1=c.MASK[diag][:], op=ALU.mult),
                                  reads=[kEt[s][i3], c.kconst], writes=[kEt[s][i3]])

                    def st2(s=s, i3=i3, i2=i2, first=first, last=last, bC=bC):
                        fw.op("pe", lambda e: e.matmul(PS[:, bC, :], c.TRI[:], spt[s][i3][:], start=True, stop=first),
                              reads=[kspt[s][i3], c.kconst], writes=[kPS[bC]])
                        if not first:
                            fw.op("pe", lambda e: e.matmul(PS[:, bC, :], c.ONESB[:], sacc[s][:], start=False, stop=True),
                                  reads=[ksacc[s], c.kconst], writes=[kPS[bC]])
                        fw.op("act", lambda e: e.activation(out=e2t[s][i2][:], in_=PS[:, bC, :], func=AF.Exp, scale=-1.0),
                              reads=[kPS[bC]], writes=[ke2t[s][i2]])
                        if not last:
                            if first:
                                fw.op("pool", lambda e: e.tensor_copy(out=sacc[s][:], in_=spt[s][i3][:]),
                                      reads=[kspt[s][i3]], writes=[ksacc[s]])
                            else:
                                fw.op("pool", lambda e: e.tensor_tensor(out=sacc[s][:], in0=sacc[s][:], in1=spt[s][i3][:], op=ALU.add),
                                      reads=[kspt[s][i3], ksacc[s]], writes=[ksacc[s]])

                    def st3(s=s, h=h, po=po, ch=ch, q0=q0, qt=qt, kb=kb, i3=i3, i2=i2, first=first, last=last, bC=bC, bO=bO):
                        fw.op("dve", lambda e: e.tensor_tensor(out=wt[s][i2][:], in0=Et[s][i3][:], in1=e2t[s][i2][:], op=ALU.mult),
                              reads=[kEt[s][i3], ke2t[s][i2]], writes=[kwt[s][i2]])
                        fw.op("pe", lambda e: e.matmul(PS[po:po + 64, bO, :], Vtok[:, kb, h * 64:(h + 1) * 64], wt[s][i2][:],
                                                       start=first, stop=last),
                              reads=[kV[kb], kwt[s][i2]], writes=[kPS[bO]])
                        if last:
                            fw.op("act", lambda e: e.activation(out=OSB[po:po + 64, ch, q0:q0 + 512], in_=PS[po:po + 64, bO, :], func=AF.Identity),
                                  reads=[kPS[bO]], writes=[kOSB[ch][qt]])
                    blocks.append([st1, st2, st3])
        streams.append(blocks)
    pipeline(streams)


CS = 14


def transpose_to(c, out_ps, in_ap, kin, kout, ident):
    c.fw.op("pe", lambda e: e.transpose(out_ps, in_ap, ident), reads=kin + [c.kconst], writes=kout)


def rwkv_all(c, d, st, ORW, kORW):
    nc, fw = c.nc, c.fw
    XF, kXF, PS, kPS = c.XF, c.kXF, c.PS, c.kPS
    IDF = c.IDF
    ST = sb(st, nc, "ST", [128, 256], F32); kST = K()
    STw = sb(st, nc, "STw", [128, 256], F32); kSTw = K()
    SAY = sb(st, nc, "SAY", [128, 64], BF16); kSAY = K()
    PB = sb(st, nc, "PB", [128, 14, TW + 1], F32); kPB = [K() for _ in range(14)]
    PM = sb(st, nc, "PM", [128, 14, TW], F32); kPM = [K() for _ in range(14)]
    XBt = sb(st, nc, "XBt", [128, KC, TW], BF16); kXBt = [K() for _ in range(KC)]
    WRb = [sb(st, nc, "WRb%d" % i, [128, KC, 256], BF16) for i in range(2)]; kWRb = [K(), K()]
    WW2 = sb(st, nc, "WW2", [128, 512], BF16)
    WA2 = sb(st, nc, "WA2", [128, 512], BF16)
    WG2 = sb(st, nc, "WG2", [128, 512], BF16)
    PC = sb(st, nc, "PC", [128, 48], F32)
    GNG = sb(st, nc, "GNG", [128, 512], F32)
    GNB = sb(st, nc, "GNB", [128, 512], F32)
    BLK = sb(st, nc, "BLK", [128, 128], F32)
    kW = K()
    tmpA = sb(st, nc, "tmpA", [128, TW], F32); ktA = K()
    tmpB = sb(st, nc, "tmpB", [128, TW], F32); ktB = K()
    tmpH = sb(st, nc, "tmpH", [128, TW], BF16); ktH = K()
    NBrow = [sb(st, nc, "NBrow%d" % i, [128, CS, 128], BF16) for i in range(2)]
    KBrow = [sb(st, nc, "KBrow%d" % i, [128, CS, 128], BF16) for i in range(2)]
    Vrow = [sb(st, nc, "Vrow%d" % i, [128, CS, 64], BF16) for i in range(2)]
    kRow = [K(), K()]
    TOK = sb(st, nc, "TOK", [128, 3, 512], BF16); kTOK = [K(), K(), K()]
    LKc = [sb(st, nc, "LKc%d" % i, [128, CS, 4, 2], F32) for i in range(2)]
    RKc = [sb(st, nc, "RKc%d" % i, [128, CS, 4, 2], F32) for i in range(2)]
    kLK = [K(), K()]
    Ybuf = [sb(st, nc, "Ybuf%d" % i, [128, CS, 64], BF16) for i in range(2)]; kY = [K(), K()]
    YTOK = sb(st, nc, "YTOK", [128, 512], BF16); kYT = K()
    YC = sb(st, nc, "YC", [128, 512], F32); kYC = K()
    YS = sb(st, nc, "YS", [128, 512], F32); kYS = K()
    gst = sb(st, nc, "gst", [128, 32], F32); kgst = K()
    SHX = sb(st, nc, "SHX", [NS + 1, RWC], F32); kSHT = K()
    SHTOK = SHX
    SHT = sb(st, nc, "SHT", [128, 14, NS], F32); kSH = K()
    SSH = SHX; kSSH = kSHT
    SLD = sb(st, nc, "SLD", [64, 4, 128], F32); kSLD = K()
    SSTt = SLD; kSST = kSLD
    fw.dma("pool", WW2[0:64, :], d["w_w2"], writes=[kW])
    fw.dma("pool", WA2[64:128, :], d["w_a2"], writes=[kW])
    fw.dma("pool", WG2[:, :], d["w_g2"], writes=[kW])
    fw.dma("pool", PC[:], d["pcol"], writes=[kW])
    fw.dma("pool", GNG[:], d["gng"], writes=[kW])
    fw.dma("pool", GNB[:], d["gnb"], writes=[kW])
    fw.dma("pool", SHTOK[0:NS, :], d["sshift0"], writes=[kSHT])
    fw.op("pool", lambda e: e.memset(BLK[:], 0.0), writes=[kW])
    fw.op("pool", lambda e: e.memset(BLK[0:64, 0:64], 1.0), reads=[kW], writes=[kW])
    fw.op("pool", lambda e: e.memset(BLK[64:128, 64:128], 1.0), reads=[kW], writes=[kW])
    fw.op("pool", lambda e: e.memset(ST[:], 0.0), writes=[kST])
    for i_ in range(2):
        fw.op("pool", lambda e, i_=i_: e.memset(NBrow[i_][:], 0.0), writes=[kRow[i_]])
        fw.op("pool", lambda e, i_=i_: e.memset(KBrow[i_][:], 0.0), writes=[kRow[i_]])
        fw.op("pool", lambda e, i_=i_: e.memset(Vrow[i_][:], 0.0), writes=[kRow[i_]])
        fw.op("pool", lambda e, i_=i_: e.memset(LKc[i_][:], 0.0), writes=[kLK[i_]])
        fw.op("pool", lambda e, i_=i_: e.memset(RKc[i_][:], 0.0), writes=[kLK[i_]])
    fw.op("pool", lambda e: e.memset(PB[:, :, 0:1], 0.0), writes=kPB)
    MU, W0, A0, KKc, KAc, RKp = 0, 14, 18, 22, 26, 30
    for bz in (2, 3, 4, 5, 6, 7):
        fw.op("dve", lambda e, bz=bz: e.memset(PS[:, bz, :], 0.0), writes=[kPS[bz]])
    for m in range(14):
        transpose_to(c, PS[:, 7, m * NS:(m + 1) * NS], SHTOK[0:NS, m * 128:(m + 1) * 128], [kSHT], [kPS[7]], IDF[0:NS, 0:NS])
    fw.op("act", lambda e: e.activation(out=SHT[:].rearrange("p m b -> p (m b)"), in_=PS[:, 7, 0:14 * NS], func=AF.Identity),
          reads=[kPS[7]], writes=[kSH])
    wr_v = d["w_in"].rearrange("(kc p) n -> p kc n", p=128)
    nwr = 0
    nbk = 0

    def store_state(dst):
        for h4 in range(4):
            transpose_to(c, PS[0:64, 0, h4 * 128:(h4 + 1) * 128], ST[:, h4 * 64:(h4 + 1) * 64], [kST], [kPS[0]], IDF[:, :])
        fw.op("act", lambda e: e.activation(out=SSTt[:].rearrange("p a b -> p (a b)"), in_=PS[0:64, 0, :], func=AF.Identity),
              reads=[kPS[0]], writes=[kSST])
        fw.dma("pool", dst.rearrange("(h4 h2) i j -> i h4 h2 j", h2=2), SSTt[:].rearrange("p a (h2 j) -> p a h2 j", h2=2), reads=[kSST])

    def load_state(src):
        fw.dma("pool", SLD[:].rearrange("p a (h2 j) -> p a h2 j", h2=2), src.rearrange("(h4 h2) i j -> i h4 h2 j", h2=2), writes=[kSLD])
        for h4 in range(4):
            transpose_to(c, PS[:, 0, h4 * 64:(h4 + 1) * 64], SLD[:, h4, :], [kSLD], [kPS[0]], IDF[0:64, 0:64])
        fw.op("act", lambda e: e.activation(out=ST[:], in_=PS[:, 0, 0:256], func=AF.Identity), reads=[kPS[0]], writes=[kST])

    for ti in ([5] if 'rw1' in VAR else [0] if 'rw0' in VAR else range(NTL)):
        c0 = ti * TW
        npr = min(TW, NP - c0)
        for kc in range(KC):
            fw.op("act", lambda e, kc=kc: e.activation(out=XBt[:, kc, :], in_=XF[:, kc, c0:c0 + TW], func=AF.Identity),
                  reads=[kXF[kc][ti]], writes=[kXBt[kc]])
        for mp in range(7):
            sl = nwr % 2
            nwr += 1
            fw.dma("pool", WRb[sl][:], wr_v[:, :, 1536 + mp * 256:1536 + (mp + 1) * 256], writes=[kWRb[sl]])
            for mm in range(2):
                m = 2 * mp + mm
                bk = nbk % 2
                nbk += 1
                for kc in range(KC):
                    fw.op("pe", lambda e, kc=kc: e.matmul(PS[:, bk, 0:TW], WRb[sl][:, kc, mm * 128:(mm + 1) * 128], XBt[:, kc, :],
                                                          start=(kc == 0), stop=(kc == KC - 1)),
                          reads=[kWRb[sl], kXBt[kc]], writes=[kPS[bk]])
                fw.op("act", lambda e: e.activation(out=PB[:, m, 1:TW + 1], in_=PS[:, bk, 0:TW], func=AF.Identity),
                      reads=[kPS[bk]], writes=[kPB[m]])
        for m in range(14):
            fw.op("pool", lambda e, m=m: e.tensor_tensor(out=PM[:, m, 0:npr], in0=PB[:, m, 0:npr], in1=PB[:, m, 1:npr + 1], op=ALU.subtract),
                  reads=[kPB[m]], writes=[kPM[m]])
            if npr < TW:
                fw.op("pool", lambda e, m=m: e.tensor_tensor(out=PM[:, m, npr:TW], in0=SHT[:, m, :], in1=PB[:, m, npr + 1:TW + 1], op=ALU.subtract),
                      reads=[kPB[m], kSH], writes=[kPM[m]])
            fw.op("dve", lambda e, m=m: e.scalar_tensor_tensor(out=PM[:, m, :], in0=PM[:, m, :], scalar=PC[:, MU + m:MU + m + 1],
                                                               in1=PB[:, m, 1:TW + 1], op0=ALU.mult, op1=ALU.add),
                  reads=[kPB[m], kPM[m], kW], writes=[kPM[m]])
        if npr < TW:
            for m in range(14):
                transpose_to(c, PS[0:NS + 1, 7, (m % 4) * 128:(m % 4 + 1) * 128], PB[:, m, npr:TW + 1], [kPB[m]], [kPS[7]], IDF[:, :])
                if m % 4 == 3 or m == 13:
                    m0 = (m // 4) * 4
                    fw.op("act", lambda e, m0=m0, m=m: e.activation(out=SSH[:, m0 * 128:(m + 1) * 128], in_=PS[0:NS + 1, 7, 0:(m - m0 + 1) * 128], func=AF.Identity),
                          reads=[kPS[7]], writes=[kSSH])
            fw.dma("pool", d["pshift"], SSH[0:1, :], reads=[kSSH])
            fw.dma("pool", d["sshift"], SSH[1:NS + 1, :], reads=[kSSH])
        fw.op("dve", lambda e: e.tensor_copy(out=PB[:, :, 0:1], in_=PB[:, :, TW:TW + 1]), reads=kPB, writes=kPB)
        Wt = lambda cc: PB[:, cc, 1:TW + 1]
        KKt = lambda cc: PB[:, 4 + cc, 1:TW + 1]
        NBt = lambda cc: PB[:, 8 + cc, 1:TW + 1]
        fw.op("act", lambda e: e.activation(out=tmpH[0:64, :], in_=PM[0:64, 12, :], func=AF.Tanh), reads=[kPM[12]], writes=[ktH])
        fw.op("act", lambda e: e.activation(out=tmpH[64:128, :], in_=PM[64:128, 12, :], func=AF.Identity), reads=[kPM[12]], writes=[ktH])
        for cc in range(4):
            bk = nbk % 2
            nbk += 1
            fw.op("pe", lambda e: e.matmul(PS[:, bk, 0:TW], WW2[0:64, cc * 128:(cc + 1) * 128], tmpH[0:64, :], start=True, stop=True),
                  reads=[kW, ktH], writes=[kPS[bk]])
            fw.op("act", lambda e: e.activation(out=tmpA[:], in_=PS[:, bk, 0:TW], func=AF.Sigmoid, bias=PC[:, W0 + cc:W0 + cc + 1]),
                  reads=[kPS[bk], kW], writes=[ktA])
            fw.op("act", lambda e: e.activation(out=Wt(cc), in_=tmpA[:], func=AF.Exp, scale=-0.6065306597126334),
                  reads=[ktA], writes=[kPB[cc]])
        for cc in range(4):
            bk = nbk % 2
            nbk += 1
            fw.op("pe", lambda e: e.matmul(PS[:, bk, 0:TW], WA2[64:128, cc * 128:(cc + 1) * 128], tmpH[64:128, :], start=True, stop=True),
                  reads=[kW, ktH], writes=[kPS[bk]])
            fw.op("act", lambda e: e.activation(out=tmpA[:], in_=PS[:, bk, 0:TW], func=AF.Sigmoid, bias=PC[:, A0 + cc:A0 + cc + 1]),
                  reads=[kPS[bk], kW], writes=[ktA])
            fw.op("dve", lambda e: e.tensor_scalar(out=KKt(cc), in0=PM[:, 4 + cc, :], scalar1=PC[:, KKc + cc:KKc + cc + 1], scalar2=None, op0=ALU.mult),
                  reads=[kPM[4 + cc], kW], writes=[kPB[4 + cc]])
            fw.op("act", lambda e: e.activation(out=tmpB[:], in_=KKt(cc), func=AF.Square), reads=[kPB[4 + cc]], writes=[ktB])
            b2 = 2 + (nbk % 2)
            fw.op("pe", lambda e: e.matmul(PS[:, b2, 0:TW], BLK[:], tmpB[:], start=True, stop=True), reads=[kW, ktB], writes=[kPS[b2]])
            fw.op("dve", lambda e: e.tensor_scalar(out=tmpB[:], in0=PS[:, b2, 0:TW], scalar1=1e-24, scalar2=None, op0=ALU.max),
                  reads=[kPS[b2]], writes=[ktB])
            fw.op("act", lambda e: e.activation(out=tmpB[:], in_=tmpB[:], func=AF.Sqrt), reads=[ktB], writes=[ktB])
            fw.op("dve", lambda e: e.reciprocal(out=tmpB[:], in_=tmpB[:]), reads=[ktB], writes=[ktB])
            fw.op("dve", lambda e: e.tensor_tensor(out=KKt(cc), in0=KKt(cc), in1=tmpB[:], op=ALU.mult), reads=[kPB[4 + cc], ktB], writes=[kPB[4 + cc]])
            fw.op("dve", lambda e: e.scalar_tensor_tensor(out=NBt(cc), in0=KKt(cc), scalar=-1.0, in1=tmpA[:], op0=ALU.mult, op1=ALU.mult),
                  reads=[kPB[4 + cc], ktA], writes=[kPB[8 + cc]])
            fw.op("dve", lambda e: e.tensor_scalar(out=tmpA[:], in0=tmpA[:], scalar1=-1.0, scalar2=PC[:, KAc + cc:KAc + cc + 1], op0=ALU.add, op1=ALU.mult),
                  reads=[ktA, kW], writes=[ktA])
            fw.op("dve", lambda e: e.scalar_tensor_tensor(out=PM[:, 4 + cc, :], in0=tmpA[:], scalar=1.0, in1=PM[:, 4 + cc, :], op0=ALU.add, op1=ALU.mult),
                  reads=[ktA, kPM[4 + cc]], writes=[kPM[4 + cc]])
        if 'rwA' in VAR:
            continue
        subs = [(a_, min(CS, TW - a_)) for a_ in range(0, TW, CS)]
        cblocks = [(0, 128), (128, 128), (256, TW - 256)]
        for (cb0, ncb) in cblocks:
            for vi, src in enumerate((lambda cc: PM[:, 4 + cc, cb0:cb0 + ncb], lambda cc: NBt(cc)[:, cb0:cb0 + ncb], lambda cc: PM[:, 8 + cc, cb0:cb0 + ncb])):
                kk_ = (lambda cc: kPM[4 + cc], lambda cc: kPB[8 + cc], lambda cc: kPM[8 + cc])[vi]
                bk = nbk % 2
                nbk += 1
                for cc in range(4):
                    transpose_to(c, PS[0:ncb, bk, cc * 128:(cc + 1) * 128], src(cc), [kk_(cc)], [kPS[bk]], IDF[:, :])
                fw.op("act", lambda e: e.activation(out=TOK[0:ncb, vi, :], in_=PS[0:ncb, bk, :], func=AF.Identity), reads=[kPS[bk]], writes=[kTOK[vi]])
            fw.dma("pool", c.TOKd[cb0:cb0 + ncb], TOK[0:ncb, :, :], reads=kTOK, writes=[c.kTOKd])

        def fetch_rows(a, ncol, r):
            for h2 in range(2):
                for vi, dstt in enumerate((KBrow[r], NBrow[r])):
                    fw.dma("pool", dstt[h2:128:32, 0:ncol, h2 * 64:(h2 + 1) * 64],
                           c.TOKd[a:a + ncol, vi, :].rearrange("s (h4 h2 j) -> h4 s h2 j", h4=4, h2=2)[:, :, h2, :], reads=[c.kTOKd], writes=[kRow[r]])
                fw.dma("pool", Vrow[r][h2:128:32, 0:ncol, :],
                       c.TOKd[a:a + ncol, 2, :].rearrange("s (h4 h2 j) -> h4 s h2 j", h4=4, h2=2)[:, :, h2, :], reads=[c.kTOKd], writes=[kRow[r]])
            for h2 in range(2):
                ps_ = slice(h2 * 64, (h2 + 1) * 64)
                fw.op("pool", lambda e: e.tensor_copy(out=LKc[r][ps_, 0:ncol, :, h2], in_=PB[ps_, 4:8, 1 + a:1 + a + ncol].rearrange("p c s -> p s c")),
                      reads=kPB[4:8], writes=[kLK[r]])
                fw.op("pool", lambda e: e.tensor_copy(out=RKc[r][ps_, 0:ncol, :, h2], in_=PM[ps_, 0:4, a:a + ncol].rearrange("p c s -> p s c")),
                      reads=kPM[0:4], writes=[kLK[r]])

        def run_steps(a, ncol, r):
            def emit_y(sy):
                yb = sy % 8
                for h4 in range(4):
                    fw.op("pe", lambda e, h4=h4: e.matmul(PS[32 * h4:32 * h4 + 2, 7, yb * 64:(yb + 1) * 64], RKc[r][:, sy, h4, :], ST[:, h4 * 64:(h4 + 1) * 64],
                                                          start=True, stop=True, tile_position=(0, 32 * h4)),
                          reads=[kLK[r], kST], writes=[kPS[7]])
                if yb == 7 or sy == ncol - 1:
                    s0_ = sy - yb
                    fw.op("act", lambda e: e.activation(out=Ybuf[r][:, s0_:sy + 1, :].rearrange("p s i -> p (s i)"), in_=PS[:, 7, 0:(yb + 1) * 64], func=AF.Identity),
                          reads=[kPS[7]], writes=[kY[r]])

            pending_y = None
            for s_ in range(ncol):
                gcol = c0 + a + s_
                is_sample = gcol >= NP
                if is_sample:
                    if pending_y is not None:
                        emit_y(pending_y); pending_y = None
                    load_state(d["swkv0"][gcol - NP])
                for h4 in range(4):
                    fw.op("pe", lambda e, h4=h4: e.matmul(PS[32 * h4:32 * h4 + 2, 2, 0:64], LKc[r][:, s_, h4, :], ST[:, h4 * 64:(h4 + 1) * 64],
                                                          start=True, stop=True, tile_position=(0, 32 * h4)),
                          reads=[kLK[r], kST], writes=[kPS[2]])
                fw.op("dve", lambda e: e.tensor_tensor(out=STw[:].rearrange("p (a b) -> p a b", a=4), in0=ST[:].rearrange("p (a b) -> p a b", a=4),
                                                       in1=PB[:, 0:4, 1 + a + s_:2 + a + s_].to_broadcast([128, 4, 64]), op=ALU.mult),
                      reads=[kST] + kPB[0:4], writes=[kSTw])
                if pending_y is not None:
                    emit_y(pending_y); pending_y = None
                for h4 in range(4):
                    fw.op("pe", lambda e, h4=h4: e.matmul(PS[:, 3 + h4, 0:64], KBrow[r][32 * h4:32 * h4 + 2, s_, :], Vrow[r][32 * h4:32 * h4 + 2, s_, :],
                                                          start=True, stop=False, tile_position=(32 * h4, 0)),
                          reads=[kRow[r]], writes=[kPS[3 + h4]])
                fw.op("act", lambda e: e.activation(out=SAY[:], in_=PS[:, 2, 0:64], func=AF.Identity), reads=[kPS[2]], writes=[kSAY])
                for h4 in range(4):
                    fw.op("pe", lambda e, h4=h4: e.matmul(PS[:, 3 + h4, 0:64], NBrow[r][32 * h4:32 * h4 + 2, s_, :], SAY[32 * h4:32 * h4 + 2, :],
                                                          start=False, stop=True, tile_position=(32 * h4, 0)),
                          reads=[kRow[r], kSAY], writes=[kPS[3 + h4]])
                fw.op("dve", lambda e: e.tensor_tensor(out=ST[:].rearrange("p (a b) -> p a b", a=4), in0=STw[:].rearrange("p (a b) -> p a b", a=4),
                                                       in1=PS[:, 3:7, 0:64], op=ALU.add),
                      reads=[kSTw] + kPS[3:7], writes=[kST])
                pending_y = s_
                if is_sample or gcol == NP - 1 or s_ == ncol - 1:
                    emit_y(pending_y); pending_y = None
                if is_sample:
                    store_state(d["swkv"][gcol - NP])
                if gcol == NP - 1:
                    store_state(d["pwkv"])
            for h2 in range(2):
                fw.dma("pool", c.Yd[:, h2, a:a + ncol, :], Ybuf[r][h2:128:32, 0:ncol, :], reads=[kY[r]], writes=[c.kYd])

        fetch_rows(subs[0][0], subs[0][1], 0)
        for n_, (a_, nc_) in enumerate(subs):
            r_ = n_ % 2
            if n_ + 1 < len(subs):
                fetch_rows(subs[n_ + 1][0], subs[n_ + 1][1], 1 - r_)
            run_steps(a_, nc_, r_)

        for (cb0, ncol) in cblocks:
            a = cb0
            fw.dma("pool", YTOK[0:ncol, :].rearrange("s (h i) -> s h i", h=8), c.Yd[:, :, a:a + ncol, :].rearrange("a b s i -> s (a b) i"),
                   reads=[c.kYd], writes=[kYT])
            Y3 = YTOK[0:ncol, :].rearrange("s (h i) -> s h i", h=8)
            C3 = YC[0:ncol, :].rearrange("s (h i) -> s h i", h=8)
            S3 = YS[0:ncol, :].rearrange("s (h i) -> s h i", h=8)
            fw.op("dve", lambda e: e.tensor_reduce(out=gst[0:ncol, 0:8], in_=Y3, axis=AX.X, op=ALU.add), reads=[kYT], writes=[kgst])
            fw.op("dve", lambda e: e.tensor_scalar(out=gst[0:ncol, 0:8], in0=gst[0:ncol, 0:8], scalar1=1.0 / 64, scalar2=None, op0=ALU.mult), reads=[kgst], writes=[kgst])
            fw.op("dve", lambda e: e.tensor_tensor(out=C3, in0=Y3, in1=gst[0:ncol, 0:8].unsqueeze(2).to_broadcast([ncol, 8, 64]), op=ALU.subtract),
                  reads=[kYT, kgst], writes=[kYC])
            fw.op("act", lambda e: e.activation(out=YS[0:ncol, :], in_=YC[0:ncol, :], func=AF.Square), reads=[kYC], writes=[kYS])
            fw.op("dve", lambda e: e.tensor_reduce(out=gst[0:ncol, 8:16], in_=S3, axis=AX.X, op=ALU.add), reads=[kYS], writes=[kgst])
            fw.op("act", lambda e: e.activation(out=gst[0:ncol, 8:16], in_=gst[0:ncol, 8:16], func=AF.Sqrt, scale=1.0 / 64, bias=c.cst[0:ncol, 2:3]),
                  reads=[kgst, c.kconst], writes=[kgst])
            fw.op("dve", lambda e: e.reciprocal(out=gst[0:ncol, 8:16], in_=gst[0:ncol, 8:16]), reads=[kgst], writes=[kgst])
            fw.op("dve", lambda e: e.tensor_tensor(out=C3, in0=C3, in1=gst[0:ncol, 8:16].unsqueeze(2).to_broadcast([ncol, 8, 64]), op=ALU.mult),
                  reads=[kYC, kgst], writes=[kYC])
            fw.op("pool", lambda e: e.tensor_tensor(out=YC[0:ncol, :], in0=YC[0:ncol, :], in1=GNG[0:ncol, :], op=ALU.mult), reads=[kYC, kW], writes=[kYC])
            fw.op("pool", lambda e: e.tensor_tensor(out=YC[0:ncol, :], in0=YC[0:ncol, :], in1=GNB[0:ncol, :], op=ALU.add), reads=[kYC, kW], writes=[kYC])
            for cc in range(4):
                transpose_to(c, PS[:, 0, cc * 128:cc * 128 + ncol], YC[0:ncol, cc * 128:(cc + 1) * 128], [kYC], [kPS[0]], IDF[0:ncol, 0:ncol])
            fw.op("act", lambda e: e.activation(out=tmpH[:, 0:ncol], in_=PM[:, 13, a:a + ncol], func=AF.Sigmoid), reads=[kPM[13]], writes=[ktH])
            for cc in range(4):
                fw.op("dve", lambda e: e.scalar_tensor_tensor(out=tmpA[:, 0:ncol], in0=PM[:, cc, a:a + ncol], scalar=PC[:, RKp + cc:RKp + cc + 1],
                                                              in1=PM[:, 4 + cc, a:a + ncol], op0=ALU.mult, op1=ALU.mult),
                      reads=[kPM[cc], kPM[4 + cc], kW], writes=[ktA])
                fw.op("pe", lambda e: e.matmul(PS[:, 1, 0:ncol], BLK[:], tmpA[:, 0:ncol], start=True, stop=True), reads=[kW, ktA], writes=[kPS[1]])
                fw.op("pe", lambda e: e.matmul(PS[:, 1, 256:256 + ncol], WG2[:, cc * 128:(cc + 1) * 128], tmpH[:, 0:ncol], start=True, stop=True),
                      reads=[kW, ktH], writes=[kPS[1]])
                fw.op("dve", lambda e: e.tensor_tensor(out=tmpB[:, 0:ncol], in0=PS[:, 1, 0:ncol], in1=PM[:, 8 + cc, a:a + ncol], op=ALU.mult),
                      reads=[kPS[1], kPM[8 + cc]], writes=[ktB])
                fw.op("dve", lambda e: e.tensor_tensor(out=tmpB[:, 0:ncol], in0=tmpB[:, 0:ncol], in1=PS[:, 0, cc * 128:cc * 128 + ncol], op=ALU.add),
                      reads=[ktB, kPS[0]], writes=[ktB])
                fw.op("dve", lambda e: e.tensor_tensor(out=ORW[:, cc, c0 + a:c0 + a + ncol], in0=tmpB[:, 0:ncol], in1=PS[:, 1, 256:256 + ncol], op=ALU.mult),
                      reads=[ktB, kPS[1]], writes=[kORW[cc]])


TWO_PI = 6.283185307179586
C1 = 6.28125
C2 = TWO_PI - 6.28125
PI = 3.141592653589793


def trig_tables(c, X, kX, Sout, Cout, kS, kC, tmpI, tmpF, ktmp, width):
    fw = c.fw
    w = slice(0, width)
    fw.op("dve", lambda e: e.tensor_scalar(out=tmpI[:, w], in0=X[:, w], scalar1=1.0 / TWO_PI, scalar2=None, op0=ALU.mult), reads=[kX], writes=[ktmp])
    fw.op("dve", lambda e: e.tensor_copy(out=tmpF[:, w], in_=tmpI[:, w]), reads=[ktmp], writes=[ktmp])
    fw.op("dve", lambda e: e.scalar_tensor_tensor(out=X[:, w], in0=tmpF[:, w], scalar=-C1, in1=X[:, w], op0=ALU.mult, op1=ALU.add), reads=[ktmp, kX], writes=[kX])
    fw.op("dve", lambda e: e.scalar_tensor_tensor(out=X[:, w], in0=tmpF[:, w], scalar=-C2, in1=X[:, w], op0=ALU.mult, op1=ALU.add), reads=[ktmp, kX], writes=[kX])
    fw.op("dve", lambda e: e.tensor_scalar(out=X[:, w], in0=X[:, w], scalar1=PI, scalar2=-PI, op0=ALU.min, op1=ALU.max), reads=[kX], writes=[kX])
    fw.op("act", lambda e: e.activation(out=Sout, in_=X[:, w], func=AF.Sin), reads=[kX], writes=[kS])
    fw.op("act", lambda e: e.activation(out=tmpF[:, w], in_=X[:, w], func=AF.Abs), reads=[kX, ktmp], writes=[ktmp])
    fw.op("act", lambda e: e.activation(out=Cout, in_=tmpF[:, w], func=AF.Sin, scale=-1.0, bias=c.cst[:, 3:4]), reads=[ktmp, c.kconst], writes=[kC])


def s5_mixer(c, d):
    nc, fw = c.nc, c.fw
    XF, kXF, PS, kPS = c.XF, c.kXF, c.PS, c.kPS
    IDF = c.IDF
    with ExitStack() as st:
        ZG = sb(st, nc, "ZG", [128, KC, NT], BF16); kZG = [[K() for _ in range(NTL)] for _ in range(KC)]
        with ExitStack() as s2:
            XB = sb(s2, nc, "XBs", [128, KC, NT], BF16); kXB = [K() for _ in range(KC)]
            for kc in range(KC):
                fw.op("act", lambda e, kc=kc: e.activation(out=XB[:, kc, :], in_=XF[:, kc, :], func=AF.Identity), reads=kXF[kc], writes=[kXB[kc]])
            PRM = sb(s2, nc, "PRM", [128, 3, 32], F32); kP = K()
            fw.dma("sp", PRM[:], d["s5prm"], writes=[kP])
            DSK = sb(s2, nc, "DSK", [128, 8], F32)
            fw.dma("sp", DSK[:], d["dskip"], writes=[kP])
            S0 = sb(s2, nc, "S0", [128, 32, NS, 2], F32); kS0 = K()
            for t_ in range(32):
                fw.dma("sp", S0[:, t_, :, :], d["s5_0"][:, t_ * 128:(t_ + 1) * 128, :].rearrange("b p r -> p b r"), writes=[kS0])
            cn = {nm: sb(s2, nc, "c_" + nm, [128, 32], F32) for nm in ("DT", "MAGL", "TH", "MAG", "X", "SI", "CO", "AR", "AI", "CR", "CI", "T1", "T2", "RD")}
            tI = sb(s2, nc, "tI32", [128, TW], I32)
            tF = sb(s2, nc, "tF32", [128, TW], F32)
            ktmp = K()
            LR, LI, LDT = PRM[:, 0, :], PRM[:, 1, :], PRM[:, 2, :]
            kc_ = K()

            def o(eng, fn, r=(), w=()):
                fw.op(eng, fn, reads=[kP, kc_] + list(r), writes=[kc_] + list(w))
            o("act", lambda e: e.activation(out=cn["DT"][:], in_=LDT, func=AF.Exp))
            o("dve", lambda e: e.tensor_tensor(out=cn["MAGL"][:], in0=LR, in1=cn["DT"][:], op=ALU.mult))
            o("dve", lambda e: e.tensor_tensor(out=cn["TH"][:], in0=LI, in1=cn["DT"][:], op=ALU.mult))
            o("act", lambda e: e.activation(out=cn["MAG"][:], in_=cn["MAGL"][:], func=AF.Exp))
            o("dve", lambda e: e.tensor_copy(out=cn["X"][:], in_=cn["TH"][:]))
            trig_tables(c, cn["X"], kc_, cn["SI"][:], cn["CO"][:], kc_, kc_, tI, tF, ktmp, 32)
            o("dve", lambda e: e.tensor_tensor(out=cn["AR"][:], in0=cn["MAG"][:], in1=cn["CO"][:], op=ALU.mult))
            o("dve", lambda e: e.tensor_tensor(out=cn["AI"][:], in0=cn["MAG"][:], in1=cn["SI"][:], op=ALU.mult))
            o("dve", lambda e: e.tensor_tensor(out=cn["T1"][:], in0=LR, in1=LR, op=ALU.mult))
            o("dve", lambda e: e.tensor_tensor(out=cn["T2"][:], in0=LI, in1=LI, op=ALU.mult))
            o("dve", lambda e: e.tensor_tensor(out=cn["T1"][:], in0=cn["T1"][:], in1=cn["T2"][:], op=ALU.add))
            o("dve", lambda e: e.reciprocal(out=cn["RD"][:], in_=cn["T1"][:]))
            o("dve", lambda e: e.tensor_scalar(out=cn["T1"][:], in0=cn["AR"][:], scalar1=-1.0, scalar2=None, op0=ALU.add))
            o("dve", lambda e: e.tensor_tensor(out=cn["CR"][:], in0=cn["T1"][:], in1=LR, op=ALU.mult))
            o("dve", lambda e: e.tensor_tensor(out=cn["T2"][:], in0=cn["AI"][:], in1=LI, op=ALU.mult))
            o("dve", lambda e: e.tensor_tensor(out=cn["CR"][:], in0=cn["CR"][:], in1=cn["T2"][:], op=ALU.add))
            o("dve", lambda e: e.tensor_tensor(out=cn["CR"][:], in0=cn["CR"][:], in1=cn["RD"][:], op=ALU.mult))
            o("dve", lambda e: e.tensor_tensor(out=cn["CI"][:], in0=cn["AI"][:], in1=LR, op=ALU.mult))
            o("dve", lambda e: e.tensor_tensor(out=cn["T2"][:], in0=cn["T1"][:], in1=LI, op=ALU.mult))
            o("dve", lambda e: e.tensor_tensor(out=cn["CI"][:], in0=cn["CI"][:], in1=cn["T2"][:], op=ALU.subtract))
            o("dve", lambda e: e.tensor_tensor(out=cn["CI"][:], in0=cn["CI"][:], in1=cn["RD"][:], op=ALU.mult))
            IOT = sb(s2, nc, "IOT", [128, TW], F32)
            o("pool", lambda e: e.iota(IOT[:], [[1, TW]], base=1, channel_multiplier=0, allow_small_or_imprecise_dtypes=True))
            TB = [[sb(s2, nc, "TB%d_%d" % (k, j), [128, TW], F32) for j in range(4)] for k in range(4)]
            kTB = [[K() for _ in range(4)] for _ in range(4)]
            XA = sb(s2, nc, "XA", [128, TW], F32); kXA = K()
            Pt = [sb(s2, nc, "Pt%d" % i, [128, TW], F32) for i in range(6)]; kPt = [K() for _ in range(6)]
            Qr = sb(s2, nc, "Qr", [128, TW], F32); Qi = sb(s2, nc, "Qi", [128, TW], F32); kQ = [K(), K()]
            S16 = [sb(s2, nc, "S16_%d" % i, [128, TW], BF16) for i in range(2)]; kS16 = [K(), K()]
            BCm = sb(s2, nc, "BCm", [128, 4, 4, 128], BF16); kBC = K()
            SE = sb(s2, nc, "SE", [128, 32, 2], F32); kSE = K()
            SN = sb(s2, nc, "SN", [128, 2, NS], F32); kSN = K()
            OUT16 = [sb(s2, nc, "OUT16_%d" % i, [NS, 256], F32) for i in range(2)]; kO16 = [K(), K()]
            RHO = sb(s2, nc, "RHO", [128, TW], F32); kRHO = K()
            fw.op("pool", lambda e: e.memset(SE[:], 0.0), writes=[kSE])
            fw.op("pool", lambda e: e.memset(RHO[:], 1.0), writes=[kRHO])
            nb = 0
            for m in range(KC):
                fw.dma("pool", BCm[:].rearrange("p a b n -> p (a b) n"), d["s5bc"][m].rearrange("a p n -> p a n"), writes=[kBC])
                for k in range(4):
                    stn = 4 * m + k
                    col = slice(stn, stn + 1)
                    TR, TI, CC, SS = TB[k]
                    fw.op("dve", lambda e: e.tensor_scalar(out=XA[:], in0=IOT[:], scalar1=cn["TH"][:, col], scalar2=None, op0=ALU.mult),
                          reads=[kc_], writes=[kXA])
                    trig_tables(c, XA, kXA, SS[:], CC[:], kTB[k][3], kTB[k][2], tI, tF, ktmp, TW)
                    fw.op("dve", lambda e: e.tensor_scalar(out=TR[:], in0=CC[:], scalar1=cn["CR"][:, col], scalar2=None, op0=ALU.mult), reads=[kTB[k][2], kc_], writes=[kTB[k][0]])
                    fw.op("dve", lambda e: e.scalar_tensor_tensor(out=TR[:], in0=SS[:], scalar=cn["CI"][:, col], in1=TR[:], op0=ALU.mult, op1=ALU.add), reads=[kTB[k][3], kTB[k][0], kc_], writes=[kTB[k][0]])
                    fw.op("dve", lambda e: e.tensor_scalar(out=TI[:], in0=CC[:], scalar1=cn["CI"][:, col], scalar2=None, op0=ALU.mult), reads=[kTB[k][2], kc_], writes=[kTB[k][1]])
                    fw.op("dve", lambda e: e.tensor_scalar(out=XA[:], in0=SS[:], scalar1=cn["CR"][:, col], scalar2=None, op0=ALU.mult), reads=[kTB[k][3], kc_], writes=[kXA])
                    fw.op("dve", lambda e: e.tensor_tensor(out=TI[:], in0=TI[:], in1=XA[:], op=ALU.subtract), reads=[kTB[k][1], kXA], writes=[kTB[k][1]])
                for ti in range(NTL):
                    c0 = ti * TW
                    npr = min(TW, NP - c0)
                    yb = 6 + (ti % 2)
                    for k in range(4):
                        stn = 4 * m + k
                        col = slice(stn, stn + 1)
                        TR, TI, CC, SS = TB[k]
                        br, bi = 2 * (nb % 2), 2 * (nb % 2) + 1
                        nb += 1
                        fw.op("pe", lambda e: e.matmul(PS[:, br, 0:TW], BCm[:, k, 0, :], XB[:, m, c0:c0 + TW], start=True, stop=True), reads=[kBC, kXB[m]], writes=[kPS[br]])
                        fw.op("pe", lambda e: e.matmul(PS[:, bi, 0:TW], BCm[:, k, 1, :], XB[:, m, c0:c0 + TW], start=True, stop=True), reads=[kBC, kXB[m]], writes=[kPS[bi]])
                        w = slice(0, npr)
                        fw.op("pool", lambda e: e.tensor_scalar(out=RHO[:, w], in0=IOT[:, w], scalar1=0.0, scalar2=cn["MAG"][:, col], op0=ALU.mult, op1=ALU.add),
                              reads=[kc_], writes=[kRHO])
                        fw.op("dve", lambda e: e.tensor_tensor(out=Pt[0][:, w], in0=PS[:, br, w], in1=TR[:, w], op=ALU.mult), reads=[kPS[br], kTB[k][0]], writes=[kPt[0]])
                        fw.op("dve", lambda e: e.tensor_tensor(out=Pt[1][:, w], in0=PS[:, bi, w], in1=TI[:, w], op=ALU.mult), reads=[kPS[bi], kTB[k][1]], writes=[kPt[1]])
                        fw.op("dve", lambda e: e.tensor_tensor(out=Pt[2][:, w], in0=PS[:, bi, w], in1=TR[:, w], op=ALU.mult), reads=[kPS[bi], kTB[k][0]], writes=[kPt[2]])
                        fw.op("dve", lambda e: e.tensor_tensor(out=Pt[3][:, w], in0=PS[:, br, w], in1=TI[:, w], op=ALU.mult), reads=[kPS[br], kTB[k][1]], writes=[kPt[3]])
                        fw.op("pool", lambda e: e.tensor_tensor(out=Pt[0][:, w], in0=Pt[0][:, w], in1=Pt[1][:, w], op=ALU.subtract), reads=[kPt[0], kPt[1]], writes=[kPt[0]])
                        fw.op("pool", lambda e: e.tensor_tensor(out=Pt[2][:, w], in0=Pt[2][:, w], in1=Pt[3][:, w], op=ALU.add), reads=[kPt[2], kPt[3]], writes=[kPt[2]])
                        fw.op("dve", lambda e: e.tensor_tensor_scan(Qr[:, w], RHO[:, w], Pt[0][:, w], SE[:, stn, 0:1], ALU.mult, ALU.add), reads=[kRHO, kPt[0], kSE], writes=[kQ[0]])
                        fw.op("dve", lambda e: e.tensor_tensor_scan(Qi[:, w], RHO[:, w], Pt[2][:, w], SE[:, stn, 1:2], ALU.mult, ALU.add), reads=[kRHO, kPt[2], kSE], writes=[kQ[1]])
                        fw.op("pool", lambda e: e.tensor_tensor(out=Pt[4][:, w], in0=CC[:, w], in1=Qr[:, w], op=ALU.mult), reads=[kTB[k][2], kQ[0]], writes=[kPt[4]])
                        fw.op("pool", lambda e: e.tensor_tensor(out=Pt[5][:, w], in0=SS[:, w], in1=Qi[:, w], op=ALU.mult), reads=[kTB[k][3], kQ[1]], writes=[kPt[5]])
                        fw.op("dve", lambda e: e.tensor_tensor(out=S16[0][:, w], in0=Pt[4][:, w], in1=Pt[5][:, w], op=ALU.subtract), reads=[kPt[4], kPt[5]], writes=[kS16[0]])
                        fw.op("pool", lambda e: e.tensor_tensor(out=Pt[1][:, w], in0=SS[:, w], in1=Qr[:, w], op=ALU.mult), reads=[kTB[k][3], kQ[0], kPt[1]], writes=[kPt[1]])
                        fw.op("pool", lambda e: e.tensor_tensor(out=Pt[3][:, w], in0=CC[:, w], in1=Qi[:, w], op=ALU.mult), reads=[kTB[k][2], kQ[1], kPt[3]], writes=[kPt[3]])
                        fw.op("dve", lambda e: e.scalar_tensor_tensor(out=S16[1][:, w], in0=Pt[1][:, w], scalar=-1.0, in1=Pt[3][:, w], op0=ALU.mult, op1=ALU.subtract),
                              reads=[kPt[1], kPt[3]], writes=[kS16[1]])
                        L = npr - 1
                        fw.op("dve", lambda e: e.tensor_tensor(out=SE[:, stn, 0:1], in0=Pt[4][:, L:L + 1], in1=Pt[5][:, L:L + 1], op=ALU.subtract), reads=[kPt[4], kPt[5], kSE], writes=[kSE])
                        fw.op("dve", lambda e: e.tensor_tensor(out=SE[:, stn, 1:2], in0=Pt[1][:, L:L + 1], in1=Pt[3][:, L:L + 1], op=ALU.add), reads=[kPt[1], kPt[3], kSE], writes=[kSE])
                        if npr < TW:
                            ws = slice(npr, TW)
                            fw.op("dve", lambda e: e.tensor_scalar(out=Pt[0][:, ws], in0=PS[:, bi, ws], scalar1=cn["CI"][:, col], scalar2=None, op0=ALU.mult), reads=[kPS[bi], kc_, kPt[0]], writes=[kPt[0]])
                            fw.op("dve", lambda e: e.scalar_tensor_tensor(out=Pt[0][:, ws], in0=PS[:, br, ws], scalar=cn["CR"][:, col], in1=Pt[0][:, ws], op0=ALU.mult, op1=ALU.subtract), reads=[kPS[br], kPt[0], kc_], writes=[kPt[0]])
                            fw.op("dve", lambda e: e.tensor_scalar(out=Pt[2][:, ws], in0=PS[:, br, ws], scalar1=cn["CI"][:, col], scalar2=None, op0=ALU.mult), reads=[kPS[br], kc_, kPt[2]], writes=[kPt[2]])
                            fw.op("dve", lambda e: e.scalar_tensor_tensor(out=Pt[2][:, ws], in0=PS[:, bi, ws], scalar=cn["CR"][:, col], in1=Pt[2][:, ws], op0=ALU.mult, op1=ALU.add), reads=[kPS[bi], kPt[2], kc_], writes=[kPt[2]])
                            s0r, s0i = S0[:, stn, :, 0], S0[:, stn, :, 1]
                            fw.op("dve", lambda e: e.scalar_tensor_tensor(out=Pt[0][:, ws], in0=s0r, scalar=cn["AR"][:, col], in1=Pt[0][:, ws], op0=ALU.mult, op1=ALU.add), reads=[kS0, kPt[0], kc_], writes=[kPt[0]])
                            fw.op("dve", lambda e: e.tensor_scalar(out=Pt[1][:, ws], in0=s0i, scalar1=cn["AI"][:, col], scalar2=None, op0=ALU.mult), reads=[kS0, kc_, kPt[1]], writes=[kPt[1]])
                            fw.op("dve", lambda e: e.tensor_tensor(out=SN[:, 0, :], in0=Pt[0][:, ws], in1=Pt[1][:, ws], op=ALU.subtract), reads=[kPt[0], kPt[1]], writes=[kSN])
                            fw.op("dve", lambda e: e.scalar_tensor_tensor(out=Pt[2][:, ws], in0=s0i, scalar=cn["AR"][:, col], in1=Pt[2][:, ws], op0=ALU.mult, op1=ALU.add), reads=[kS0, kPt[2], kc_], writes=[kPt[2]])
                            fw.op("dve", lambda e: e.scalar_tensor_tensor(out=SN[:, 1, :], in0=s0r, scalar=cn["AI"][:, col], in1=Pt[2][:, ws], op0=ALU.mult, op1=ALU.add), reads=[kS0, kPt[2], kc_], writes=[kSN])
                            fw.op("act", lambda e: e.activation(out=S16[0][:, ws], in_=SN[:, 0, :], func=AF.Identity), reads=[kSN, kS16[0]], writes=[kS16[0]])
                            fw.op("act", lambda e: e.activation(out=S16[1][:, ws], in_=SN[:, 1, :], func=AF.Identity, scale=-1.0), reads=[kSN, kS16[1]], writes=[kS16[1]])
                            for r_ in range(2):
                                transpose_to(c, PS[0:NS, 5, r_ * 128:(r_ + 1) * 128], SN[:, r_, :], [kSN], [kPS[5]], IDF[:, :])
                            oi = stn % 2
                            fw.op("act", lambda e: e.activation(out=OUT16[oi][:, :].rearrange("b (p r) -> b r p", r=2),
                                                                in_=PS[0:NS, 5, 0:256].rearrange("b (r p) -> b r p", r=2), func=AF.Identity),
                                  reads=[kPS[5]], writes=[kO16[oi]])
                            fw.dma("sp", d["ss5"][:, stn * 256:(stn + 1) * 256], OUT16[oi][:, :], reads=[kO16[oi]])
                        fw.op("pe", lambda e: e.matmul(PS[:, yb, 0:TW], BCm[:, k, 2, :], S16[0][:], start=(k == 0), stop=False), reads=[kBC, kS16[0]], writes=[kPS[yb]])
                        fw.op("pe", lambda e: e.matmul(PS[:, yb, 0:TW], BCm[:, k, 3, :], S16[1][:], start=False, stop=(k == 3)), reads=[kBC, kS16[1]], writes=[kPS[yb]])
                    fw.op("dve", lambda e: e.scalar_tensor_tensor(out=XA[:], in0=XF[:, m, c0:c0 + TW], scalar=DSK[:, m:m + 1], in1=PS[:, yb, 0:TW], op0=ALU.mult, op1=ALU.add),
                          reads=[kXF[m][ti], kPS[yb], kP, kXA], writes=[kXA])
                    fw.op("act", lambda e: e.activation(out=ZG[:, m, c0:c0 + TW], in_=XA[:], func=AF.Gelu), reads=[kXA], writes=[kZG[m][ti]])
            fw.dma("sp", d["ps5"].rearrange("(t p) r -> p t r", p=128), SE[:], reads=[kSE])
            fw.barrier()
        with ExitStack() as s3:
            alloc_ln(c, s3)
            WGb = [[sb(s3, nc, "WG%d_%d" % (a, i), [128, KC, 256], BF16) for i in range(2)] for a in range(2)]
            kWG = [[K(), K()], [K(), K()]]
            sgl = [sb(s3, nc, "sgl%d" % i, [128, TW], F32) for i in range(2)]; ksgl = [K(), K()]
            wv = [d["w_glu_out"].rearrange("(kc p) n -> p kc n", p=128), d["w_glu_gate"].rearrange("(kc p) n -> p kc n", p=128)]
            nb = 0
            for mp in range(4):
                sl = mp % 2
                for a in range(2):
                    fw.dma("pool", WGb[a][sl][:], wv[a][:, :, mp * 256:(mp + 1) * 256], writes=[kWG[a][sl]])
                for mm in range(2):
                    m = 2 * mp + mm
                    for ti in range(NTL):
                        cs = slice(ti * TW, (ti + 1) * TW)
                        bo, bg = 2 * (nb % 2), 2 * (nb % 2) + 1
                        nb += 1
                        for a, bk in ((0, bo), (1, bg)):
                            for kc in range(KC):
                                fw.op("pe", lambda e, kc=kc: e.matmul(PS[:, bk, 0:TW], WGb[a][sl][:, kc, mm * 128:(mm + 1) * 128], ZG[:, kc, cs], start=(kc == 0), stop=(kc == KC - 1)),
                                      reads=[kWG[a][sl], kZG[kc][ti]], writes=[kPS[bk]])
                        ss = nb % 2
                        fw.op("act", lambda e: e.activation(out=sgl[ss][:], in_=PS[:, bg, 0:TW], func=AF.Sigmoid), reads=[kPS[bg]], writes=[ksgl[ss]])
                        fw.op("dve", lambda e: e.tensor_tensor(out=sgl[ss][:], in0=PS[:, bo, 0:TW], in1=sgl[ss][:], op=ALU.mult), reads=[kPS[bo], ksgl[ss]], writes=[ksgl[ss]])
                        fw.op("dve", lambda e: e.scalar_tensor_tensor(out=XF[:, m, cs], in0=XF[:, m, cs], scalar=ALPHA, in1=sgl[ss][:], op0=ALU.mult, op1=ALU.add),
                              reads=[ksgl[ss], kXF[m][ti]], writes=[kXF[m][ti]])
            for ti in range(NTL):
                layer_norm_tile(c, ti, 4)
            fw.barrier()


def sb_attention_sample(c, d, st, QS, kQS, OSB, kOSB):
    nc, fw = c.nc, c.fw
    PS, kPS = c.PS, c.kPS
    NE = NS * 16
    nrows = c.npool * 128
    QBC = sb(st, nc, "QBC", [128, NS, 512], F32); kQBC = K()
    fw.dma("sp", c.Qd, QS[0:NS, :], reads=[kQS], writes=[c.kQd])
    fw.dma("sp", QBC[:].rearrange("p b n -> p (b n)"), c.Qd.rearrange("b n -> (b n)").unsqueeze(0).to_broadcast([128, NS * 512]), reads=[c.kQd], writes=[kQBC])
    PTB = sb(st, nc, "PTB", [128, NE], I32); kPT = K()
    IDXF = sb(st, nc, "IDXF", [128, NE], F32)
    IOP = sb(st, nc, "IOP", [128, NE], F32)
    fw.dma("sp", PTB[:], d["pt"].to_broadcast([128, NE]), writes=[kPT])
    fw.op("pool", lambda e: e.iota(IOP[:], [[0, NE]], base=0, channel_multiplier=1, allow_small_or_imprecise_dtypes=True), writes=[kPT])
    fw.op("dve", lambda e: e.tensor_copy(out=IDXF[:], in_=PTB[:]), reads=[kPT], writes=[kPT])
    fw.op("dve", lambda e: e.scalar_tensor_tensor(out=IDXF[:], in0=IDXF[:], scalar=128.0, in1=IOP[:], op0=ALU.mult, op1=ALU.add), reads=[kPT], writes=[kPT])
    fw.op("dve", lambda e: e.tensor_copy(out=PTB[:], in_=IDXF[:]), reads=[kPT], writes=[kPT])
    NSL = 3
    IDc = [sb(st, nc, "IDc%d" % i, [128, 1], I32) for i in range(NSL)]; kID = [K() for _ in range(NSL)]
    PG = [sb(st, nc, "PG%d" % i, [128, 512], F32) for i in range(NSL)]; kPG = [K() for _ in range(NSL)]
    PR = [sb(st, nc, "PR%d" % i, [128, 512], F32) for i in range(2)]; kPR = [K(), K()]
    ZA = sb(st, nc, "ZA", [128, NE * 8], F32); kZA = K()
    EA = sb(st, nc, "EA", [128, NE * 8], F32); kEA = K()
    SPA = sb(st, nc, "SPA", [128, NE * 8], F32); kSPA = K()
    CIN = sb(st, nc, "CIN", [128, NE * 8], F32); kCIN = K()
    TOT = sb(st, nc, "TOT", [128, NE * 8], F32); kTOT = K()
    TRIF = sb(st, nc, "TRIF", [128, 128], F32); kTF = K()
    fw.op("pool", lambda e: e.affine_select(out=TRIF[:], in_=c.onesf[:], pattern=[[-1, 128]], base=0, channel_multiplier=1,
                                            compare_op=ALU.is_ge, fill=0.0), reads=[c.kconst], writes=[kTF])
    PRb = [sb(st, nc, "PRb%d" % i, [128, 512], BF16) for i in range(2)]; kPRb = [K(), K()]
    OH = sb(st, nc, "OH", [128, NS, NS], BF16); kOH = K()
    fw.op("pool", lambda e: e.memset(OH[:], 0.0), writes=[kOH])
    for b_ in range(NS):
        fw.op("pool", lambda e, b_=b_: e.memset(OH[:, b_, b_:b_ + 1], 1.0), reads=[kOH], writes=[kOH])
    n = 0
    for e_ in range(NE):
        b = e_ // 16
        sl = n % NSL
        n += 1
        fw.op("dve", lambda e: e.tensor_copy(out=IDc[sl][:], in_=PTB[:, e_:e_ + 1]), reads=[kPT], writes=[kID[sl]])
        fw.gather(PG[sl][:, :], d["ck"], IDc[sl][:, :], nrows, reads=[kID[sl]], writes=[kPG[sl]])
        ps = e_ % 2
        fw.op("dve", lambda e: e.tensor_tensor(out=PR[ps][:], in0=PG[sl][:], in1=QBC[:, b, :], op=ALU.mult), reads=[kPG[sl], kQBC], writes=[kPR[ps]])
        fw.op("dve", lambda e: e.tensor_reduce(out=ZA[:, e_ * 8:(e_ + 1) * 8], in_=PR[ps][:].rearrange("p (h d) -> p h d", h=8), axis=AX.X, op=ALU.add),
              reads=[kPR[ps]], writes=[kZA])
    fw.op("dve", lambda e: e.scalar_tensor_tensor(out=ZA[:].rearrange("p (e h) -> p e h", h=8), in0=ZA[:].rearrange("p (e h) -> p e h", h=8), scalar=0.125,
                                                  in1=c.sbb[:, :].unsqueeze(1).to_broadcast([128, NE, 8]), op0=ALU.mult, op1=ALU.add),
          reads=[kZA, c.kconst], writes=[kZA])
    fw.op("act", lambda e: e.activation(out=EA[:], in_=ZA[:], func=AF.Exp), reads=[kZA], writes=[kEA])
    fw.op("act", lambda e: e.activation(out=SPA[:], in_=EA[:], func=AF.Ln, bias=c.cst[:, 1:2]), reads=[kEA, c.kconst], writes=[kSPA])
    for q4 in range(4):
        fw.op("pe", lambda e: e.matmul(PS[:, q4, :], TRIF[:], SPA[:, q4 * 512:(q4 + 1) * 512], start=True, stop=True), reads=[kTF, kSPA], writes=[kPS[q4]])
        fw.op("pe", lambda e: e.matmul(PS[:, 4 + q4, :], c.onesf[:], SPA[:, q4 * 512:(q4 + 1) * 512], start=True, stop=True), reads=[c.kconst, kSPA], writes=[kPS[4 + q4]])
        fw.op("act", lambda e: e.activation(out=CIN[:, q4 * 512:(q4 + 1) * 512], in_=PS[:, q4, :], func=AF.Identity), reads=[kPS[q4]], writes=[kCIN])
        fw.op("act", lambda e: e.activation(out=TOT[:, q4 * 512:(q4 + 1) * 512], in_=PS[:, 4 + q4, :], func=AF.Identity), reads=[kPS[4 + q4]], writes=[kTOT])
    T4 = TOT[:].rearrange("p (b q h) -> p b q h", b=NS, q=16)
    C4 = CIN[:].rearrange("p (b q h) -> p b q h", b=NS, q=16)
    RUN = sb(st, nc, "RUN", [128, NS, 8], F32); kRUN = K()
    fw.op("pool", lambda e: e.memset(RUN[:], 0.0), writes=[kRUN])
    for p_ in range(14, -1, -1):
        fw.op("dve", lambda e: e.tensor_tensor(out=RUN[:], in0=RUN[:], in1=T4[:, :, p_ + 1, :], op=ALU.add), reads=[kRUN, kTOT], writes=[kRUN])
        fw.op("dve", lambda e: e.tensor_tensor(out=C4[:, :, p_, :], in0=C4[:, :, p_, :], in1=RUN[:], op=ALU.add), reads=[kRUN, kCIN], writes=[kCIN])
    fw.op("act", lambda e: e.activation(out=CIN[:], in_=CIN[:], func=AF.Exp, scale=-1.0), reads=[kCIN], writes=[kCIN])
    fw.op("dve", lambda e: e.tensor_tensor(out=EA[:], in0=EA[:], in1=CIN[:], op=ALU.mult), reads=[kEA, kCIN], writes=[kEA])
    for e_ in range(NE):
        b, p_ = e_ // 16, e_ % 16
        sl = n % NSL
        n += 1
        fw.op("dve", lambda e: e.tensor_copy(out=IDc[sl][:], in_=PTB[:, e_:e_ + 1]), reads=[kPT], writes=[kID[sl]])
        fw.gather(PG[sl][:, :], d["cv"], IDc[sl][:, :], nrows, reads=[kID[sl]], writes=[kPG[sl]])
        ps = e_ % 2
        fw.op("dve", lambda e: e.tensor_tensor(out=PRb[ps][:].rearrange("p (h d) -> p h d", h=8), in0=PG[sl][:].rearrange("p (h d) -> p h d", h=8),
                                               in1=EA[:, e_ * 8:(e_ + 1) * 8].unsqueeze(2).to_broadcast([128, 8, 64]), op=ALU.mult),
              reads=[kPG[sl], kEA], writes=[kPRb[ps]])
        fw.op("pe", lambda e: e.matmul(PS[0:NS, 4, :], OH[:, b, :], PRb[ps][:], start=(e_ == 0), stop=(e_ == NE - 1)),
              reads=[kPRb[ps], kOH], writes=[kPS[4]])
    OT = sb(st, nc, "OTs", [NS, 512], F32); kOT = K()
    fw.op("act", lambda e: e.activation(out=OT[:], in_=PS[0:NS, 4, :], func=AF.Identity), reads=[kPS[4]], writes=[kOT])
    for c4 in range(4):
        transpose_to(c, PS[:, 5, c4 * NS:(c4 + 1) * NS], OT[0:NS, c4 * 128:(c4 + 1) * 128], [kOT], [kPS[5]], c.IDF[0:NS, 0:NS])
    fw.op("act", lambda e: e.activation(out=OSB[:, :, NP:NT], in_=PS[:, 5, 0:4 * NS].rearrange("p (c b) -> p c b", c=4), func=AF.Identity),
          reads=[kPS[5]], writes=[kOSB[c4][4] for c4 in range(4)])

def build(stage=99, dbg=False, npool=2560):
    nc = bass.Bass("TRN2", target_bir_lowering=False)
    c = Ctx()
    c.nc = nc

    DECL.clear()

    def din(name, shape, dt=F32):
        DECL.append(name)
        return nc.dram_tensor(name, list(shape), dt, kind="ExternalInput").ap()

    def dout(name, shape, dt=F32):
        return nc.dram_tensor(name, list(shape), dt, kind="ExternalOutput").ap()

    xT = din("xT", [D, NT])
    lngT = din("lngT", [128, 6 * KC])
    lnbT = din("lnbT", [128, 6 * KC])
    fw_ = {}
    for nm in ("ffn1_wg", "ffn1_wu", "ffn2_wg", "ffn2_wu"):
        fw_[nm] = din(nm, [2, D, DFF])
    for nm in ("ffn1_wd", "ffn2_wd"):
        fw_[nm] = din(nm, [2, DFF, D])
    yT = dout("yT", [D, NT])
    d = {}
    c.dbg = dbg
    if stage >= 2:
        d["w_in"] = din("w_in", [D, INC])
        d["sbb"] = din("sbb", [128, 8])
        d["pk"] = dout("pk", [NP, 512]); d["pv"] = dout("pv", [NP, 512])
        d["sk"] = dout("sk", [NS, 512]); d["sv"] = dout("sv", [NS, 512])
        if dbg:
            d["dbg_osb"] = dout("dbg_osb", [128, 4, NT], BF16)
    c.npool = npool
    if stage >= 4:
        d["ck"] = din("ck", [npool * 128, 512]); d["cv"] = din("cv", [npool * 128, 512])
        d["pt"] = din("pt", [1, NS * 16], I32)
        c.Qd = nc.dram_tensor("Qd", [NS, 512], F32, kind="Internal").ap(); c.kQd = K()
    if stage >= 5:
        c.TOKd = nc.dram_tensor("TOKd", [TW, 3, 512], BF16, kind="Internal").ap()
        c.Yd = nc.dram_tensor("Yd", [4, 2, TW, 64], BF16, kind="Internal").ap()
        c.kTOKd = K(); c.kYd = K()
        d["w_w2"] = din("w_w2", [64, 512]); d["w_a2"] = din("w_a2", [64, 512]); d["w_g2"] = din("w_g2", [128, 512])
        d["pcol"] = din("pcol", [128, 48]); d["gng"] = din("gng", [128, 512]); d["gnb"] = din("gnb", [128, 512])
        d["sshift0"] = din("sshift0", [NS, RWC]); d["swkv0"] = din("swkv0", [NS, 8, 64, 64])
        d["pwkv"] = dout("pwkv", [8, 64, 64]); d["swkv"] = dout("swkv", [NS, 8, 64, 64])
        d["pshift"] = dout("pshift", [1, RWC]); d["sshift"] = dout("sshift", [NS, RWC])
        d["w_out"] = din("w_out", [D, D])
    if stage >= 8:
        d["s5prm"] = din("s5prm", [128, 3, 32]); d["dskip"] = din("dskip", [128, 8])
        d["s5_0"] = din("s5_0", [NS, 4096, 2]); d["s5bc"] = din("s5bc", [8, 16, 128, 128])
        d["w_glu_out"] = din("w_glu_out", [D, D]); d["w_glu_gate"] = din("w_glu_gate", [D, D])
        d["ps5"] = dout("ps5", [4096, 2]); d["ss5"] = dout("ss5", [NS, 8192])

    with ExitStack() as st:
        fw = FW(nc, st)
        c.fw = fw
        c.XF = sb(st, nc, "XF", [128, KC, NT], F32)
        c.kXF = [[K() for _ in range(NTL)] for _ in range(KC)]
        c.PS = st.enter_context(nc.psum_tensor("PS", [128, 8, 512], F32))
        c.kPS = [K() for _ in range(8)]
        c.onesf = sb(st, nc, "onesf", [128, 128], F32)
        c.epsc = sb(st, nc, "epsc", [128, 1], F32)
        c.lng = sb(st, nc, "lng", [128, 6 * KC], F32)
        c.lnb = sb(st, nc, "lnb", [128, 6 * KC], F32)
        c.kconst = K()
        c.nsq = 0
        c.nln = 0
        fw.op("pool", lambda e: e.memset(c.onesf[:], 1.0), writes=[c.kconst])
        fw.op("pool", lambda e: e.memset(c.epsc[:], LN_EPS), writes=[c.kconst])
        fw.dma("sp", c.lng[:], lngT, writes=[c.kconst])
        fw.dma("sp", c.lnb[:], lnbT, writes=[c.kconst])
        xv = xT.rearrange("(kc p) t -> p kc t", p=128)
        for kc in range(KC):
            fw.dma("sp", c.XF[:, kc, :], xv[:, kc, :], writes=c.kXF[kc])

        if not SKIP_FFN:
            ffn_ln(c, fw_["ffn1_wg"][0], fw_["ffn1_wu"][0], fw_["ffn1_wd"][0], 0, "a")
        fw.barrier()
        if stage >= 2:
            c.cst = sb(st, nc, "cst", [128, 4], F32)
            c.sbb = sb(st, nc, "sbb_s", [128, 8], F32)
            fw.op("pool", lambda e: e.memset(c.cst[:, 0:1], LN_EPS), writes=[c.kconst])
            fw.op("pool", lambda e: e.memset(c.cst[:, 1:2], 1.0), writes=[c.kconst])
            fw.op("pool", lambda e: e.memset(c.cst[:, 2:3], GN_EPS), writes=[c.kconst])
            fw.op("pool", lambda e: e.memset(c.cst[:, 3:4], 1.5707963267948966), writes=[c.kconst])
            fw.dma("sp", c.sbb[:], d["sbb"], writes=[c.kconst])
            onesb = sb(st, nc, "onesb", [128, 512], BF16)
            fw.op("pool", lambda e: e.memset(onesb[:], 1.0), writes=[c.kconst])
            c.ONESB = onesb[:, 0:128]
            c.onesb = onesb
            c.IDF = sb(st, nc, "IDF", [128, 128], F32)
            fw.op("pool", lambda e: e.memset(c.IDF[:], 1.0), writes=[c.kconst])
            fw.op("pool", lambda e: e.affine_select(out=c.IDF[:], in_=c.IDF[:], pattern=[[-1, 128]], base=0, channel_multiplier=1,
                                                    compare_op=ALU.is_equal, fill=0.0), reads=[c.kconst], writes=[c.kconst])
            mixer_even(c, d, stage)
        if stage >= 7 and not SKIP_FFN:
            ffn_ln(c, fw_["ffn2_wg"][0], fw_["ffn2_wu"][0], fw_["ffn2_wd"][0], 2, "b")
            fw.barrier()
            ffn_ln(c, fw_["ffn1_wg"][1], fw_["ffn1_wu"][1], fw_["ffn1_wd"][1], 3, "c")
            fw.barrier()
        if stage >= 8:
            s5_mixer(c, d)
        if stage >= 9 and not SKIP_FFN:
            ffn_ln(c, fw_["ffn2_wg"][1], fw_["ffn2_wu"][1], fw_["ffn2_wd"][1], 5, "d")
            fw.barrier()

        yv = yT.rearrange("(kc p) t -> p kc t", p=128)
        for kc in range(KC):
            fw.dma("sp", yv[:, kc, :], c.XF[:, kc, :], reads=c.kXF[kc])
        fw.finish()
        print("instructions:", fw.ninst, {e: fw.cnt[e] for e in fw.cnt})
    return nc


DECL = []


def make_in_maps(inp, n_cores=8):
    f = np.float32
    maps = []
    ln_g = np.ascontiguousarray(np.asarray(inp["ln_g"], f).reshape(6, KC, 128).transpose(2, 0, 1).reshape(128, 6 * KC))
    ln_b = np.ascontiguousarray(np.asarray(inp["ln_b"], f).reshape(6, KC, 128).transpose(2, 0, 1).reshape(128, 6 * KC))
    shared = {"lngT": ln_g, "lnbT": ln_b}
    for nm in ("ffn1_wg", "ffn1_wu", "ffn1_wd", "ffn2_wg", "ffn2_wu", "ffn2_wd"):
        shared[nm] = np.asarray(inp[nm], f)
    shared["w_in"] = np.asarray(inp["w_in_even"][0], f)
    for nm in ("w_w2", "w_a2", "w_g2"):
        shared[nm] = np.asarray(inp[nm][0], f)
    shared["w_out"] = np.asarray(inp["w_out_even"][0], f)
    def colT(v, n):
        return np.asarray(v, f).reshape(n, 128).T
    pc = np.zeros((128, 48), f)
    pc[:, 0:14] = colT(inp["mu_shift"][0], 14)
    pc[:, 14:18] = colT(inp["w0"][0], 4); pc[:, 18:22] = colT(inp["a0"][0], 4)
    pc[:, 22:26] = colT(inp["k_k"][0], 4); pc[:, 26:30] = colT(inp["k_a"][0], 4)
    pc[:, 30:34] = colT(inp["r_k"][0].reshape(-1), 4)
    shared["pcol"] = pc
    shared["gng"] = np.ascontiguousarray(np.broadcast_to(np.asarray(inp["gn_g"][0], f)[None, :], (128, 512)))
    shared["gnb"] = np.ascontiguousarray(np.broadcast_to(np.asarray(inp["gn_b"][0], f)[None, :], (128, 512)))
    lre = np.asarray(inp["lam_re"][0], f); lim = np.asarray(inp["lam_im"][0], f); ldt = np.asarray(inp["log_dt"][0], f)
    prm = np.zeros((128, 3, 32), f)
    prm[:, 0, :] = lre.reshape(32, 128).T; prm[:, 1, :] = lim.reshape(32, 128).T
    prm[:, 2, :] = np.repeat(ldt, 64).reshape(32, 128).T
    shared["s5prm"] = prm
    shared["dskip"] = np.ascontiguousarray(np.asarray(inp["d_skip"][0], f).reshape(8, 128).T)
    bre = np.asarray(inp["b_re"][0], f); bim = np.asarray(inp["b_im"][0], f)
    cre = np.asarray(inp["c_re"][0], f); cim = np.asarray(inp["c_im"][0], f)
    bc = np.zeros((8, 4, 4, 128, 128), f)
    for g in range(64):
        m_, gl = g // 8, g % 8
        k_, g2 = (g % 8) // 2, g % 2
        rows = slice(gl * 16, gl * 16 + 16); cols = slice(g2 * 64, g2 * 64 + 64)
        bc[m_, k_, 0][rows, cols] = bre[g].T
        bc[m_, k_, 1][rows, cols] = bim[g].T
        bc[m_, k_, 2][cols, rows] = cre[g].T
        bc[m_, k_, 3][cols, rows] = cim[g].T
    shared["s5bc"] = bc.reshape(8, 16, 128, 128)
    shared["w_glu_out"] = np.asarray(inp["w_glu_out"][0], f); shared["w_glu_gate"] = np.asarray(inp["w_glu_gate"][0], f)
    shared["sbb"] = np.ascontiguousarray(np.broadcast_to(np.asarray(inp["sb_bias"][0], f)[None, :], (128, 8)))
    for cidx in range(n_cores):
        m = dict(shared)
        xp = np.asarray(inp["x_prompt"][cidx], f)
        xs = np.asarray(inp["x_sample"][cidx * NS:(cidx + 1) * NS, 0], f)
        m["xT"] = np.ascontiguousarray(np.concatenate([xp, xs], axis=0).T)
        sl = slice(cidx * NS, (cidx + 1) * NS)
        m["sshift0"] = np.asarray(inp["state_shift"][0, sl], f)
        m["swkv0"] = np.asarray(inp["state_wkv"][0, sl], f)
        m["pt"] = np.ascontiguousarray(np.asarray(inp["page_table"][sl], np.int32).reshape(1, NS * 16))
        m["ck"] = np.asarray(inp["cache_k_sb"][0], f).reshape(-1, 512)
        m["cv"] = np.asarray(inp["cache_v_sb"][0], f).reshape(-1, 512)
        m["s5_0"] = np.asarray(inp["state_s5"][0, sl], f).reshape(NS, 4096, 2)
        maps.append({k: v for k, v in m.items() if k in DECL})
    return maps


def dev_compare(stage, r, ref, cmp):
    if stage >= 2:
        pp = ref["p_proj"][0]; sp_ = ref["s_proj"][:, 0]
        cmp("pk", r["pk"], pp[:, 512:1024]); cmp("pv", r["pv"], pp[:, 1024:1536])
        cmp("sk", r["sk"], sp_[:, 512:1024]); cmp("sv", r["sv"], sp_[:, 1024:1536])
    if stage >= 4 and "dbg_osb" in r:
        import ml_dtypes
        x_ = r["dbg_osb"]
        if x_.dtype.kind == "V":
            x_ = x_.view(ml_dtypes.bfloat16)
        o = np.asarray(x_).astype(np.float32).transpose(1, 0, 2).reshape(512, NT).T
        cmp("osb_s", o[NP:], ref["s_osb"][:, 0])
    if stage >= 3 and "dbg_osb" in r and False:
        o = np.asarray(r["dbg_osb"]).astype(np.float32).transpose(1, 0, 2).reshape(512, NT).T
        cmp("osb_p", o[:NP], ref["p_osb"][0])
        for qq in range(4):
            cmp("osb_p q%d" % qq, o[qq*512:(qq+1)*512], ref["p_osb"][0][qq*512:(qq+1)*512])
    if stage >= 5:
        cmp("pwkv", r["pwkv"], ref["p_wkv"][0]); cmp("swkv", r["swkv"], ref["s_wkv"])
        cmp("pshift", r["pshift"][0], ref["p_proj"][0, -1, 1536:]); cmp("sshift", r["sshift"], ref["s_proj"][:, 0, 1536:])
    if stage >= 8:
        cmp("ps5", r["ps5"].reshape(64, 64, 2), ref["p_s5"][0]); cmp("ss5", r["ss5"].reshape(NS, 64, 64, 2), ref["s_s5"])
    y = r["yT"].T
    key = {1: "L0_x1", 2: "L0_x1", 3: "L0_x1", 4: "L0_x1", 5: "L0_x1", 6: "L0_x2", 7: "L1_x1", 8: "L1_x2", 9: "L1_x3"}.get(stage, "L1_x3")
    cmp("y_prompt", y[:NP], ref["p_" + key][0])
    cmp("y_sample", y[NP:], ref["s_" + key][:, 0])


def kernel(**inputs):
    n = 8
    npool = int(np.asarray(inputs["cache_k_sb"]).shape[1])
    nc = build(stage=9, dbg=False, npool=npool)
    maps = make_in_maps(inputs, n_cores=n)
    res = run_bass_kernel_spmd(nc, maps, core_ids=list(range(n)))
    R = res.results
    f = np.float32
    yp = np.stack([np.asarray(R[i]["yT"], f).T[:NP] for i in range(n)], axis=0)
    ys = np.concatenate([np.asarray(R[i]["yT"], f).T[NP:] for i in range(n)], axis=0)[:, None, :]
    pk = np.stack([np.asarray(R[i]["pk"], f).reshape(NP, 8, 64) for i in range(n)], axis=0)[None]
    pv = np.stack([np.asarray(R[i]["pv"], f).reshape(NP, 8, 64) for i in range(n)], axis=0)[None]
    pwkv = np.stack([np.asarray(R[i]["pwkv"], f) for i in range(n)], axis=0)[None]
    pshift = np.stack([np.asarray(R[i]["pshift"], f).reshape(RWC) for i in range(n)], axis=0)[None]
    ps5 = np.stack([np.asarray(R[i]["ps5"], f).reshape(64, 64, 2) for i in range(n)], axis=0)[None]
    sk = np.concatenate([np.asarray(R[i]["sk"], f).reshape(NS, 1, 8, 64) for i in range(n)], axis=0)[None]
    sv = np.concatenate([np.asarray(R[i]["sv"], f).reshape(NS, 1, 8, 64) for i in range(n)], axis=0)[None]
    swkv = np.concatenate([np.asarray(R[i]["swkv"], f) for i in range(n)], axis=0)[None]
    sshift = np.concatenate([np.asarray(R[i]["sshift"], f) for i in range(n)], axis=0)[None]
    ss5 = np.concatenate([np.asarray(R[i]["ss5"], f).reshape(NS, 64, 64, 2) for i in range(n)], axis=0)[None]
    return (yp, ys, pk, pv, pwkv, pshift, ps5, sk, sv, swkv, sshift, ss5)
```

```python
from contextlib import ExitStack
import numpy as np
import concourse.bass as bass
import concourse.mybir as mybir
from concourse.bass_utils import run_bass_kernel_spmd

F32 = mybir.dt.float32
BF16 = mybir.dt.bfloat16
I32 = mybir.dt.int32
AF = mybir.ActivationFunctionType
ALU = mybir.AluOpType
AX = mybir.AxisListType

D = 1024
KC = 8
DFF = 2816
JC = 22
NP = 2048
NS = 16
NT = NP + NS
TW = 344
NTL = NT // TW
NG = 3
GW = NT // NG
ALPHA = 4.0 ** 0.25
LN_EPS = 1e-5
GN_EPS = 64e-5
INC = 3328
RWC = 1792
SKIP_FFN = False
import os
VAR = os.environ.get('KVAR', '')


class K:
    __slots__ = ("name", "lw", "rd")

    def __init__(self, name=""):
        self.name = name
        self.lw = None
        self.rd = []


class FW:
    def __init__(self, nc, stack, n_dma_sems=8):
        self.nc = nc
        self.eng = {"pe": nc.tensor, "dve": nc.vector, "act": nc.scalar,
                    "pool": nc.gpsimd, "sp": nc.sync}
        self.sem = {}
        self.cnt = {}
        self.seen = {e: {} for e in self.eng}
        for e in self.eng:
            self.sem[e] = stack.enter_context(nc.semaphore("s_" + e))
            self.cnt[e] = 0
        self.dsem = {}
        self.dcnt = {}
        self.dnext = {}
        for q in ("sp", "pool"):
            self.dsem[q] = [stack.enter_context(nc.semaphore("d_%s_%d" % (q, i)))
                            for i in range(n_dma_sems)]
            self.dcnt[q] = [0] * n_dma_sems
            self.dnext[q] = 0
        self.ninst = 0
        self.dram_writes = []
        self.psum_keys = set()
        self.pe_defer = False
        self._pend_r = []
        self._pend_w = []

    def _semobj(self, sk):
        if isinstance(sk, tuple):
            return self.dsem[sk[0]][sk[1]]
        return self.sem[sk]

    def _wait(self, e, sk, val):
        if sk == "pe" and e == "pe":
            return
        if self.seen[e].get(sk, 0) >= val:
            return
        self.seen[e][sk] = val
        self.eng[e].wait_ge(self._semobj(sk), val)
        self.ninst += 1

    def _deps(self, e, reads, writes):
        need = {}
        for k in reads:
            if k.lw is not None and need.get(k.lw[0], 0) < k.lw[1]:
                need[k.lw[0]] = k.lw[1]
        for k in writes:
            if k.lw is not None and need.get(k.lw[0], 0) < k.lw[1]:
                need[k.lw[0]] = k.lw[1]
            for (sk, v) in k.rd:
                if need.get(sk, 0) < v:
                    need[sk] = v
        for sk, v in need.items():
            self._wait(e, sk, v)

    def _mark(self, sk, val, reads, writes):
        for k in reads:
            k.rd.append((sk, val))
            if len(k.rd) > 16:
                m = {}
                for (s, v) in k.rd:
                    if m.get(s, 0) < v:
                        m[s] = v
                k.rd = list(m.items())
        for k in writes:
            k.lw = (sk, val)
            k.rd = []

    def op(self, e, fn, reads=(), writes=()):
        if e == "pe" and self.pe_defer:
            self._deps(e, reads, writes)
            fn(self.eng[e])
            self._pend_r.extend(reads)
            self._pend_w.extend(writes)
            self.ninst += 1
            return None
        if e == "pe" and self._pend_r:
            reads = list(reads) + self._pend_r
            writes = list(dict.fromkeys(list(writes) + self._pend_w))
            self._pend_r = []
            self._pend_w = []
        if e == "dve" and self.dram_writes and any(id(k) in self.psum_keys for k in reads):
            for (sk, v) in self.dram_writes:
                self._wait(e, sk, v)
            self.dram_writes = []
        self._deps(e, reads, writes)
        ins = fn(self.eng[e])
        self.cnt[e] += 1
        ins.then_inc(self.sem[e], 1)
        self._mark(e, self.cnt[e], reads, writes)
        self.ninst += 1
        return ins

    def _dslot(self, q):
        i = self.dnext[q]
        self.dnext[q] = (i + 1) % len(self.dsem[q])
        if self.dcnt[q][i] > 0:
            self._wait(q, (q, i), self.dcnt[q][i])
        return i

    def dma(self, q, out, in_, reads=(), writes=(), **kw):
        i = self._dslot(q)
        self._deps(q, reads, writes)
        ins = self.eng[q].dma_start(out=out, in_=in_, **kw)
        self.dcnt[q][i] += 16
        ins.then_inc(self.dsem[q][i], 16)
        self._mark((q, i), self.dcnt[q][i], reads, writes)
        self.ninst += 1
        if "DRAM" in str(getattr(out.tensor, "space", "")).upper() or type(out.tensor).__name__.startswith("DRam"):
            self.dram_writes.append(((q, i), self.dcnt[q][i]))
        return ins

    def gather(self, out, in_rows, idx_ap, nrows, reads=(), writes=()):
        q = "pool"
        i = self._dslot(q)
        self._deps(q, reads, writes)
        if getattr(self, "_breg", None) is None or self._breg[0] != nrows:
            self._breg = (nrows, self.nc.gpsimd.to_reg(nrows - 1))
        ins = self.nc.gpsimd.indirect_dma_start(
            out=out, out_offset=None, in_=in_rows,
            in_offset=bass.IndirectOffsetOnAxis(ap=idx_ap, axis=0),
            bounds_check=self._breg[1], oob_is_err=False)
        self.dcnt[q][i] += 16
        ins.then_inc(self.dsem[q][i], 16)
        self._mark((q, i), self.dcnt[q][i], reads, writes)
        self.ninst += 1
        return ins

    def barrier(self):
        for e in self.eng:
            for q in self.dsem:
                for i, v in enumerate(self.dcnt[q]):
                    if v > 0:
                        self._wait(e, (q, i), v)
            for e2 in self.eng:
                if self.cnt[e2] > 0 and not (e2 == e and e == "pe"):
                    self._wait(e, e2, self.cnt[e2])

    def finish(self):
        for q in self.dsem:
            for i, v in enumerate(self.dcnt[q]):
                if v > 0:
                    self._wait("sp", (q, i), v)
        for e in self.eng:
            if e != "sp" and self.cnt[e] > 0:
                self._wait("sp", e, self.cnt[e])


class Ctx:
    pass


def sb(st, nc, name, shape, dt):
    return st.enter_context(nc.sbuf_tensor(name, list(shape), dt))


def ffn_ln(c, wg, wu, wd, lnidx, tag):
    nc, fw = c.nc, c.fw
    XF, kXF = c.XF, c.kXF
    with ExitStack() as st:
        XB = sb(st, nc, "XB" + tag, [128, KC, GW], BF16)
        H = sb(st, nc, "H" + tag, [128, JC, GW], BF16)
        wgb = [sb(st, nc, "wgb%d%s" % (i, tag), [128, KC, 256], BF16) for i in range(2)]
        wub = [sb(st, nc, "wub%d%s" % (i, tag), [128, KC, 256], BF16) for i in range(2)]
        wdb = [sb(st, nc, "wdb%d%s" % (i, tag), [128, JC, 256], BF16) for i in range(2)]
        sgt = [sb(st, nc, "sgt%d%s" % (i, tag), [128, TW], F32) for i in range(2)]
        alloc_ln(c, st)
        kXB = [K() for _ in range(KC)]
        kH = [[K() for _ in range(2)] for _ in range(JC)]
        kwg = [K(), K()]
        kwu = [K(), K()]
        kwd = [K(), K()]
        ksg = [K(), K()]
        wgv = wg.rearrange("(kc p) n -> p kc n", p=128)
        wuv = wu.rearrange("(kc p) n -> p kc n", p=128)
        wdv = wd.rearrange("(j p) n -> p j n", p=128)
        PS, kPS = c.PS, c.kPS
        nsg = 0
        nb = 0
        for g in range(NG):
            c0 = g * GW
            for kc in range(KC):
                fw.op("act", lambda e, kc=kc: e.activation(out=XB[:, kc, :], in_=XF[:, kc, c0:c0 + GW], func=AF.Identity),
                      reads=[kXF[kc][2 * g], kXF[kc][2 * g + 1]], writes=[kXB[kc]])
            for jp in range(JC // 2):
                s = jp % 2
                fw.dma("pool", wgb[s][:], wgv[:, :, jp * 256:(jp + 1) * 256], writes=[kwg[s]])
                fw.dma("pool", wub[s][:], wuv[:, :, jp * 256:(jp + 1) * 256], writes=[kwu[s]])
                for jj in range(2):
                    j = 2 * jp + jj
                    for tl in range(2):
                        bg, bu = 2 * (nb % 2), 2 * (nb % 2) + 1
                        nb += 1
                        cs = slice(tl * TW, (tl + 1) * TW)
                        for kc in range(KC):
                            fw.pe_defer = (kc != KC - 1)
                            fw.op("pe", lambda e, kc=kc: e.matmul(PS[:, bg, 0:TW], wgb[s][:, kc, jj * 128:(jj + 1) * 128],
                                                                  XB[:, kc, cs], start=(kc == 0), stop=(kc == KC - 1)),
                                  reads=[kwg[s], kXB[kc]], writes=[kPS[bg]])
                        for kc in range(KC):
                            fw.pe_defer = (kc != KC - 1)
                            fw.op("pe", lambda e, kc=kc: e.matmul(PS[:, bu, 0:TW], wub[s][:, kc, jj * 128:(jj + 1) * 128],
                                                                  XB[:, kc, cs], start=(kc == 0), stop=(kc == KC - 1)),
                                  reads=[kwu[s], kXB[kc]], writes=[kPS[bu]])
                        ss = nsg % 2
                        nsg += 1
                        fw.op("act", lambda e: e.activation(out=sgt[ss][:], in_=PS[:, bg, 0:TW], func=AF.Silu),
                              reads=[kPS[bg]], writes=[ksg[ss]])
                        fw.op("dve", lambda e: e.scalar_tensor_tensor(out=H[:, j, cs], in0=PS[:, bu, 0:TW], scalar=0.5,
                                                                      in1=sgt[ss][:], op0=ALU.mult, op1=ALU.mult),
                              reads=[kPS[bu], ksg[ss]], writes=[kH[j][tl]])
            for mp in range(KC // 2):
                s = mp % 2
                fw.dma("pool", wdb[s][:], wdv[:, :, mp * 256:(mp + 1) * 256], writes=[kwd[s]])
                for mm in range(2):
                    m = 2 * mp + mm
                    for tl in range(2):
                        by = 4 + (nb % 2)
                        nb += 1
                        cs = slice(tl * TW, (tl + 1) * TW)
                        gc = slice(c0 + tl * TW, c0 + (tl + 1) * TW)
                        for j in range(JC):
                            fw.pe_defer = (j != JC - 1)
                            fw.op("pe", lambda e, j=j: e.matmul(PS[:, by, 0:TW], wdb[s][:, j, mm * 128:(mm + 1) * 128],
                                                                H[:, j, cs], start=(j == 0), stop=(j == JC - 1)),
                                  reads=[kwd[s], kH[j][tl]], writes=[kPS[by]])
                        fw.op("dve", lambda e: e.scalar_tensor_tensor(out=XF[:, m, gc], in0=XF[:, m, gc], scalar=ALPHA,
                                                                      in1=PS[:, by, 0:TW], op0=ALU.mult, op1=ALU.add),
                              reads=[kPS[by], kXF[m][2 * g + tl]], writes=[kXF[m][2 * g + tl]])
            for tl in range(2):
                layer_norm_tile(c, 2 * g + tl, lnidx)


def alloc_ln(c, st):
    c.nln += 1
    c.sq = [sb(st, c.nc, "sq%d_%d" % (i, c.nln), [128, TW], F32) for i in range(2)]
    c.ksq = [K(), K()]
    c.sqb = [sb(st, c.nc, "sqb%d_%d" % (i, c.nln), [128, TW], BF16) for i in range(2)]
    c.ksqb = [K(), K()]
    c.xbb = [sb(st, c.nc, "xbb%d_%d" % (i, c.nln), [128, TW], BF16) for i in range(2)]
    c.kxbb = [K(), K()]
    c.lnt = [sb(st, c.nc, "lnt%d_%d" % (i, c.nln), [128, TW], F32) for i in range(4)]
    c.kln = [K() for _ in range(4)]


def layer_norm_tile(c, ti, lnidx):
    nc, fw = c.nc, c.fw
    XF, kXF, PS, kPS = c.XF, c.kXF, c.PS, c.kPS
    cs = slice(ti * TW, (ti + 1) * TW)
    b1, b2 = 6, 7
    for m in range(KC):
        s = c.nsq % 2
        c.nsq += 1
        fw.op("act", lambda e: e.activation(out=c.sq[s][:], in_=XF[:, m, cs], func=AF.Square),
              reads=[kXF[m][ti]], writes=[c.ksq[s]])
        fw.pe_defer = True
        fw.op("pe", lambda e: e.matmul(PS[:, b1, 0:TW], c.onesf[:], XF[:, m, cs], start=(m == 0), stop=(m == KC - 1)),
              reads=[kXF[m][ti], c.kconst], writes=[kPS[b1]])
        fw.pe_defer = False
        fw.op("pe", lambda e: e.matmul(PS[:, b2, 0:TW], c.onesf[:], c.sq[s][:], start=(m == 0), stop=(m == KC - 1)),
              reads=[c.ksq[s], c.kconst], writes=[kPS[b2]])
    mean, msq, var, rstd = c.lnt
    kln = c.kln
    fw.op("dve", lambda e: e.tensor_scalar(out=mean[:], in0=PS[:, b1, 0:TW], scalar1=1.0 / D, scalar2=None, op0=ALU.mult),
          reads=[kPS[b1]], writes=[kln[0]])
    fw.op("dve", lambda e: e.tensor_tensor(out=msq[:], in0=mean[:], in1=mean[:], op=ALU.mult),
          reads=[kln[0]], writes=[kln[1]])
    fw.op("dve", lambda e: e.scalar_tensor_tensor(out=var[:], in0=PS[:, b2, 0:TW], scalar=1.0 / D, in1=msq[:],
                                                  op0=ALU.mult, op1=ALU.subtract),
          reads=[kPS[b2], kln[1]], writes=[kln[2]])
    fw.op("act", lambda e: e.activation(out=var[:], in_=var[:], func=AF.Sqrt, bias=c.epsc[:, 0:1]),
          reads=[kln[2], c.kconst], writes=[kln[2]])
    fw.op("dve", lambda e: e.reciprocal(out=rstd[:], in_=var[:]), reads=[kln[2]], writes=[kln[3]])
    for m in range(KC):
        s = c.nsq % 2
        c.nsq += 1
        t = c.sq[s]
        fw.op("pool" if m % 2 == 0 else "dve", lambda e: e.tensor_tensor(out=t[:], in0=XF[:, m, cs], in1=mean[:], op=ALU.subtract),
              reads=[kXF[m][ti], kln[0]], writes=[c.ksq[s]])
        fw.op("dve", lambda e: e.tensor_tensor(out=t[:], in0=t[:], in1=rstd[:], op=ALU.mult),
              reads=[c.ksq[s], kln[3]], writes=[c.ksq[s]])
        col = lnidx * KC + m
        fw.op("act", lambda e: e.activation(out=XF[:, m, cs], in_=t[:], func=AF.Identity,
                                            scale=c.lng[:, col:col + 1], bias=c.lnb[:, col:col + 1]),
              reads=[c.ksq[s], c.kconst], writes=[kXF[m][ti]])


def pipeline(streams):
    nst = max(len(b) for s in streams for b in s)
    nmax = max(len(s) for s in streams)
    for t in range(nmax + nst - 1):
        for s in streams:
            for k in range(nst - 1, -1, -1):
                bi = t - k
                if 0 <= bi < len(s) and k < len(s[bi]):
                    s[bi][k]()


def mixer_even(c, d, stage):
    nc, fw = c.nc, c.fw
    XF, kXF, PS, kPS = c.XF, c.kXF, c.PS, c.kPS
    st = ExitStack()
    with st:
        OSB = sb(st, nc, "OSB", [128, 4, NT], BF16)
        kOSB = [[K() for _ in range(5)] for _ in range(4)]
        ORW = sb(st, nc, "ORW", [128, 4, NT], BF16)
        kORW = [K() for _ in range(4)]
        fw.op("pool", lambda e: e.memset(OSB[:, :, NP:NT], 0.0), writes=[kOSB[cc][4] for cc in range(4)])
        QS = sb(st, nc, "QS", [NS, 512], F32); kQS = K()
        win_v = d["w_in"].rearrange("(kc p) n -> p kc n", p=128)
        with ExitStack() as sta:
            QT = sb(sta, nc, "QT", [128, 4, NT], BF16)
            KT = sb(sta, nc, "KT", [128, 4, NT], BF16)
            kQT = [K() for _ in range(4)]
            kKT = [K() for _ in range(4)]
            Vtok = sb(sta, nc, "Vtok", [128, 17, 512], BF16)
            kV = [K() for _ in range(17)]
            with ExitStack() as st2:
                XB = sb(st2, nc, "XBm", [128, KC, NT], BF16)
                kXB = [K() for _ in range(KC)]
                for kc in range(KC):
                    fw.op("act" if kc % 2 else "dve",
                          (lambda e, kc=kc: e.activation(out=XB[:, kc, :], in_=XF[:, kc, :], func=AF.Identity)) if kc % 2 else
                          (lambda e, kc=kc: e.tensor_copy(out=XB[:, kc, :], in_=XF[:, kc, :])),
                          reads=kXF[kc], writes=[kXB[kc]])
                WB = [sb(st2, nc, "WBm%d" % i, [128, KC, 256], BF16) for i in range(2)]
                kWB = [K(), K()]
                stg = [sb(st2, nc, "stg%d" % i, [128, 256], F32) for i in range(2)]
                kstg = [K(), K()]
                nb = 0
                nstg = 0
                for wc in range(6):
                    sl = wc % 2
                    fw.dma("pool", WB[sl][:], win_v[:, :, wc * 256:(wc + 1) * 256], writes=[kWB[sl]])
                    if wc < 4:
                        for oo in range(2):
                            oc = 2 * wc + oo
                            dst, kd = (QT, kQT) if oc < 4 else (KT, kKT)
                            for ti in range(NTL):
                                bk = nb % 2
                                nb += 1
                                cs = slice(ti * TW, (ti + 1) * TW)
                                for kc in range(KC):
                                    fw.pe_defer = (kc != KC - 1)
                                    fw.op("pe", lambda e, kc=kc: e.matmul(PS[:, bk, 0:TW], WB[sl][:, kc, oo * 128:(oo + 1) * 128], XB[:, kc, cs],
                                                                          start=(kc == 0), stop=(kc == KC - 1)),
                                          reads=[kWB[sl], kXB[kc]], writes=[kPS[bk]])
                                fw.op("act", lambda e: e.activation(out=dst[:, oc % 4, cs], in_=PS[:, bk, 0:TW], func=AF.Identity),
                                      reads=[kPS[bk]], writes=[kd[oc % 4]])
                    if wc < 2 and stage >= 4:
                        bk = 2 + (nb % 2)
                        nb += 1
                        for kc in range(KC):
                            fw.pe_defer = (kc != KC - 1)
                            fw.op("pe", lambda e, kc=kc: e.matmul(PS[0:NS, bk, 0:256], XB[:, kc, NP:NT], WB[sl][:, kc, :], start=(kc == 0), stop=(kc == KC - 1)),
                                  reads=[kWB[sl], kXB[kc]], writes=[kPS[bk]])
                        fw.op("act", lambda e: e.activation(out=QS[0:NS, wc * 256:(wc + 1) * 256], in_=PS[0:NS, bk, 0:256], func=AF.Identity), reads=[kPS[bk]], writes=[kQS])
                    if wc >= 2:
                        which = 0 if wc < 4 else 1
                        coff = (wc % 2) * 256
                        for tt in range(17):
                            rows = 128 if tt < 16 else NS
                            bk = 2 + (nb % 2)
                            nb += 1
                            for kc in range(KC):
                                fw.pe_defer = (kc != KC - 1)
                                fw.op("pe", lambda e, kc=kc: e.matmul(PS[0:rows, bk, 0:256], XB[:, kc, tt * 128:tt * 128 + rows], WB[sl][:, kc, :],
                                                                      start=(kc == 0), stop=(kc == KC - 1)),
                                      reads=[kWB[sl], kXB[kc]], writes=[kPS[bk]])
                            ss = nstg % 2
                            nstg += 1
                            fw.op("act", lambda e: e.activation(out=stg[ss][0:rows, :], in_=PS[0:rows, bk, 0:256], func=AF.Identity),
                                  reads=[kPS[bk]], writes=[kstg[ss]])
                            if tt < 16:
                                dstd = (d["pk"] if which == 0 else d["pv"])[tt * 128:(tt + 1) * 128, coff:coff + 256]
                            else:
                                dstd = (d["sk"] if which == 0 else d["sv"])[:, coff:coff + 256]
                            fw.dma("sp", dstd, stg[ss][0:rows, :], reads=[kstg[ss]])
                            if which == 1:
                                fw.op("act", lambda e: e.activation(out=Vtok[0:rows, tt, coff:coff + 256], in_=PS[0:rows, bk, 0:256], func=AF.Identity),
                                      reads=[kPS[bk]], writes=[kV[tt]])
                fw.barrier()
            with ExitStack() as st2:
                if stage >= 3:
                    sb_attention_prompt(c, d, st2, QT, KT, kQT, kKT, Vtok, kV, OSB, kOSB)
                fw.barrier()
            fw.barrier()
        if stage >= 4 and 'nosamp' not in VAR:
            with ExitStack() as st3:
                sb_attention_sample(c, d, st3, QS, kQS, OSB, kOSB)
                fw.barrier()
        if stage >= 5:
            with ExitStack() as st3:
                rwkv_all(c, d, st3, ORW, kORW)
                fw.barrier()
        if stage >= 6:
            with ExitStack() as st3:
                alloc_ln(c, st3)
                WOUT = sb(st3, nc, "WOUT", [128, KC, D], BF16)
                kWO = [K() for _ in range(4)]
                wo_v = d["w_out"].rearrange("(kc p) n -> p kc n", p=128)
                for i in range(4):
                    fw.dma("pool", WOUT[:, :, i * 256:(i + 1) * 256], wo_v[:, :, i * 256:(i + 1) * 256], writes=[kWO[i]])
                nb2 = 0
                for ti in range(NTL):
                    cs = slice(ti * TW, (ti + 1) * TW)
                    for m in range(KC):
                        bk = nb2 % 2
                        nb2 += 1
                        for kc in range(KC):
                            src, ks = (OSB, kOSB[kc % 4]) if kc < 4 else (ORW, [kORW[kc % 4]])
                            fw.op("pe", lambda e, kc=kc, src=src: e.matmul(PS[:, bk, 0:TW], WOUT[:, kc, m * 128:(m + 1) * 128], src[:, kc % 4, cs],
                                                                           start=(kc == 0), stop=(kc == KC - 1)),
                                  reads=[kWO[m // 2]] + list(ks), writes=[kPS[bk]])
                        fw.op("dve", lambda e: e.scalar_tensor_tensor(out=XF[:, m, cs], in0=XF[:, m, cs], scalar=ALPHA, in1=PS[:, bk, 0:TW],
                                                                      op0=ALU.mult, op1=ALU.add),
                              reads=[kPS[bk], kXF[m][ti]], writes=[kXF[m][ti]])
                    layer_norm_tile(c, ti, 1)
                fw.barrier()
        if c.dbg and 'nodbg' not in VAR:
            fw.dma("sp", d["dbg_osb"], OSB[:], reads=[k for kk in kOSB for k in kk])
        fw.barrier()


def sb_attention_prompt(c, d, st, QT, KT, kQT, kKT, Vtok, kV, OSB, kOSB):
    nc, fw = c.nc, c.fw
    PS, kPS = c.PS, c.kPS
    NSTR = 2
    onesb = c.onesb
    c.TRI = sb(st, nc, "TRI", [128, 128], BF16)
    c.STRICT = sb(st, nc, "STRICT", [128, 128], BF16)
    c.MASK = [sb(st, nc, "MASK%d" % i, [128, 512], BF16) for i in range(4)]
    fw.op("pool", lambda e: e.affine_select(out=c.TRI[:], in_=onesb[:, 0:128], pattern=[[-1, 128]], base=0, channel_multiplier=1,
                                            compare_op=ALU.is_ge, fill=0.0), reads=[c.kconst], writes=[c.kconst])
    for i in range(4):
        fw.op("pool", lambda e, i=i: e.affine_select(out=c.MASK[i][:], in_=onesb[:], pattern=[[1, 512]], base=-128 * i, channel_multiplier=-1,
                                                     compare_op=ALU.is_gt, fill=0.0), reads=[c.kconst], writes=[c.kconst])
    Et = [[sb(st, nc, "Et%d_%d" % (s, i), [128, 512], F32) for i in range(3)] for s in range(NSTR)]
    spt = [[sb(st, nc, "spt%d_%d" % (s, i), [128, 512], BF16) for i in range(3)] for s in range(NSTR)]
    e2t = [[sb(st, nc, "e2t%d_%d" % (s, i), [128, 512], F32) for i in range(2)] for s in range(NSTR)]
    wt = [[sb(st, nc, "wt%d_%d" % (s, i), [128, 512], BF16) for i in range(2)] for s in range(NSTR)]
    kEt = [[K() for _ in range(3)] for _ in range(NSTR)]
    kspt = [[K() for _ in range(3)] for _ in range(NSTR)]
    ke2t = [[K() for _ in range(2)] for _ in range(NSTR)]
    kwt = [[K() for _ in range(2)] for _ in range(NSTR)]
    sacc = [sb(st, nc, "sacc%d" % s, [128, 512], BF16) for s in range(NSTR)]
    ksacc = [K() for _ in range(NSTR)]
    streams = []
    for s in range(NSTR):
        blocks = []
        bi = 0
        bA = [4 * s, 4 * s + 1]
        bC = 4 * s + 2
        bO = 4 * s + 3
        for h in range(s, 8, NSTR):
            po = (h % 2) * 64
            ch = h // 2
            for qt in range(4):
                q0 = qt * 512
                nkb = 4 * qt + 4
                for n, kb in enumerate(range(nkb - 1, -1, -1)):
                    first = (n == 0)
                    last = (kb == 0)
                    diag = kb - 4 * qt
                    i3, i2 = bi % 3, bi % 2
                    bi += 1

                    def st1(s=s, h=h, po=po, ch=ch, q0=q0, kb=kb, diag=diag, i3=i3, bA=bA[bi % 2]):
                        fw.op("pe", lambda e: e.matmul(PS[:, bA, :], KT[po:po + 64, ch, kb * 128:(kb + 1) * 128], QT[po:po + 64, ch, q0:q0 + 512],
                                                       start=True, stop=True),
                              reads=[kQT[ch], kKT[ch]], writes=[kPS[bA]])
                        fw.op("act", lambda e: e.activation(out=Et[s][i3][:], in_=PS[:, bA, :], func=AF.Exp, scale=0.125,
                                                            bias=c.sbb[:, h:h + 1]),
                              reads=[kPS[bA], c.kconst], writes=[kEt[s][i3]])
                        fw.op("act", lambda e: e.activation(out=spt[s][i3][:], in_=Et[s][i3][:], func=AF.Ln, bias=c.cst[:, 1:2]),
                              reads=[kEt[s][i3], c.kconst], writes=[kspt[s][i3]])
                        if diag >= 0:
                            fw.op("dve", lambda e: e.tensor_tensor(out=spt[s][i3][:], in0=spt[s][i3][:], in1=c.MASK[diag][:], op=ALU.mult),
                                  reads=[kspt[s][i3], c.kconst], writes=[kspt[s][i3]])
                            fw.op("pool", lambda e: e.tensor_tensor(out=Et[s][i3][:], in0=Et[s][i3][:], in1=c.MASK[diag][:], op=ALU.mult),
                                  reads=[kEt[s][i3], c.kconst], writes=[kEt[s][i3]])

                    def st2(s=s, i3=i3, i2=i2, first=first, last=last, bC=bC):
                        fw.op("pe", lambda e: e.matmul(PS[:, bC, :], c.TRI[:], spt[s][i3][:], start=True, stop=first),
                              reads=[kspt[s][i3], c.kconst], writes=[kPS[bC]])
                        if not first:
                            fw.op("pe", lambda e: e.matmul(PS[:, bC, :], c.ONESB[:], sacc[s][:], start=False, stop=True),
                                  reads=[ksacc[s], c.kconst], writes=[kPS[bC]])
                        fw.op("act", lambda e: e.activation(out=e2t[s][i2][:], in_=PS[:, bC, :], func=AF.Exp, scale=-1.0),
                              reads=[kPS[bC]], writes=[ke2t[s][i2]])
                        if not last:
                            if first:
                                fw.op("pool", lambda e: e.tensor_copy(out=sacc[s][:], in_=spt[s][i3][:]),
                                      reads=[kspt[s][i3]], writes=[ksacc[s]])
                            else:
                                fw.op("pool", lambda e: e.tensor_tensor(out=sacc[s][:], in0=sacc[s][:], in1=spt[s][i3][:], op=ALU.add),
                                      reads=[kspt[s][i3], ksacc[s]], writes=[ksacc[s]])

                    def st3(s=s, h=h, po=po, ch=ch, q0=q0, qt=qt, kb=kb, i3=i3, i2=i2, first=first, last=last, bC=bC, bO=bO):
                        fw.op("dve", lambda e: e.tensor_tensor(out=wt[s][i2][:], in0=Et[s][i3][:], in1=e2t[s][i2][:], op=ALU.mult),
                              reads=[kEt[s][i3], ke2t[s][i2]], writes=[kwt[s][i2]])
                        fw.op("pe", lambda e: e.matmul(PS[po:po + 64, bO, :], Vtok[:, kb, h * 64:(h + 1) * 64], wt[s][i2][:],
                                                       start=first, stop=last),
                              reads=[kV[kb], kwt[s][i2]], writes=[kPS[bO]])
                        if last:
                            fw.op("act", lambda e: e.activation(out=OSB[po:po + 64, ch, q0:q0 + 512], in_=PS[po:po + 64, bO, :], func=AF.Identity),
                                  reads=[kPS[bO]], writes=[kOSB[ch][qt]])
                    blocks.append([st1, st2, st3])
        streams.append(blocks)
    pipeline(streams)


CS = 14


def transpose_to(c, out_ps, in_ap, kin, kout, ident):
    c.fw.op("pe", lambda e: e.transpose(out_ps, in_ap, ident), reads=kin + [c.kconst], writes=kout)


def rwkv_all(c, d, st, ORW, kORW):
    nc, fw = c.nc, c.fw
    XF, kXF, PS, kPS = c.XF, c.kXF, c.PS, c.kPS
    IDF = c.IDF
    ST = sb(st, nc, "ST", [128, 256], F32); kST = K()
    STw = sb(st, nc, "STw", [128, 256], F32); kSTw = K()
    SAY = sb(st, nc, "SAY", [128, 64], BF16); kSAY = K()
    PB = sb(st, nc, "PB", [128, 14, TW + 1], F32); kPB = [K() for _ in range(14)]
    PM = sb(st, nc, "PM", [128, 14, TW], F32); kPM = [K() for _ in range(14)]
    XBt = sb(st, nc, "XBt", [128, KC, TW], BF16); kXBt = [K() for _ in range(KC)]
    WRb = [sb(st, nc, "WRb%d" % i, [128, KC, 256], BF16) for i in range(2)]; kWRb = [K(), K()]
    WW2 = sb(st, nc, "WW2", [128, 512], BF16)
    WA2 = sb(st, nc, "WA2", [128, 512], BF16)
    WG2 = sb(st, nc, "WG2", [128, 512], BF16)
    PC = sb(st, nc, "PC", [128, 48], F32)
    GNG = sb(st, nc, "GNG", [128, 512], F32)
    GNB = sb(st, nc, "GNB", [128, 512], F32)
    BLK = sb(st, nc, "BLK", [128, 128], F32)
    kW = K()
    tmpA = sb(st, nc, "tmpA", [128, TW], F32); ktA = K()
    tmpB = sb(st, nc, "tmpB", [128, TW], F32); ktB = K()
    tmpH = sb(st, nc, "tmpH", [128, TW], BF16); ktH = K()
    NBrow = [sb(st, nc, "NBrow%d" % i, [128, CS, 128], BF16) for i in range(2)]
    KBrow = [sb(st, nc, "KBrow%d" % i, [128, CS, 128], BF16) for i in range(2)]
    Vrow = [sb(st, nc, "Vrow%d" % i, [128, CS, 64], BF16) for i in range(2)]
    kRow = [K(), K()]
    TOK = sb(st, nc, "TOK", [128, 3, 512], BF16); kTOK = [K(), K(), K()]
    LKc = [sb(st, nc, "LKc%d" % i, [128, CS, 4, 2], F32) for i in range(2)]
    RKc = [sb(st, nc, "RKc%d" % i, [128, CS, 4, 2], F32) for i in range(2)]
    kLK = [K(), K()]
    Ybuf = [sb(st, nc, "Ybuf%d" % i, [128, CS, 64], BF16) for i in range(2)]; kY = [K(), K()]
    YTOK = sb(st, nc, "YTOK", [128, 512], BF16); kYT = K()
    YC = sb(st, nc, "YC", [128, 512], F32); kYC = K()
    YS = sb(st, nc, "YS", [128, 512], F32); kYS = K()
    gst = sb(st, nc, "gst", [128, 32], F32); kgst = K()
    SHX = sb(st, nc, "SHX", [NS + 1, RWC], F32); kSHT = K()
    SHTOK = SHX
    SHT = sb(st, nc, "SHT", [128, 14, NS], F32); kSH = K()
    SSH = SHX; kSSH = kSHT
    SLD = sb(st, nc, "SLD", [64, 4, 128], F32); kSLD = K()
    SSTt = SLD; kSST = kSLD
    fw.dma("pool", WW2[0:64, :], d["w_w2"], writes=[kW])
    fw.dma("pool", WA2[64:128, :], d["w_a2"], writes=[kW])
    fw.dma("pool", WG2[:, :], d["w_g2"], writes=[kW])
    fw.dma("pool", PC[:], d["pcol"], writes=[kW])
    fw.dma("pool", GNG[:], d["gng"], writes=[kW])
    fw.dma("pool", GNB[:], d["gnb"], writes=[kW])
    fw.dma("pool", SHTOK[0:NS, :], d["sshift0"], writes=[kSHT])
    fw.op("pool", lambda e: e.memset(BLK[:], 0.0), writes=[kW])
    fw.op("pool", lambda e: e.memset(BLK[0:64, 0:64], 1.0), reads=[kW], writes=[kW])
    fw.op("pool", lambda e: e.memset(BLK[64:128, 64:128], 1.0), reads=[kW], writes=[kW])
    fw.op("pool", lambda e: e.memset(ST[:], 0.0), writes=[kST])
    for i_ in range(2):
        fw.op("pool", lambda e, i_=i_: e.memset(NBrow[i_][:], 0.0), writes=[kRow[i_]])
        fw.op("pool", lambda e, i_=i_: e.memset(KBrow[i_][:], 0.0), writes=[kRow[i_]])
        fw.op("pool", lambda e, i_=i_: e.memset(Vrow[i_][:], 0.0), writes=[kRow[i_]])
        fw.op("pool", lambda e, i_=i_: e.memset(LKc[i_][:], 0.0), writes=[kLK[i_]])
        fw.op("pool", lambda e, i_=i_: e.memset(RKc[i_][:], 0.0), writes=[kLK[i_]])
    fw.op("pool", lambda e: e.memset(PB[:, :, 0:1], 0.0), writes=kPB)
    MU, W0, A0, KKc, KAc, RKp = 0, 14, 18, 22, 26, 30
    for bz in (2, 3, 4, 5, 6, 7):
        fw.op("dve", lambda e, bz=bz: e.memset(PS[:, bz, :], 0.0), writes=[kPS[bz]])
    for m in range(14):
        transpose_to(c, PS[:, 7, m * NS:(m + 1) * NS], SHTOK[0:NS, m * 128:(m + 1) * 128], [kSHT], [kPS[7]], IDF[0:NS, 0:NS])
    fw.op("act", lambda e: e.activation(out=SHT[:].rearrange("p m b -> p (m b)"), in_=PS[:, 7, 0:14 * NS], func=AF.Identity),
          reads=[kPS[7]], writes=[kSH])
    wr_v = d["w_in"].rearrange("(kc p) n -> p kc n", p=128)
    nwr = 0
    nbk = 0

    def store_state(dst):
        for h4 in range(4):
            transpose_to(c, PS[0:64, 0, h4 * 128:(h4 + 1) * 128], ST[:, h4 * 64:(h4 + 1) * 64], [kST], [kPS[0]], IDF[:, :])
        fw.op("act", lambda e: e.activation(out=SSTt[:].rearrange("p a b -> p (a b)"), in_=PS[0:64, 0, :], func=AF.Identity),
              reads=[kPS[0]], writes=[kSST])
        fw.dma("pool", dst.rearrange("(h4 h2) i j -> i h4 h2 j", h2=2), SSTt[:].rearrange("p a (h2 j) -> p a h2 j", h2=2), reads=[kSST])

    def load_state(src):
        fw.dma("pool", SLD[:].rearrange("p a (h2 j) -> p a h2 j", h2=2), src.rearrange("(h4 h2) i j -> i h4 h2 j", h2=2), writes=[kSLD])
        for h4 in range(4):
            transpose_to(c, PS[:, 0, h4 * 64:(h4 + 1) * 64], SLD[:, h4, :], [kSLD], [kPS[0]], IDF[0:64, 0:64])
        fw.op("act", lambda e: e.activation(out=ST[:], in_=PS[:, 0, 0:256], func=AF.Identity), reads=[kPS[0]], writes=[kST])

    for ti in ([5] if 'rw1' in VAR else [0] if 'rw0' in VAR else range(NTL)):
        c0 = ti * TW
        npr = min(TW, NP - c0)
        for kc in range(KC):
            fw.op("act", lambda e, kc=kc: e.activation(out=XBt[:, kc, :], in_=XF[:, kc, c0:c0 + TW], func=AF.Identity),
                  reads=[kXF[kc][ti]], writes=[kXBt[kc]])
        for mp in range(7):
            sl = nwr % 2
            nwr += 1
            fw.dma("pool", WRb[sl][:], wr_v[:, :, 1536 + mp * 256:1536 + (mp + 1) * 256], writes=[kWRb[sl]])
            for mm in range(2):
                m = 2 * mp + mm
                bk = nbk % 2
                nbk += 1
                for kc in range(KC):
                    fw.pe_defer = (kc != KC - 1)
                    fw.op("pe", lambda e, kc=kc: e.matmul(PS[:, bk, 0:TW], WRb[sl][:, kc, mm * 128:(mm + 1) * 128], XBt[:, kc, :],
                                                          start=(kc == 0), stop=(kc == KC - 1)),
                          reads=[kWRb[sl], kXBt[kc]], writes=[kPS[bk]])
                fw.op("act", lambda e: e.activation(out=PB[:, m, 1:TW + 1], in_=PS[:, bk, 0:TW], func=AF.Identity),
                      reads=[kPS[bk]], writes=[kPB[m]])
        for m in range(14):
            fw.op("pool", lambda e, m=m: e.tensor_tensor(out=PM[:, m, 0:npr], in0=PB[:, m, 0:npr], in1=PB[:, m, 1:npr + 1], op=ALU.subtract),
                  reads=[kPB[m]], writes=[kPM[m]])
            if npr < TW:
                fw.op("pool", lambda e, m=m: e.tensor_tensor(out=PM[:, m, npr:TW], in0=SHT[:, m, :], in1=PB[:, m, npr + 1:TW + 1], op=ALU.subtract),
                      reads=[kPB[m], kSH], writes=[kPM[m]])
            fw.op("dve", lambda e, m=m: e.scalar_tensor_tensor(out=PM[:, m, :], in0=PM[:, m, :], scalar=PC[:, MU + m:MU + m + 1],
                                                               in1=PB[:, m, 1:TW + 1], op0=ALU.mult, op1=ALU.add),
                  reads=[kPB[m], kPM[m], kW], writes=[kPM[m]])
        if npr < TW:
            for m in range(14):
                transpose_to(c, PS[0:NS + 1, 7, (m % 4) * 128:(m % 4 + 1) * 128], PB[:, m, npr:TW + 1], [kPB[m]], [kPS[7]], IDF[:, :])
                if m % 4 == 3 or m == 13:
                    m0 = (m // 4) * 4
                    fw.op("act", lambda e, m0=m0, m=m: e.activation(out=SSH[:, m0 * 128:(m + 1) * 128], in_=PS[0:NS + 1, 7, 0:(m - m0 + 1) * 128], func=AF.Identity),
                          reads=[kPS[7]], writes=[kSSH])
            fw.dma("pool", d["pshift"], SSH[0:1, :], reads=[kSSH])
            fw.dma("pool", d["sshift"], SSH[1:NS + 1, :], reads=[kSSH])
        fw.op("dve", lambda e: e.tensor_copy(out=PB[:, :, 0:1], in_=PB[:, :, TW:TW + 1]), reads=kPB, writes=kPB)
        Wt = lambda cc: PB[:, cc, 1:TW + 1]
        KKt = lambda cc: PB[:, 4 + cc, 1:TW + 1]
        NBt = lambda cc: PB[:, 8 + cc, 1:TW + 1]
        fw.op("act", lambda e: e.activation(out=tmpH[0:64, :], in_=PM[0:64, 12, :], func=AF.Tanh), reads=[kPM[12]], writes=[ktH])
        fw.op("act", lambda e: e.activation(out=tmpH[64:128, :], in_=PM[64:128, 12, :], func=AF.Identity), reads=[kPM[12]], writes=[ktH])
        for cc in range(4):
            bk = nbk % 2
            nbk += 1
            fw.op("pe", lambda e: e.matmul(PS[:, bk, 0:TW], WW2[0:64, cc * 128:(cc + 1) * 128], tmpH[0:64, :], start=True, stop=True),
                  reads=[kW, ktH], writes=[kPS[bk]])
            fw.op("act", lambda e: e.activation(out=tmpA[:], in_=PS[:, bk, 0:TW], func=AF.Sigmoid, bias=PC[:, W0 + cc:W0 + cc + 1]),
                  reads=[kPS[bk], kW], writes=[ktA])
            fw.op("act", lambda e: e.activation(out=Wt(cc), in_=tmpA[:], func=AF.Exp, scale=-0.6065306597126334),
                  reads=[ktA], writes=[kPB[cc]])
        for cc in range(4):
            bk = nbk % 2
            nbk += 1
            fw.op("pe", lambda e: e.matmul(PS[:, bk, 0:TW], WA2[64:128, cc * 128:(cc + 1) * 128], tmpH[64:128, :], start=True, stop=True),
                  reads=[kW, ktH], writes=[kPS[bk]])
            fw.op("act", lambda e: e.activation(out=tmpA[:], in_=PS[:, bk, 0:TW], func=AF.Sigmoid, bias=PC[:, A0 + cc:A0 + cc + 1]),
                  reads=[kPS[bk], kW], writes=[ktA])
            fw.op("dve", lambda e: e.tensor_scalar(out=KKt(cc), in0=PM[:, 4 + cc, :], scalar1=PC[:, KKc + cc:KKc + cc + 1], scalar2=None, op0=ALU.mult),
                  reads=[kPM[4 + cc], kW], writes=[kPB[4 + cc]])
            fw.op("act", lambda e: e.activation(out=tmpB[:], in_=KKt(cc), func=AF.Square), reads=[kPB[4 + cc]], writes=[ktB])
            b2 = 2 + (nbk % 2)
            fw.op("pe", lambda e: e.matmul(PS[:, b2, 0:TW], BLK[:], tmpB[:], start=True, stop=True), reads=[kW, ktB], writes=[kPS[b2]])
            fw.op("dve", lambda e: e.tensor_scalar(out=tmpB[:], in0=PS[:, b2, 0:TW], scalar1=1e-24, scalar2=None, op0=ALU.max),
                  reads=[kPS[b2]], writes=[ktB])
            fw.op("act", lambda e: e.activation(out=tmpB[:], in_=tmpB[:], func=AF.Sqrt), reads=[ktB], writes=[ktB])
            fw.op("dve", lambda e: e.reciprocal(out=tmpB[:], in_=tmpB[:]), reads=[ktB], writes=[ktB])
            fw.op("dve", lambda e: e.tensor_tensor(out=KKt(cc), in0=KKt(cc), in1=tmpB[:], op=ALU.mult), reads=[kPB[4 + cc], ktB], writes=[kPB[4 + cc]])
            fw.op("dve", lambda e: e.scalar_tensor_tensor(out=NBt(cc), in0=KKt(cc), scalar=-1.0, in1=tmpA[:], op0=ALU.mult, op1=ALU.mult),
                  reads=[kPB[4 + cc], ktA], writes=[kPB[8 + cc]])
            fw.op("dve", lambda e: e.tensor_scalar(out=tmpA[:], in0=tmpA[:], scalar1=-1.0, scalar2=PC[:, KAc + cc:KAc + cc + 1], op0=ALU.add, op1=ALU.mult),
                  reads=[ktA, kW], writes=[ktA])
            fw.op("dve", lambda e: e.scalar_tensor_tensor(out=PM[:, 4 + cc, :], in0=tmpA[:], scalar=1.0, in1=PM[:, 4 + cc, :], op0=ALU.add, op1=ALU.mult),
                  reads=[ktA, kPM[4 + cc]], writes=[kPM[4 + cc]])
        if 'rwA' in VAR:
            continue
        subs = [(a_, min(CS, TW - a_)) for a_ in range(0, TW, CS)]
        cblocks = [(0, 128), (128, 128), (256, TW - 256)]
        for (cb0, ncb) in cblocks:
            for vi, src in enumerate((lambda cc: PM[:, 4 + cc, cb0:cb0 + ncb], lambda cc: NBt(cc)[:, cb0:cb0 + ncb], lambda cc: PM[:, 8 + cc, cb0:cb0 + ncb])):
                kk_ = (lambda cc: kPM[4 + cc], lambda cc: kPB[8 + cc], lambda cc: kPM[8 + cc])[vi]
                bk = nbk % 2
                nbk += 1
                for cc in range(4):
                    transpose_to(c, PS[0:ncb, bk, cc * 128:(cc + 1) * 128], src(cc), [kk_(cc)], [kPS[bk]], IDF[:, :])
                fw.op("act", lambda e: e.activation(out=TOK[0:ncb, vi, :], in_=PS[0:ncb, bk, :], func=AF.Identity), reads=[kPS[bk]], writes=[kTOK[vi]])
            fw.dma("pool", c.TOKd[cb0:cb0 + ncb], TOK[0:ncb, :, :], reads=kTOK, writes=[c.kTOKd])

        def fetch_rows(a, ncol, r):
            for h2 in range(2):
                for vi, dstt in enumerate((KBrow[r], NBrow[r])):
                    fw.dma("pool", dstt[h2:128:32, 0:ncol, h2 * 64:(h2 + 1) * 64],
                           c.TOKd[a:a + ncol, vi, :].rearrange("s (h4 h2 j) -> h4 s h2 j", h4=4, h2=2)[:, :, h2, :], reads=[c.kTOKd], writes=[kRow[r]])
                fw.dma("pool", Vrow[r][h2:128:32, 0:ncol, :],
                       c.TOKd[a:a + ncol, 2, :].rearrange("s (h4 h2 j) -> h4 s h2 j", h4=4, h2=2)[:, :, h2, :], reads=[c.kTOKd], writes=[kRow[r]])
            for h2 in range(2):
                ps_ = slice(h2 * 64, (h2 + 1) * 64)
                fw.op("pool", lambda e: e.tensor_copy(out=LKc[r][ps_, 0:ncol, :, h2], in_=PB[ps_, 4:8, 1 + a:1 + a + ncol].rearrange("p c s -> p s c")),
                      reads=kPB[4:8], writes=[kLK[r]])
                fw.op("pool", lambda e: e.tensor_copy(out=RKc[r][ps_, 0:ncol, :, h2], in_=PM[ps_, 0:4, a:a + ncol].rearrange("p c s -> p s c")),
                      reads=kPM[0:4], writes=[kLK[r]])

        def run_steps(a, ncol, r):
            def emit_y(sy):
                yb = sy % 8
                for h4 in range(4):
                    fw.op("pe", lambda e, h4=h4: e.matmul(PS[32 * h4:32 * h4 + 2, 7, yb * 64:(yb + 1) * 64], RKc[r][:, sy, h4, :], ST[:, h4 * 64:(h4 + 1) * 64],
                                                          start=True, stop=True, tile_position=(0, 32 * h4)),
                          reads=[kLK[r], kST], writes=[kPS[7]])
                if yb == 7 or sy == ncol - 1:
                    s0_ = sy - yb
                    fw.op("act", lambda e: e.activation(out=Ybuf[r][:, s0_:sy + 1, :].rearrange("p s i -> p (s i)"), in_=PS[:, 7, 0:(yb + 1) * 64], func=AF.Identity),
                          reads=[kPS[7]], writes=[kY[r]])

            pending_y = None
            for s_ in range(ncol):
                gcol = c0 + a + s_
                is_sample = gcol >= NP
                if is_sample:
                    if pending_y is not None:
                        emit_y(pending_y); pending_y = None
                    load_state(d["swkv0"][gcol - NP])
                for h4 in range(4):
                    fw.op("pe", lambda e, h4=h4: e.matmul(PS[32 * h4:32 * h4 + 2, 2, 0:64], LKc[r][:, s_, h4, :], ST[:, h4 * 64:(h4 + 1) * 64],
                                                          start=True, stop=True, tile_position=(0, 32 * h4)),
                          reads=[kLK[r], kST], writes=[kPS[2]])
                fw.op("dve", lambda e: e.tensor_tensor(out=STw[:].rearrange("p (a b) -> p a b", a=4), in0=ST[:].rearrange("p (a b) -> p a b", a=4),
                                                       in1=PB[:, 0:4, 1 + a + s_:2 + a + s_].to_broadcast([128, 4, 64]), op=ALU.mult),
                      reads=[kST] + kPB[0:4], writes=[kSTw])
                if pending_y is not None:
                    emit_y(pending_y); pending_y = None
                for h4 in range(4):
                    fw.op("pe", lambda e, h4=h4: e.matmul(PS[:, 3 + h4, 0:64], KBrow[r][32 * h4:32 * h4 + 2, s_, :], Vrow[r][32 * h4:32 * h4 + 2, s_, :],
                                                          start=True, stop=False, tile_position=(32 * h4, 0)),
                          reads=[kRow[r]], writes=[kPS[3 + h4]])
                fw.op("act", lambda e: e.activation(out=SAY[:], in_=PS[:, 2, 0:64], func=AF.Identity), reads=[kPS[2]], writes=[kSAY])
                for h4 in range(4):
                    fw.op("pe", lambda e, h4=h4: e.matmul(PS[:, 3 + h4, 0:64], NBrow[r][32 * h4:32 * h4 + 2, s_, :], SAY[32 * h4:32 * h4 + 2, :],
                                                          start=False, stop=True, tile_position=(32 * h4, 0)),
                          reads=[kRow[r], kSAY], writes=[kPS[3 + h4]])
                fw.op("dve", lambda e: e.tensor_tensor(out=ST[:].rearrange("p (a b) -> p a b", a=4), in0=STw[:].rearrange("p (a b) -> p a b", a=4),
                                                       in1=PS[:, 3:7, 0:64], op=ALU.add),
                      reads=[kSTw] + kPS[3:7], writes=[kST])
                pending_y = s_
                if is_sample or gcol == NP - 1 or s_ == ncol - 1:
                    emit_y(pending_y); pending_y = None
                if is_sample:
                    store_state(d["swkv"][gcol - NP])
                if gcol == NP - 1:
                    store_state(d["pwkv"])
            for h2 in range(2):
                fw.dma("pool", c.Yd[:, h2, a:a + ncol, :], Ybuf[r][h2:128:32, 0:ncol, :], reads=[kY[r]], writes=[c.kYd])

        fetch_rows(subs[0][0], subs[0][1], 0)
        for n_, (a_, nc_) in enumerate(subs):
            r_ = n_ % 2
            if n_ + 1 < len(subs):
                fetch_rows(subs[n_ + 1][0], subs[n_ + 1][1], 1 - r_)
            run_steps(a_, nc_, r_)

        for (cb0, ncol) in cblocks:
            a = cb0
            fw.dma("pool", YTOK[0:ncol, :].rearrange("s (h i) -> s h i", h=8), c.Yd[:, :, a:a + ncol, :].rearrange("a b s i -> s (a b) i"),
                   reads=[c.kYd], writes=[kYT])
            Y3 = YTOK[0:ncol, :].rearrange("s (h i) -> s h i", h=8)
            C3 = YC[0:ncol, :].rearrange("s (h i) -> s h i", h=8)
            S3 = YS[0:ncol, :].rearrange("s (h i) -> s h i", h=8)
            fw.op("dve", lambda e: e.tensor_reduce(out=gst[0:ncol, 0:8], in_=Y3, axis=AX.X, op=ALU.add), reads=[kYT], writes=[kgst])
            fw.op("dve", lambda e: e.tensor_scalar(out=gst[0:ncol, 0:8], in0=gst[0:ncol, 0:8], scalar1=1.0 / 64, scalar2=None, op0=ALU.mult), reads=[kgst], writes=[kgst])
            fw.op("dve", lambda e: e.tensor_tensor(out=C3, in0=Y3, in1=gst[0:ncol, 0:8].unsqueeze(2).to_broadcast([ncol, 8, 64]), op=ALU.subtract),
                  reads=[kYT, kgst], writes=[kYC])
            fw.op("act", lambda e: e.activation(out=YS[0:ncol, :], in_=YC[0:ncol, :], func=AF.Square), reads=[kYC], writes=[kYS])
            fw.op("dve", lambda e: e.tensor_reduce(out=gst[0:ncol, 8:16], in_=S3, axis=AX.X, op=ALU.add), reads=[kYS], writes=[kgst])
            fw.op("act", lambda e: e.activation(out=gst[0:ncol, 8:16], in_=gst[0:ncol, 8:16], func=AF.Sqrt, scale=1.0 / 64, bias=c.cst[0:ncol, 2:3]),
                  reads=[kgst, c.kconst], writes=[kgst])
            fw.op("dve", lambda e: e.reciprocal(out=gst[0:ncol, 8:16], in_=gst[0:ncol, 8:16]), reads=[kgst], writes=[kgst])
            fw.op("dve", lambda e: e.tensor_tensor(out=C3, in0=C3, in1=gst[0:ncol, 8:16].unsqueeze(2).to_broadcast([ncol, 8, 64]), op=ALU.mult),
                  reads=[kYC, kgst], writes=[kYC])
            fw.op("pool", lambda e: e.tensor_tensor(out=YC[0:ncol, :], in0=YC[0:ncol, :], in1=GNG[0:ncol, :], op=ALU.mult), reads=[kYC, kW], writes=[kYC])
            fw.op("pool", lambda e: e.tensor_tensor(out=YC[0:ncol, :], in0=YC[0:ncol, :], in1=GNB[0:ncol, :], op=ALU.add), reads=[kYC, kW], writes=[kYC])
            for cc in range(4):
                transpose_to(c, PS[:, 0, cc * 128:cc * 128 + ncol], YC[0:ncol, cc * 128:(cc + 1) * 128], [kYC], [kPS[0]], IDF[0:ncol, 0:ncol])
            fw.op("act", lambda e: e.activation(out=tmpH[:, 0:ncol], in_=PM[:, 13, a:a + ncol], func=AF.Sigmoid), reads=[kPM[13]], writes=[ktH])
            for cc in range(4):
                fw.op("dve", lambda e: e.scalar_tensor_tensor(out=tmpA[:, 0:ncol], in0=PM[:, cc, a:a + ncol], scalar=PC[:, RKp + cc:RKp + cc + 1],
                                                              in1=PM[:, 4 + cc, a:a + ncol], op0=ALU.mult, op1=ALU.mult),
                      reads=[kPM[cc], kPM[4 + cc], kW], writes=[ktA])
                fw.op("pe", lambda e: e.matmul(PS[:, 1, 0:ncol], BLK[:], tmpA[:, 0:ncol], start=True, stop=True), reads=[kW, ktA], writes=[kPS[1]])
                fw.op("pe", lambda e: e.matmul(PS[:, 1, 256:256 + ncol], WG2[:, cc * 128:(cc + 1) * 128], tmpH[:, 0:ncol], start=True, stop=True),
                      reads=[kW, ktH], writes=[kPS[1]])
                fw.op("dve", lambda e: e.tensor_tensor(out=tmpB[:, 0:ncol], in0=PS[:, 1, 0:ncol], in1=PM[:, 8 + cc, a:a + ncol], op=ALU.mult),
                      reads=[kPS[1], kPM[8 + cc]], writes=[ktB])
                fw.op("dve", lambda e: e.tensor_tensor(out=tmpB[:, 0:ncol], in0=tmpB[:, 0:ncol], in1=PS[:, 0, cc * 128:cc * 128 + ncol], op=ALU.add),
                      reads=[ktB, kPS[0]], writes=[ktB])
                fw.op("dve", lambda e: e.tensor_tensor(out=ORW[:, cc, c0 + a:c0 + a + ncol], in0=tmpB[:, 0:ncol], in1=PS[:, 1, 256:256 + ncol], op=ALU.mult),
                      reads=[ktB, kPS[1]], writes=[kORW[cc]])


TWO_PI = 6.283185307179586
C1 = 6.28125
C2 = TWO_PI - 6.28125
PI = 3.141592653589793


def trig_tables(c, X, kX, Sout, Cout, kS, kC, tmpI, tmpF, ktmp, width):
    fw = c.fw
    w = slice(0, width)
    fw.op("dve", lambda e: e.tensor_scalar(out=tmpI[:, w], in0=X[:, w], scalar1=1.0 / TWO_PI, scalar2=None, op0=ALU.mult), reads=[kX], writes=[ktmp])
    fw.op("dve", lambda e: e.tensor_copy(out=tmpF[:, w], in_=tmpI[:, w]), reads=[ktmp], writes=[ktmp])
    fw.op("dve", lambda e: e.scalar_tensor_tensor(out=X[:, w], in0=tmpF[:, w], scalar=-C1, in1=X[:, w], op0=ALU.mult, op1=ALU.add), reads=[ktmp, kX], writes=[kX])
    fw.op("dve", lambda e: e.scalar_tensor_tensor(out=X[:, w], in0=tmpF[:, w], scalar=-C2, in1=X[:, w], op0=ALU.mult, op1=ALU.add), reads=[ktmp, kX], writes=[kX])
    fw.op("dve", lambda e: e.tensor_scalar(out=X[:, w], in0=X[:, w], scalar1=PI, scalar2=-PI, op0=ALU.min, op1=ALU.max), reads=[kX], writes=[kX])
    fw.op("act", lambda e: e.activation(out=Sout, in_=X[:, w], func=AF.Sin), reads=[kX], writes=[kS])
    fw.op("act", lambda e: e.activation(out=tmpF[:, w], in_=X[:, w], func=AF.Abs), reads=[kX, ktmp], writes=[ktmp])
    fw.op("act", lambda e: e.activation(out=Cout, in_=tmpF[:, w], func=AF.Sin, scale=-1.0, bias=c.cst[:, 3:4]), reads=[ktmp, c.kconst], writes=[kC])


def s5_mixer(c, d):
    nc, fw = c.nc, c.fw
    XF, kXF, PS, kPS = c.XF, c.kXF, c.PS, c.kPS
    IDF = c.IDF
    with ExitStack() as st:
        ZG = sb(st, nc, "ZG", [128, KC, NT], BF16); kZG = [[K() for _ in range(NTL)] for _ in range(KC)]
        with ExitStack() as s2:
            XB = sb(s2, nc, "XBs", [128, KC, NT], BF16); kXB = [K() for _ in range(KC)]
            for kc in range(KC):
                fw.op("act", lambda e, kc=kc: e.activation(out=XB[:, kc, :], in_=XF[:, kc, :], func=AF.Identity), reads=kXF[kc], writes=[kXB[kc]])
            PRM = sb(s2, nc, "PRM", [128, 3, 32], F32); kP = K()
            fw.dma("sp", PRM[:], d["s5prm"], writes=[kP])
            DSK = sb(s2, nc, "DSK", [128, 8], F32)
            fw.dma("sp", DSK[:], d["dskip"], writes=[kP])
            S0 = sb(s2, nc, "S0", [128, 32, NS, 2], F32); kS0 = K()
            for t_ in range(32):
                fw.dma("sp", S0[:, t_, :, :], d["s5_0"][:, t_ * 128:(t_ + 1) * 128, :].rearrange("b p r -> p b r"), writes=[kS0])
            cn = {nm: sb(s2, nc, "c_" + nm, [128, 32], F32) for nm in ("DT", "MAGL", "TH", "MAG", "X", "SI", "CO", "AR", "AI", "CR", "CI", "T1", "T2", "RD")}
            tI = sb(s2, nc, "tI32", [128, TW], I32)
            tF = sb(s2, nc, "tF32", [128, TW], F32)
            ktmp = K()
            LR, LI, LDT = PRM[:, 0, :], PRM[:, 1, :], PRM[:, 2, :]
            kc_ = K()

            def o(eng, fn, r=(), w=()):
                fw.op(eng, fn, reads=[kP, kc_] + list(r), writes=[kc_] + list(w))
            o("act", lambda e: e.activation(out=cn["DT"][:], in_=LDT, func=AF.Exp))
            o("dve", lambda e: e.tensor_tensor(out=cn["MAGL"][:], in0=LR, in1=cn["DT"][:], op=ALU.mult))
            o("dve", lambda e: e.tensor_tensor(out=cn["TH"][:], in0=LI, in1=cn["DT"][:], op=ALU.mult))
            o("act", lambda e: e.activation(out=cn["MAG"][:], in_=cn["MAGL"][:], func=AF.Exp))
            o("dve", lambda e: e.tensor_copy(out=cn["X"][:], in_=cn["TH"][:]))
            trig_tables(c, cn["X"], kc_, cn["SI"][:], cn["CO"][:], kc_, kc_, tI, tF, ktmp, 32)
            o("dve", lambda e: e.tensor_tensor(out=cn["AR"][:], in0=cn["MAG"][:], in1=cn["CO"][:], op=ALU.mult))
            o("dve", lambda e: e.tensor_tensor(out=cn["AI"][:], in0=cn["MAG"][:], in1=cn["SI"][:], op=ALU.mult))
            o("dve", lambda e: e.tensor_tensor(out=cn["T1"][:], in0=LR, in1=LR, op=ALU.mult))
            o("dve", lambda e: e.tensor_tensor(out=cn["T2"][:], in0=LI, in1=LI, op=ALU.mult))
            o("dve", lambda e: e.tensor_tensor(out=cn["T1"][:], in0=cn["T1"][:], in1=cn["T2"][:], op=ALU.add))
            o("dve", lambda e: e.reciprocal(out=cn["RD"][:], in_=cn["T1"][:]))
            o("dve", lambda e: e.tensor_scalar(out=cn["T1"][:], in0=cn["AR"][:], scalar1=-1.0, scalar2=None, op0=ALU.add))
            o("dve", lambda e: e.tensor_tensor(out=cn["CR"][:], in0=cn["T1"][:], in1=LR, op=ALU.mult))
            o("dve", lambda e: e.tensor_tensor(out=cn["T2"][:], in0=cn["AI"][:], in1=LI, op=ALU.mult))
            o("dve", lambda e: e.tensor_tensor(out=cn["CR"][:], in0=cn["CR"][:], in1=cn["T2"][:], op=ALU.add))
            o("dve", lambda e: e.tensor_tensor(out=cn["CR"][:], in0=cn["CR"][:], in1=cn["RD"][:], op=ALU.mult))
            o("dve", lambda e: e.tensor_tensor(out=cn["CI"][:], in0=cn["AI"][:], in1=LR, op=ALU.mult))
            o("dve", lambda e: e.tensor_tensor(out=cn["T2"][:], in0=cn["T1"][:], in1=LI, op=ALU.mult))
            o("dve", lambda e: e.tensor_tensor(out=cn["CI"][:], in0=cn["CI"][:], in1=cn["T2"][:], op=ALU.subtract))
            o("dve", lambda e: e.tensor_tensor(out=cn["CI"][:], in0=cn["CI"][:], in1=cn["RD"][:], op=ALU.mult))
            IOT = sb(s2, nc, "IOT", [128, TW], F32)
            o("pool", lambda e: e.iota(IOT[:], [[1, TW]], base=1, channel_multiplier=0, allow_small_or_imprecise_dtypes=True))
            TB = [[sb(s2, nc, "TB%d_%d" % (k, j), [128, TW], F32) for j in range(4)] for k in range(4)]
            kTB = [[K() for _ in range(4)] for _ in range(4)]
            XA = sb(s2, nc, "XA", [128, TW], F32); kXA = K()
            Pt = [sb(s2, nc, "Pt%d" % i, [128, TW], F32) for i in range(6)]; kPt = [K() for _ in range(6)]
            Qr = sb(s2, nc, "Qr", [128, TW], F32); Qi = sb(s2, nc, "Qi", [128, TW], F32); kQ = [K(), K()]
            S16 = [sb(s2, nc, "S16_%d" % i, [128, TW], BF16) for i in range(2)]; kS16 = [K(), K()]
            BCm = sb(s2, nc, "BCm", [128, 4, 4, 128], BF16); kBC = K()
            SE = sb(s2, nc, "SE", [128, 32, 2], F32); kSE = K()
            SN = sb(s2, nc, "SN", [128, 2, NS], F32); kSN = K()
            OUT16 = [sb(s2, nc, "OUT16_%d" % i, [NS, 256], F32) for i in range(2)]; kO16 = [K(), K()]
            RHO = sb(s2, nc, "RHO", [128, TW], F32); kRHO = K()
            fw.op("pool", lambda e: e.memset(SE[:], 0.0), writes=[kSE])
            fw.op("pool", lambda e: e.memset(RHO[:], 1.0), writes=[kRHO])
            nb = 0
            for m in range(KC):
                fw.dma("pool", BCm[:].rearrange("p a b n -> p (a b) n"), d["s5bc"][m].rearrange("a p n -> p a n"), writes=[kBC])
                for k in range(4):
                    stn = 4 * m + k
                    col = slice(stn, stn + 1)
                    TR, TI, CC, SS = TB[k]
                    fw.op("dve", lambda e: e.tensor_scalar(out=XA[:], in0=IOT[:], scalar1=cn["TH"][:, col], scalar2=None, op0=ALU.mult),
                          reads=[kc_], writes=[kXA])
                    trig_tables(c, XA, kXA, SS[:], CC[:], kTB[k][3], kTB[k][2], tI, tF, ktmp, TW)
                    fw.op("dve", lambda e: e.tensor_scalar(out=TR[:], in0=CC[:], scalar1=cn["CR"][:, col], scalar2=None, op0=ALU.mult), reads=[kTB[k][2], kc_], writes=[kTB[k][0]])
                    fw.op("dve", lambda e: e.scalar_tensor_tensor(out=TR[:], in0=SS[:], scalar=cn["CI"][:, col], in1=TR[:], op0=ALU.mult, op1=ALU.add), reads=[kTB[k][3], kTB[k][0], kc_], writes=[kTB[k][0]])
                    fw.op("dve", lambda e: e.tensor_scalar(out=TI[:], in0=CC[:], scalar1=cn["CI"][:, col], scalar2=None, op0=ALU.mult), reads=[kTB[k][2], kc_], writes=[kTB[k][1]])
                    fw.op("dve", lambda e: e.tensor_scalar(out=XA[:], in0=SS[:], scalar1=cn["CR"][:, col], scalar2=None, op0=ALU.mult), reads=[kTB[k][3], kc_], writes=[kXA])
                    fw.op("dve", lambda e: e.tensor_tensor(out=TI[:], in0=TI[:], in1=XA[:], op=ALU.subtract), reads=[kTB[k][1], kXA], writes=[kTB[k][1]])
                for ti in range(NTL):
                    c0 = ti * TW
                    npr = min(TW, NP - c0)
                    yb = 6 + (ti % 2)
                    for k in range(4):
                        stn = 4 * m + k
                        col = slice(stn, stn + 1)
                        TR, TI, CC, SS = TB[k]
                        br, bi = 2 * (nb % 2), 2 * (nb % 2) + 1
                        nb += 1
                        fw.op("pe", lambda e: e.matmul(PS[:, br, 0:TW], BCm[:, k, 0, :], XB[:, m, c0:c0 + TW], start=True, stop=True), reads=[kBC, kXB[m]], writes=[kPS[br]])
                        fw.op("pe", lambda e: e.matmul(PS[:, bi, 0:TW], BCm[:, k, 1, :], XB[:, m, c0:c0 + TW], start=True, stop=True), reads=[kBC, kXB[m]], writes=[kPS[bi]])
                        w = slice(0, npr)
                        fw.op("dve", lambda e: e.tensor_tensor(out=Pt[0][:, w], in0=PS[:, br, w], in1=TR[:, w], op=ALU.mult), reads=[kPS[br], kTB[k][0]], writes=[kPt[0]])
                        fw.op("dve", lambda e: e.tensor_tensor(out=Pt[1][:, w], in0=PS[:, bi, w], in1=TI[:, w], op=ALU.mult), reads=[kPS[bi], kTB[k][1]], writes=[kPt[1]])
                        fw.op("dve", lambda e: e.tensor_tensor(out=Pt[2][:, w], in0=PS[:, bi, w], in1=TR[:, w], op=ALU.mult), reads=[kPS[bi], kTB[k][0]], writes=[kPt[2]])
                        fw.op("dve", lambda e: e.tensor_tensor(out=Pt[3][:, w], in0=PS[:, br, w], in1=TI[:, w], op=ALU.mult), reads=[kPS[br], kTB[k][1]], writes=[kPt[3]])
                        fw.op("pool", lambda e: e.tensor_tensor(out=Pt[0][:, w], in0=Pt[0][:, w], in1=Pt[1][:, w], op=ALU.subtract), reads=[kPt[0], kPt[1]], writes=[kPt[0]])
                        fw.op("pool", lambda e: e.tensor_tensor(out=Pt[2][:, w], in0=Pt[2][:, w], in1=Pt[3][:, w], op=ALU.add), reads=[kPt[2], kPt[3]], writes=[kPt[2]])
                        fw.op("dve", lambda e: e.tensor_tensor_scan(Qr[:, w], cn["MAG"][:, col].to_broadcast([128, npr]), Pt[0][:, w], SE[:, stn, 0:1], ALU.mult, ALU.add), reads=[kc_, kPt[0], kSE], writes=[kQ[0]])
                        fw.op("dve", lambda e: e.tensor_tensor_scan(Qi[:, w], cn["MAG"][:, col].to_broadcast([128, npr]), Pt[2][:, w], SE[:, stn, 1:2], ALU.mult, ALU.add), reads=[kc_, kPt[2], kSE], writes=[kQ[1]])
                        fw.op("dve", lambda e: e.tensor_tensor(out=Pt[4][:, w], in0=CC[:, w], in1=Qr[:, w], op=ALU.mult), reads=[kTB[k][2], kQ[0]], writes=[kPt[4]])
                        fw.op("pool", lambda e: e.tensor_tensor(out=Pt[5][:, w], in0=SS[:, w], in1=Qi[:, w], op=ALU.mult), reads=[kTB[k][3], kQ[1]], writes=[kPt[5]])
                        fw.op("dve", lambda e: e.tensor_tensor(out=S16[0][:, w], in0=Pt[4][:, w], in1=Pt[5][:, w], op=ALU.subtract), reads=[kPt[4], kPt[5]], writes=[kS16[0]])
                        fw.op("pool", lambda e: e.tensor_tensor(out=Pt[1][:, w], in0=SS[:, w], in1=Qr[:, w], op=ALU.mult), reads=[kTB[k][3], kQ[0], kPt[1]], writes=[kPt[1]])
                        fw.op("dve", lambda e: e.tensor_tensor(out=Pt[3][:, w], in0=CC[:, w], in1=Qi[:, w], op=ALU.mult), reads=[kTB[k][2], kQ[1], kPt[3]], writes=[kPt[3]])
                        fw.op("dve", lambda e: e.scalar_tensor_tensor(out=S16[1][:, w], in0=Pt[1][:, w], scalar=-1.0, in1=Pt[3][:, w], op0=ALU.mult, op1=ALU.subtract),
                              reads=[kPt[1], kPt[3]], writes=[kS16[1]])
                        L = npr - 1
                        fw.op("dve", lambda e: e.tensor_tensor(out=SE[:, stn, 0:1], in0=Pt[4][:, L:L + 1], in1=Pt[5][:, L:L + 1], op=ALU.subtract), reads=[kPt[4], kPt[5], kSE], writes=[kSE])
                        fw.op("dve", lambda e: e.tensor_tensor(out=SE[:, stn, 1:2], in0=Pt[1][:, L:L + 1], in1=Pt[3][:, L:L + 1], op=ALU.add), reads=[kPt[1], kPt[3], kSE], writes=[kSE])
                        if npr < TW:
                            ws = slice(npr, TW)
                            fw.op("dve", lambda e: e.tensor_scalar(out=Pt[0][:, ws], in0=PS[:, bi, ws], scalar1=cn["CI"][:, col], scalar2=None, op0=ALU.mult), reads=[kPS[bi], kc_, kPt[0]], writes=[kPt[0]])
                            fw.op("dve", lambda e: e.scalar_tensor_tensor(out=Pt[0][:, ws], in0=PS[:, br, ws], scalar=cn["CR"][:, col], in1=Pt[0][:, ws], op0=ALU.mult, op1=ALU.subtract), reads=[kPS[br], kPt[0], kc_], writes=[kPt[0]])
                            fw.op("dve", lambda e: e.tensor_scalar(out=Pt[2][:, ws], in0=PS[:, br, ws], scalar1=cn["CI"][:, col], scalar2=None, op0=ALU.mult), reads=[kPS[br], kc_, kPt[2]], writes=[kPt[2]])
                            fw.op("dve", lambda e: e.scalar_tensor_tensor(out=Pt[2][:, ws], in0=PS[:, bi, ws], scalar=cn["CR"][:, col], in1=Pt[2][:, ws], op0=ALU.mult, op1=ALU.add), reads=[kPS[bi], kPt[2], kc_], writes=[kPt[2]])
                            s0r, s0i = S0[:, stn, :, 0], S0[:, stn, :, 1]
                            fw.op("dve", lambda e: e.scalar_tensor_tensor(out=Pt[0][:, ws], in0=s0r, scalar=cn["AR"][:, col], in1=Pt[0][:, ws], op0=ALU.mult, op1=ALU.add), reads=[kS0, kPt[0], kc_], writes=[kPt[0]])
                            fw.op("dve", lambda e: e.tensor_scalar(out=Pt[1][:, ws], in0=s0i, scalar1=cn["AI"][:, col], scalar2=None, op0=ALU.mult), reads=[kS0, kc_, kPt[1]], writes=[kPt[1]])
                            fw.op("dve", lambda e: e.tensor_tensor(out=SN[:, 0, :], in0=Pt[0][:, ws], in1=Pt[1][:, ws], op=ALU.subtract), reads=[kPt[0], kPt[1]], writes=[kSN])
                            fw.op("dve", lambda e: e.scalar_tensor_tensor(out=Pt[2][:, ws], in0=s0i, scalar=cn["AR"][:, col], in1=Pt[2][:, ws], op0=ALU.mult, op1=ALU.add), reads=[kS0, kPt[2], kc_], writes=[kPt[2]])
                            fw.op("dve", lambda e: e.scalar_tensor_tensor(out=SN[:, 1, :], in0=s0r, scalar=cn["AI"][:, col], in1=Pt[2][:, ws], op0=ALU.mult, op1=ALU.add), reads=[kS0, kPt[2], kc_], writes=[kSN])
                            fw.op("act", lambda e: e.activation(out=S16[0][:, ws], in_=SN[:, 0, :], func=AF.Identity), reads=[kSN, kS16[0]], writes=[kS16[0]])
                            fw.op("act", lambda e: e.activation(out=S16[1][:, ws], in_=SN[:, 1, :], func=AF.Identity, scale=-1.0), reads=[kSN, kS16[1]], writes=[kS16[1]])
                            for r_ in range(2):
                                transpose_to(c, PS[0:NS, 5, r_ * 128:(r_ + 1) * 128], SN[:, r_, :], [kSN], [kPS[5]], IDF[:, :])
                            oi = stn % 2
                            fw.op("act", lambda e: e.activation(out=OUT16[oi][:, :].rearrange("b (p r) -> b r p", r=2),
                                                                in_=PS[0:NS, 5, 0:256].rearrange("b (r p) -> b r p", r=2), func=AF.Identity),
                                  reads=[kPS[5]], writes=[kO16[oi]])
                            fw.dma("sp", d["ss5"][:, stn * 256:(stn + 1) * 256], OUT16[oi][:, :], reads=[kO16[oi]])
                        fw.op("pe", lambda e: e.matmul(PS[:, yb, 0:TW], BCm[:, k, 2, :], S16[0][:], start=(k == 0), stop=False), reads=[kBC, kS16[0]], writes=[kPS[yb]])
                        fw.op("pe", lambda e: e.matmul(PS[:, yb, 0:TW], BCm[:, k, 3, :], S16[1][:], start=False, stop=(k == 3)), reads=[kBC, kS16[1]], writes=[kPS[yb]])
                    fw.op("dve", lambda e: e.scalar_tensor_tensor(out=XA[:], in0=XF[:, m, c0:c0 + TW], scalar=DSK[:, m:m + 1], in1=PS[:, yb, 0:TW], op0=ALU.mult, op1=ALU.add),
                          reads=[kXF[m][ti], kPS[yb], kP, kXA], writes=[kXA])
                    fw.op("act", lambda e: e.activation(out=ZG[:, m, c0:c0 + TW], in_=XA[:], func=AF.Gelu), reads=[kXA], writes=[kZG[m][ti]])
            fw.dma("sp", d["ps5"].rearrange("(t p) r -> p t r", p=128), SE[:], reads=[kSE])
            fw.barrier()
        with ExitStack() as s3:
            alloc_ln(c, s3)
            WGb = [[sb(s3, nc, "WG%d_%d" % (a, i), [128, KC, 256], BF16) for i in range(2)] for a in range(2)]
            kWG = [[K(), K()], [K(), K()]]
            sgl = [sb(s3, nc, "sgl%d" % i, [128, TW], F32) for i in range(2)]; ksgl = [K(), K()]
            wv = [d["w_glu_out"].rearrange("(kc p) n -> p kc n", p=128), d["w_glu_gate"].rearrange("(kc p) n -> p kc n", p=128)]
            nb = 0
            for mp in range(4):
                sl = mp % 2
                for a in range(2):
                    fw.dma("pool", WGb[a][sl][:], wv[a][:, :, mp * 256:(mp + 1) * 256], writes=[kWG[a][sl]])
                for mm in range(2):
                    m = 2 * mp + mm
                    for ti in range(NTL):
                        cs = slice(ti * TW, (ti + 1) * TW)
                        bo, bg = 2 * (nb % 2), 2 * (nb % 2) + 1
                        nb += 1
                        for a, bk in ((0, bo), (1, bg)):
                            for kc in range(KC):
                                fw.pe_defer = (kc != KC - 1)
                                fw.op("pe", lambda e, kc=kc: e.matmul(PS[:, bk, 0:TW], WGb[a][sl][:, kc, mm * 128:(mm + 1) * 128], ZG[:, kc, cs], start=(kc == 0), stop=(kc == KC - 1)),
                                      reads=[kWG[a][sl], kZG[kc][ti]], writes=[kPS[bk]])
                        ss = nb % 2
                        fw.op("act", lambda e: e.activation(out=sgl[ss][:], in_=PS[:, bg, 0:TW], func=AF.Sigmoid), reads=[kPS[bg]], writes=[ksgl[ss]])
                        fw.op("dve", lambda e: e.tensor_tensor(out=sgl[ss][:], in0=PS[:, bo, 0:TW], in1=sgl[ss][:], op=ALU.mult), reads=[kPS[bo], ksgl[ss]], writes=[ksgl[ss]])
                        fw.op("dve", lambda e: e.scalar_tensor_tensor(out=XF[:, m, cs], in0=XF[:, m, cs], scalar=ALPHA, in1=sgl[ss][:], op0=ALU.mult, op1=ALU.add),
                              reads=[ksgl[ss], kXF[m][ti]], writes=[kXF[m][ti]])
            for ti in range(NTL):
                layer_norm_tile(c, ti, 4)
            fw.barrier()


def sb_attention_sample(c, d, st, QS, kQS, OSB, kOSB):
    nc, fw = c.nc, c.fw
    PS, kPS = c.PS, c.kPS
    NE = NS * 16
    NGp = NS * 4
    nrows4 = c.npool * 32
    ck4 = d["ck"].rearrange("(n r) x -> n (r x)", r=4)
    cv4 = d["cv"].rearrange("(n r) x -> n (r x)", r=4)
    fw.dma("sp", c.Qd, QS[0:NS, :], reads=[kQS], writes=[c.kQd])
    QB = [sb(st, nc, "QB%d" % i, [128, 512], F32) for i in range(2)]; kQB = [K(), K()]
    PTB = sb(st, nc, "PTB", [128, NE], I32); kPT = K()
    PTF = sb(st, nc, "PTF", [128, NE], F32)
    IOQ = sb(st, nc, "IOQ", [128, NGp], F32)
    IDXF = sb(st, nc, "IDXF", [128, NGp], F32)
    IDX = sb(st, nc, "IDX", [128, NGp], I32)
    fw.dma("sp", PTB[:], d["pt"].to_broadcast([128, NE]), writes=[kPT])
    fw.op("dve", lambda e: e.tensor_copy(out=PTF[:], in_=PTB[:]), reads=[kPT], writes=[kPT])
    PT4 = PTF[:].rearrange("p (g q) -> p g q", q=4)
    for q4 in range(4):
        ps_ = slice(32 * q4, 32 * q4 + 32)
        fw.op("pool", lambda e: e.iota(IOQ[ps_, :], [[0, NGp]], base=0, channel_multiplier=1, allow_small_or_imprecise_dtypes=True), reads=[kPT], writes=[kPT])
        fw.op("dve", lambda e: e.scalar_tensor_tensor(out=IDXF[ps_, :], in0=PT4[ps_, :, q4], scalar=32.0, in1=IOQ[ps_, :], op0=ALU.mult, op1=ALU.add),
              reads=[kPT], writes=[kPT])
    fw.op("dve", lambda e: e.tensor_copy(out=IDX[:], in_=IDXF[:]), reads=[kPT], writes=[kPT])
    NSL = 3
    IDc = [sb(st, nc, "IDc%d" % i, [128, 1], I32) for i in range(NSL)]; kID = [K() for _ in range(NSL)]
    PG = [sb(st, nc, "PG%d" % i, [128, 2048], F32) for i in range(NSL)]; kPG = [K() for _ in range(NSL)]
    PRb = [sb(st, nc, "PRb%d" % i, [128, 2048], BF16) for i in range(2)]; kPRb = [K(), K()]
    W_ = NGp * 32
    ZA = sb(st, nc, "ZA", [128, W_], F32); kZA = K()
    EA = sb(st, nc, "EA", [128, W_], F32); kEA = K()
    SPA = sb(st, nc, "SPA", [128, W_], F32); kSPA = K()
    TT = sb(st, nc, "TT", [128, NGp * 8], F32); kTT = K()
    CP = sb(st, nc, "CP", [128, NGp * 8], F32); kCP = K()
    TG = sb(st, nc, "TG", [128, NGp * 8], F32); kTG = K()
    STRF = sb(st, nc, "STRF", [128, 128], F32); kTF = K()
    fw.op("pool", lambda e: e.affine_select(out=STRF[:], in_=c.onesf[:], pattern=[[-1, 128]], base=0, channel_multiplier=1,
                                            compare_op=ALU.is_gt, fill=0.0), reads=[c.kconst], writes=[kTF])
    OH = sb(st, nc, "OH", [128, NS, NS], BF16); kOH = K()
    fw.op("pool", lambda e: e.memset(OH[:], 0.0), writes=[kOH])
    for b_ in range(NS):
        fw.op("pool", lambda e, b_=b_: e.memset(OH[:, b_, b_:b_ + 1], 1.0), reads=[kOH], writes=[kOH])
    n = 0
    for G in range(NGp):
        b = G // 4
        sl = n % NSL
        n += 1
        if G % 4 == 0:
            fw.dma("sp", QB[b % 2][:], c.Qd[b:b + 1, :].to_broadcast([128, 512]), reads=[c.kQd], writes=[kQB[b % 2]])
        fw.op("dve", lambda e: e.tensor_copy(out=IDc[sl][:], in_=IDX[:, G:G + 1]), reads=[kPT], writes=[kID[sl]])
        fw.gather(PG[sl][:, :], ck4, IDc[sl][:, :], nrows4, reads=[kID[sl]], writes=[kPG[sl]])
        P3 = PG[sl][:].rearrange("p (r x) -> p r x", r=4)
        fw.op("dve", lambda e: e.tensor_tensor(out=P3, in0=P3, in1=QB[b % 2][:].unsqueeze(1).to_broadcast([128, 4, 512]), op=ALU.mult),
              reads=[kPG[sl], kQB[b % 2]], writes=[kPG[sl]])
        fw.op("dve", lambda e: e.tensor_reduce(out=ZA[:, G * 32:(G + 1) * 32], in_=PG[sl][:].rearrange("p (rh d) -> p rh d", d=64), axis=AX.X, op=ALU.add),
              reads=[kPG[sl]], writes=[kZA])
    fw.op("dve", lambda e: e.scalar_tensor_tensor(out=ZA[:].rearrange("p (e h) -> p e h", h=8), in0=ZA[:].rearrange("p (e h) -> p e h", h=8), scalar=0.125,
                                                  in1=c.sbb[:, :].unsqueeze(1).to_broadcast([128, NGp * 4, 8]), op0=ALU.mult, op1=ALU.add),
          reads=[kZA, c.kconst], writes=[kZA])
    fw.op("act", lambda e: e.activation(out=EA[:], in_=ZA[:], func=AF.Exp), reads=[kZA], writes=[kEA])
    fw.op("act", lambda e: e.activation(out=SPA[:], in_=EA[:], func=AF.Ln, bias=c.cst[:, 1:2]), reads=[kEA, c.kconst], writes=[kSPA])
    S4 = SPA[:].rearrange("p (g r h) -> p g r h", r=4, h=8)
    T3 = TT[:].rearrange("p (g h) -> p g h", h=8)
    fw.op("dve", lambda e: e.tensor_tensor(out=T3, in0=S4[:, :, 0, :], in1=S4[:, :, 1, :], op=ALU.add), reads=[kSPA], writes=[kTT])
    fw.op("dve", lambda e: e.tensor_tensor(out=T3, in0=T3, in1=S4[:, :, 2, :], op=ALU.add), reads=[kSPA, kTT], writes=[kTT])
    fw.op("dve", lambda e: e.tensor_tensor(out=T3, in0=T3, in1=S4[:, :, 3, :], op=ALU.add), reads=[kSPA, kTT], writes=[kTT])
    fw.op("pe", lambda e: e.matmul(PS[:, 0, :], STRF[:], TT[:], start=True, stop=True), reads=[kTF, kTT], writes=[kPS[0]])
    fw.op("pe", lambda e: e.matmul(PS[:, 1, :], c.onesf[:], TT[:], start=True, stop=True), reads=[c.kconst, kTT], writes=[kPS[1]])
    fw.op("act", lambda e: e.activation(out=CP[:], in_=PS[:, 0, :], func=AF.Identity), reads=[kPS[0]], writes=[kCP])
    fw.op("act", lambda e: e.activation(out=TG[:], in_=PS[:, 1, :], func=AF.Identity), reads=[kPS[1]], writes=[kTG])
    C4 = CP[:].rearrange("p (b g h) -> p b g h", g=4, h=8)
    G4 = TG[:].rearrange("p (b g h) -> p b g h", g=4, h=8)
    RUN = sb(st, nc, "RUN", [128, NS, 8], F32); kRUN = K()
    fw.op("pool", lambda e: e.memset(RUN[:], 0.0), writes=[kRUN])
    for g_ in range(2, -1, -1):
        fw.op("dve", lambda e: e.tensor_tensor(out=RUN[:], in0=RUN[:], in1=G4[:, :, g_ + 1, :], op=ALU.add), reads=[kRUN, kTG], writes=[kRUN])
        fw.op("dve", lambda e: e.tensor_tensor(out=C4[:, :, g_, :], in0=C4[:, :, g_, :], in1=RUN[:], op=ALU.add), reads=[kRUN, kCP], writes=[kCP])
    C3 = CP[:].rearrange("p (g h) -> p g h", h=8)
    fw.op("dve", lambda e: e.tensor_tensor(out=S4[:, :, 3, :], in0=S4[:, :, 3, :], in1=C3, op=ALU.add), reads=[kSPA, kCP], writes=[kSPA])
    for r_ in (2, 1, 0):
        fw.op("dve", lambda e, r_=r_: e.tensor_tensor(out=S4[:, :, r_, :], in0=S4[:, :, r_, :], in1=S4[:, :, r_ + 1, :], op=ALU.add), reads=[kSPA], writes=[kSPA])
    fw.op("act", lambda e: e.activation(out=SPA[:], in_=SPA[:], func=AF.Exp, scale=-1.0), reads=[kSPA], writes=[kSPA])
    fw.op("dve", lambda e: e.tensor_tensor(out=EA[:], in0=EA[:], in1=SPA[:], op=ALU.mult), reads=[kEA, kSPA], writes=[kEA])
    for G in range(NGp):
        b = G // 4
        sl = n % NSL
        n += 1
        fw.op("dve", lambda e: e.tensor_copy(out=IDc[sl][:], in_=IDX[:, G:G + 1]), reads=[kPT], writes=[kID[sl]])
        fw.gather(PG[sl][:, :], cv4, IDc[sl][:, :], nrows4, reads=[kID[sl]], writes=[kPG[sl]])
        ps = G % 2
        fw.op("dve", lambda e: e.tensor_tensor(out=PRb[ps][:].rearrange("p (rh d) -> p rh d", d=64), in0=PG[sl][:].rearrange("p (rh d) -> p rh d", d=64),
                                               in1=EA[:, G * 32:(G + 1) * 32].unsqueeze(2).to_broadcast([128, 32, 64]), op=ALU.mult),
              reads=[kPG[sl], kEA], writes=[kPRb[ps]])
        for r_ in range(4):
            fw.op("pe", lambda e, r_=r_: e.matmul(PS[0:NS, 4, :], OH[:, b, :], PRb[ps][:, r_ * 512:(r_ + 1) * 512], start=(G == 0 and r_ == 0), stop=(G == NGp - 1 and r_ == 3)),
                  reads=[kPRb[ps], kOH], writes=[kPS[4]])
    OT = sb(st, nc, "OTs", [NS, 512], F32); kOT = K()
    fw.op("act", lambda e: e.activation(out=OT[:], in_=PS[0:NS, 4, :], func=AF.Identity), reads=[kPS[4]], writes=[kOT])
    for c4 in range(4):
        transpose_to(c, PS[:, 5, c4 * NS:(c4 + 1) * NS], OT[0:NS, c4 * 128:(c4 + 1) * 128], [kOT], [kPS[5]], c.IDF[0:NS, 0:NS])
    fw.op("act", lambda e: e.activation(out=OSB[:, :, NP:NT], in_=PS[:, 5, 0:4 * NS].rearrange("p (c b) -> p c b", c=4), func=AF.Identity),
          reads=[kPS[5]], writes=[kOSB[c4][4] for c4 in range(4)])


def build(stage=99, dbg=False, npool=2560):
    nc = bass.Bass("TRN2", target_bir_lowering=False)
    c = Ctx()
    c.nc = nc

    DECL.clear()

    def din(name, shape, dt=F32):
        DECL.append(name)
        return nc.dram_tensor(name, list(shape), dt, kind="ExternalInput").ap()

    def dout(name, shape, dt=F32):
        return nc.dram_tensor(name, list(shape), dt, kind="ExternalOutput").ap()

    xT = din("xT", [D, NT])
    lngT = din("lngT", [128, 6 * KC])
    lnbT = din("lnbT", [128, 6 * KC])
    fw_ = {}
    for nm in ("ffn1_wg", "ffn1_wu", "ffn2_wg", "ffn2_wu"):
        fw_[nm] = din(nm, [2, D, DFF])
    for nm in ("ffn1_wd", "ffn2_wd"):
        fw_[nm] = din(nm, [2, DFF, D])
    yT = dout("yT", [D, NT])
    d = {}
    c.dbg = dbg
    if stage >= 2:
        d["w_in"] = din("w_in", [D, INC])
        d["sbb"] = din("sbb", [128, 8])
        d["pk"] = dout("pk", [NP, 512]); d["pv"] = dout("pv", [NP, 512])
        d["sk"] = dout("sk", [NS, 512]); d["sv"] = dout("sv", [NS, 512])
        if dbg:
            d["dbg_osb"] = dout("dbg_osb", [128, 4, NT], BF16)
    c.npool = npool
    if stage >= 4:
        d["ck"] = din("ck", [npool * 128, 512]); d["cv"] = din("cv", [npool * 128, 512])
        d["pt"] = din("pt", [1, NS * 16], I32)
        c.Qd = nc.dram_tensor("Qd", [NS, 512], F32, kind="Internal").ap(); c.kQd = K()
    if stage >= 5:
        c.TOKd = nc.dram_tensor("TOKd", [TW, 3, 512], BF16, kind="Internal").ap()
        c.Yd = nc.dram_tensor("Yd", [4, 2, TW, 64], BF16, kind="Internal").ap()
        c.kTOKd = K(); c.kYd = K()
        d["w_w2"] = din("w_w2", [64, 512]); d["w_a2"] = din("w_a2", [64, 512]); d["w_g2"] = din("w_g2", [128, 512])
        d["pcol"] = din("pcol", [128, 48]); d["gng"] = din("gng", [128, 512]); d["gnb"] = din("gnb", [128, 512])
        d["sshift0"] = din("sshift0", [NS, RWC]); d["swkv0"] = din("swkv0", [NS, 8, 64, 64])
        d["pwkv"] = dout("pwkv", [8, 64, 64]); d["swkv"] = dout("swkv", [NS, 8, 64, 64])
        d["pshift"] = dout("pshift", [1, RWC]); d["sshift"] = dout("sshift", [NS, RWC])
        d["w_out"] = din("w_out", [D, D])
    if stage >= 8:
        d["s5prm"] = din("s5prm", [128, 3, 32]); d["dskip"] = din("dskip", [128, 8])
        d["s5_0"] = din("s5_0", [NS, 4096, 2]); d["s5bc"] = din("s5bc", [8, 16, 128, 128])
        d["w_glu_out"] = din("w_glu_out", [D, D]); d["w_glu_gate"] = din("w_glu_gate", [D, D])
        d["ps5"] = dout("ps5", [4096, 2]); d["ss5"] = dout("ss5", [NS, 8192])

    with ExitStack() as st:
        fw = FW(nc, st)
        c.fw = fw
        c.XF = sb(st, nc, "XF", [128, KC, NT], F32)
        c.kXF = [[K() for _ in range(NTL)] for _ in range(KC)]
        c.PS = st.enter_context(nc.psum_tensor("PS", [128, 8, 512], F32))
        c.kPS = [K() for _ in range(8)]
        fw.psum_keys = set(id(k) for k in c.kPS)
        c.onesf = sb(st, nc, "onesf", [128, 128], F32)
        c.epsc = sb(st, nc, "epsc", [128, 1], F32)
        c.lng = sb(st, nc, "lng", [128, 6 * KC], F32)
        c.lnb = sb(st, nc, "lnb", [128, 6 * KC], F32)
        c.kconst = K()
        c.nsq = 0
        c.nln = 0
        fw.op("pool", lambda e: e.memset(c.onesf[:], 1.0), writes=[c.kconst])
        c.ones16 = sb(st, nc, "ones16", [128, 128], BF16)
        fw.op("pool", lambda e: e.memset(c.ones16[:], 1.0), writes=[c.kconst])
        fw.op("pool", lambda e: e.memset(c.epsc[:], LN_EPS), writes=[c.kconst])
        fw.dma("sp", c.lng[:], lngT, writes=[c.kconst])
        fw.dma("sp", c.lnb[:], lnbT, writes=[c.kconst])
        xv = xT.rearrange("(kc p) t -> p kc t", p=128)
        for kc in range(KC):
            fw.dma("sp", c.XF[:, kc, :], xv[:, kc, :], writes=c.kXF[kc])

        if not SKIP_FFN:
            ffn_ln(c, fw_["ffn1_wg"][0], fw_["ffn1_wu"][0], fw_["ffn1_wd"][0], 0, "a")
        fw.barrier()
        if stage >= 2:
            c.cst = sb(st, nc, "cst", [128, 4], F32)
            c.sbb = sb(st, nc, "sbb_s", [128, 8], F32)
            fw.op("pool", lambda e: e.memset(c.cst[:, 0:1], LN_EPS), writes=[c.kconst])
            fw.op("pool", lambda e: e.memset(c.cst[:, 1:2], 1.0), writes=[c.kconst])
            fw.op("pool", lambda e: e.memset(c.cst[:, 2:3], GN_EPS), writes=[c.kconst])
            fw.op("pool", lambda e: e.memset(c.cst[:, 3:4], 1.5707963267948966), writes=[c.kconst])
            fw.dma("sp", c.sbb[:], d["sbb"], writes=[c.kconst])
            onesb = sb(st, nc, "onesb", [128, 512], BF16)
            fw.op("pool", lambda e: e.memset(onesb[:], 1.0), writes=[c.kconst])
            c.ONESB = onesb[:, 0:128]
            c.onesb = onesb
            c.IDF = sb(st, nc, "IDF", [128, 128], F32)
            fw.op("pool", lambda e: e.memset(c.IDF[:], 1.0), writes=[c.kconst])
            fw.op("pool", lambda e: e.affine_select(out=c.IDF[:], in_=c.IDF[:], pattern=[[-1, 128]], base=0, channel_multiplier=1,
                                                    compare_op=ALU.is_equal, fill=0.0), reads=[c.kconst], writes=[c.kconst])
            mixer_even(c, d, stage)
        if stage >= 7 and not SKIP_FFN:
            ffn_ln(c, fw_["ffn2_wg"][0], fw_["ffn2_wu"][0], fw_["ffn2_wd"][0], 2, "b")
            fw.barrier()
            ffn_ln(c, fw_["ffn1_wg"][1], fw_["ffn1_wu"][1], fw_["ffn1_wd"][1], 3, "c")
            fw.barrier()
        if stage >= 8:
            s5_mixer(c, d)
        if stage >= 9 and not SKIP_FFN:
            ffn_ln(c, fw_["ffn2_wg"][1], fw_["ffn2_wu"][1], fw_["ffn2_wd"][1], 5, "d")
            fw.barrier()

        yv = yT.rearrange("(kc p) t -> p kc t", p=128)
        for kc in range(KC):
            fw.dma("sp", yv[:, kc, :], c.XF[:, kc, :], reads=c.kXF[kc])
        fw.finish()
        print("instructions:", fw.ninst, {e: fw.cnt[e] for e in fw.cnt})
    return nc


DECL = []


def make_in_maps(inp, n_cores=8):
    f = np.float32
    maps = []
    ln_g = np.ascontiguousarray(np.asarray(inp["ln_g"], f).reshape(6, KC, 128).transpose(2, 0, 1).reshape(128, 6 * KC))
    ln_b = np.ascontiguousarray(np.asarray(inp["ln_b"], f).reshape(6, KC, 128).transpose(2, 0, 1).reshape(128, 6 * KC))
    shared = {"lngT": ln_g, "lnbT": ln_b}
    for nm in ("ffn1_wg", "ffn1_wu", "ffn1_wd", "ffn2_wg", "ffn2_wu", "ffn2_wd"):
        shared[nm] = np.asarray(inp[nm], f)
    shared["w_in"] = np.asarray(inp["w_in_even"][0], f)
    for nm in ("w_w2", "w_a2", "w_g2"):
        shared[nm] = np.asarray(inp[nm][0], f)
    shared["w_out"] = np.asarray(inp["w_out_even"][0], f)
    def colT(v, n):
        return np.asarray(v, f).reshape(n, 128).T
    pc = np.zeros((128, 48), f)
    pc[:, 0:14] = colT(inp["mu_shift"][0], 14)
    pc[:, 14:18] = colT(inp["w0"][0], 4); pc[:, 18:22] = colT(inp["a0"][0], 4)
    pc[:, 22:26] = colT(inp["k_k"][0], 4); pc[:, 26:30] = colT(inp["k_a"][0], 4)
    pc[:, 30:34] = colT(inp["r_k"][0].reshape(-1), 4)
    shared["pcol"] = pc
    shared["gng"] = np.ascontiguousarray(np.broadcast_to(np.asarray(inp["gn_g"][0], f)[None, :], (128, 512)))
    shared["gnb"] = np.ascontiguousarray(np.broadcast_to(np.asarray(inp["gn_b"][0], f)[None, :], (128, 512)))
    lre = np.asarray(inp["lam_re"][0], f); lim = np.asarray(inp["lam_im"][0], f); ldt = np.asarray(inp["log_dt"][0], f)
    prm = np.zeros((128, 3, 32), f)
    prm[:, 0, :] = lre.reshape(32, 128).T; prm[:, 1, :] = lim.reshape(32, 128).T
    prm[:, 2, :] = np.repeat(ldt, 64).reshape(32, 128).T
    shared["s5prm"] = prm
    shared["dskip"] = np.ascontiguousarray(np.asarray(inp["d_skip"][0], f).reshape(8, 128).T)
    bre = np.asarray(inp["b_re"][0], f); bim = np.asarray(inp["b_im"][0], f)
    cre = np.asarray(inp["c_re"][0], f); cim = np.asarray(inp["c_im"][0], f)
    bc = np.zeros((8, 4, 4, 128, 128), f)
    for g in range(64):
        m_, gl = g // 8, g % 8
        k_, g2 = (g % 8) // 2, g % 2
        rows = slice(gl * 16, gl * 16 + 16); cols = slice(g2 * 64, g2 * 64 + 64)
        bc[m_, k_, 0][rows, cols] = bre[g].T
        bc[m_, k_, 1][rows, cols] = bim[g].T
        bc[m_, k_, 2][cols, rows] = cre[g].T
        bc[m_, k_, 3][cols, rows] = cim[g].T
    shared["s5bc"] = bc.reshape(8, 16, 128, 128)
    shared["w_glu_out"] = np.asarray(inp["w_glu_out"][0], f); shared["w_glu_gate"] = np.asarray(inp["w_glu_gate"][0], f)
    shared["sbb"] = np.ascontiguousarray(np.broadcast_to(np.asarray(inp["sb_bias"][0], f)[None, :], (128, 8)))
    for cidx in range(n_cores):
        m = dict(shared)
        xp = np.asarray(inp["x_prompt"][cidx], f)
        xs = np.asarray(inp["x_sample"][cidx * NS:(cidx + 1) * NS, 0], f)
        m["xT"] = np.ascontiguousarray(np.concatenate([xp, xs], axis=0).T)
        sl = slice(cidx * NS, (cidx + 1) * NS)
        m["sshift0"] = np.asarray(inp["state_shift"][0, sl], f)
        m["swkv0"] = np.asarray(inp["state_wkv"][0, sl], f)
        m["pt"] = np.ascontiguousarray(np.asarray(inp["page_table"][sl], np.int32).reshape(1, NS * 16))
        m["ck"] = np.asarray(inp["cache_k_sb"][0], f).reshape(-1, 512)
        m["cv"] = np.asarray(inp["cache_v_sb"][0], f).reshape(-1, 512)
        m["s5_0"] = np.asarray(inp["state_s5"][0, sl], f).reshape(NS, 4096, 2)
        maps.append({k: v for k, v in m.items() if k in DECL})
    return maps


def dev_compare(stage, r, ref, cmp):
    if stage >= 2:
        pp = ref["p_proj"][0]; sp_ = ref["s_proj"][:, 0]
        cmp("pk", r["pk"], pp[:, 512:1024]); cmp("pv", r["pv"], pp[:, 1024:1536])
        cmp("sk", r["sk"], sp_[:, 512:1024]); cmp("sv", r["sv"], sp_[:, 1024:1536])
    if stage >= 4 and "dbg_osb" in r:
        import ml_dtypes
        x_ = r["dbg_osb"]
        if x_.dtype.kind == "V":
            x_ = x_.view(ml_dtypes.bfloat16)
        o = np.asarray(x_).astype(np.float32).transpose(1, 0, 2).reshape(512, NT).T
        cmp("osb_s", o[NP:], ref["s_osb"][:, 0])
    if stage >= 3 and "dbg_osb" in r and False:
        o = np.asarray(r["dbg_osb"]).astype(np.float32).transpose(1, 0, 2).reshape(512, NT).T
        cmp("osb_p", o[:NP], ref["p_osb"][0])
        for qq in range(4):
            cmp("osb_p q%d" % qq, o[qq*512:(qq+1)*512], ref["p_osb"][0][qq*512:(qq+1)*512])
    if stage >= 5:
        cmp("pwkv", r["pwkv"], ref["p_wkv"][0]); cmp("swkv", r["swkv"], ref["s_wkv"])
        cmp("pshift", r["pshift"][0], ref["p_proj"][0, -1, 1536:]); cmp("sshift", r["sshift"], ref["s_proj"][:, 0, 1536:])
    if stage >= 8:
        cmp("ps5", r["ps5"].reshape(64, 64, 2), ref["p_s5"][0]); cmp("ss5", r["ss5"].reshape(NS, 64, 64, 2), ref["s_s5"])
    y = r["yT"].T
    key = {1: "L0_x1", 2: "L0_x1", 3: "L0_x1", 4: "L0_x1", 5: "L0_x1", 6: "L0_x2", 7: "L1_x1", 8: "L1_x2", 9: "L1_x3"}.get(stage, "L1_x3")
    cmp("y_prompt", y[:NP], ref["p_" + key][0])
    cmp("y_sample", y[NP:], ref["s_" + key][:, 0])


def kernel(**inputs):
    n = 8
    npool = int(np.asarray(inputs["cache_k_sb"]).shape[1])
    nc = build(stage=9, dbg=False, npool=npool)
    maps = make_in_maps(inputs, n_cores=n)
    res = run_bass_kernel_spmd(nc, maps, core_ids=list(range(n)))
    R = res.results
    f = np.float32
    yp = np.stack([np.asarray(R[i]["yT"], f).T[:NP] for i in range(n)], axis=0)
    ys = np.concatenate([np.asarray(R[i]["yT"], f).T[NP:] for i in range(n)], axis=0)[:, None, :]
    pk = np.stack([np.asarray(R[i]["pk"], f).reshape(NP, 8, 64) for i in range(n)], axis=0)[None]
    pv = np.stack([np.asarray(R[i]["pv"], f).reshape(NP, 8, 64) for i in range(n)], axis=0)[None]
    pwkv = np.stack([np.asarray(R[i]["pwkv"], f) for i in range(n)], axis=0)[None]
    pshift = np.stack([np.asarray(R[i]["pshift"], f).reshape(RWC) for i in range(n)], axis=0)[None]
    ps5 = np.stack([np.asarray(R[i]["ps5"], f).reshape(64, 64, 2) for i in range(n)], axis=0)[None]
    sk = np.concatenate([np.asarray(R[i]["sk"], f).reshape(NS, 1, 8, 64) for i in range(n)], axis=0)[None]
    sv = np.concatenate([np.asarray(R[i]["sv"], f).reshape(NS, 1, 8, 64) for i in range(n)], axis=0)[None]
    swkv = np.concatenate([np.asarray(R[i]["swkv"], f) for i in range(n)], axis=0)[None]
    sshift = np.concatenate([np.asarray(R[i]["sshift"], f) for i in range(n)], axis=0)[None]
    ss5 = np.concatenate([np.asarray(R[i]["ss5"], f).reshape(NS, 64, 64, 2) for i in range(n)], axis=0)[None]
    return (yp, ys, pk, pv, pwkv, pshift, ps5, sk, sv, swkv, sshift, ss5)
```

```python
from contextlib import ExitStack
import numpy as np
import concourse.bass as bass
import concourse.mybir as mybir
from concourse.bass_utils import run_bass_kernel_spmd

F32 = mybir.dt.float32
BF16 = mybir.dt.bfloat16
I32 = mybir.dt.int32
AF = mybir.ActivationFunctionType
ALU = mybir.AluOpType
AX = mybir.AxisListType

D = 1024
KC = 8
DFF = 2816
JC = 22
NP = 2048
NS = 16
NT = NP + NS
TW = 344
NTL = NT // TW
NG = 3
GW = NT // NG
ALPHA = 4.0 ** 0.25
LN_EPS = 1e-5
GN_EPS = 64e-5
INC = 3328
RWC = 1792
SKIP_FFN = False
import os
VAR = os.environ.get('KVAR', '')


class K:
    __slots__ = ("name", "lw", "rd")

    def __init__(self, name=""):
        self.name = name
        self.lw = None
        self.rd = []


class FW:
    def __init__(self, nc, stack, n_dma_sems=8):
        self.nc = nc
        self.eng = {"pe": nc.tensor, "dve": nc.vector, "act": nc.scalar,
                    "pool": nc.gpsimd, "sp": nc.sync}
        self.sem = {}
        self.cnt = {}
        self.seen = {e: {} for e in self.eng}
        for e in self.eng:
            self.sem[e] = stack.enter_context(nc.semaphore("s_" + e))
            self.cnt[e] = 0
        self.dsem = {}
        self.dcnt = {}
        self.dnext = {}
        for q in ("sp", "pool"):
            self.dsem[q] = [stack.enter_context(nc.semaphore("d_%s_%d" % (q, i)))
                            for i in range(n_dma_sems)]
            self.dcnt[q] = [0] * n_dma_sems
            self.dnext[q] = 0
        self.ninst = 0
        self.dram_writes = []
        self.psum_keys = set()
        self.pe_defer = False
        self._pend_r = []
        self._pend_w = []

    def _semobj(self, sk):
        if isinstance(sk, tuple):
            return self.dsem[sk[0]][sk[1]]
        return self.sem[sk]

    def _wait(self, e, sk, val):
        if sk == "pe" and e == "pe":
            return
        if self.seen[e].get(sk, 0) >= val:
            return
        self.seen[e][sk] = val
        self.eng[e].wait_ge(self._semobj(sk), val)
        self.ninst += 1

    def _deps(self, e, reads, writes):
        need = {}
        for k in reads:
            if k.lw is not None and need.get(k.lw[0], 0) < k.lw[1]:
                need[k.lw[0]] = k.lw[1]
        for k in writes:
            if k.lw is not None and need.get(k.lw[0], 0) < k.lw[1]:
                need[k.lw[0]] = k.lw[1]
            for (sk, v) in k.rd:
                if need.get(sk, 0) < v:
                    need[sk] = v
        for sk, v in need.items():
            self._wait(e, sk, v)

    def _mark(self, sk, val, reads, writes):
        for k in reads:
            k.rd.append((sk, val))
            if len(k.rd) > 16:
                m = {}
                for (s, v) in k.rd:
                    if m.get(s, 0) < v:
                        m[s] = v
                k.rd = list(m.items())
        for k in writes:
            k.lw = (sk, val)
            k.rd = []

    def op(self, e, fn, reads=(), writes=()):
        if e == "pe" and self.pe_defer:
            self._deps(e, reads, writes)
            fn(self.eng[e])
            self._pend_r.extend(reads)
            self._pend_w.extend(writes)
            self.ninst += 1
            return None
        if e == "pe" and self._pend_r:
            reads = list(reads) + self._pend_r
            writes = list(dict.fromkeys(list(writes) + self._pend_w))
            self._pend_r = []
            self._pend_w = []
        if e == "dve" and self.dram_writes and any(id(k) in self.psum_keys for k in reads):
            for (sk, v) in self.dram_writes:
                self._wait(e, sk, v)
            self.dram_writes = []
        self._deps(e, reads, writes)
        ins = fn(self.eng[e])
        self.cnt[e] += 1
        ins.then_inc(self.sem[e], 1)
        self._mark(e, self.cnt[e], reads, writes)
        self.ninst += 1
        return ins

    def _dslot(self, q):
        i = self.dnext[q]
        self.dnext[q] = (i + 1) % len(self.dsem[q])
        if self.dcnt[q][i] > 0:
            self._wait(q, (q, i), self.dcnt[q][i])
        return i

    def dma(self, q, out, in_, reads=(), writes=(), **kw):
        i = self._dslot(q)
        self._deps(q, reads, writes)
        ins = self.eng[q].dma_start(out=out, in_=in_, **kw)
        self.dcnt[q][i] += 16
        ins.then_inc(self.dsem[q][i], 16)
        self._mark((q, i), self.dcnt[q][i], reads, writes)
        self.ninst += 1
        if "DRAM" in str(getattr(out.tensor, "space", "")).upper() or type(out.tensor).__name__.startswith("DRam"):
            self.dram_writes.append(((q, i), self.dcnt[q][i]))
        return ins

    def gather(self, out, in_rows, idx_ap, nrows, reads=(), writes=()):
        q = "pool"
        i = self._dslot(q)
        self._deps(q, reads, writes)
        if getattr(self, "_breg", None) is None or self._breg[0] != nrows:
            self._breg = (nrows, self.nc.gpsimd.to_reg(nrows - 1))
        ins = self.nc.gpsimd.indirect_dma_start(
            out=out, out_offset=None, in_=in_rows,
            in_offset=bass.IndirectOffsetOnAxis(ap=idx_ap, axis=0),
            bounds_check=self._breg[1], oob_is_err=False)
        self.dcnt[q][i] += 16
        ins.then_inc(self.dsem[q][i], 16)
        self._mark((q, i), self.dcnt[q][i], reads, writes)
        self.ninst += 1
        return ins

    def barrier(self):
        for e in self.eng:
            for q in self.dsem:
                for i, v in enumerate(self.dcnt[q]):
                    if v > 0:
                        self._wait(e, (q, i), v)
            for e2 in self.eng:
                if self.cnt[e2] > 0 and not (e2 == e and e == "pe"):
                    self._wait(e, e2, self.cnt[e2])

    def finish(self):
        for q in self.dsem:
            for i, v in enumerate(self.dcnt[q]):
                if v > 0:
                    self._wait("sp", (q, i), v)
        for e in self.eng:
            if e != "sp" and self.cnt[e] > 0:
                self._wait("sp", e, self.cnt[e])


class Ctx:
    pass


def sb(st, nc, name, shape, dt):
    return st.enter_context(nc.sbuf_tensor(name, list(shape), dt))


def ffn_ln(c, wg, wu, wd, lnidx, tag):
    nc, fw = c.nc, c.fw
    XF, kXF = c.XF, c.kXF
    with ExitStack() as st:
        XB = sb(st, nc, "XB" + tag, [128, KC, GW], BF16)
        H = sb(st, nc, "H" + tag, [128, JC, GW], BF16)
        wgb = [sb(st, nc, "wgb%d%s" % (i, tag), [128, KC, 256], BF16) for i in range(2)]
        wub = [sb(st, nc, "wub%d%s" % (i, tag), [128, KC, 256], BF16) for i in range(2)]
        wdb = [sb(st, nc, "wdb%d%s" % (i, tag), [128, JC, 256], BF16) for i in range(2)]
        sgt = [sb(st, nc, "sgt%d%s" % (i, tag), [128, TW], F32) for i in range(2)]
        alloc_ln(c, st)
        kXB = [K() for _ in range(KC)]
        kH = [[K() for _ in range(2)] for _ in range(JC)]
        kwg = [K(), K()]
        kwu = [K(), K()]
        kwd = [K(), K()]
        ksg = [K(), K()]
        wgv = wg.rearrange("(kc p) n -> p kc n", p=128)
        wuv = wu.rearrange("(kc p) n -> p kc n", p=128)
        wdv = wd.rearrange("(j p) n -> p j n", p=128)
        PS, kPS = c.PS, c.kPS
        nsg = 0
        nb = 0
        for g in range(NG):
            c0 = g * GW
            for kc in range(KC):
                fw.op("act", lambda e, kc=kc: e.activation(out=XB[:, kc, :], in_=XF[:, kc, c0:c0 + GW], func=AF.Identity),
                      reads=[kXF[kc][2 * g], kXF[kc][2 * g + 1]], writes=[kXB[kc]])
            for jp in range(JC // 2):
                s = jp % 2
                fw.dma("pool", wgb[s][:], wgv[:, :, jp * 256:(jp + 1) * 256], writes=[kwg[s]])
                fw.dma("pool", wub[s][:], wuv[:, :, jp * 256:(jp + 1) * 256], writes=[kwu[s]])
                for jj in range(2):
                    j = 2 * jp + jj
                    for tl in range(2):
                        bg, bu = 2 * (nb % 2), 2 * (nb % 2) + 1
                        nb += 1
                        cs = slice(tl * TW, (tl + 1) * TW)
                        for kc in range(KC):
                            fw.pe_defer = (kc != KC - 1)
                            fw.op("pe", lambda e, kc=kc: e.matmul(PS[:, bg, 0:TW], wgb[s][:, kc, jj * 128:(jj + 1) * 128],
                                                                  XB[:, kc, cs], start=(kc == 0), stop=(kc == KC - 1)),
                                  reads=[kwg[s], kXB[kc]], writes=[kPS[bg]])
                        for kc in range(KC):
                            fw.pe_defer = (kc != KC - 1)
                            fw.op("pe", lambda e, kc=kc: e.matmul(PS[:, bu, 0:TW], wub[s][:, kc, jj * 128:(jj + 1) * 128],
                                                                  XB[:, kc, cs], start=(kc == 0), stop=(kc == KC - 1)),
                                  reads=[kwu[s], kXB[kc]], writes=[kPS[bu]])
                        ss = nsg % 2
                        nsg += 1
                        fw.op("act", lambda e: e.activation(out=sgt[ss][:], in_=PS[:, bg, 0:TW], func=AF.Silu),
                              reads=[kPS[bg]], writes=[ksg[ss]])
                        fw.op("dve", lambda e: e.scalar_tensor_tensor(out=H[:, j, cs], in0=PS[:, bu, 0:TW], scalar=0.5,
                                                                      in1=sgt[ss][:], op0=ALU.mult, op1=ALU.mult),
                              reads=[kPS[bu], ksg[ss]], writes=[kH[j][tl]])
            for mp in range(KC // 2):
                s = mp % 2
                fw.dma("pool", wdb[s][:], wdv[:, :, mp * 256:(mp + 1) * 256], writes=[kwd[s]])
                for mm in range(2):
                    m = 2 * mp + mm
                    for tl in range(2):
                        by = 4 + (nb % 2)
                        nb += 1
                        cs = slice(tl * TW, (tl + 1) * TW)
                        gc = slice(c0 + tl * TW, c0 + (tl + 1) * TW)
                        for j in range(JC):
                            fw.pe_defer = (j != JC - 1)
                            fw.op("pe", lambda e, j=j: e.matmul(PS[:, by, 0:TW], wdb[s][:, j, mm * 128:(mm + 1) * 128],
                                                                H[:, j, cs], start=(j == 0), stop=(j == JC - 1)),
                                  reads=[kwd[s], kH[j][tl]], writes=[kPS[by]])
                        fw.op("dve", lambda e: e.scalar_tensor_tensor(out=XF[:, m, gc], in0=XF[:, m, gc], scalar=ALPHA,
                                                                      in1=PS[:, by, 0:TW], op0=ALU.mult, op1=ALU.add),
                              reads=[kPS[by], kXF[m][2 * g + tl]], writes=[kXF[m][2 * g + tl]])
            for tl in range(2):
                layer_norm_tile(c, 2 * g + tl, lnidx)


def alloc_ln(c, st):
    c.nln += 1
    c.sq = [sb(st, c.nc, "sq%d_%d" % (i, c.nln), [128, TW], F32) for i in range(2)]
    c.ksq = [K(), K()]
    c.sqb = [sb(st, c.nc, "sqb%d_%d" % (i, c.nln), [128, TW], BF16) for i in range(2)]
    c.ksqb = [K(), K()]
    c.xbb = [sb(st, c.nc, "xbb%d_%d" % (i, c.nln), [128, TW], BF16) for i in range(2)]
    c.kxbb = [K(), K()]
    c.lnt = [sb(st, c.nc, "lnt%d_%d" % (i, c.nln), [128, TW], F32) for i in range(4)]
    c.kln = [K() for _ in range(4)]


def layer_norm_tile(c, ti, lnidx):
    nc, fw = c.nc, c.fw
    XF, kXF, PS, kPS = c.XF, c.kXF, c.PS, c.kPS
    cs = slice(ti * TW, (ti + 1) * TW)
    b1, b2 = 6, 7
    for m in range(KC):
        s = c.nsq % 2
        c.nsq += 1
        fw.op("act", lambda e: e.activation(out=c.sq[s][:], in_=XF[:, m, cs], func=AF.Square),
              reads=[kXF[m][ti]], writes=[c.ksq[s]])
        fw.pe_defer = True
        fw.op("pe", lambda e: e.matmul(PS[:, b1, 0:TW], c.onesf[:], XF[:, m, cs], start=(m == 0), stop=(m == KC - 1)),
              reads=[kXF[m][ti], c.kconst], writes=[kPS[b1]])
        fw.pe_defer = False
        fw.op("pe", lambda e: e.matmul(PS[:, b2, 0:TW], c.onesf[:], c.sq[s][:], start=(m == 0), stop=(m == KC - 1)),
              reads=[c.ksq[s], c.kconst], writes=[kPS[b2]])
    mean, msq, var, rstd = c.lnt
    kln = c.kln
    fw.op("dve", lambda e: e.tensor_scalar(out=mean[:], in0=PS[:, b1, 0:TW], scalar1=1.0 / D, scalar2=None, op0=ALU.mult),
          reads=[kPS[b1]], writes=[kln[0]])
    fw.op("dve", lambda e: e.tensor_tensor(out=msq[:], in0=mean[:], in1=mean[:], op=ALU.mult),
          reads=[kln[0]], writes=[kln[1]])
    fw.op("dve", lambda e: e.scalar_tensor_tensor(out=var[:], in0=PS[:, b2, 0:TW], scalar=1.0 / D, in1=msq[:],
                                                  op0=ALU.mult, op1=ALU.subtract),
          reads=[kPS[b2], kln[1]], writes=[kln[2]])
    fw.op("act", lambda e: e.activation(out=var[:], in_=var[:], func=AF.Sqrt, bias=c.epsc[:, 0:1]),
          reads=[kln[2], c.kconst], writes=[kln[2]])
    fw.op("dve", lambda e: e.reciprocal(out=rstd[:], in_=var[:]), reads=[kln[2]], writes=[kln[3]])
    for m in range(KC):
        s = c.nsq % 2
        c.nsq += 1
        t = c.sq[s]
        fw.op("pool" if m % 2 == 0 else "dve", lambda e: e.tensor_tensor(out=t[:], in0=XF[:, m, cs], in1=mean[:], op=ALU.subtract),
              reads=[kXF[m][ti], kln[0]], writes=[c.ksq[s]])
        fw.op("dve", lambda e: e.tensor_tensor(out=t[:], in0=t[:], in1=rstd[:], op=ALU.mult),
              reads=[c.ksq[s], kln[3]], writes=[c.ksq[s]])
        col = lnidx * KC + m
        fw.op("act", lambda e: e.activation(out=XF[:, m, cs], in_=t[:], func=AF.Identity,
                                            scale=c.lng[:, col:col + 1], bias=c.lnb[:, col:col + 1]),
              reads=[c.ksq[s], c.kconst], writes=[kXF[m][ti]])


def pipeline(streams):
    nst = max(len(b) for s in streams for b in s)
    nmax = max(len(s) for s in streams)
    for t in range(nmax + nst - 1):
        for s in streams:
            for k in range(nst - 1, -1, -1):
                bi = t - k
                if 0 <= bi < len(s) and k < len(s[bi]):
                    s[bi][k]()


def mixer_even(c, d, stage):
    nc, fw = c.nc, c.fw
    XF, kXF, PS, kPS = c.XF, c.kXF, c.PS, c.kPS
    st = ExitStack()
    with st:
        OSB = sb(st, nc, "OSB", [128, 4, NT], BF16)
        kOSB = [[K() for _ in range(5)] for _ in range(4)]
        ORW = sb(st, nc, "ORW", [128, 4, NT], BF16)
        kORW = [K() for _ in range(4)]
        fw.op("pool", lambda e: e.memset(OSB[:, :, NP:NT], 0.0), writes=[kOSB[cc][4] for cc in range(4)])
        QS = sb(st, nc, "QS", [NS, 512], F32); kQS = K()
        win_v = d["w_in"].rearrange("(kc p) n -> p kc n", p=128)
        with ExitStack() as sta:
            QT = sb(sta, nc, "QT", [128, 4, NT], BF16)
            KT = sb(sta, nc, "KT", [128, 4, NT], BF16)
            kQT = [K() for _ in range(4)]
            kKT = [K() for _ in range(4)]
            Vtok = sb(sta, nc, "Vtok", [128, 17, 512], BF16)
            kV = [K() for _ in range(17)]
            with ExitStack() as st2:
                XB = sb(st2, nc, "XBm", [128, KC, NT], BF16)
                kXB = [K() for _ in range(KC)]
                for kc in range(KC):
                    fw.op("act" if kc % 2 else "dve",
                          (lambda e, kc=kc: e.activation(out=XB[:, kc, :], in_=XF[:, kc, :], func=AF.Identity)) if kc % 2 else
                          (lambda e, kc=kc: e.tensor_copy(out=XB[:, kc, :], in_=XF[:, kc, :])),
                          reads=kXF[kc], writes=[kXB[kc]])
                WB = [sb(st2, nc, "WBm%d" % i, [128, KC, 256], BF16) for i in range(2)]
                kWB = [K(), K()]
                stg = [sb(st2, nc, "stg%d" % i, [128, 256], F32) for i in range(2)]
                kstg = [K(), K()]
                nb = 0
                nstg = 0
                for wc in range(6):
                    sl = wc % 2
                    fw.dma("pool", WB[sl][:], win_v[:, :, wc * 256:(wc + 1) * 256], writes=[kWB[sl]])
                    if wc < 4:
                        for oo in range(2):
                            oc = 2 * wc + oo
                            dst, kd = (QT, kQT) if oc < 4 else (KT, kKT)
                            for ti in range(NTL):
                                bk = nb % 2
                                nb += 1
                                cs = slice(ti * TW, (ti + 1) * TW)
                                for kc in range(KC):
                                    fw.pe_defer = (kc != KC - 1)
                                    fw.op("pe", lambda e, kc=kc: e.matmul(PS[:, bk, 0:TW], WB[sl][:, kc, oo * 128:(oo + 1) * 128], XB[:, kc, cs],
                                                                          start=(kc == 0), stop=(kc == KC - 1)),
                                          reads=[kWB[sl], kXB[kc]], writes=[kPS[bk]])
                                fw.op("act", lambda e: e.activation(out=dst[:, oc % 4, cs], in_=PS[:, bk, 0:TW], func=AF.Identity),
                                      reads=[kPS[bk]], writes=[kd[oc % 4]])
                    if wc < 2 and stage >= 4:
                        bk = 2 + (nb % 2)
                        nb += 1
                        for kc in range(KC):
                            fw.pe_defer = (kc != KC - 1)
                            fw.op("pe", lambda e, kc=kc: e.matmul(PS[0:NS, bk, 0:256], XB[:, kc, NP:NT], WB[sl][:, kc, :], start=(kc == 0), stop=(kc == KC - 1)),
                                  reads=[kWB[sl], kXB[kc]], writes=[kPS[bk]])
                        fw.op("act", lambda e: e.activation(out=QS[0:NS, wc * 256:(wc + 1) * 256], in_=PS[0:NS, bk, 0:256], func=AF.Identity), reads=[kPS[bk]], writes=[kQS])
                    if wc >= 2:
                        which = 0 if wc < 4 else 1
                        coff = (wc % 2) * 256
                        for tt in range(17):
                            rows = 128 if tt < 16 else NS
                            bk = 2 + (nb % 2)
                            nb += 1
                            for kc in range(KC):
                                fw.pe_defer = (kc != KC - 1)
                                fw.op("pe", lambda e, kc=kc: e.matmul(PS[0:rows, bk, 0:256], XB[:, kc, tt * 128:tt * 128 + rows], WB[sl][:, kc, :],
                                                                      start=(kc == 0), stop=(kc == KC - 1)),
                                      reads=[kWB[sl], kXB[kc]], writes=[kPS[bk]])
                            ss = nstg % 2
                            nstg += 1
                            fw.op("act", lambda e: e.activation(out=stg[ss][0:rows, :], in_=PS[0:rows, bk, 0:256], func=AF.Identity),
                                  reads=[kPS[bk]], writes=[kstg[ss]])
                            if tt < 16:
                                dstd = (d["pk"] if which == 0 else d["pv"])[tt * 128:(tt + 1) * 128, coff:coff + 256]
                            else:
                                dstd = (d["sk"] if which == 0 else d["sv"])[:, coff:coff + 256]
                            fw.dma("sp", dstd, stg[ss][0:rows, :], reads=[kstg[ss]])
                            if which == 1:
                                fw.op("act", lambda e: e.activation(out=Vtok[0:rows, tt, coff:coff + 256], in_=PS[0:rows, bk, 0:256], func=AF.Identity),
                                      reads=[kPS[bk]], writes=[kV[tt]])
                fw.barrier()
            with ExitStack() as st2:
                if stage >= 3:
                    sb_attention_prompt(c, d, st2, QT, KT, kQT, kKT, Vtok, kV, OSB, kOSB)
                fw.barrier()
            fw.barrier()
        if stage >= 4 and 'nosamp' not in VAR:
            with ExitStack() as st3:
                sb_attention_sample(c, d, st3, QS, kQS, OSB, kOSB)
                fw.barrier()
        if stage >= 5:
            with ExitStack() as st3:
                rwkv_all(c, d, st3, ORW, kORW)
                fw.barrier()
        if stage >= 6:
            with ExitStack() as st3:
                alloc_ln(c, st3)
                WOUT = sb(st3, nc, "WOUT", [128, KC, D], BF16)
                kWO = [K() for _ in range(4)]
                wo_v = d["w_out"].rearrange("(kc p) n -> p kc n", p=128)
                for i in range(4):
                    fw.dma("pool", WOUT[:, :, i * 256:(i + 1) * 256], wo_v[:, :, i * 256:(i + 1) * 256], writes=[kWO[i]])
                nb2 = 0
                for ti in range(NTL):
                    cs = slice(ti * TW, (ti + 1) * TW)
                    for m in range(KC):
                        bk = nb2 % 2
                        nb2 += 1
                        for kc in range(KC):
                            src, ks = (OSB, kOSB[kc % 4]) if kc < 4 else (ORW, [kORW[kc % 4]])
                            fw.op("pe", lambda e, kc=kc, src=src: e.matmul(PS[:, bk, 0:TW], WOUT[:, kc, m * 128:(m + 1) * 128], src[:, kc % 4, cs],
                                                                           start=(kc == 0), stop=(kc == KC - 1)),
                                  reads=[kWO[m // 2]] + list(ks), writes=[kPS[bk]])
                        fw.op("dve", lambda e: e.scalar_tensor_tensor(out=XF[:, m, cs], in0=XF[:, m, cs], scalar=ALPHA, in1=PS[:, bk, 0:TW],
                                                                      op0=ALU.mult, op1=ALU.add),
                              reads=[kPS[bk], kXF[m][ti]], writes=[kXF[m][ti]])
                    layer_norm_tile(c, ti, 1)
                fw.barrier()
        if c.dbg and 'nodbg' not in VAR:
            fw.dma("sp", d["dbg_osb"], OSB[:], reads=[k for kk in kOSB for k in kk])
        fw.barrier()


def sb_attention_prompt(c, d, st, QT, KT, kQT, kKT, Vtok, kV, OSB, kOSB):
    nc, fw = c.nc, c.fw
    PS, kPS = c.PS, c.kPS
    NSTR = 2
    onesb = c.onesb
    c.TRI = sb(st, nc, "TRI", [128, 128], BF16)
    c.STRICT = sb(st, nc, "STRICT", [128, 128], BF16)
    c.MASK = [sb(st, nc, "MASK%d" % i, [128, 512], BF16) for i in range(4)]
    fw.op("pool", lambda e: e.affine_select(out=c.TRI[:], in_=onesb[:, 0:128], pattern=[[-1, 128]], base=0, channel_multiplier=1,
                                            compare_op=ALU.is_ge, fill=0.0), reads=[c.kconst], writes=[c.kconst])
    for i in range(4):
        fw.op("pool", lambda e, i=i: e.affine_select(out=c.MASK[i][:], in_=onesb[:], pattern=[[1, 512]], base=-128 * i, channel_multiplier=-1,
                                                     compare_op=ALU.is_gt, fill=0.0), reads=[c.kconst], writes=[c.kconst])
    Et = [[sb(st, nc, "Et%d_%d" % (s, i), [128, 512], F32) for i in range(3)] for s in range(NSTR)]
    spt = [[sb(st, nc, "spt%d_%d" % (s, i), [128, 512], BF16) for i in range(3)] for s in range(NSTR)]
    e2t = [[sb(st, nc, "e2t%d_%d" % (s, i), [128, 512], F32) for i in range(2)] for s in range(NSTR)]
    wt = [[sb(st, nc, "wt%d_%d" % (s, i), [128, 512], BF16) for i in range(2)] for s in range(NSTR)]
    kEt = [[K() for _ in range(3)] for _ in range(NSTR)]
    kspt = [[K() for _ in range(3)] for _ in range(NSTR)]
    ke2t = [[K() for _ in range(2)] for _ in range(NSTR)]
    kwt = [[K() for _ in range(2)] for _ in range(NSTR)]
    sacc = [sb(st, nc, "sacc%d" % s, [128, 512], BF16) for s in range(NSTR)]
    ksacc = [K() for _ in range(NSTR)]
    streams = []
    for s in range(NSTR):
        blocks = []
        bi = 0
        bA = [4 * s, 4 * s + 1]
        bC = 4 * s + 2
        bO = 4 * s + 3
        for h in range(s, 8, NSTR):
            po = (h % 2) * 64
            ch = h // 2
            for qt in range(4):
                q0 = qt * 512
                nkb = 4 * qt + 4
                for n, kb in enumerate(range(nkb - 1, -1, -1)):
                    first = (n == 0)
                    last = (kb == 0)
                    diag = kb - 4 * qt
                    i3, i2 = bi % 3, bi % 2
                    bi += 1

                    def st1(s=s, h=h, po=po, ch=ch, q0=q0, kb=kb, diag=diag, i3=i3, bA=bA[bi % 2]):
                        fw.op("pe", lambda e: e.matmul(PS[:, bA, :], KT[po:po + 64, ch, kb * 128:(kb + 1) * 128], QT[po:po + 64, ch, q0:q0 + 512],
                                                       start=True, stop=True),
                              reads=[kQT[ch], kKT[ch]], writes=[kPS[bA]])
                        fw.op("act", lambda e: e.activation(out=Et[s][i3][:], in_=PS[:, bA, :], func=AF.Exp, scale=0.125,
                                                            bias=c.sbb[:, h:h + 1]),
                              reads=[kPS[bA], c.kconst], writes=[kEt[s][i3]])
                        fw.op("act", lambda e: e.activation(out=spt[s][i3][:], in_=Et[s][i3][:], func=AF.Ln, bias=c.cst[:, 1:2]),
                              reads=[kEt[s][i3], c.kconst], writes=[kspt[s][i3]])
                        if diag >= 0:
                            fw.op("dve", lambda e: e.tensor_tensor(out=spt[s][i3][:], in0=spt[s][i3][:], in1=c.MASK[diag][:], op=ALU.mult),
                                  reads=[kspt[s][i3], c.kconst], writes=[kspt[s][i3]])
                            fw.op("pool", lambda e: e.tensor_tensor(out=Et[s][i3][:], in0=Et[s][i3][:], in1=c.MASK[diag][:], op=ALU.mult),
                                  reads=[kEt[s][i3], c.kconst], writes=[kEt[s][i3]])

                    def st2(s=s, i3=i3, i2=i2, first=first, last=last, bC=bC):
                        fw.op("pe", lambda e: e.matmul(PS[:, bC, :], c.TRI[:], spt[s][i3][:], start=True, stop=first),
                              reads=[kspt[s][i3], c.kconst], writes=[kPS[bC]])
                        if not first:
                            fw.op("pe", lambda e: e.matmul(PS[:, bC, :], c.ONESB[:], sacc[s][:], start=False, stop=True),
                                  reads=[ksacc[s], c.kconst], writes=[kPS[bC]])
                        fw.op("act", lambda e: e.activation(out=e2t[s][i2][:], in_=PS[:, bC, :], func=AF.Exp, scale=-1.0),
                              reads=[kPS[bC]], writes=[ke2t[s][i2]])
                        if not last:
                            if first:
                                fw.op("pool", lambda e: e.tensor_copy(out=sacc[s][:], in_=spt[s][i3][:]),
                                      reads=[kspt[s][i3]], writes=[ksacc[s]])
                            else:
                                fw.op("pool", lambda e: e.tensor_tensor(out=sacc[s][:], in0=sacc[s][:], in1=spt[s][i3][:], op=ALU.add),
                                      reads=[kspt[s][i3], ksacc[s]], writes=[ksacc[s]])

                    def st3(s=s, h=h, po=po, ch=ch, q0=q0, qt=qt, kb=kb, i3=i3, i2=i2, first=first, last=last, bC=bC, bO=bO):
                        fw.op("dve", lambda e: e.tensor_tensor(out=wt[s][i2][:], in0=Et[s][i3][:], in1=e2t[s][i2][:], op=ALU.mult),
                              reads=[kEt[s][i3], ke2t[s][i2]], writes=[kwt[s][i2]])
                        fw.op("pe", lambda e: e.matmul(PS[po:po + 64, bO, :], Vtok[:, kb, h * 64:(h + 1) * 64], wt[s][i2][:],
                                                       start=first, stop=last),
                              reads=[kV[kb], kwt[s][i2]], writes=[kPS[bO]])
                        if last:
                            fw.op("act", lambda e: e.activation(out=OSB[po:po + 64, ch, q0:q0 + 512], in_=PS[po:po + 64, bO, :], func=AF.Identity),
                                  reads=[kPS[bO]], writes=[kOSB[ch][qt]])
                    blocks.append([st1, st2, st3])
        streams.append(blocks)
    pipeline(streams)


CS = 14


def transpose_to(c, out_ps, in_ap, kin, kout, ident):
    c.fw.op("pe", lambda e: e.transpose(out_ps, in_ap, ident), reads=kin + [c.kconst], writes=kout)


def rwkv_all(c, d, st, ORW, kORW):
    nc, fw = c.nc, c.fw
    XF, kXF, PS, kPS = c.XF, c.kXF, c.PS, c.kPS
    IDF = c.IDF
    ST = sb(st, nc, "ST", [128, 256], F32); kST = K()
    STw = sb(st, nc, "STw", [128, 256], F32); kSTw = K()
    SAY = sb(st, nc, "SAY", [128, 64], BF16); kSAY = K()
    PB = sb(st, nc, "PB", [128, 14, TW + 1], F32); kPB = [K() for _ in range(14)]
    PM = sb(st, nc, "PM", [128, 14, TW], F32); kPM = [K() for _ in range(14)]
    XBt = sb(st, nc, "XBt", [128, KC, TW], BF16); kXBt = [K() for _ in range(KC)]
    WRb = [sb(st, nc, "WRb%d" % i, [128, KC, 256], BF16) for i in range(2)]; kWRb = [K(), K()]
    WW2 = sb(st, nc, "WW2", [128, 512], BF16)
    WA2 = sb(st, nc, "WA2", [128, 512], BF16)
    WG2 = sb(st, nc, "WG2", [128, 512], BF16)
    PC = sb(st, nc, "PC", [128, 48], F32)
    GNG = sb(st, nc, "GNG", [128, 512], F32)
    GNB = sb(st, nc, "GNB", [128, 512], F32)
    BLK = sb(st, nc, "BLK", [128, 128], F32)
    kW = K()
    tmpA = sb(st, nc, "tmpA", [128, TW], F32); ktA = K()
    tmpB = sb(st, nc, "tmpB", [128, TW], F32); ktB = K()
    tmpH = sb(st, nc, "tmpH", [128, TW], BF16); ktH = K()
    NBrow = [sb(st, nc, "NBrow%d" % i, [128, CS, 128], BF16) for i in range(2)]
    KBrow = [sb(st, nc, "KBrow%d" % i, [128, CS, 128], BF16) for i in range(2)]
    Vrow = [sb(st, nc, "Vrow%d" % i, [128, CS, 64], BF16) for i in range(2)]
    kRow = [K(), K()]
    TOK = sb(st, nc, "TOK", [128, 3, 512], BF16); kTOK = [K(), K(), K()]
    LKc = [sb(st, nc, "LKc%d" % i, [128, CS, 4, 2], F32) for i in range(2)]
    RKc = [sb(st, nc, "RKc%d" % i, [128, CS, 4, 2], F32) for i in range(2)]
    kLK = [K(), K()]
    Ybuf = [sb(st, nc, "Ybuf%d" % i, [128, CS, 64], BF16) for i in range(2)]; kY = [K(), K()]
    YTOK = sb(st, nc, "YTOK", [128, 512], BF16); kYT = K()
    YC = sb(st, nc, "YC", [128, 512], F32); kYC = K()
    YS = sb(st, nc, "YS", [128, 512], F32); kYS = K()
    gst = sb(st, nc, "gst", [128, 32], F32); kgst = K()
    SHX = sb(st, nc, "SHX", [NS + 1, RWC], F32); kSHT = K()
    SHTOK = SHX
    SHT = sb(st, nc, "SHT", [128, 14, NS], F32); kSH = K()
    SSH = SHX; kSSH = kSHT
    SLD = sb(st, nc, "SLD", [64, 4, 128], F32); kSLD = K()
    SSTt = SLD; kSST = kSLD
    fw.dma("pool", WW2[0:64, :], d["w_w2"], writes=[kW])
    fw.dma("pool", WA2[64:128, :], d["w_a2"], writes=[kW])
    fw.dma("pool", WG2[:, :], d["w_g2"], writes=[kW])
    fw.dma("pool", PC[:], d["pcol"], writes=[kW])
    fw.dma("pool", GNG[:], d["gng"], writes=[kW])
    fw.dma("pool", GNB[:], d["gnb"], writes=[kW])
    fw.dma("pool", SHTOK[0:NS, :], d["sshift0"], writes=[kSHT])
    fw.op("pool", lambda e: e.memset(BLK[:], 0.0), writes=[kW])
    fw.op("pool", lambda e: e.memset(BLK[0:64, 0:64], 1.0), reads=[kW], writes=[kW])
    fw.op("pool", lambda e: e.memset(BLK[64:128, 64:128], 1.0), reads=[kW], writes=[kW])
    fw.op("pool", lambda e: e.memset(ST[:], 0.0), writes=[kST])
    for i_ in range(2):
        fw.op("pool", lambda e, i_=i_: e.memset(NBrow[i_][:], 0.0), writes=[kRow[i_]])
        fw.op("pool", lambda e, i_=i_: e.memset(KBrow[i_][:], 0.0), writes=[kRow[i_]])
        fw.op("pool", lambda e, i_=i_: e.memset(Vrow[i_][:], 0.0), writes=[kRow[i_]])
        fw.op("pool", lambda e, i_=i_: e.memset(LKc[i_][:], 0.0), writes=[kLK[i_]])
        fw.op("pool", lambda e, i_=i_: e.memset(RKc[i_][:], 0.0), writes=[kLK[i_]])
    fw.op("pool", lambda e: e.memset(PB[:, :, 0:1], 0.0), writes=kPB)
    MU, W0, A0, KKc, KAc, RKp = 0, 14, 18, 22, 26, 30
    for bz in (2, 3, 4, 5, 6, 7):
        fw.op("dve", lambda e, bz=bz: e.memset(PS[:, bz, :], 0.0), writes=[kPS[bz]])
    for m in range(14):
        transpose_to(c, PS[:, 7, m * NS:(m + 1) * NS], SHTOK[0:NS, m * 128:(m + 1) * 128], [kSHT], [kPS[7]], IDF[0:NS, 0:NS])
    fw.op("act", lambda e: e.activation(out=SHT[:].rearrange("p m b -> p (m b)"), in_=PS[:, 7, 0:14 * NS], func=AF.Identity),
          reads=[kPS[7]], writes=[kSH])
    wr_v = d["w_in"].rearrange("(kc p) n -> p kc n", p=128)
    nwr = 0
    nbk = 0

    def store_state(dst):
        for h4 in range(4):
            transpose_to(c, PS[0:64, 0, h4 * 128:(h4 + 1) * 128], ST[:, h4 * 64:(h4 + 1) * 64], [kST], [kPS[0]], IDF[:, :])
        fw.op("act", lambda e: e.activation(out=SSTt[:].rearrange("p a b -> p (a b)"), in_=PS[0:64, 0, :], func=AF.Identity),
              reads=[kPS[0]], writes=[kSST])
        fw.dma("pool", dst.rearrange("(h4 h2) i j -> i h4 h2 j", h2=2), SSTt[:].rearrange("p a (h2 j) -> p a h2 j", h2=2), reads=[kSST])

    def load_state(src):
        fw.dma("pool", SLD[:].rearrange("p a (h2 j) -> p a h2 j", h2=2), src.rearrange("(h4 h2) i j -> i h4 h2 j", h2=2), writes=[kSLD])
        for h4 in range(4):
            transpose_to(c, PS[:, 0, h4 * 64:(h4 + 1) * 64], SLD[:, h4, :], [kSLD], [kPS[0]], IDF[0:64, 0:64])
        fw.op("act", lambda e: e.activation(out=ST[:], in_=PS[:, 0, 0:256], func=AF.Identity), reads=[kPS[0]], writes=[kST])

    for ti in ([5] if 'rw1' in VAR else [0] if 'rw0' in VAR else range(NTL)):
        c0 = ti * TW
        npr = min(TW, NP - c0)
        for kc in range(KC):
            fw.op("act", lambda e, kc=kc: e.activation(out=XBt[:, kc, :], in_=XF[:, kc, c0:c0 + TW], func=AF.Identity),
                  reads=[kXF[kc][ti]], writes=[kXBt[kc]])
        for mp in range(7):
            sl = nwr % 2
            nwr += 1
            fw.dma("pool", WRb[sl][:], wr_v[:, :, 1536 + mp * 256:1536 + (mp + 1) * 256], writes=[kWRb[sl]])
            for mm in range(2):
                m = 2 * mp + mm
                bk = nbk % 2
                nbk += 1
                for kc in range(KC):
                    fw.pe_defer = (kc != KC - 1)
                    fw.op("pe", lambda e, kc=kc: e.matmul(PS[:, bk, 0:TW], WRb[sl][:, kc, mm * 128:(mm + 1) * 128], XBt[:, kc, :],
                                                          start=(kc == 0), stop=(kc == KC - 1)),
                          reads=[kWRb[sl], kXBt[kc]], writes=[kPS[bk]])
                fw.op("act", lambda e: e.activation(out=PB[:, m, 1:TW + 1], in_=PS[:, bk, 0:TW], func=AF.Identity),
                      reads=[kPS[bk]], writes=[kPB[m]])
        for m in range(14):
            fw.op("pool", lambda e, m=m: e.tensor_tensor(out=PM[:, m, 0:npr], in0=PB[:, m, 0:npr], in1=PB[:, m, 1:npr + 1], op=ALU.subtract),
                  reads=[kPB[m]], writes=[kPM[m]])
            if npr < TW:
                fw.op("pool", lambda e, m=m: e.tensor_tensor(out=PM[:, m, npr:TW], in0=SHT[:, m, :], in1=PB[:, m, npr + 1:TW + 1], op=ALU.subtract),
                      reads=[kPB[m], kSH], writes=[kPM[m]])
            fw.op("dve", lambda e, m=m: e.scalar_tensor_tensor(out=PM[:, m, :], in0=PM[:, m, :], scalar=PC[:, MU + m:MU + m + 1],
                                                               in1=PB[:, m, 1:TW + 1], op0=ALU.mult, op1=ALU.add),
                  reads=[kPB[m], kPM[m], kW], writes=[kPM[m]])
        if npr < TW:
            for m in range(14):
                transpose_to(c, PS[0:NS + 1, 7, (m % 4) * 128:(m % 4 + 1) * 128], PB[:, m, npr:TW + 1], [kPB[m]], [kPS[7]], IDF[:, :])
                if m % 4 == 3 or m == 13:
                    m0 = (m // 4) * 4
                    fw.op("act", lambda e, m0=m0, m=m: e.activation(out=SSH[:, m0 * 128:(m + 1) * 128], in_=PS[0:NS + 1, 7, 0:(m - m0 + 1) * 128], func=AF.Identity),
                          reads=[kPS[7]], writes=[kSSH])
            fw.dma("pool", d["pshift"], SSH[0:1, :], reads=[kSSH])
            fw.dma("pool", d["sshift"], SSH[1:NS + 1, :], reads=[kSSH])
        fw.op("dve", lambda e: e.tensor_copy(out=PB[:, :, 0:1], in_=PB[:, :, TW:TW + 1]), reads=kPB, writes=kPB)
        Wt = lambda cc: PB[:, cc, 1:TW + 1]
        KKt = lambda cc: PB[:, 4 + cc, 1:TW + 1]
        NBt = lambda cc: PB[:, 8 + cc, 1:TW + 1]
        fw.op("act", lambda e: e.activation(out=tmpH[0:64, :], in_=PM[0:64, 12, :], func=AF.Tanh), reads=[kPM[12]], writes=[ktH])
        fw.op("act", lambda e: e.activation(out=tmpH[64:128, :], in_=PM[64:128, 12, :], func=AF.Identity), reads=[kPM[12]], writes=[ktH])
        for cc in range(4):
            bk = nbk % 2
            nbk += 1
            fw.op("pe", lambda e: e.matmul(PS[:, bk, 0:TW], WW2[0:64, cc * 128:(cc + 1) * 128], tmpH[0:64, :], start=True, stop=True),
                  reads=[kW, ktH], writes=[kPS[bk]])
            fw.op("act", lambda e: e.activation(out=tmpA[:], in_=PS[:, bk, 0:TW], func=AF.Sigmoid, bias=PC[:, W0 + cc:W0 + cc + 1]),
                  reads=[kPS[bk], kW], writes=[ktA])
            fw.op("act", lambda e: e.activation(out=Wt(cc), in_=tmpA[:], func=AF.Exp, scale=-0.6065306597126334),
                  reads=[ktA], writes=[kPB[cc]])
        for cc in range(4):
            bk = nbk % 2
            nbk += 1
            fw.op("pe", lambda e: e.matmul(PS[:, bk, 0:TW], WA2[64:128, cc * 128:(cc + 1) * 128], tmpH[64:128, :], start=True, stop=True),
                  reads=[kW, ktH], writes=[kPS[bk]])
            fw.op("act", lambda e: e.activation(out=tmpA[:], in_=PS[:, bk, 0:TW], func=AF.Sigmoid, bias=PC[:, A0 + cc:A0 + cc + 1]),
                  reads=[kPS[bk], kW], writes=[ktA])
            fw.op("dve", lambda e: e.tensor_scalar(out=KKt(cc), in0=PM[:, 4 + cc, :], scalar1=PC[:, KKc + cc:KKc + cc + 1], scalar2=None, op0=ALU.mult),
                  reads=[kPM[4 + cc], kW], writes=[kPB[4 + cc]])
            fw.op("act", lambda e: e.activation(out=tmpB[:], in_=KKt(cc), func=AF.Square), reads=[kPB[4 + cc]], writes=[ktB])
            b2 = 2 + (nbk % 2)
            fw.op("pe", lambda e: e.matmul(PS[:, b2, 0:TW], BLK[:], tmpB[:], start=True, stop=True), reads=[kW, ktB], writes=[kPS[b2]])
            fw.op("dve", lambda e: e.tensor_scalar(out=tmpB[:], in0=PS[:, b2, 0:TW], scalar1=1e-24, scalar2=None, op0=ALU.max),
                  reads=[kPS[b2]], writes=[ktB])
            fw.op("act", lambda e: e.activation(out=tmpB[:], in_=tmpB[:], func=AF.Sqrt), reads=[ktB], writes=[ktB])
            fw.op("dve", lambda e: e.reciprocal(out=tmpB[:], in_=tmpB[:]), reads=[ktB], writes=[ktB])
            fw.op("dve", lambda e: e.tensor_tensor(out=KKt(cc), in0=KKt(cc), in1=tmpB[:], op=ALU.mult), reads=[kPB[4 + cc], ktB], writes=[kPB[4 + cc]])
            fw.op("dve", lambda e: e.scalar_tensor_tensor(out=NBt(cc), in0=KKt(cc), scalar=-1.0, in1=tmpA[:], op0=ALU.mult, op1=ALU.mult),
                  reads=[kPB[4 + cc], ktA], writes=[kPB[8 + cc]])
            fw.op("dve", lambda e: e.tensor_scalar(out=tmpA[:], in0=tmpA[:], scalar1=-1.0, scalar2=PC[:, KAc + cc:KAc + cc + 1], op0=ALU.add, op1=ALU.mult),
                  reads=[ktA, kW], writes=[ktA])
            fw.op("dve", lambda e: e.scalar_tensor_tensor(out=PM[:, 4 + cc, :], in0=tmpA[:], scalar=1.0, in1=PM[:, 4 + cc, :], op0=ALU.add, op1=ALU.mult),
                  reads=[ktA, kPM[4 + cc]], writes=[kPM[4 + cc]])
        if 'rwA' in VAR:
            continue
        subs = [(a_, min(CS, TW - a_)) for a_ in range(0, TW, CS)]
        cblocks = [(0, 128), (128, 128), (256, TW - 256)]
        for (cb0, ncb) in cblocks:
            for vi, src in enumerate((lambda cc: PM[:, 4 + cc, cb0:cb0 + ncb], lambda cc: NBt(cc)[:, cb0:cb0 + ncb], lambda cc: PM[:, 8 + cc, cb0:cb0 + ncb])):
                kk_ = (lambda cc: kPM[4 + cc], lambda cc: kPB[8 + cc], lambda cc: kPM[8 + cc])[vi]
                bk = nbk % 2
                nbk += 1
                for cc in range(4):
                    transpose_to(c, PS[0:ncb, bk, cc * 128:(cc + 1) * 128], src(cc), [kk_(cc)], [kPS[bk]], IDF[:, :])
                fw.op("act", lambda e: e.activation(out=TOK[0:ncb, vi, :], in_=PS[0:ncb, bk, :], func=AF.Identity), reads=[kPS[bk]], writes=[kTOK[vi]])
            fw.dma("pool", c.TOKd[cb0:cb0 + ncb], TOK[0:ncb, :, :], reads=kTOK, writes=[c.kTOKd])

        def fetch_rows(a, ncol, r):
            for h2 in range(2):
                for vi, dstt in enumerate((KBrow[r], NBrow[r])):
                    fw.dma("pool", dstt[h2:128:32, 0:ncol, h2 * 64:(h2 + 1) * 64],
                           c.TOKd[a:a + ncol, vi, :].rearrange("s (h4 h2 j) -> h4 s h2 j", h4=4, h2=2)[:, :, h2, :], reads=[c.kTOKd], writes=[kRow[r]])
                fw.dma("pool", Vrow[r][h2:128:32, 0:ncol, :],
                       c.TOKd[a:a + ncol, 2, :].rearrange("s (h4 h2 j) -> h4 s h2 j", h4=4, h2=2)[:, :, h2, :], reads=[c.kTOKd], writes=[kRow[r]])
            for h2 in range(2):
                ps_ = slice(h2 * 64, (h2 + 1) * 64)
                fw.op("pool", lambda e: e.tensor_copy(out=LKc[r][ps_, 0:ncol, :, h2], in_=PB[ps_, 4:8, 1 + a:1 + a + ncol].rearrange("p c s -> p s c")),
                      reads=kPB[4:8], writes=[kLK[r]])
                fw.op("pool", lambda e: e.tensor_copy(out=RKc[r][ps_, 0:ncol, :, h2], in_=PM[ps_, 0:4, a:a + ncol].rearrange("p c s -> p s c")),
                      reads=kPM[0:4], writes=[kLK[r]])

        def run_steps(a, ncol, r):
            def emit_y(sy):
                yb = sy % 8
                for h4 in range(4):
                    fw.op("pe", lambda e, h4=h4: e.matmul(PS[32 * h4:32 * h4 + 2, 7, yb * 64:(yb + 1) * 64], RKc[r][:, sy, h4, :], ST[:, h4 * 64:(h4 + 1) * 64],
                                                          start=True, stop=True, tile_position=(0, 32 * h4)),
                          reads=[kLK[r], kST], writes=[kPS[7]])
                if yb == 7 or sy == ncol - 1:
                    s0_ = sy - yb
                    fw.op("act", lambda e: e.activation(out=Ybuf[r][:, s0_:sy + 1, :].rearrange("p s i -> p (s i)"), in_=PS[:, 7, 0:(yb + 1) * 64], func=AF.Identity),
                          reads=[kPS[7]], writes=[kY[r]])

            pending_y = None
            for s_ in range(ncol):
                gcol = c0 + a + s_
                is_sample = gcol >= NP
                if is_sample:
                    if pending_y is not None:
                        emit_y(pending_y); pending_y = None
                    load_state(d["swkv0"][gcol - NP])
                for h4 in range(4):
                    fw.op("pe", lambda e, h4=h4: e.matmul(PS[32 * h4:32 * h4 + 2, 2, 0:64], LKc[r][:, s_, h4, :], ST[:, h4 * 64:(h4 + 1) * 64],
                                                          start=True, stop=True, tile_position=(0, 32 * h4)),
                          reads=[kLK[r], kST], writes=[kPS[2]])
                fw.op("dve", lambda e: e.tensor_tensor(out=STw[:].rearrange("p (a b) -> p a b", a=4), in0=ST[:].rearrange("p (a b) -> p a b", a=4),
                                                       in1=PB[:, 0:4, 1 + a + s_:2 + a + s_].to_broadcast([128, 4, 64]), op=ALU.mult),
                      reads=[kST] + kPB[0:4], writes=[kSTw])
                if pending_y is not None:
                    emit_y(pending_y); pending_y = None
                for h4 in range(4):
                    fw.op("pe", lambda e, h4=h4: e.matmul(PS[:, 3 + h4, 0:64], KBrow[r][32 * h4:32 * h4 + 2, s_, :], Vrow[r][32 * h4:32 * h4 + 2, s_, :],
                                                          start=True, stop=False, tile_position=(32 * h4, 0)),
                          reads=[kRow[r]], writes=[kPS[3 + h4]])
                fw.op("dve", lambda e: e.tensor_copy(out=SAY[:], in_=PS[:, 2, 0:64]), reads=[kPS[2]], writes=[kSAY])
                for h4 in range(4):
                    fw.op("pe", lambda e, h4=h4: e.matmul(PS[:, 3 + h4, 0:64], NBrow[r][32 * h4:32 * h4 + 2, s_, :], SAY[32 * h4:32 * h4 + 2, :],
                                                          start=False, stop=True, tile_position=(32 * h4, 0)),
                          reads=[kRow[r], kSAY], writes=[kPS[3 + h4]])
                fw.op("dve", lambda e: e.tensor_tensor(out=ST[:].rearrange("p (a b) -> p a b", a=4), in0=STw[:].rearrange("p (a b) -> p a b", a=4),
                                                       in1=PS[:, 3:7, 0:64], op=ALU.add),
                      reads=[kSTw] + kPS[3:7], writes=[kST])
                pending_y = s_
                if is_sample or gcol == NP - 1 or s_ == ncol - 1:
                    emit_y(pending_y); pending_y = None
                if is_sample:
                    store_state(d["swkv"][gcol - NP])
                if gcol == NP - 1:
                    store_state(d["pwkv"])
            for h2 in range(2):
                fw.dma("pool", c.Yd[:, h2, a:a + ncol, :], Ybuf[r][h2:128:32, 0:ncol, :], reads=[kY[r]], writes=[c.kYd])

        fetch_rows(subs[0][0], subs[0][1], 0)
        for n_, (a_, nc_) in enumerate(subs):
            r_ = n_ % 2
            if n_ + 1 < len(subs):
                fetch_rows(subs[n_ + 1][0], subs[n_ + 1][1], 1 - r_)
            run_steps(a_, nc_, r_)

        for (cb0, ncol) in cblocks:
            a = cb0
            fw.dma("pool", YTOK[0:ncol, :].rearrange("s (h i) -> s h i", h=8), c.Yd[:, :, a:a + ncol, :].rearrange("a b s i -> s (a b) i"),
                   reads=[c.kYd], writes=[kYT])
            Y3 = YTOK[0:ncol, :].rearrange("s (h i) -> s h i", h=8)
            C3 = YC[0:ncol, :].rearrange("s (h i) -> s h i", h=8)
            S3 = YS[0:ncol, :].rearrange("s (h i) -> s h i", h=8)
            fw.op("dve", lambda e: e.tensor_reduce(out=gst[0:ncol, 0:8], in_=Y3, axis=AX.X, op=ALU.add), reads=[kYT], writes=[kgst])
            fw.op("dve", lambda e: e.tensor_scalar(out=gst[0:ncol, 0:8], in0=gst[0:ncol, 0:8], scalar1=1.0 / 64, scalar2=None, op0=ALU.mult), reads=[kgst], writes=[kgst])
            fw.op("dve", lambda e: e.tensor_tensor(out=C3, in0=Y3, in1=gst[0:ncol, 0:8].unsqueeze(2).to_broadcast([ncol, 8, 64]), op=ALU.subtract),
                  reads=[kYT, kgst], writes=[kYC])
            fw.op("act", lambda e: e.activation(out=YS[0:ncol, :], in_=YC[0:ncol, :], func=AF.Square), reads=[kYC], writes=[kYS])
            fw.op("dve", lambda e: e.tensor_reduce(out=gst[0:ncol, 8:16], in_=S3, axis=AX.X, op=ALU.add), reads=[kYS], writes=[kgst])
            fw.op("act", lambda e: e.activation(out=gst[0:ncol, 8:16], in_=gst[0:ncol, 8:16], func=AF.Sqrt, scale=1.0 / 64, bias=c.cst[0:ncol, 2:3]),
                  reads=[kgst, c.kconst], writes=[kgst])
            fw.op("dve", lambda e: e.reciprocal(out=gst[0:ncol, 8:16], in_=gst[0:ncol, 8:16]), reads=[kgst], writes=[kgst])
            fw.op("dve", lambda e: e.tensor_tensor(out=C3, in0=C3, in1=gst[0:ncol, 8:16].unsqueeze(2).to_broadcast([ncol, 8, 64]), op=ALU.mult),
                  reads=[kYC, kgst], writes=[kYC])
            fw.op("pool", lambda e: e.tensor_tensor(out=YC[0:ncol, :], in0=YC[0:ncol, :], in1=GNG[0:ncol, :], op=ALU.mult), reads=[kYC, kW], writes=[kYC])
            fw.op("pool", lambda e: e.tensor_tensor(out=YC[0:ncol, :], in0=YC[0:ncol, :], in1=GNB[0:ncol, :], op=ALU.add), reads=[kYC, kW], writes=[kYC])
            for cc in range(4):
                transpose_to(c, PS[:, 0, cc * 128:cc * 128 + ncol], YC[0:ncol, cc * 128:(cc + 1) * 128], [kYC], [kPS[0]], IDF[0:ncol, 0:ncol])
            fw.op("act", lambda e: e.activation(out=tmpH[:, 0:ncol], in_=PM[:, 13, a:a + ncol], func=AF.Sigmoid), reads=[kPM[13]], writes=[ktH])
            for cc in range(4):
                fw.op("dve", lambda e: e.scalar_tensor_tensor(out=tmpA[:, 0:ncol], in0=PM[:, cc, a:a + ncol], scalar=PC[:, RKp + cc:RKp + cc + 1],
                                                              in1=PM[:, 4 + cc, a:a + ncol], op0=ALU.mult, op1=ALU.mult),
                      reads=[kPM[cc], kPM[4 + cc], kW], writes=[ktA])
                fw.op("pe", lambda e: e.matmul(PS[:, 1, 0:ncol], BLK[:], tmpA[:, 0:ncol], start=True, stop=True), reads=[kW, ktA], writes=[kPS[1]])
                fw.op("pe", lambda e: e.matmul(PS[:, 1, 256:256 + ncol], WG2[:, cc * 128:(cc + 1) * 128], tmpH[:, 0:ncol], start=True, stop=True),
                      reads=[kW, ktH], writes=[kPS[1]])
                fw.op("dve", lambda e: e.tensor_tensor(out=tmpB[:, 0:ncol], in0=PS[:, 1, 0:ncol], in1=PM[:, 8 + cc, a:a + ncol], op=ALU.mult),
                      reads=[kPS[1], kPM[8 + cc]], writes=[ktB])
                fw.op("dve", lambda e: e.tensor_tensor(out=tmpB[:, 0:ncol], in0=tmpB[:, 0:ncol], in1=PS[:, 0, cc * 128:cc * 128 + ncol], op=ALU.add),
                      reads=[ktB, kPS[0]], writes=[ktB])
                fw.op("dve", lambda e: e.tensor_tensor(out=ORW[:, cc, c0 + a:c0 + a + ncol], in0=tmpB[:, 0:ncol], in1=PS[:, 1, 256:256 + ncol], op=ALU.mult),
                      reads=[ktB, kPS[1]], writes=[kORW[cc]])


TWO_PI = 6.283185307179586
C1 = 6.28125
C2 = TWO_PI - 6.28125
PI = 3.141592653589793


def trig_tables(c, X, kX, Sout, Cout, kS, kC, tmpI, tmpF, ktmp, width):
    fw = c.fw
    w = slice(0, width)
    fw.op("dve", lambda e: e.tensor_scalar(out=tmpI[:, w], in0=X[:, w], scalar1=1.0 / TWO_PI, scalar2=None, op0=ALU.mult), reads=[kX], writes=[ktmp])
    fw.op("dve", lambda e: e.tensor_copy(out=tmpF[:, w], in_=tmpI[:, w]), reads=[ktmp], writes=[ktmp])
    fw.op("dve", lambda e: e.scalar_tensor_tensor(out=X[:, w], in0=tmpF[:, w], scalar=-C1, in1=X[:, w], op0=ALU.mult, op1=ALU.add), reads=[ktmp, kX], writes=[kX])
    fw.op("dve", lambda e: e.scalar_tensor_tensor(out=X[:, w], in0=tmpF[:, w], scalar=-C2, in1=X[:, w], op0=ALU.mult, op1=ALU.add), reads=[ktmp, kX], writes=[kX])
    fw.op("dve", lambda e: e.tensor_scalar(out=X[:, w], in0=X[:, w], scalar1=PI, scalar2=-PI, op0=ALU.min, op1=ALU.max), reads=[kX], writes=[kX])
    fw.op("act", lambda e: e.activation(out=Sout, in_=X[:, w], func=AF.Sin), reads=[kX], writes=[kS])
    fw.op("act", lambda e: e.activation(out=tmpF[:, w], in_=X[:, w], func=AF.Abs), reads=[kX, ktmp], writes=[ktmp])
    fw.op("act", lambda e: e.activation(out=Cout, in_=tmpF[:, w], func=AF.Sin, scale=-1.0, bias=c.cst[:, 3:4]), reads=[ktmp, c.kconst], writes=[kC])


def s5_mixer(c, d):
    nc, fw = c.nc, c.fw
    XF, kXF, PS, kPS = c.XF, c.kXF, c.PS, c.kPS
    IDF = c.IDF
    with ExitStack() as st:
        ZG = sb(st, nc, "ZG", [128, KC, NT], BF16); kZG = [[K() for _ in range(NTL)] for _ in range(KC)]
        with ExitStack() as s2:
            XB = sb(s2, nc, "XBs", [128, KC, NT], BF16); kXB = [K() for _ in range(KC)]
            for kc in range(KC):
                fw.op("act", lambda e, kc=kc: e.activation(out=XB[:, kc, :], in_=XF[:, kc, :], func=AF.Identity), reads=kXF[kc], writes=[kXB[kc]])
            PRM = sb(s2, nc, "PRM", [128, 3, 32], F32); kP = K()
            fw.dma("sp", PRM[:], d["s5prm"], writes=[kP])
            DSK = sb(s2, nc, "DSK", [128, 8], F32)
            fw.dma("sp", DSK[:], d["dskip"], writes=[kP])
            S0 = sb(s2, nc, "S0", [128, 32, NS, 2], F32); kS0 = K()
            for t_ in range(32):
                fw.dma("sp", S0[:, t_, :, :], d["s5_0"][:, t_ * 128:(t_ + 1) * 128, :].rearrange("b p r -> p b r"), writes=[kS0])
            cn = {nm: sb(s2, nc, "c_" + nm, [128, 32], F32) for nm in ("DT", "MAGL", "TH", "MAG", "X", "SI", "CO", "AR", "AI", "CR", "CI", "T1", "T2", "RD")}
            tI = sb(s2, nc, "tI32", [128, TW], I32)
            tF = sb(s2, nc, "tF32", [128, TW], F32)
            ktmp = K()
            LR, LI, LDT = PRM[:, 0, :], PRM[:, 1, :], PRM[:, 2, :]
            kc_ = K()

            def o(eng, fn, r=(), w=()):
                fw.op(eng, fn, reads=[kP, kc_] + list(r), writes=[kc_] + list(w))
            o("act", lambda e: e.activation(out=cn["DT"][:], in_=LDT, func=AF.Exp))
            o("dve", lambda e: e.tensor_tensor(out=cn["MAGL"][:], in0=LR, in1=cn["DT"][:], op=ALU.mult))
            o("dve", lambda e: e.tensor_tensor(out=cn["TH"][:], in0=LI, in1=cn["DT"][:], op=ALU.mult))
            o("act", lambda e: e.activation(out=cn["MAG"][:], in_=cn["MAGL"][:], func=AF.Exp))
            o("dve", lambda e: e.tensor_copy(out=cn["X"][:], in_=cn["TH"][:]))
            trig_tables(c, cn["X"], kc_, cn["SI"][:], cn["CO"][:], kc_, kc_, tI, tF, ktmp, 32)
            o("dve", lambda e: e.tensor_tensor(out=cn["AR"][:], in0=cn["MAG"][:], in1=cn["CO"][:], op=ALU.mult))
            o("dve", lambda e: e.tensor_tensor(out=cn["AI"][:], in0=cn["MAG"][:], in1=cn["SI"][:], op=ALU.mult))
            o("dve", lambda e: e.tensor_tensor(out=cn["T1"][:], in0=LR, in1=LR, op=ALU.mult))
            o("dve", lambda e: e.tensor_tensor(out=cn["T2"][:], in0=LI, in1=LI, op=ALU.mult))
            o("dve", lambda e: e.tensor_tensor(out=cn["T1"][:], in0=cn["T1"][:], in1=cn["T2"][:], op=ALU.add))
            o("dve", lambda e: e.reciprocal(out=cn["RD"][:], in_=cn["T1"][:]))
            o("dve", lambda e: e.tensor_scalar(out=cn["T1"][:], in0=cn["AR"][:], scalar1=-1.0, scalar2=None, op0=ALU.add))
            o("dve", lambda e: e.tensor_tensor(out=cn["CR"][:], in0=cn["T1"][:], in1=LR, op=ALU.mult))
            o("dve", lambda e: e.tensor_tensor(out=cn["T2"][:], in0=cn["AI"][:], in1=LI, op=ALU.mult))
            o("dve", lambda e: e.tensor_tensor(out=cn["CR"][:], in0=cn["CR"][:], in1=cn["T2"][:], op=ALU.add))
            o("dve", lambda e: e.tensor_tensor(out=cn["CR"][:], in0=cn["CR"][:], in1=cn["RD"][:], op=ALU.mult))
            o("dve", lambda e: e.tensor_tensor(out=cn["CI"][:], in0=cn["AI"][:], in1=LR, op=ALU.mult))
            o("dve", lambda e: e.tensor_tensor(out=cn["T2"][:], in0=cn["T1"][:], in1=LI, op=ALU.mult))
            o("dve", lambda e: e.tensor_tensor(out=cn["CI"][:], in0=cn["CI"][:], in1=cn["T2"][:], op=ALU.subtract))
            o("dve", lambda e: e.tensor_tensor(out=cn["CI"][:], in0=cn["CI"][:], in1=cn["RD"][:], op=ALU.mult))
            IOT = sb(s2, nc, "IOT", [128, TW], F32)
            o("pool", lambda e: e.iota(IOT[:], [[1, TW]], base=1, channel_multiplier=0, allow_small_or_imprecise_dtypes=True))
            TB = [[sb(s2, nc, "TB%d_%d" % (k, j), [128, TW], F32) for j in range(4)] for k in range(4)]
            kTB = [[K() for _ in range(4)] for _ in range(4)]
            XA = sb(s2, nc, "XA", [128, TW], F32); kXA = K()
            Pt = [sb(s2, nc, "Pt%d" % i, [128, TW], F32) for i in range(6)]; kPt = [K() for _ in range(6)]
            Qr = sb(s2, nc, "Qr", [128, TW], F32); Qi = sb(s2, nc, "Qi", [128, TW], F32); kQ = [K(), K()]
            S16 = [sb(s2, nc, "S16_%d" % i, [128, TW], BF16) for i in range(2)]; kS16 = [K(), K()]
            BCm = sb(s2, nc, "BCm", [128, 4, 4, 128], BF16); kBC = K()
            SE = sb(s2, nc, "SE", [128, 32, 2], F32); kSE = K()
            SN = sb(s2, nc, "SN", [128, 2, NS], F32); kSN = K()
            OUT16 = [sb(s2, nc, "OUT16_%d" % i, [NS, 256], F32) for i in range(2)]; kO16 = [K(), K()]
            RHO = sb(s2, nc, "RHO", [128, TW], F32); kRHO = K()
            fw.op("pool", lambda e: e.memset(SE[:], 0.0), writes=[kSE])
            fw.op("pool", lambda e: e.memset(RHO[:], 1.0), writes=[kRHO])
            nb = 0
            for m in range(KC):
                fw.dma("pool", BCm[:].rearrange("p a b n -> p (a b) n"), d["s5bc"][m].rearrange("a p n -> p a n"), writes=[kBC])
                for k in range(4):
                    stn = 4 * m + k
                    col = slice(stn, stn + 1)
                    TR, TI, CC, SS = TB[k]
                    fw.op("dve", lambda e: e.tensor_scalar(out=XA[:], in0=IOT[:], scalar1=cn["TH"][:, col], scalar2=None, op0=ALU.mult),
                          reads=[kc_], writes=[kXA])
                    trig_tables(c, XA, kXA, SS[:], CC[:], kTB[k][3], kTB[k][2], tI, tF, ktmp, TW)
                    fw.op("dve", lambda e: e.tensor_scalar(out=TR[:], in0=CC[:], scalar1=cn["CR"][:, col], scalar2=None, op0=ALU.mult), reads=[kTB[k][2], kc_], writes=[kTB[k][0]])
                    fw.op("dve", lambda e: e.scalar_tensor_tensor(out=TR[:], in0=SS[:], scalar=cn["CI"][:, col], in1=TR[:], op0=ALU.mult, op1=ALU.add), reads=[kTB[k][3], kTB[k][0], kc_], writes=[kTB[k][0]])
                    fw.op("dve", lambda e: e.tensor_scalar(out=TI[:], in0=CC[:], scalar1=cn["CI"][:, col], scalar2=None, op0=ALU.mult), reads=[kTB[k][2], kc_], writes=[kTB[k][1]])
                    fw.op("dve", lambda e: e.tensor_scalar(out=XA[:], in0=SS[:], scalar1=cn["CR"][:, col], scalar2=None, op0=ALU.mult), reads=[kTB[k][3], kc_], writes=[kXA])
                    fw.op("dve", lambda e: e.tensor_tensor(out=TI[:], in0=TI[:], in1=XA[:], op=ALU.subtract), reads=[kTB[k][1], kXA], writes=[kTB[k][1]])
                for ti in range(NTL):
                    c0 = ti * TW
                    npr = min(TW, NP - c0)
                    yb = 6 + (ti % 2)
                    for k in range(4):
                        stn = 4 * m + k
                        col = slice(stn, stn + 1)
                        TR, TI, CC, SS = TB[k]
                        br, bi = 2 * (nb % 2), 2 * (nb % 2) + 1
                        nb += 1
                        fw.op("pe", lambda e: e.matmul(PS[:, br, 0:TW], BCm[:, k, 0, :], XB[:, m, c0:c0 + TW], start=True, stop=True), reads=[kBC, kXB[m]], writes=[kPS[br]])
                        fw.op("pe", lambda e: e.matmul(PS[:, bi, 0:TW], BCm[:, k, 1, :], XB[:, m, c0:c0 + TW], start=True, stop=True), reads=[kBC, kXB[m]], writes=[kPS[bi]])
                        w = slice(0, npr)
                        fw.op("dve", lambda e: e.tensor_tensor(out=Pt[0][:, w], in0=PS[:, br, w], in1=TR[:, w], op=ALU.mult), reads=[kPS[br], kTB[k][0]], writes=[kPt[0]])
                        fw.op("dve", lambda e: e.tensor_tensor(out=Pt[1][:, w], in0=PS[:, bi, w], in1=TI[:, w], op=ALU.mult), reads=[kPS[bi], kTB[k][1]], writes=[kPt[1]])
                        fw.op("dve", lambda e: e.tensor_tensor(out=Pt[2][:, w], in0=PS[:, bi, w], in1=TR[:, w], op=ALU.mult), reads=[kPS[bi], kTB[k][0]], writes=[kPt[2]])
                        fw.op("dve", lambda e: e.tensor_tensor(out=Pt[3][:, w], in0=PS[:, br, w], in1=TI[:, w], op=ALU.mult), reads=[kPS[br], kTB[k][1]], writes=[kPt[3]])
                        fw.op("pool", lambda e: e.tensor_tensor(out=Pt[0][:, w], in0=Pt[0][:, w], in1=Pt[1][:, w], op=ALU.subtract), reads=[kPt[0], kPt[1]], writes=[kPt[0]])
                        fw.op("pool", lambda e: e.tensor_tensor(out=Pt[2][:, w], in0=Pt[2][:, w], in1=Pt[3][:, w], op=ALU.add), reads=[kPt[2], kPt[3]], writes=[kPt[2]])
                        fw.op("dve", lambda e: e.tensor_tensor_scan(Qr[:, w], cn["MAG"][:, col].to_broadcast([128, npr]), Pt[0][:, w], SE[:, stn, 0:1], ALU.mult, ALU.add), reads=[kc_, kPt[0], kSE], writes=[kQ[0]])
                        fw.op("dve", lambda e: e.tensor_tensor_scan(Qi[:, w], cn["MAG"][:, col].to_broadcast([128, npr]), Pt[2][:, w], SE[:, stn, 1:2], ALU.mult, ALU.add), reads=[kc_, kPt[2], kSE], writes=[kQ[1]])
                        fw.op("dve", lambda e: e.tensor_tensor(out=Pt[4][:, w], in0=CC[:, w], in1=Qr[:, w], op=ALU.mult), reads=[kTB[k][2], kQ[0]], writes=[kPt[4]])
                        fw.op("pool", lambda e: e.tensor_tensor(out=Pt[5][:, w], in0=SS[:, w], in1=Qi[:, w], op=ALU.mult), reads=[kTB[k][3], kQ[1]], writes=[kPt[5]])
                        fw.op("dve", lambda e: e.tensor_tensor(out=S16[0][:, w], in0=Pt[4][:, w], in1=Pt[5][:, w], op=ALU.subtract), reads=[kPt[4], kPt[5]], writes=[kS16[0]])
                        fw.op("pool", lambda e: e.tensor_tensor(out=Pt[1][:, w], in0=SS[:, w], in1=Qr[:, w], op=ALU.mult), reads=[kTB[k][3], kQ[0], kPt[1]], writes=[kPt[1]])
                        fw.op("dve", lambda e: e.tensor_tensor(out=Pt[3][:, w], in0=CC[:, w], in1=Qi[:, w], op=ALU.mult), reads=[kTB[k][2], kQ[1], kPt[3]], writes=[kPt[3]])
                        fw.op("dve", lambda e: e.scalar_tensor_tensor(out=S16[1][:, w], in0=Pt[1][:, w], scalar=-1.0, in1=Pt[3][:, w], op0=ALU.mult, op1=ALU.subtract),
                              reads=[kPt[1], kPt[3]], writes=[kS16[1]])
                        L = npr - 1
                        fw.op("dve", lambda e: e.tensor_tensor(out=SE[:, stn, 0:1], in0=Pt[4][:, L:L + 1], in1=Pt[5][:, L:L + 1], op=ALU.subtract), reads=[kPt[4], kPt[5], kSE], writes=[kSE])
                        fw.op("dve", lambda e: e.tensor_tensor(out=SE[:, stn, 1:2], in0=Pt[1][:, L:L + 1], in1=Pt[3][:, L:L + 1], op=ALU.add), reads=[kPt[1], kPt[3], kSE], writes=[kSE])
                        if npr < TW:
                            ws = slice(npr, TW)
                            fw.op("dve", lambda e: e.tensor_scalar(out=Pt[0][:, ws], in0=PS[:, bi, ws], scalar1=cn["CI"][:, col], scalar2=None, op0=ALU.mult), reads=[kPS[bi], kc_, kPt[0]], writes=[kPt[0]])
                            fw.op("dve", lambda e: e.scalar_tensor_tensor(out=Pt[0][:, ws], in0=PS[:, br, ws], scalar=cn["CR"][:, col], in1=Pt[0][:, ws], op0=ALU.mult, op1=ALU.subtract), reads=[kPS[br], kPt[0], kc_], writes=[kPt[0]])
                            fw.op("dve", lambda e: e.tensor_scalar(out=Pt[2][:, ws], in0=PS[:, br, ws], scalar1=cn["CI"][:, col], scalar2=None, op0=ALU.mult), reads=[kPS[br], kc_, kPt[2]], writes=[kPt[2]])
                            fw.op("dve", lambda e: e.scalar_tensor_tensor(out=Pt[2][:, ws], in0=PS[:, bi, ws], scalar=cn["CR"][:, col], in1=Pt[2][:, ws], op0=ALU.mult, op1=ALU.add), reads=[kPS[bi], kPt[2], kc_], writes=[kPt[2]])
                            s0r, s0i = S0[:, stn, :, 0], S0[:, stn, :, 1]
                            fw.op("dve", lambda e: e.scalar_tensor_tensor(out=Pt[0][:, ws], in0=s0r, scalar=cn["AR"][:, col], in1=Pt[0][:, ws], op0=ALU.mult, op1=ALU.add), reads=[kS0, kPt[0], kc_], writes=[kPt[0]])
                            fw.op("dve", lambda e: e.tensor_scalar(out=Pt[1][:, ws], in0=s0i, scalar1=cn["AI"][:, col], scalar2=None, op0=ALU.mult), reads=[kS0, kc_, kPt[1]], writes=[kPt[1]])
                            fw.op("dve", lambda e: e.tensor_tensor(out=SN[:, 0, :], in0=Pt[0][:, ws], in1=Pt[1][:, ws], op=ALU.subtract), reads=[kPt[0], kPt[1]], writes=[kSN])
                            fw.op("dve", lambda e: e.scalar_tensor_tensor(out=Pt[2][:, ws], in0=s0i, scalar=cn["AR"][:, col], in1=Pt[2][:, ws], op0=ALU.mult, op1=ALU.add), reads=[kS0, kPt[2], kc_], writes=[kPt[2]])
                            fw.op("dve", lambda e: e.scalar_tensor_tensor(out=SN[:, 1, :], in0=s0r, scalar=cn["AI"][:, col], in1=Pt[2][:, ws], op0=ALU.mult, op1=ALU.add), reads=[kS0, kPt[2], kc_], writes=[kSN])
                            fw.op("act", lambda e: e.activation(out=S16[0][:, ws], in_=SN[:, 0, :], func=AF.Identity), reads=[kSN, kS16[0]], writes=[kS16[0]])
                            fw.op("act", lambda e: e.activation(out=S16[1][:, ws], in_=SN[:, 1, :], func=AF.Identity, scale=-1.0), reads=[kSN, kS16[1]], writes=[kS16[1]])
                            for r_ in range(2):
                                transpose_to(c, PS[0:NS, 5, r_ * 128:(r_ + 1) * 128], SN[:, r_, :], [kSN], [kPS[5]], IDF[:, :])
                            oi = stn % 2
                            fw.op("act", lambda e: e.activation(out=OUT16[oi][:, :].rearrange("b (p r) -> b r p", r=2),
                                                                in_=PS[0:NS, 5, 0:256].rearrange("b (r p) -> b r p", r=2), func=AF.Identity),
                                  reads=[kPS[5]], writes=[kO16[oi]])
                            fw.dma("sp", d["ss5"][:, stn * 256:(stn + 1) * 256], OUT16[oi][:, :], reads=[kO16[oi]])
                        fw.op("pe", lambda e: e.matmul(PS[:, yb, 0:TW], BCm[:, k, 2, :], S16[0][:], start=(k == 0), stop=False), reads=[kBC, kS16[0]], writes=[kPS[yb]])
                        fw.op("pe", lambda e: e.matmul(PS[:, yb, 0:TW], BCm[:, k, 3, :], S16[1][:], start=False, stop=(k == 3)), reads=[kBC, kS16[1]], writes=[kPS[yb]])
                    fw.op("dve", lambda e: e.scalar_tensor_tensor(out=XA[:], in0=XF[:, m, c0:c0 + TW], scalar=DSK[:, m:m + 1], in1=PS[:, yb, 0:TW], op0=ALU.mult, op1=ALU.add),
                          reads=[kXF[m][ti], kPS[yb], kP, kXA], writes=[kXA])
                    fw.op("act", lambda e: e.activation(out=ZG[:, m, c0:c0 + TW], in_=XA[:], func=AF.Gelu), reads=[kXA], writes=[kZG[m][ti]])
            fw.dma("sp", d["ps5"].rearrange("(t p) r -> p t r", p=128), SE[:], reads=[kSE])
            fw.barrier()
        with ExitStack() as s3:
            alloc_ln(c, s3)
            WGb = [[sb(s3, nc, "WG%d_%d" % (a, i), [128, KC, 256], BF16) for i in range(2)] for a in range(2)]
            kWG = [[K(), K()], [K(), K()]]
            sgl = [sb(s3, nc, "sgl%d" % i, [128, TW], F32) for i in range(2)]; ksgl = [K(), K()]
            wv = [d["w_glu_out"].rearrange("(kc p) n -> p kc n", p=128), d["w_glu_gate"].rearrange("(kc p) n -> p kc n", p=128)]
            nb = 0
            for mp in range(4):
                sl = mp % 2
                for a in range(2):
                    fw.dma("pool", WGb[a][sl][:], wv[a][:, :, mp * 256:(mp + 1) * 256], writes=[kWG[a][sl]])
                for mm in range(2):
                    m = 2 * mp + mm
                    for ti in range(NTL):
                        cs = slice(ti * TW, (ti + 1) * TW)
                        bo, bg = 2 * (nb % 2), 2 * (nb % 2) + 1
                        nb += 1
                        for a, bk in ((0, bo), (1, bg)):
                            for kc in range(KC):
                                fw.pe_defer = (kc != KC - 1)
                                fw.op("pe", lambda e, kc=kc: e.matmul(PS[:, bk, 0:TW], WGb[a][sl][:, kc, mm * 128:(mm + 1) * 128], ZG[:, kc, cs], start=(kc == 0), stop=(kc == KC - 1)),
                                      reads=[kWG[a][sl], kZG[kc][ti]], writes=[kPS[bk]])
                        ss = nb % 2
                        fw.op("act", lambda e: e.activation(out=sgl[ss][:], in_=PS[:, bg, 0:TW], func=AF.Sigmoid), reads=[kPS[bg]], writes=[ksgl[ss]])
                        fw.op("dve", lambda e: e.tensor_tensor(out=sgl[ss][:], in0=PS[:, bo, 0:TW], in1=sgl[ss][:], op=ALU.mult), reads=[kPS[bo], ksgl[ss]], writes=[ksgl[ss]])
                        fw.op("dve", lambda e: e.scalar_tensor_tensor(out=XF[:, m, cs], in0=XF[:, m, cs], scalar=ALPHA, in1=sgl[ss][:], op0=ALU.mult, op1=ALU.add),
                              reads=[ksgl[ss], kXF[m][ti]], writes=[kXF[m][ti]])
            for ti in range(NTL):
                layer_norm_tile(c, ti, 4)
            fw.barrier()


def sb_attention_sample(c, d, st, QS, kQS, OSB, kOSB):
    nc, fw = c.nc, c.fw
    PS, kPS = c.PS, c.kPS
    NE = NS * 16
    NGp = NS * 4
    nrows4 = c.npool * 32
    ck4 = d["ck"].rearrange("(n r) x -> n (r x)", r=4)
    cv4 = d["cv"].rearrange("(n r) x -> n (r x)", r=4)
    fw.dma("sp", c.Qd, QS[0:NS, :], reads=[kQS], writes=[c.kQd])
    QB = [sb(st, nc, "QB%d" % i, [128, 512], F32) for i in range(2)]; kQB = [K(), K()]
    PTB = sb(st, nc, "PTB", [128, NE], I32); kPT = K()
    PTF = sb(st, nc, "PTF", [128, NE], F32)
    IOQ = sb(st, nc, "IOQ", [128, NGp], F32)
    IDXF = sb(st, nc, "IDXF", [128, NGp], F32)
    IDX = sb(st, nc, "IDX", [128, NGp], I32)
    fw.dma("sp", PTB[:], d["pt"].to_broadcast([128, NE]), writes=[kPT])
    fw.op("dve", lambda e: e.tensor_copy(out=PTF[:], in_=PTB[:]), reads=[kPT], writes=[kPT])
    PT4 = PTF[:].rearrange("p (g q) -> p g q", q=4)
    for q4 in range(4):
        ps_ = slice(32 * q4, 32 * q4 + 32)
        fw.op("pool", lambda e: e.iota(IOQ[ps_, :], [[0, NGp]], base=0, channel_multiplier=1, allow_small_or_imprecise_dtypes=True), reads=[kPT], writes=[kPT])
        fw.op("dve", lambda e: e.scalar_tensor_tensor(out=IDXF[ps_, :], in0=PT4[ps_, :, q4], scalar=32.0, in1=IOQ[ps_, :], op0=ALU.mult, op1=ALU.add),
              reads=[kPT], writes=[kPT])
    fw.op("dve", lambda e: e.tensor_copy(out=IDX[:], in_=IDXF[:]), reads=[kPT], writes=[kPT])
    NSL = 3
    IDc = [sb(st, nc, "IDc%d" % i, [128, 1], I32) for i in range(NSL)]; kID = [K() for _ in range(NSL)]
    PG = [sb(st, nc, "PG%d" % i, [128, 2048], F32) for i in range(NSL)]; kPG = [K() for _ in range(NSL)]
    PRb = [sb(st, nc, "PRb%d" % i, [128, 2048], BF16) for i in range(2)]; kPRb = [K(), K()]
    W_ = NGp * 32
    ZA = sb(st, nc, "ZA", [128, W_], F32); kZA = K()
    EA = sb(st, nc, "EA", [128, W_], F32); kEA = K()
    SPA = sb(st, nc, "SPA", [128, W_], F32); kSPA = K()
    TT = sb(st, nc, "TT", [128, NGp * 8], F32); kTT = K()
    CP = sb(st, nc, "CP", [128, NGp * 8], F32); kCP = K()
    TG = sb(st, nc, "TG", [128, NGp * 8], F32); kTG = K()
    STRF = sb(st, nc, "STRF", [128, 128], F32); kTF = K()
    fw.op("pool", lambda e: e.affine_select(out=STRF[:], in_=c.onesf[:], pattern=[[-1, 128]], base=0, channel_multiplier=1,
                                            compare_op=ALU.is_gt, fill=0.0), reads=[c.kconst], writes=[kTF])
    OH = sb(st, nc, "OH", [128, NS, NS], BF16); kOH = K()
    fw.op("pool", lambda e: e.memset(OH[:], 0.0), writes=[kOH])
    for b_ in range(NS):
        fw.op("pool", lambda e, b_=b_: e.memset(OH[:, b_, b_:b_ + 1], 1.0), reads=[kOH], writes=[kOH])
    n = 0
    for G in range(NGp):
        b = G // 4
        sl = n % NSL
        n += 1
        if G % 4 == 0:
            fw.dma("sp", QB[b % 2][:], c.Qd[b:b + 1, :].to_broadcast([128, 512]), reads=[c.kQd], writes=[kQB[b % 2]])
        fw.op("dve", lambda e: e.tensor_copy(out=IDc[sl][:], in_=IDX[:, G:G + 1]), reads=[kPT], writes=[kID[sl]])
        fw.gather(PG[sl][:, :], ck4, IDc[sl][:, :], nrows4, reads=[kID[sl]], writes=[kPG[sl]])
        P3 = PG[sl][:].rearrange("p (r x) -> p r x", r=4)
        fw.op("dve", lambda e: e.tensor_tensor(out=P3, in0=P3, in1=QB[b % 2][:].unsqueeze(1).to_broadcast([128, 4, 512]), op=ALU.mult),
              reads=[kPG[sl], kQB[b % 2]], writes=[kPG[sl]])
        fw.op("dve", lambda e: e.tensor_reduce(out=ZA[:, G * 32:(G + 1) * 32], in_=PG[sl][:].rearrange("p (rh d) -> p rh d", d=64), axis=AX.X, op=ALU.add),
              reads=[kPG[sl]], writes=[kZA])
    fw.op("dve", lambda e: e.scalar_tensor_tensor(out=ZA[:].rearrange("p (e h) -> p e h", h=8), in0=ZA[:].rearrange("p (e h) -> p e h", h=8), scalar=0.125,
                                                  in1=c.sbb[:, :].unsqueeze(1).to_broadcast([128, NGp * 4, 8]), op0=ALU.mult, op1=ALU.add),
          reads=[kZA, c.kconst], writes=[kZA])
    fw.op("act", lambda e: e.activation(out=EA[:], in_=ZA[:], func=AF.Exp), reads=[kZA], writes=[kEA])
    fw.op("act", lambda e: e.activation(out=SPA[:], in_=EA[:], func=AF.Ln, bias=c.cst[:, 1:2]), reads=[kEA, c.kconst], writes=[kSPA])
    S4 = SPA[:].rearrange("p (g r h) -> p g r h", r=4, h=8)
    T3 = TT[:].rearrange("p (g h) -> p g h", h=8)
    fw.op("dve", lambda e: e.tensor_tensor(out=T3, in0=S4[:, :, 0, :], in1=S4[:, :, 1, :], op=ALU.add), reads=[kSPA], writes=[kTT])
    fw.op("dve", lambda e: e.tensor_tensor(out=T3, in0=T3, in1=S4[:, :, 2, :], op=ALU.add), reads=[kSPA, kTT], writes=[kTT])
    fw.op("dve", lambda e: e.tensor_tensor(out=T3, in0=T3, in1=S4[:, :, 3, :], op=ALU.add), reads=[kSPA, kTT], writes=[kTT])
    fw.op("pe", lambda e: e.matmul(PS[:, 0, :], STRF[:], TT[:], start=True, stop=True), reads=[kTF, kTT], writes=[kPS[0]])
    fw.op("pe", lambda e: e.matmul(PS[:, 1, :], c.onesf[:], TT[:], start=True, stop=True), reads=[c.kconst, kTT], writes=[kPS[1]])
    fw.op("act", lambda e: e.activation(out=CP[:], in_=PS[:, 0, :], func=AF.Identity), reads=[kPS[0]], writes=[kCP])
    fw.op("act", lambda e: e.activation(out=TG[:], in_=PS[:, 1, :], func=AF.Identity), reads=[kPS[1]], writes=[kTG])
    C4 = CP[:].rearrange("p (b g h) -> p b g h", g=4, h=8)
    G4 = TG[:].rearrange("p (b g h) -> p b g h", g=4, h=8)
    RUN = sb(st, nc, "RUN", [128, NS, 8], F32); kRUN = K()
    fw.op("pool", lambda e: e.memset(RUN[:], 0.0), writes=[kRUN])
    for g_ in range(2, -1, -1):
        fw.op("dve", lambda e: e.tensor_tensor(out=RUN[:], in0=RUN[:], in1=G4[:, :, g_ + 1, :], op=ALU.add), reads=[kRUN, kTG], writes=[kRUN])
        fw.op("dve", lambda e: e.tensor_tensor(out=C4[:, :, g_, :], in0=C4[:, :, g_, :], in1=RUN[:], op=ALU.add), reads=[kRUN, kCP], writes=[kCP])
    C3 = CP[:].rearrange("p (g h) -> p g h", h=8)
    fw.op("dve", lambda e: e.tensor_tensor(out=S4[:, :, 3, :], in0=S4[:, :, 3, :], in1=C3, op=ALU.add), reads=[kSPA, kCP], writes=[kSPA])
    for r_ in (2, 1, 0):
        fw.op("dve", lambda e, r_=r_: e.tensor_tensor(out=S4[:, :, r_, :], in0=S4[:, :, r_, :], in1=S4[:, :, r_ + 1, :], op=ALU.add), reads=[kSPA], writes=[kSPA])
    fw.op("act", lambda e: e.activation(out=SPA[:], in_=SPA[:], func=AF.Exp, scale=-1.0), reads=[kSPA], writes=[kSPA])
    fw.op("dve", lambda e: e.tensor_tensor(out=EA[:], in0=EA[:], in1=SPA[:], op=ALU.mult), reads=[kEA, kSPA], writes=[kEA])
    for G in range(NGp):
        b = G // 4
        sl = n % NSL
        n += 1
        fw.op("dve", lambda e: e.tensor_copy(out=IDc[sl][:], in_=IDX[:, G:G + 1]), reads=[kPT], writes=[kID[sl]])
        fw.gather(PG[sl][:, :], cv4, IDc[sl][:, :], nrows4, reads=[kID[sl]], writes=[kPG[sl]])
        ps = G % 2
        fw.op("dve", lambda e: e.tensor_tensor(out=PRb[ps][:].rearrange("p (rh d) -> p rh d", d=64), in0=PG[sl][:].rearrange("p (rh d) -> p rh d", d=64),
                                               in1=EA[:, G * 32:(G + 1) * 32].unsqueeze(2).to_broadcast([128, 32, 64]), op=ALU.mult),
              reads=[kPG[sl], kEA], writes=[kPRb[ps]])
        for r_ in range(4):
            fw.op("pe", lambda e, r_=r_: e.matmul(PS[0:NS, 4, :], OH[:, b, :], PRb[ps][:, r_ * 512:(r_ + 1) * 512], start=(G == 0 and r_ == 0), stop=(G == NGp - 1 and r_ == 3)),
                  reads=[kPRb[ps], kOH], writes=[kPS[4]])
    OT = sb(st, nc, "OTs", [NS, 512], F32); kOT = K()
    fw.op("act", lambda e: e.activation(out=OT[:], in_=PS[0:NS, 4, :], func=AF.Identity), reads=[kPS[4]], writes=[kOT])
    for c4 in range(4):
        transpose_to(c, PS[:, 5, c4 * NS:(c4 + 1) * NS], OT[0:NS, c4 * 128:(c4 + 1) * 128], [kOT], [kPS[5]], c.IDF[0:NS, 0:NS])
    fw.op("act", lambda e: e.activation(out=OSB[:, :, NP:NT], in_=PS[:, 5, 0:4 * NS].rearrange("p (c b) -> p c b", c=4), func=AF.Identity),
          reads=[kPS[5]], writes=[kOSB[c4][4] for c4 in range(4)])


def build(stage=99, dbg=False, npool=2560):
    nc = bass.Bass("TRN2", target_bir_lowering=False)
    c = Ctx()
    c.nc = nc

    DECL.clear()

    def din(name, shape, dt=F32):
        DECL.append(name)
        return nc.dram_tensor(name, list(shape), dt, kind="ExternalInput").ap()

    def dout(name, shape, dt=F32):
        return nc.dram_tensor(name, list(shape), dt, kind="ExternalOutput").ap()

    xT = din("xT", [D, NT])
    lngT = din("lngT", [128, 6 * KC])
    lnbT = din("lnbT", [128, 6 * KC])
    fw_ = {}
    for nm in ("ffn1_wg", "ffn1_wu", "ffn2_wg", "ffn2_wu"):
        fw_[nm] = din(nm, [2, D, DFF])
    for nm in ("ffn1_wd", "ffn2_wd"):
        fw_[nm] = din(nm, [2, DFF, D])
    yT = dout("yT", [D, NT])
    d = {}
    c.dbg = dbg
    if stage >= 2:
        d["w_in"] = din("w_in", [D, INC])
        d["sbb"] = din("sbb", [128, 8])
        d["pk"] = dout("pk", [NP, 512]); d["pv"] = dout("pv", [NP, 512])
        d["sk"] = dout("sk", [NS, 512]); d["sv"] = dout("sv", [NS, 512])
        if dbg:
            d["dbg_osb"] = dout("dbg_osb", [128, 4, NT], BF16)
    c.npool = npool
    if stage >= 4:
        d["ck"] = din("ck", [npool * 128, 512]); d["cv"] = din("cv", [npool * 128, 512])
        d["pt"] = din("pt", [1, NS * 16], I32)
        c.Qd = nc.dram_tensor("Qd", [NS, 512], F32, kind="Internal").ap(); c.kQd = K()
    if stage >= 5:
        c.TOKd = nc.dram_tensor("TOKd", [TW, 3, 512], BF16, kind="Internal").ap()
        c.Yd = nc.dram_tensor("Yd", [4, 2, TW, 64], BF16, kind="Internal").ap()
        c.kTOKd = K(); c.kYd = K()
        d["w_w2"] = din("w_w2", [64, 512]); d["w_a2"] = din("w_a2", [64, 512]); d["w_g2"] = din("w_g2", [128, 512])
        d["pcol"] = din("pcol", [128, 48]); d["gng"] = din("gng", [128, 512]); d["gnb"] = din("gnb", [128, 512])
        d["sshift0"] = din("sshift0", [NS, RWC]); d["swkv0"] = din("swkv0", [NS, 8, 64, 64])
        d["pwkv"] = dout("pwkv", [8, 64, 64]); d["swkv"] = dout("swkv", [NS, 8, 64, 64])
        d["pshift"] = dout("pshift", [1, RWC]); d["sshift"] = dout("sshift", [NS, RWC])
        d["w_out"] = din("w_out", [D, D])
    if stage >= 8:
        d["s5prm"] = din("s5prm", [128, 3, 32]); d["dskip"] = din("dskip", [128, 8])
        d["s5_0"] = din("s5_0", [NS, 4096, 2]); d["s5bc"] = din("s5bc", [8, 16, 128, 128])
        d["w_glu_out"] = din("w_glu_out", [D, D]); d["w_glu_gate"] = din("w_glu_gate", [D, D])
        d["ps5"] = dout("ps5", [4096, 2]); d["ss5"] = dout("ss5", [NS, 8192])

    with ExitStack() as st:
        fw = FW(nc, st)
        c.fw = fw
        c.XF = sb(st, nc, "XF", [128, KC, NT], F32)
        c.kXF = [[K() for _ in range(NTL)] for _ in range(KC)]
        c.PS = st.enter_context(nc.psum_tensor("PS", [128, 8, 512], F32))
        c.kPS = [K() for _ in range(8)]
        fw.psum_keys = set(id(k) for k in c.kPS)
        c.onesf = sb(st, nc, "onesf", [128, 128], F32)
        c.epsc = sb(st, nc, "epsc", [128, 1], F32)
        c.lng = sb(st, nc, "lng", [128, 6 * KC], F32)
        c.lnb = sb(st, nc, "lnb", [128, 6 * KC], F32)
        c.kconst = K()
        c.nsq = 0
        c.nln = 0
        fw.op("pool", lambda e: e.memset(c.onesf[:], 1.0), writes=[c.kconst])
        c.ones16 = sb(st, nc, "ones16", [128, 128], BF16)
        fw.op("pool", lambda e: e.memset(c.ones16[:], 1.0), writes=[c.kconst])
        fw.op("pool", lambda e: e.memset(c.epsc[:], LN_EPS), writes=[c.kconst])
        fw.dma("sp", c.lng[:], lngT, writes=[c.kconst])
        fw.dma("sp", c.lnb[:], lnbT, writes=[c.kconst])
        xv = xT.rearrange("(kc p) t -> p kc t", p=128)
        for kc in range(KC):
            fw.dma("sp", c.XF[:, kc, :], xv[:, kc, :], writes=c.kXF[kc])

        if not SKIP_FFN:
            ffn_ln(c, fw_["ffn1_wg"][0], fw_["ffn1_wu"][0], fw_["ffn1_wd"][0], 0, "a")
        fw.barrier()
        if stage >= 2:
            c.cst = sb(st, nc, "cst", [128, 4], F32)
            c.sbb = sb(st, nc, "sbb_s", [128, 8], F32)
            fw.op("pool", lambda e: e.memset(c.cst[:, 0:1], LN_EPS), writes=[c.kconst])
            fw.op("pool", lambda e: e.memset(c.cst[:, 1:2], 1.0), writes=[c.kconst])
            fw.op("pool", lambda e: e.memset(c.cst[:, 2:3], GN_EPS), writes=[c.kconst])
            fw.op("pool", lambda e: e.memset(c.cst[:, 3:4], 1.5707963267948966), writes=[c.kconst])
            fw.dma("sp", c.sbb[:], d["sbb"], writes=[c.kconst])
            onesb = sb(st, nc, "onesb", [128, 512], BF16)
            fw.op("pool", lambda e: e.memset(onesb[:], 1.0), writes=[c.kconst])
            c.ONESB = onesb[:, 0:128]
            c.onesb = onesb
            c.IDF = sb(st, nc, "IDF", [128, 128], F32)
            fw.op("pool", lambda e: e.memset(c.IDF[:], 1.0), writes=[c.kconst])
            fw.op("pool", lambda e: e.affine_select(out=c.IDF[:], in_=c.IDF[:], pattern=[[-1, 128]], base=0, channel_multiplier=1,
                                                    compare_op=ALU.is_equal, fill=0.0), reads=[c.kconst], writes=[c.kconst])
            mixer_even(c, d, stage)
        if stage >= 7 and not SKIP_FFN:
            ffn_ln(c, fw_["ffn2_wg"][0], fw_["ffn2_wu"][0], fw_["ffn2_wd"][0], 2, "b")
            fw.barrier()
            ffn_ln(c, fw_["ffn1_wg"][1], fw_["ffn1_wu"][1], fw_["ffn1_wd"][1], 3, "c")
            fw.barrier()
        if stage >= 8:
            s5_mixer(c, d)
        if stage >= 9 and not SKIP_FFN:
            ffn_ln(c, fw_["ffn2_wg"][1], fw_["ffn2_wu"][1], fw_["ffn2_wd"][1], 5, "d")
            fw.barrier()

        yv = yT.rearrange("(kc p) t -> p kc t", p=128)
        for kc in range(KC):
            fw.dma("sp", yv[:, kc, :], c.XF[:, kc, :], reads=c.kXF[kc])
        fw.finish()
        print("instructions:", fw.ninst, {e: fw.cnt[e] for e in fw.cnt})
    return nc


DECL = []


def make_in_maps(inp, n_cores=8):
    f = np.float32
    maps = []
    ln_g = np.ascontiguousarray(np.asarray(inp["ln_g"], f).reshape(6, KC, 128).transpose(2, 0, 1).reshape(128, 6 * KC))
    ln_b = np.ascontiguousarray(np.asarray(inp["ln_b"], f).reshape(6, KC, 128).transpose(2, 0, 1).reshape(128, 6 * KC))
    shared = {"lngT": ln_g, "lnbT": ln_b}
    for nm in ("ffn1_wg", "ffn1_wu", "ffn1_wd", "ffn2_wg", "ffn2_wu", "ffn2_wd"):
        shared[nm] = np.asarray(inp[nm], f)
    shared["w_in"] = np.asarray(inp["w_in_even"][0], f)
    for nm in ("w_w2", "w_a2", "w_g2"):
        shared[nm] = np.asarray(inp[nm][0], f)
    shared["w_out"] = np.asarray(inp["w_out_even"][0], f)
    def colT(v, n):
        return np.asarray(v, f).reshape(n, 128).T
    pc = np.zeros((128, 48), f)
    pc[:, 0:14] = colT(inp["mu_shift"][0], 14)
    pc[:, 14:18] = colT(inp["w0"][0], 4); pc[:, 18:22] = colT(inp["a0"][0], 4)
    pc[:, 22:26] = colT(inp["k_k"][0], 4); pc[:, 26:30] = colT(inp["k_a"][0], 4)
    pc[:, 30:34] = colT(inp["r_k"][0].reshape(-1), 4)
    shared["pcol"] = pc
    shared["gng"] = np.ascontiguousarray(np.broadcast_to(np.asarray(inp["gn_g"][0], f)[None, :], (128, 512)))
    shared["gnb"] = np.ascontiguousarray(np.broadcast_to(np.asarray(inp["gn_b"][0], f)[None, :], (128, 512)))
    lre = np.asarray(inp["lam_re"][0], f); lim = np.asarray(inp["lam_im"][0], f); ldt = np.asarray(inp["log_dt"][0], f)
    prm = np.zeros((128, 3, 32), f)
    prm[:, 0, :] = lre.reshape(32, 128).T; prm[:, 1, :] = lim.reshape(32, 128).T
    prm[:, 2, :] = np.repeat(ldt, 64).reshape(32, 128).T
    shared["s5prm"] = prm
    shared["dskip"] = np.ascontiguousarray(np.asarray(inp["d_skip"][0], f).reshape(8, 128).T)
    bre = np.asarray(inp["b_re"][0], f); bim = np.asarray(inp["b_im"][0], f)
    cre = np.asarray(inp["c_re"][0], f); cim = np.asarray(inp["c_im"][0], f)
    bc = np.zeros((8, 4, 4, 128, 128), f)
    for g in range(64):
        m_, gl = g // 8, g % 8
        k_, g2 = (g % 8) // 2, g % 2
        rows = slice(gl * 16, gl * 16 + 16); cols = slice(g2 * 64, g2 * 64 + 64)
        bc[m_, k_, 0][rows, cols] = bre[g].T
        bc[m_, k_, 1][rows, cols] = bim[g].T
        bc[m_, k_, 2][cols, rows] = cre[g].T
        bc[m_, k_, 3][cols, rows] = cim[g].T
    shared["s5bc"] = bc.reshape(8, 16, 128, 128)
    shared["w_glu_out"] = np.asarray(inp["w_glu_out"][0], f); shared["w_glu_gate"] = np.asarray(inp["w_glu_gate"][0], f)
    shared["sbb"] = np.ascontiguousarray(np.broadcast_to(np.asarray(inp["sb_bias"][0], f)[None, :], (128, 8)))
    for cidx in range(n_cores):
        m = dict(shared)
        xp = np.asarray(inp["x_prompt"][cidx], f)
        xs = np.asarray(inp["x_sample"][cidx * NS:(cidx + 1) * NS, 0], f)
        m["xT"] = np.ascontiguousarray(np.concatenate([xp, xs], axis=0).T)
        sl = slice(cidx * NS, (cidx + 1) * NS)
        m["sshift0"] = np.asarray(inp["state_shift"][0, sl], f)
        m["swkv0"] = np.asarray(inp["state_wkv"][0, sl], f)
        m["pt"] = np.ascontiguousarray(np.asarray(inp["page_table"][sl], np.int32).reshape(1, NS * 16))
        m["ck"] = np.asarray(inp["cache_k_sb"][0], f).reshape(-1, 512)
        m["cv"] = np.asarray(inp["cache_v_sb"][0], f).reshape(-1, 512)
        m["s5_0"] = np.asarray(inp["state_s5"][0, sl], f).reshape(NS, 4096, 2)
        maps.append({k: v for k, v in m.items() if k in DECL})
    return maps


def dev_compare(stage, r, ref, cmp):
    if stage >= 2:
        pp = ref["p_proj"][0]; sp_ = ref["s_proj"][:, 0]
        cmp("pk", r["pk"], pp[:, 512:1024]); cmp("pv", r["pv"], pp[:, 1024:1536])
        cmp("sk", r["sk"], sp_[:, 512:1024]); cmp("sv", r["sv"], sp_[:, 1024:1536])
    if stage >= 4 and "dbg_osb" in r:
        import ml_dtypes
        x_ = r["dbg_osb"]
        if x_.dtype.kind == "V":
            x_ = x_.view(ml_dtypes.bfloat16)
        o = np.asarray(x_).astype(np.float32).transpose(1, 0, 2).reshape(512, NT).T
        cmp("osb_s", o[NP:], ref["s_osb"][:, 0])
    if stage >= 3 and "dbg_osb" in r and False:
        o = np.asarray(r["dbg_osb"]).astype(np.float32).transpose(1, 0, 2).reshape(512, NT).T
        cmp("osb_p", o[:NP], ref["p_osb"][0])
        for qq in range(4):
            cmp("osb_p q%d" % qq, o[qq*512:(qq+1)*512], ref["p_osb"][0][qq*512:(qq+1)*512])
    if stage >= 5:
        cmp("pwkv", r["pwkv"], ref["p_wkv"][0]); cmp("swkv", r["swkv"], ref["s_wkv"])
        cmp("pshift", r["pshift"][0], ref["p_proj"][0, -1, 1536:]); cmp("sshift", r["sshift"], ref["s_proj"][:, 0, 1536:])
    if stage >= 8:
        cmp("ps5", r["ps5"].reshape(64, 64, 2), ref["p_s5"][0]); cmp("ss5", r["ss5"].reshape(NS, 64, 64, 2), ref["s_s5"])
    y = r["yT"].T
    key = {1: "L0_x1", 2: "L0_x1", 3: "L0_x1", 4: "L0_x1", 5: "L0_x1", 6: "L0_x2", 7: "L1_x1", 8: "L1_x2", 9: "L1_x3"}.get(stage, "L1_x3")
    cmp("y_prompt", y[:NP], ref["p_" + key][0])
    cmp("y_sample", y[NP:], ref["s_" + key][:, 0])


def kernel(**inputs):
    n = 8
    npool = int(np.asarray(inputs["cache_k_sb"]).shape[1])
    nc = build(stage=9, dbg=False, npool=npool)
    maps = make_in_maps(inputs, n_cores=n)
    res = run_bass_kernel_spmd(nc, maps, core_ids=list(range(n)))
    R = res.results
    f = np.float32
    yp = np.stack([np.asarray(R[i]["yT"], f).T[:NP] for i in range(n)], axis=0)
    ys = np.concatenate([np.asarray(R[i]["yT"], f).T[NP:] for i in range(n)], axis=0)[:, None, :]
    pk = np.stack([np.asarray(R[i]["pk"], f).reshape(NP, 8, 64) for i in range(n)], axis=0)[None]
    pv = np.stack([np.asarray(R[i]["pv"], f).reshape(NP, 8, 64) for i in range(n)], axis=0)[None]
    pwkv = np.stack([np.asarray(R[i]["pwkv"], f) for i in range(n)], axis=0)[None]
    pshift = np.stack([np.asarray(R[i]["pshift"], f).reshape(RWC) for i in range(n)], axis=0)[None]
    ps5 = np.stack([np.asarray(R[i]["ps5"], f).reshape(64, 64, 2) for i in range(n)], axis=0)[None]
    sk = np.concatenate([np.asarray(R[i]["sk"], f).reshape(NS, 1, 8, 64) for i in range(n)], axis=0)[None]
    sv = np.concatenate([np.asarray(R[i]["sv"], f).reshape(NS, 1, 8, 64) for i in range(n)], axis=0)[None]
    swkv = np.concatenate([np.asarray(R[i]["swkv"], f) for i in range(n)], axis=0)[None]
    sshift = np.concatenate([np.asarray(R[i]["sshift"], f) for i in range(n)], axis=0)[None]
    ss5 = np.concatenate([np.asarray(R[i]["ss5"], f).reshape(NS, 64, 64, 2) for i in range(n)], axis=0)[None]
    return (yp, ys, pk, pv, pwkv, pshift, ps5, sk, sv, swkv, sshift, ss5)
```
